# Optimizing a Trainium2 kernel written in Bass

```python
import jax, jax.numpy as jnp
from jax import lax
import numpy as np

D_MODEL = 1024
BATCH = 8
SEQ = 4096
DEPTH = 2

GRID_W = 64
CTX_LEN = 256

ATTN_HEADS = 8
ATTN_KV_HEADS = 2
Q_PER_KV = ATTN_HEADS // ATTN_KV_HEADS
HEAD_DIM = 64
ATTN_WIDTH = ATTN_HEADS * HEAD_DIM
KV_WIDTH = ATTN_KV_HEADS * HEAD_DIM
Q_BLOCK = 128
ROPE_THETA = 10000.0
ATTN_SCALE = HEAD_DIM ** -0.5

SSD_INNER = D_MODEL // 2
SSD_HEAD_DIM = 64
SSD_HEADS = SSD_INNER // SSD_HEAD_DIM
SSD_GROUPS = 2
SSD_STATE = 128
SSD_CONV = 5
SSD_CHUNK = 128
SSD_CONV_CH = SSD_INNER + 2 * SSD_GROUPS * SSD_STATE

CM_CH = D_MODEL // 2
CM_KERNEL = 31

N_BRANCH = 3
EPS = 1e-6

IN_SPLITS = (ATTN_WIDTH, KV_WIDTH, KV_WIDTH, ATTN_WIDTH,
             SSD_CONV_CH, 2 * SSD_HEADS, SSD_INNER,
             2 * CM_CH, CM_CH,
             N_BRANCH * D_MODEL)
IN_COLS = (2 * ATTN_WIDTH + 2 * KV_WIDTH + SSD_CONV_CH + 2 * SSD_HEADS + SSD_INNER
           + 3 * CM_CH + N_BRANCH * D_MODEL)

kernel_name = "hybrid_gated_attn_ssd_conformer_block"


def split_in(t):
    idx = []
    acc = 0
    for s in IN_SPLITS[:-1]:
        acc += s
        idx.append(acc)
    return jnp.split(t, idx, axis=-1)


def rmsnorm(x, w):
    xf = x.astype(jnp.float32)
    y = xf * lax.rsqrt(jnp.mean(xf * xf, axis=-1, keepdims=True) + EPS)
    return (y * w.astype(jnp.float32)).astype(x.dtype)


def layernorm(x, w, b):
    xf = x.astype(jnp.float32)
    mu = jnp.mean(xf, axis=-1, keepdims=True)
    d = xf - mu
    var = jnp.mean(d * d, axis=-1, keepdims=True)
    return (d * lax.rsqrt(var + EPS) * w.astype(jnp.float32) + b.astype(jnp.float32)).astype(x.dtype)


def depthwise_conv(x, w, b):
    k = w.shape[0]
    y = lax.conv_general_dilated(x, w[:, None, :].astype(x.dtype), window_strides=(1,),
                                 padding=[(k // 2, k // 2)],
                                 dimension_numbers=('NWC', 'WIO', 'NWC'),
                                 feature_group_count=x.shape[-1])
    return y + b


def axial_rope_tables(length):
    rows = length // GRID_W
    row = jnp.repeat(jnp.arange(rows), GRID_W).astype(jnp.float32)
    col = jnp.tile(jnp.arange(GRID_W), rows).astype(jnp.float32)
    n_freq = HEAD_DIM // 4
    inv = 1.0 / (ROPE_THETA ** (jnp.arange(n_freq, dtype=jnp.float32) / n_freq))
    ang_r = row[:, None] * inv
    ang_c = col[:, None] * inv
    return jnp.cos(ang_r), jnp.sin(ang_r), jnp.cos(ang_c), jnp.sin(ang_c)


def _rotate(x, cos, sin):
    x1, x2 = jnp.split(x, 2, axis=-1)
    return jnp.concatenate([x1 * cos - x2 * sin, x2 * cos + x1 * sin], axis=-1)


def apply_axial_rope(x, rope):
    cos_r, sin_r, cos_c, sin_c = rope
    xf = x.astype(jnp.float32)
    xr, xc = jnp.split(xf, 2, axis=-1)
    return jnp.concatenate([_rotate(xr, cos_r, sin_r), _rotate(xc, cos_c, sin_c)], axis=-1).astype(x.dtype)


def heads_q(t):
    b, l, _ = t.shape
    return t.reshape(b, l, ATTN_KV_HEADS, Q_PER_KV, HEAD_DIM).transpose(0, 2, 3, 1, 4)


def heads_kv(t):
    b, l, _ = t.shape
    return t.reshape(b, l, ATTN_KV_HEADS, HEAD_DIM).transpose(0, 2, 1, 3)


def merge_heads(o):
    b, kv, g, l, d = o.shape
    return o.transpose(0, 3, 1, 2, 4).reshape(b, l, kv * g * d)


def _attend(q, k, v):
    s = jnp.einsum('bkgqd,bksd->bkgqs', q, k).astype(jnp.float32) * ATTN_SCALE
    p = jax.nn.softmax(s, axis=-1).astype(v.dtype)
    return jnp.einsum('bkgqs,bksd->bkgqd', p, v)


def latent_attention(q, k, v, k_c, v_c):
    b, kv, g, l, d = q.shape
    nb = l // Q_BLOCK
    k_all = jnp.concatenate([k, k_c], axis=2)
    v_all = jnp.concatenate([v, v_c], axis=2)
    qb = q.reshape(b, kv, g, nb, Q_BLOCK, d).transpose(3, 0, 1, 2, 4, 5)
    ob = lax.map(lambda blk: _attend(blk, k_all, v_all), qb)
    return ob.transpose(1, 2, 3, 0, 4, 5).reshape(b, kv, g, l, d)


def segsum(x):
    t = x.shape[-1]
    xe = jnp.broadcast_to(x[..., :, None], x.shape + (t,))
    xs = jnp.cumsum(jnp.where(jnp.tril(jnp.ones((t, t), bool), -1), xe, 0.0), axis=-2)
    return jnp.where(jnp.tril(jnp.ones((t, t), bool)), xs, -jnp.inf)


def ssd_chunked(x, dt, a_head, bm, cm, init_state, with_output):
    b, l, h, p = x.shape
    n = bm.shape[-1]
    nc = l // SSD_CHUNK
    xd = (x * dt[..., None].astype(x.dtype)).reshape(b, nc, SSD_CHUNK, h, p)
    a = (dt * a_head).reshape(b, nc, SSD_CHUNK, h).transpose(0, 3, 1, 2)
    a_cum = jnp.cumsum(a, axis=-1)
    bc = bm.reshape(b, nc, SSD_CHUNK, h, n)
    cc = cm.reshape(b, nc, SSD_CHUNK, h, n)
    decay_states = jnp.exp(a_cum[..., -1:] - a_cum).astype(x.dtype)
    states = jnp.einsum('bclhn,bhcl,bclhp->bchpn', bc, decay_states, xd)
    states = jnp.concatenate([init_state[:, None].astype(x.dtype), states], axis=1)
    decay_chunk = jnp.exp(segsum(jnp.pad(a_cum[..., -1], ((0, 0), (0, 0), (1, 0))))).astype(x.dtype)
    new_states = jnp.einsum('bhzc,bchpn->bzhpn', decay_chunk, states)
    final_state = new_states[:, -1]
    if not with_output:
        return None, final_state
    states = new_states[:, :-1]
    lmat = jnp.exp(segsum(a)).astype(x.dtype)
    cb = jnp.einsum('bclhn,bcshn->bhcls', cc, bc)
    y_diag = jnp.einsum('bhcls,bcshp->bclhp', cb * lmat, xd)
    y_off = jnp.einsum('bclhn,bchpn,bhcl->bclhp', cc, states, jnp.exp(a_cum).astype(x.dtype))
    return (y_diag + y_off).reshape(b, l, h, p), final_state


def ssd_inputs(xbc, dt_raw, conv_w, conv_b, dt_bias):
    xbc = jax.nn.silu(depthwise_conv(xbc, conv_w, conv_b))
    b, l, _ = xbc.shape
    xs, bm, cm = jnp.split(xbc, [SSD_INNER, SSD_INNER + SSD_GROUPS * SSD_STATE], axis=-1)
    rep = SSD_HEADS // SSD_GROUPS
    xs = xs.reshape(b, l, SSD_HEADS, SSD_HEAD_DIM)
    bm = jnp.repeat(bm.reshape(b, l, SSD_GROUPS, SSD_STATE), rep, axis=2)
    cm = jnp.repeat(cm.reshape(b, l, SSD_GROUPS, SSD_STATE), rep, axis=2)
    dt = jax.nn.softplus(dt_raw.astype(jnp.float32).reshape(b, l, 2, SSD_HEADS)
                         + dt_bias.astype(jnp.float32))
    return xs, bm, cm, dt[:, :, 0], dt[:, :, 1]


def _flip(t):
    return jnp.flip(t, axis=1)


def bidir_ssd(xs, bm, cm, dt_f, dt_b, a, init_f, init_b, d_skip, with_output):
    y_f, s_f = ssd_chunked(xs, dt_f, a[0], bm, cm, init_f, with_output)
    y_b, s_b = ssd_chunked(_flip(xs), _flip(dt_b), a[1], _flip(bm), _flip(cm), init_b, with_output)
    if not with_output:
        return None, s_f, s_b
    y = y_f + _flip(y_b) + d_skip[:, None].astype(xs.dtype) * xs
    return y.reshape(xs.shape[0], xs.shape[1], SSD_INNER), s_f, s_b


def conformer_conv(glu, gate, conv_w, conv_b, ln_w, ln_b):
    u, g = jnp.split(glu, 2, axis=-1)
    v = u * jax.nn.sigmoid(g)
    v = depthwise_conv(v, conv_w, conv_b)
    v = jax.nn.silu(layernorm(v, ln_w, ln_b))
    return v * jax.nn.silu(gate)


def gated_merge(ua, us, uc, gm, w_br_attn, w_br_ssd, w_br_conv, b_gate, w_out):
    g = jax.nn.sigmoid(gm.reshape(gm.shape[:-1] + (N_BRANCH, D_MODEL)) + b_gate)
    y = (g[..., 0, :] * (ua @ w_br_attn) + g[..., 1, :] * (us @ w_br_ssd)
         + g[..., 2, :] * (uc @ w_br_conv))
    return y @ w_out


def hybrid_layer(x, xc, c, c_ctx, w_mod, b_mod, norm_w, w_in, q_norm_w, k_norm_w,
                 ssd_conv_w, ssd_conv_b, ssd_A_log, ssd_dt_bias, ssd_D, ssd_norm_w,
                 cm_conv_w, cm_conv_b, cm_ln_w, cm_ln_b,
                 w_br_attn, w_br_ssd, w_br_conv, b_gate, w_out, rope, last):
    b = x.shape[0]
    mod = jax.nn.silu(c) @ w_mod + b_mod
    mod_c = jax.nn.silu(c_ctx) @ w_mod + b_mod
    shift, scale, gate = jnp.split(mod[:, None, :], 3, axis=-1)
    shift_c, scale_c, gate_c = jnp.split(mod_c, 3, axis=-1)
    h = rmsnorm(x, norm_w) * (1.0 + scale) + shift
    hc = rmsnorm(xc, norm_w) * (1.0 + scale_c) + shift_c

    q, k, v, ga, xbc, dt, z, glu, gcv, gm = split_in(h @ w_in)
    if last:
        w_parts = split_in(w_in)
        k_c, v_c, xbc_c, dt_c = (hc @ w_parts[1], hc @ w_parts[2], hc @ w_parts[4], hc @ w_parts[5])
    else:
        q_c, k_c, v_c, ga_c, xbc_c, dt_c, z_c, glu_c, gcv_c, gm_c = split_in(hc @ w_in)

    qh = apply_axial_rope(rmsnorm(heads_q(q), q_norm_w), rope)
    kh = apply_axial_rope(rmsnorm(heads_kv(k), k_norm_w), rope)
    kh_c = rmsnorm(heads_kv(k_c), k_norm_w)
    vh_c = heads_kv(v_c)
    ua = merge_heads(latent_attention(qh, kh, heads_kv(v), kh_c, vh_c)) * jax.nn.silu(ga)

    a = -jnp.exp(ssd_A_log.astype(jnp.float32))
    xs_c, bm_c, cm_c, dtf_c, dtb_c = ssd_inputs(xbc_c, dt_c, ssd_conv_w, ssd_conv_b, ssd_dt_bias)
    zero = jnp.zeros((b, SSD_HEADS, SSD_HEAD_DIM, SSD_STATE), xs_c.dtype)
    yc, s_f, s_b = bidir_ssd(xs_c, bm_c, cm_c, dtf_c, dtb_c, a, zero, zero, ssd_D, not last)
    xs, bm, cm, dtf, dtb = ssd_inputs(xbc, dt, ssd_conv_w, ssd_conv_b, ssd_dt_bias)
    y, _, _ = bidir_ssd(xs, bm, cm, dtf, dtb, a, s_f, s_b, ssd_D, True)
    us = rmsnorm(y * jax.nn.silu(z), ssd_norm_w)

    uc = conformer_conv(glu, gcv, cm_conv_w, cm_conv_b, cm_ln_w, cm_ln_b)

    out = gated_merge(ua, us, uc, gm, w_br_attn, w_br_ssd, w_br_conv, b_gate, w_out)
    x_new = x + gate * out
    if last:
        return x_new, None

    qh_c = rmsnorm(heads_q(q_c), q_norm_w)
    ua_c = merge_heads(_attend(qh_c, kh_c, vh_c)) * jax.nn.silu(ga_c)
    us_c = rmsnorm(yc * jax.nn.silu(z_c), ssd_norm_w)
    uc_c = conformer_conv(glu_c, gcv_c, cm_conv_w, cm_conv_b, cm_ln_w, cm_ln_b)
    out_c = gated_merge(ua_c, us_c, uc_c, gm_c, w_br_attn, w_br_ssd, w_br_conv, b_gate, w_out)
    return x_new, xc + gate_c * out_c


def setup_inputs(seed: int = 0) -> dict:
    key = jax.random.key(seed)
    ks = jax.random.split(key, 32)
    f32 = jnp.float32

    def nrm(k, shape, s):
        return jax.random.normal(k, shape, f32) * s

    def gain(k, shape):
        return 1.0 + 0.05 * jax.random.normal(k, shape, f32)

    dt0 = jnp.exp(jax.random.uniform(ks[10], (DEPTH, 2, SSD_HEADS), f32,
                                     math_log(0.001), math_log(0.1)))
    return {
        'x': nrm(ks[0], (BATCH, SEQ, D_MODEL), 1.0),
        'c': nrm(ks[1], (BATCH, D_MODEL), 1.0),
        'ctx': nrm(ks[2], (BATCH, CTX_LEN, D_MODEL), 1.0),
        'c_ctx': nrm(ks[3], (D_MODEL,), 1.0),
        'w_mod': nrm(ks[4], (DEPTH, D_MODEL, 3 * D_MODEL), 0.5 * D_MODEL ** -0.5),
        'b_mod': nrm(ks[5], (DEPTH, 3 * D_MODEL), 0.01),
        'norm_w': gain(ks[6], (DEPTH, D_MODEL)),
        'w_in': nrm(ks[7], (DEPTH, D_MODEL, IN_COLS), D_MODEL ** -0.5),
        'q_norm_w': gain(ks[8], (DEPTH, HEAD_DIM)),
        'k_norm_w': gain(ks[9], (DEPTH, HEAD_DIM)),
        'ssd_conv_w': nrm(ks[11], (DEPTH, SSD_CONV, SSD_CONV_CH), SSD_CONV ** -0.5),
        'ssd_conv_b': nrm(ks[12], (DEPTH, SSD_CONV_CH), 0.02),
        'ssd_A_log': jnp.log(jax.random.uniform(ks[13], (DEPTH, 2, SSD_HEADS), f32, 1.0, 16.0)),
        'ssd_dt_bias': dt0 + jnp.log(-jnp.expm1(-dt0)),
        'ssd_D': gain(ks[14], (DEPTH, SSD_HEADS)),
        'ssd_norm_w': gain(ks[15], (DEPTH, SSD_INNER)),
        'cm_conv_w': nrm(ks[16], (DEPTH, CM_KERNEL, CM_CH), CM_KERNEL ** -0.5),
        'cm_conv_b': nrm(ks[17], (DEPTH, CM_CH), 0.02),
        'cm_ln_w': gain(ks[18], (DEPTH, CM_CH)),
        'cm_ln_b': nrm(ks[19], (DEPTH, CM_CH), 0.02),
        'w_br_attn': nrm(ks[20], (DEPTH, ATTN_WIDTH, D_MODEL), ATTN_WIDTH ** -0.5),
        'w_br_ssd': nrm(ks[21], (DEPTH, SSD_INNER, D_MODEL), SSD_INNER ** -0.5),
        'w_br_conv': nrm(ks[22], (DEPTH, CM_CH, D_MODEL), CM_CH ** -0.5),
        'b_gate': nrm(ks[23], (DEPTH, N_BRANCH, D_MODEL), 0.1),
        'w_out': nrm(ks[24], (DEPTH, D_MODEL, D_MODEL), D_MODEL ** -0.5),
        'final_norm_w': gain(ks[25], (D_MODEL,)),
    }


def math_log(v):
    return float(np.log(v))


def reference(x, c, ctx, c_ctx, w_mod, b_mod, norm_w, w_in, q_norm_w, k_norm_w,
              ssd_conv_w, ssd_conv_b, ssd_A_log, ssd_dt_bias, ssd_D, ssd_norm_w,
              cm_conv_w, cm_conv_b, cm_ln_w, cm_ln_b,
              w_br_attn, w_br_ssd, w_br_conv, b_gate, w_out, final_norm_w):
    rope = axial_rope_tables(x.shape[1])
    xc = ctx
    for i in range(DEPTH):
        x, xc = hybrid_layer(x, xc, c, c_ctx, w_mod[i], b_mod[i], norm_w[i], w_in[i],
                             q_norm_w[i], k_norm_w[i], ssd_conv_w[i], ssd_conv_b[i],
                             ssd_A_log[i], ssd_dt_bias[i], ssd_D[i], ssd_norm_w[i],
                             cm_conv_w[i], cm_conv_b[i], cm_ln_w[i], cm_ln_b[i],
                             w_br_attn[i], w_br_ssd[i], w_br_conv[i], b_gate[i], w_out[i],
                             rope, i == DEPTH - 1)
    return rmsnorm(x, final_norm_w)
```

```python
import numpy as np
from contextlib import ExitStack
import concourse.bass as bass
import concourse.mybir as mybir
from concourse.bass_utils import run_bass_kernel_spmd

F32 = mybir.dt.float32
BF16 = mybir.dt.bfloat16
AF = mybir.ActivationFunctionType
ALU = mybir.AluOpType

COMPUTE = ('pe', 'act', 'dve', 'pool')
NDMASEM = 12


class Res:
    __slots__ = ('w', 'rd', 'rdma')

    def __init__(self):
        self.w = None
        self.rd = {}
        self.rdma = []


def RL(n):
    return [Res() for _ in range(n)]


class Op:
    __slots__ = ('eng', 'fn', 'deps', 'dma', 'needs_inc', 'cnt', 'sem', 'target', 'prewait', 'batch')

    def __init__(self, eng, fn, dma):
        self.batch = None
        self.eng = eng
        self.fn = fn
        self.dma = dma
        self.deps = []
        self.needs_inc = False
        self.cnt = 0
        self.sem = None
        self.target = 0
        self.prewait = None


class Prog:
    def __init__(self, nc):
        self.nc = nc
        self.streams = {e: [] for e in ('pe', 'act', 'dve', 'pool', 'sp')}
        self.pending = {}
        self.dmas = []
        self.cur_batch = None
        self.batches = []

    def batch_begin(self):
        self.cur_batch = {}
        self.batches.append(self.cur_batch)

    def batch_end(self):
        self.cur_batch = None

    def barrier(self):
        deps = []
        for e, st in self.streams.items():
            for op in reversed(st):
                if not op.dma:
                    deps.append(op)
                    break
        deps.extend(self.dmas)
        self.dmas = []
        self.pending = {e: list(deps) for e in self.streams}

    def add(self, eng, fn, reads=(), writes=(), dma=False):
        op = Op(eng, fn, dma)
        deps = {}
        if self.pending.get(eng):
            for o in self.pending.pop(eng):
                deps[id(o)] = o
        if dma:
            self.dmas.append(op)
        for r in reads:
            if r.w is not None:
                deps[id(r.w)] = r.w
        for w in writes:
            if w.w is not None:
                deps[id(w.w)] = w.w
            for o in w.rd.values():
                deps[id(o)] = o
            for o in w.rdma:
                deps[id(o)] = o
        for r in reads:
            if dma:
                r.rdma.append(op)
            else:
                r.rd[eng] = op
        for w in writes:
            w.w = op
            w.rd = {}
            w.rdma = []
        for d in deps.values():
            if d is op:
                continue
            if (not d.dma) and (not dma) and d.eng == eng and eng == 'pe':
                continue
            op.deps.append(d)
            if not d.dma:
                d.needs_inc = True
        if self.cur_batch is not None and not dma and eng == 'pe':
            op.batch = self.cur_batch
            self.cur_batch.setdefault(eng, []).append(op)
        self.streams[eng].append(op)
        return op

    def coarsen(self):
        for st in self.streams.values():
            for x in st:
                nd = []
                for d in x.deps:
                    if d.batch is not None and not d.dma and not (x.batch is d.batch and x.eng == d.eng):
                        d = d.batch[d.eng][-1]
                    nd.append(d)
                x.deps = nd
        for b in self.batches:
            for e, ops in b.items():
                union = {}
                for o in ops:
                    for d in o.deps:
                        if d.batch is b and d.eng == e and not d.dma:
                            continue
                        union[id(d)] = d
                    o.deps = []
                ops[0].deps = list(union.values())
        for st in self.streams.values():
            for x in st:
                x.needs_inc = False
        for st in self.streams.values():
            for x in st:
                for d in x.deps:
                    if not d.dma:
                        d.needs_inc = True

    def dma(self, q, out, in_, reads=(), writes=(), slow=False):
        if slow:
            return self.add(q, lambda e: e.dma_start(out=out, in_=in_, allow_slow_non_contiguous=True),
                            reads, writes, dma=True)
        return self.add(q, lambda e: e.dma_start(out=out, in_=in_), reads, writes, dma=True)

    def emit(self):
        nc = self.nc
        self.coarsen()
        with ExitStack() as es:
            csem = {e: es.enter_context(nc.semaphore('cs_' + e)) for e in COMPUTE}
            dsem = {}
            for q in ('sp', 'act', 'pool'):
                dsem[q] = [es.enter_context(nc.semaphore('ds_%s%d' % (q, i))) for i in range(NDMASEM)]
            for e, st in self.streams.items():
                c = 0
                nd = 0
                for op in st:
                    if op.dma:
                        op.sem = dsem[e][nd % NDMASEM]
                        op.target = 16 * (nd // NDMASEM + 1)
                        if nd >= NDMASEM:
                            op.prewait = (op.sem, 16 * (nd // NDMASEM))
                        nd += 1
                    else:
                        if op.needs_inc:
                            c += 1
                        op.cnt = c
            block = es.enter_context(nc.Block())
            streams = self.streams

            def run_stream(ename, eng):
                known = {}
                for op in streams[ename]:
                    need = {}
                    if op.prewait is not None:
                        need[id(op.prewait[0])] = op.prewait
                    for d in op.deps:
                        if d.dma:
                            s, v = d.sem, d.target
                        else:
                            s, v = csem[d.eng], d.cnt
                        k = id(s)
                        if k not in need or need[k][1] < v:
                            need[k] = (s, v)
                    for k, (s, v) in need.items():
                        if known.get(k, 0) >= v:
                            continue
                        eng.wait_ge(s, v)
                        known[k] = v
                    ins = op.fn(eng)
                    if op.dma:
                        ins.then_inc(op.sem, 16)
                    elif op.needs_inc:
                        ins.then_inc(csem[ename], 1)
                if ename == 'sp':
                    for q in ('sp', 'act', 'pool'):
                        last = {}
                        for op in streams[q]:
                            if op.dma:
                                last[id(op.sem)] = (op.sem, op.target)
                        for k, (s, v) in last.items():
                            if known.get(k, 0) < v:
                                eng.wait_ge(s, v)
                                known[k] = v

            @block.tensor
            def _(eng):
                run_stream('pe', eng)

            @block.scalar
            def _(eng):
                run_stream('act', eng)

            @block.vector
            def _(eng):
                run_stream('dve', eng)

            @block.gpsimd
            def _(eng):
                run_stream('pool', eng)

            @block.sync
            def _(eng):
                run_stream('sp', eng)


D = 1024
KC = 8
OQ, OK_, OV, OGA, OXBC, ODT, OZ, OGLU, OGCV, OGM = 0, 512, 640, 768, 1280, 2304, 2320, 2832, 3856, 4368
INCOLS = 7440
EPS = 1e-6
NEG = -30000.0
PP_BMOD, PP_NORMW, PP_QW, PP_KW, PP_SCW, PP_SCB, PP_SNW, PP_CCW, PP_CCB, PP_CLW, PP_CLB, PP_BG = \
    0, 24, 32, 33, 34, 74, 82, 86, 210, 214, 218, 222
NPP = 246
PR_DTB, PR_ALOG, PR_D, PR_SCB = 0, 16, 32, 40
NPR = 40
CF_ID, CF_TRIF, CF_TRIB, CF_STRIF, CF_STRIB, CF_ONES = range(6)
CB_ID, CB_NEGF, CB_NEGB, CB_BLK, CB_RROT, CB_ONES, CB_TRIF, CB_TRIB = range(8)


def host_consts():
    j = np.arange(128)[:, None]
    s = np.arange(128)[None, :]
    cf = np.zeros((128, 6, 128), np.float32)
    cf[:, CF_ID] = (j == s)
    cf[:, CF_TRIF] = (j <= s)
    cf[:, CF_TRIB] = (j >= s)
    cf[:, CF_STRIF] = (j > s)
    cf[:, CF_STRIB] = (j < s)
    cf[:, CF_ONES] = 1.0
    cb = np.zeros((128, 8, 128), np.float32)
    cb[:, CB_ID] = (j == s)
    cb[:, CB_NEGF] = NEG * (s < j)
    cb[:, CB_NEGB] = NEG * (s > j)
    cb[:, CB_BLK] = ((j // 64) == (s // 64))
    rr = np.zeros((128, 128), np.float32)
    for m in range(128):
        if (m % 32) < 16:
            rr[m + 16, m] = -1.0
        else:
            rr[m - 16, m] = 1.0
    cb[:, CB_RROT] = rr
    cb[:, CB_ONES] = 1.0
    cb[:, CB_TRIF] = (j <= s)
    cb[:, CB_TRIB] = (j >= s)
    return cf.reshape(128, 768), cb.reshape(128, 1024)


def host_rope(L, C):
    T = C + L
    t = np.arange(L)
    row = (t // 64).astype(np.float32)
    col = (t % 64).astype(np.float32)
    inv = (1.0 / (10000.0 ** (np.arange(16, dtype=np.float32) / 16))).astype(np.float32)
    cosT = np.ones((128, T), np.float32)
    sinT = np.zeros((128, T), np.float32)
    for p in range(128):
        d = p % 64
        pos = row if d < 32 else col
        ang = (pos * inv[d % 16]).astype(np.float32)
        cosT[p, C:] = np.cos(ang)
        sinT[p, C:] = np.sin(ang)
    return cosT, sinT


def build(L, C, DEPTH, dbg=False, stop_after=None):
    nc = bass.Bass("TRN2", target_bir_lowering=False)
    T = C + L
    NT = T // 128
    NTC = C // 128
    NTL = L // 128
    P = Prog(nc)

    def din(name, shape, dt=F32):
        return nc.dram_tensor(name, shape, dt, kind="ExternalInput").ap()

    def dscr(name, shape, dt=F32):
        return nc.dram_tensor(name, shape, dt, kind=("ExternalOutput" if dbg else "Internal")).ap()

    x_in = din("x", [L, D])
    ctx_in = din("ctx", [C, D])
    cvec_in = din("cvec", [128, 16])
    w_mod = din("w_mod", [DEPTH, D, 3 * D])
    w_in = din("w_in", [DEPTH, D, INCOLS])
    w_bra = din("w_bra", [DEPTH, 512, D])
    w_brs = din("w_brs", [DEPTH, 512, D])
    w_brc = din("w_brc", [DEPTH, 512, D])
    w_out = din("w_out", [DEPTH, D, D])
    pp_in = din("pp", [DEPTH, 128, NPP])
    pr_in = din("pr", [DEPTH, 128, NPR])
    fnw_in = din("fnw", [128, D])
    scb_in = din("scb", [DEPTH, 1, 1024])
    cf_in = din("cf", [128, 768])
    cb_in = din("cb", [128, 1024])
    cos_in = din("cosT", [128, T])
    sin_in = din("sinT", [128, T])
    out = nc.dram_tensor("out", [L, D], F32, kind="ExternalOutput").ap()

    x1 = dscr("x1", [L, D])
    xc1 = dscr("xc1", [C, D])
    uaT = dscr("uaT", [512, T], BF16)
    usT = dscr("usT", [512, T], BF16)
    ucT = dscr("ucT", [512, T], BF16)
    yT_d = dscr("yT", [D, T], BF16)
    xs_d = dscr("xs_tm", [T, 512], BF16)
    b_d = dscr("b_tm", [T, 256], BF16)
    yp_d = dscr("ypart", [T, 512])
    sb_d = dscr("sbst", [NT, 128, 512])

    groups = [(0, 0, C)] + [(1, C + 512 * i, 512) for i in range(L // 512)]

    with ExitStack() as es:
        def sb(name, shape, dt=F32):
            return es.enter_context(nc.sbuf_tensor("s_" + name, shape, dt))

        ARN = 49152
        AR = sb("arena", [128, ARN], BF16)
        aoff = {}

        def ar(phase, shape, dt=BF16):
            n = 1
            for v in shape[1:]:
                n *= v
            nb = n * (2 if dt == BF16 else 4)
            off = (aoff.get(phase, 0) + 3) // 4 * 4
            aoff[phase] = off + nb
            assert aoff[phase] <= ARN * 2, (phase, aoff[phase])
            ap = AR[0:shape[0], off // 2:(off + nb) // 2]
            if dt == F32:
                ap = ap.bitcast(F32)
            if len(shape) == 2:
                return ap
            names = 'abcd'[:len(shape) - 1]
            pat = "p (" + " ".join(names) + ") -> p " + " ".join(names)
            kw = {names[i]: shape[1 + i] for i in range(1, len(names))}
            return ap.rearrange(pat, **kw)

        PP2 = [es.enter_context(nc.psum_tensor("pp%d" % i, [128, 1024], F32)) for i in range(4)]
        PS = []
        for i in range(4):
            PS.append(PP2[i][:, 0:512])
            PS.append(PP2[i][:, 512:1024])
        rPS = RL(8)

        def MM(o, lhsT, rhs, start=True, stop=True, R=(), W=()):
            P.add('pe', lambda e: e.matmul(o, lhsT=lhsT, rhs=rhs, start=start, stop=stop), R, W)

        def TR(o, in_, ident, R=(), W=()):
            P.add('pe', lambda e: e.transpose(out=o, in_=in_, identity=ident), R, W)

        def ACT(o, in_, func, bias=None, scale=None, accum=None, R=(), W=()):
            kw = {}
            if bias is not None:
                kw['bias'] = bias
            if scale is not None:
                kw['scale'] = scale
            if accum is not None:
                kw['accum_out'] = accum
            P.add('act', lambda e: e.activation(out=o, in_=in_, func=func, **kw), R, W)

        def TT(eng, o, a, b, op, R=(), W=()):
            P.add(eng, lambda e: e.tensor_tensor(out=o, in0=a, in1=b, op=op), R, W)

        def TS(eng, o, a, s1, s2, op0, op1=None, R=(), W=()):
            if op1 is None:
                P.add(eng, lambda e: e.tensor_scalar(out=o, in0=a, scalar1=s1, scalar2=None, op0=op0), R, W)
            else:
                P.add(eng, lambda e: e.tensor_scalar(out=o, in0=a, scalar1=s1, scalar2=s2, op0=op0, op1=op1), R, W)

        def STT(eng, o, a, s, b, op0, op1, R=(), W=()):
            P.add(eng, lambda e: e.scalar_tensor_tensor(out=o, in0=a, scalar=s, in1=b, op0=op0, op1=op1), R, W)

        def CP(eng, o, a, R=(), W=()):
            P.add(eng, lambda e: e.tensor_copy(out=o, in_=a), R, W)

        def RCP(o, a, R=(), W=()):
            P.add('dve', lambda e: e.reciprocal(out=o, in_=a), R, W)

        def MSET(eng, o, v, W=()):
            P.add(eng, lambda e: e.memset(o, v), (), W)

        def wsrc(w, l, c0, n):
            return w[l, :, c0:c0 + n].rearrange("(c p) n -> p c n", p=128)

        cf = sb("cf", [128, 6, 128]); r_cf = Res()
        cb = sb("cb", [128, 8, 128], BF16); r_cb = Res()
        pp = sb("pp", [128, DEPTH, NPP]); r_pp = Res()
        pr = sb("pr", [128, DEPTH, NPR]); r_pr = Res()
        cv = sb("cv", [128, 16]); r_cv = Res()
        csl = [[ar('B', [128, 512], F32) for b in range(2)] for a in range(2)]; r_csl = RL(2)
        cslc = [0]
        P.dma('sp', cf[:].rearrange("p a b -> p (a b)"), cf_in[:, :], writes=[r_cf])
        P.dma('pool', cb[:].rearrange("p a b -> p (a b)"), cb_in[:, :], writes=[r_cb])
        P.dma('sp', pp[:], pp_in.rearrange("l p n -> p l n"), writes=[r_pp])
        P.dma('sp', pr[:], pr_in.rearrange("l p n -> p l n"), writes=[r_pr])
        P.dma('sp', cv[:], cvec_in[:, :], writes=[r_cv])
        ones512 = sb("ones512", [1, 512], BF16); r_ones512 = Res()
        MSET('dve', ones512[0:1, :], 1.0, W=[r_ones512])
        identF = cf[:, CF_ID, :]
        identB = cb[:, CB_ID, :]

        hT = sb("hT", [128, KC, T], BF16); r_hT = RL(NT)

        def rh(t0, n):
            return r_hT[t0 // 128:(t0 + n) // 128]

        NW = 4
        wsl = [ar('M', [128, KC, 512], BF16) for i in range(NW)]
        r_wsl = RL(NW)
        wctr = [0]

        def wslot():
            i = wctr[0] % NW
            wctr[0] += 1
            return wsl[i], r_wsl[i]

        NWF = 10
        wf = [sb("wf%d" % i, [128, 512]) for i in range(NWF)]
        r_wf = RL(NWF)
        wfc = [0]

        def wtile():
            i = wfc[0] % NWF
            wfc[0] += 1
            return wf[i], r_wf[i]

        NWB = 8
        wb = [sb("wb%d" % i, [128, 512], BF16) for i in range(NWB)]
        r_wb = RL(NWB)
        wbc = [0]

        def wbtile():
            i = wbc[0] % NWB
            wbc[0] += 1
            return wb[i], r_wb[i]

        psc = [0]

        def psbank(lo=0, hi=8):
            i = lo + psc[0] % (hi - lo)
            psc[0] += 1
            return PS[i], rPS[i]

        modv = sb("modv", [128, 24, 2]); r_modv = Res()
        A1 = sb("A1", [128, 8, 2]); r_A1 = Res()
        scb = sb("scb", [128, 8, 2], BF16); r_scb = Res()

        xtA = [ar('A', [128, D], F32) for i in range(4)]; r_xt = RL(4)
        xnA = [ar('A', [128, D], F32) for i in range(2)]; r_xn = RL(2)
        junk = ar('A', [128, D], F32); r_junk = Res()
        xt = xtA
        xn = xnA
        st4 = [sb("st4_%d" % i, [128, 4]) for i in range(2)]; r_st4 = RL(2)

        def src_tile(l, i):
            if l == 0:
                return (ctx_in if i < NTC else x_in)[((i if i < NTC else i - NTC) * 128):((i if i < NTC else i - NTC) * 128 + 128), :]
            return (xc1 if i < NTC else x1)[((i if i < NTC else i - NTC) * 128):((i if i < NTC else i - NTC) * 128 + 128), :]

        r_x1 = RL(NT)

        def phase_mod(l):
            ACT(scb[:].rearrange("p a b -> p (a b)"), cv[:, :], AF.Silu, R=[r_cv], W=[r_scb])
            pm, rpm = psbank()
            for blk in range(6):
                ws, rws = wslot()
                P.dma('pool', ws[:], wsrc(w_mod, l, blk * 512, 512), writes=[rws])
                for jj in range(4):
                    j = blk * 4 + jj
                    for k in range(KC):
                        MM(pm[:, 2 * j:2 * j + 2], ws[:, k, jj * 128:(jj + 1) * 128], scb[:, k, :],
                           start=(k == 0), stop=(k == KC - 1), R=[rws, r_scb], W=[rpm])
            TT('dve', modv[:], pm[:, 0:48].rearrange("p (j t) -> p j t", t=2),
               pp[:, l, PP_BMOD:PP_BMOD + 24].unsqueeze(2).broadcast_to([128, 24, 2]), ALU.add,
               R=[rpm, r_pp], W=[r_modv])
            STT('dve', A1[:], modv[:, 8:16, :], 1.0,
                pp[:, l, PP_NORMW:PP_NORMW + 8].unsqueeze(2).broadcast_to([128, 8, 2]),
                ALU.add, ALU.mult, R=[r_modv, r_pp], W=[r_A1])

        def phase_hT(l):
            for i in range(NT):
                s = i % 2
                sx = i % 4
                col = 1 if i < NTC else 0
                rsrc = [r_x1[i]] if l > 0 else []
                P.dma('sp', xt[sx][:], src_tile(l, i), reads=rsrc, writes=[r_xt[sx]])
                ACT(junk[:], xt[sx][:], AF.Square, accum=st4[s][:, 0:1], R=[r_xt[sx]], W=[r_junk, r_st4[s]])
                ACT(st4[s][:, 1:2], st4[s][:, 0:1], AF.Sqrt, bias=EPS, scale=1.0 / D, R=[r_st4[s]], W=[r_st4[s]])
                RCP(st4[s][:, 2:3], st4[s][:, 1:2], R=[r_st4[s]], W=[r_st4[s]])
                ACT(xn[s][:], xt[sx][:], AF.Copy, scale=st4[s][:, 2:3], R=[r_xt[sx], r_st4[s]], W=[r_xn[s]])
                for b in range(2):
                    pb, rpb = psbank()
                    for kk in range(4):
                        k = b * 4 + kk
                        TR(pb[:, kk * 128:(kk + 1) * 128], xn[s][:, k * 128:(k + 1) * 128], identF,
                           R=[r_xn[s], r_cf], W=[rpb])
                    tw, rtw = wtile()
                    TT('dve', tw[:].rearrange("p (a b) -> p a b", b=128), pb[:].rearrange("p (a b) -> p a b", b=128),
                       A1[:, 4 * b:4 * b + 4, col:col + 1].broadcast_to([128, 4, 128]), ALU.mult,
                       R=[rpb, r_A1], W=[rtw])
                    TT('pool', hT[:, 4 * b:4 * b + 4, i * 128:(i + 1) * 128], tw[:].rearrange("p (a b) -> p a b", b=128),
                       modv[:, 4 * b:4 * b + 4, col:col + 1].broadcast_to([128, 4, 128]), ALU.add,
                       R=[rtw, r_modv], W=[r_hT[i]])

        kT = ar('B', [128, 2, T]); r_kT = Res()
        Vx = ar('B', [128, NT, 2, 128]); r_Vx = Res()
        wkd = ar('B', [128, KC, 2, 128]); r_wkd = Res()
        wv = ar('B', [128, KC, 128]); r_wv = Res()

        def qk_epilogue(l, pq, rpq, n, t0, wcolidx, o, ro):
            sq, rsq = wbtile()
            ACT(sq[:, 0:n], pq[:, 0:n], AF.Square, R=[rpq], W=[rsq])
            p2, rp2 = psbank()
            MM(p2[:, 0:n], cb[:, CB_BLK, :], sq[:, 0:n], R=[rsq, r_cb], W=[rp2])
            rs, rrs = wtile()
            ACT(rs[:, 0:n], p2[:, 0:n], AF.Sqrt, bias=EPS, scale=1.0 / 64, R=[rp2], W=[rrs])
            RCP(rs[:, 0:n], rs[:, 0:n], R=[rrs], W=[rrs])
            qw, rqw = wtile()
            ACT(qw[:, 0:n], pq[:, 0:n], AF.Copy, scale=pp[:, l, wcolidx:wcolidx + 1], R=[rpq, r_pp], W=[rqw])
            TT('dve', qw[:, 0:n], qw[:, 0:n], rs[:, 0:n], ALU.mult, R=[rqw, rrs], W=[rqw])
            qb, rqb = wbtile()
            CP('pool', qb[:, 0:n], qw[:, 0:n], R=[rqw], W=[rqb])
            p3, rp3 = psbank()
            MM(p3[:, 0:n], cb[:, CB_RROT, :], qb[:, 0:n], R=[rqb, r_cb], W=[rp3])
            ci = cslc[0] % 2
            cslc[0] += 1
            P.dma('sp', csl[ci][0][:, 0:n], cos_in[:, t0:t0 + n], writes=[r_csl[ci]])
            P.dma('sp', csl[ci][1][:, 0:n], sin_in[:, t0:t0 + n], writes=[r_csl[ci]])
            t1, rt1 = wtile()
            TT('pool', t1[:, 0:n], qw[:, 0:n], csl[ci][0][:, 0:n], ALU.mult, R=[rqw, r_csl[ci]], W=[rt1])
            t2, rt2 = wtile()
            TT('dve', t2[:, 0:n], p3[:, 0:n], csl[ci][1][:, 0:n], ALU.mult, R=[rp3, r_csl[ci]], W=[rt2])
            TT('dve', o, t1[:, 0:n], t2[:, 0:n], ALU.add, R=[rt1, rt2], W=[ro])

        def phase_kv(l):
            MSET('pool', Vx[:, :, :, 64:128], 1.0, W=[r_Vx])
            for g in range(2):
                for h in range(2):
                    P.dma('pool', wkd[:, :, g, h * 64:(h + 1) * 64], wsrc(w_in, l, OK_ + g * 64, 64), writes=[r_wkd])
            P.dma('pool', wv[:], wsrc(w_in, l, OV, 128), writes=[r_wv])
            for (sq_, t0, n) in groups:
                for g in range(2):
                    pk, rpk = psbank()
                    for k in range(KC):
                        MM(pk[:, 0:n], wkd[:, k, g, :], hT[:, k, t0:t0 + n], start=(k == 0), stop=(k == KC - 1),
                           R=[r_wkd] + rh(t0, n), W=[rpk])
                    qk_epilogue(l, pk, rpk, n, t0, PP_KW, kT[:, g, t0:t0 + n], r_kT)
            for i0 in range(0, NT, 4):
                nb = min(4, NT - i0)
                pvv, rpv = psbank()
                for ii in range(nb):
                    i = i0 + ii
                    for k in range(KC):
                        MM(pvv[:, ii * 128:(ii + 1) * 128], hT[:, k, i * 128:(i + 1) * 128], wv[:, k, :],
                           start=(k == 0), stop=(k == KC - 1), R=[r_wv, r_hT[i]], W=[rpv])
                CP('dve', Vx[:, i0:i0 + nb, :, 0:64],
                   pvv[:, 0:nb * 128].rearrange("p (a g d) -> p a g d", g=2, d=64), R=[rpv], W=[r_Vx])

        qT = [ar('B', [128, T]) for i in range(2)]; r_qT = RL(2)
        sga = [ar('B', [128, T]) for i in range(2)]; r_sga = RL(2)
        wqg = [ar('B', [128, KC, 2, 128]) for i in range(2)]; r_wqg = RL(2)
        pts = [ar('B', [128, 1024]) for i in range(3)]; r_pts = RL(3)
        r_uaT = RL(4 * len(groups))
        SB_ = None
        OBK = (6, 7)
        attc = [0]

        def attend(l, c, par, q0, nq, key_tiles, gi, hooks=()):
            g = c // 2
            nk = len(key_tiles)
            LOOK = 1
            issued = []
            hooks = list(hooks)
            hstep = max(1, (nk - 2) // max(1, len(hooks))) if hooks else 1

            def issue_s(j):
                a = attc[0]
                attc[0] += 1
                pr_ = a % 2
                pt, rpt = pts[a % 3], r_pts[a % 3]
                for half in range(2):
                    hs = slice(half * 64, half * 64 + 64)
                    MM(PS[2 * pr_ + half][:, 0:nq], kT[hs, g, j * 128:(j + 1) * 128], qT[par][hs, q0:q0 + nq],
                       R=[r_kT, r_qT[par]], W=[rPS[2 * pr_ + half]])
                issued.append((pr_, pt, rpt))

            def issue_exp(n_):
                pr_, pt, rpt = issued[n_]
                if nq == 512:
                    ACT(pt[:, 0:1024], PP2[pr_][:, 0:1024], AF.Exp, scale=0.125, R=[rPS[2 * pr_], rPS[2 * pr_ + 1]], W=[rpt])
                else:
                    for half in range(2):
                        ACT(pt[:, half * 512:half * 512 + nq], PS[2 * pr_ + half][:, 0:nq], AF.Exp, scale=0.125,
                            R=[rPS[2 * pr_ + half]], W=[rpt])

            def finish():
                ua, rua = wtile()
                uab, ruab = wbtile()
                osb = []
                for half in range(2):
                    o_, ro_ = wtile()
                    CP('dve', o_[:, 0:nq], PS[OBK[half]][:, 0:nq], R=[rPS[OBK[half]]], W=[ro_])
                    osb.append((o_, ro_))
                ob, rob = osb[0]
                RCP(ob[64:128, 0:nq], ob[64:128, 0:nq], R=[rob], W=[rob])
                r2, rr2 = wtile()
                P.dma('sp', r2[0:64, 0:nq], ob[64:128, 0:nq], reads=[rob], writes=[rr2])
                TT('dve', ua[0:64, 0:nq], ob[0:64, 0:nq], r2[0:64, 0:nq], ALU.mult, R=[rob, rr2], W=[rua])
                ob1, rob1 = osb[1]
                o2, ro2 = wtile()
                P.dma('sp', o2[64:128, 0:nq], ob1[0:64, 0:nq], reads=[rob1], writes=[ro2])
                RCP(ob1[64:128, 0:nq], ob1[64:128, 0:nq], R=[rob1], W=[rob1])
                TT('dve', ua[64:128, 0:nq], o2[64:128, 0:nq], ob1[64:128, 0:nq], ALU.mult, R=[ro2, rob1], W=[rua])
                TT('pool', uab[:, 0:nq], ua[:, 0:nq], sga[par][:, q0:q0 + nq], ALU.mult,
                   R=[rua, r_sga[par]], W=[ruab])
                P.dma('sp', uaT[c * 128:(c + 1) * 128, q0:q0 + nq], uab[:, 0:nq], reads=[ruab],
                      writes=[r_uaT[c * len(groups) + gi]])

            for n_ in range(min(LOOK, nk)):
                P.batch_begin()
                issue_s(key_tiles[n_])
                P.batch_end()
                issue_exp(n_)
            for n_, j in enumerate(key_tiles):
                if n_ + LOOK < nk:
                    P.batch_begin()
                    issue_s(key_tiles[n_ + LOOK])
                    P.batch_end()
                P.batch_begin()
                pr_, pt, rpt = issued[n_]
                for half in range(2):
                    MM(PS[OBK[half]][:, 0:nq], Vx[:, j, g, :], pt[:, half * 512:half * 512 + nq], start=(n_ == 0),
                       stop=(n_ == nk - 1), R=[r_Vx, rpt], W=[rPS[OBK[half]]])
                P.batch_end()
                if n_ + LOOK < nk:
                    issue_exp(n_ + LOOK)
                if hooks and n_ >= 1 and (n_ - 1) % hstep == 0:
                    hooks.pop(0)()
            finish()
            while hooks:
                hooks.pop(0)()

        def qga_stages(l, c, gi, lo, hi):
            par = c % 2
            (sq_, t0, n) = groups[gi]
            st = {}

            def bank(k):
                if lo is None:
                    return psbank()
                return PS[lo + k % (hi - lo)], rPS[lo + k % (hi - lo)]

            def s1():
                st['pq'] = bank(0)
                pq, rpq = st['pq']
                for k in range(KC):
                    MM(pq[:, 0:n], wqg[par][:, k, 0, :], hT[:, k, t0:t0 + n], start=(k == 0), stop=(k == KC - 1),
                       R=[r_wqg[par]] + rh(t0, n), W=[rpq])
                ci = cslc[0] % 2
                cslc[0] += 1
                st['ci'] = ci
                P.dma('sp', csl[ci][0][:, 0:n], cos_in[:, t0:t0 + n], writes=[r_csl[ci]])
                P.dma('sp', csl[ci][1][:, 0:n], sin_in[:, t0:t0 + n], writes=[r_csl[ci]])

            def s2():
                pq, rpq = st['pq']
                st['sq'] = wbtile()
                sq, rsq = st['sq']
                ACT(sq[:, 0:n], pq[:, 0:n], AF.Square, R=[rpq], W=[rsq])
                st['qw'] = wtile()
                qw, rqw = st['qw']
                ACT(qw[:, 0:n], pq[:, 0:n], AF.Copy, scale=pp[:, l, PP_QW:PP_QW + 1], R=[rpq, r_pp], W=[rqw])

            def s3():
                st['p2'] = bank(1)
                p2, rp2 = st['p2']
                sq, rsq = st['sq']
                MM(p2[:, 0:n], cb[:, CB_BLK, :], sq[:, 0:n], R=[rsq, r_cb], W=[rp2])

            def s4():
                p2, rp2 = st['p2']
                st['rs'] = wtile()
                rs, rrs = st['rs']
                ACT(rs[:, 0:n], p2[:, 0:n], AF.Ln, bias=EPS, scale=1.0 / 64, R=[rp2], W=[rrs])
                ACT(rs[:, 0:n], rs[:, 0:n], AF.Exp, scale=-0.5, R=[rrs], W=[rrs])

            def s5():
                rs, rrs = st['rs']
                qw, rqw = st['qw']
                ci = st['ci']
                TT('dve', qw[:, 0:n], qw[:, 0:n], rs[:, 0:n], ALU.mult, R=[rqw, rrs], W=[rqw])
                st['qb'] = wbtile()
                qb, rqb = st['qb']
                CP('pool', qb[:, 0:n], qw[:, 0:n], R=[rqw], W=[rqb])
                st['t1'] = wtile()
                t1, rt1 = st['t1']
                TT('pool', t1[:, 0:n], qw[:, 0:n], csl[ci][0][:, 0:n], ALU.mult, R=[rqw, r_csl[ci]], W=[rt1])

            def s6():
                st['p3'] = bank(2)
                p3, rp3 = st['p3']
                qb, rqb = st['qb']
                MM(p3[:, 0:n], cb[:, CB_RROT, :], qb[:, 0:n], R=[rqb, r_cb], W=[rp3])
                st['pg'] = bank(3)
                pg, rpg = st['pg']
                for k in range(KC):
                    MM(pg[:, 0:n], wqg[par][:, k, 1, :], hT[:, k, t0:t0 + n], start=(k == 0), stop=(k == KC - 1),
                       R=[r_wqg[par]] + rh(t0, n), W=[rpg])

            def s7():
                p3, rp3 = st['p3']
                pg, rpg = st['pg']
                t1, rt1 = st['t1']
                ci = st['ci']
                t2, rt2 = wtile()
                TT('dve', t2[:, 0:n], p3[:, 0:n], csl[ci][1][:, 0:n], ALU.mult, R=[rp3, r_csl[ci]], W=[rt2])
                TT('dve', qT[par][:, t0:t0 + n], t1[:, 0:n], t2[:, 0:n], ALU.add, R=[rt1, rt2], W=[r_qT[par]])
                e_, re_ = wtile()
                ACT(e_[:, 0:n], pg[:, 0:n], AF.Exp, scale=-1.0, R=[rpg], W=[re_])
                g_, rg_ = wtile()
                ACT(g_[:, 0:n], pg[:, 0:n], AF.Copy, R=[rpg], W=[rg_])
                TS('dve', e_[:, 0:n], e_[:, 0:n], 1.0, None, ALU.add, R=[re_], W=[re_])
                RCP(e_[:, 0:n], e_[:, 0:n], R=[re_], W=[re_])
                TT('pool', sga[par][:, t0:t0 + n], g_[:, 0:n], e_[:, 0:n], ALU.mult, R=[rg_, re_], W=[r_sga[par]])

            return [s1, s2, s3, s4, s5, s6, s7]

        def phase_attn(l, last):
            glist = [gi for gi, (sq_, t0, n) in enumerate(groups) if not (last and sq_ == 0)]

            def load_w(c):
                par = c % 2
                P.dma('pool', wqg[par][:, :, 0, :], wsrc(w_in, l, OQ + c * 128, 128), writes=[r_wqg[par]])
                P.dma('pool', wqg[par][:, :, 1, :], wsrc(w_in, l, OGA + c * 128, 128), writes=[r_wqg[par]])

            load_w(0)
            for gi in glist:
                for f_ in qga_stages(l, 0, gi, None, None):
                    f_()
            for c in range(4):
                par = c % 2
                if c + 1 < 4:
                    load_w(c + 1)
                order = [gi for gi in glist if groups[gi][0] == 1] + [gi for gi in glist if groups[gi][0] == 0]
                nxt = list(glist) if c + 1 < 4 else []
                for gi in order:
                    (sq_, t0, n) = groups[gi]
                    hk = qga_stages(l, c + 1, nxt.pop(0), 4, 6) if nxt else []
                    if sq_ == 0:
                        attend(l, c, par, t0, n, list(range(NTC)), gi, hk)
                    else:
                        attend(l, c, par, t0, n, list(range(NT)), gi, hk)

        def qk_epilogue_b(l, pq, rpq, n, t0, wcolidx, o, ro):
            old = psc[0]
            qk_epilogue_r(l, pq, rpq, n, t0, wcolidx, o, ro, 0, 8)

        def qk_epilogue_r(l, pq, rpq, n, t0, wcolidx, o, ro, lo, hi):
            sq, rsq = wbtile()
            ACT(sq[:, 0:n], pq[:, 0:n], AF.Square, R=[rpq], W=[rsq])
            p2, rp2 = psbank(lo, hi)
            MM(p2[:, 0:n], cb[:, CB_BLK, :], sq[:, 0:n], R=[rsq, r_cb], W=[rp2])
            rs, rrs = wtile()
            ACT(rs[:, 0:n], p2[:, 0:n], AF.Sqrt, bias=EPS, scale=1.0 / 64, R=[rp2], W=[rrs])
            RCP(rs[:, 0:n], rs[:, 0:n], R=[rrs], W=[rrs])
            qw, rqw = wtile()
            ACT(qw[:, 0:n], pq[:, 0:n], AF.Copy, scale=pp[:, l, wcolidx:wcolidx + 1], R=[rpq, r_pp], W=[rqw])
            TT('dve', qw[:, 0:n], qw[:, 0:n], rs[:, 0:n], ALU.mult, R=[rqw, rrs], W=[rqw])
            qb, rqb = wbtile()
            CP('pool', qb[:, 0:n], qw[:, 0:n], R=[rqw], W=[rqb])
            p3, rp3 = psbank(lo, hi)
            MM(p3[:, 0:n], cb[:, CB_RROT, :], qb[:, 0:n], R=[rqb, r_cb], W=[rp3])
            ci = cslc[0] % 2
            cslc[0] += 1
            P.dma('sp', csl[ci][0][:, 0:n], cos_in[:, t0:t0 + n], writes=[r_csl[ci]])
            P.dma('sp', csl[ci][1][:, 0:n], sin_in[:, t0:t0 + n], writes=[r_csl[ci]])
            t1, rt1 = wtile()
            TT('pool', t1[:, 0:n], qw[:, 0:n], csl[ci][0][:, 0:n], ALU.mult, R=[rqw, r_csl[ci]], W=[rt1])
            t2, rt2 = wtile()
            TT('dve', t2[:, 0:n], p3[:, 0:n], csl[ci][1][:, 0:n], ALU.mult, R=[rp3, r_csl[ci]], W=[rt2])
            TT('dve', o, t1[:, 0:n], t2[:, 0:n], ALU.add, R=[rt1, rt2], W=[ro])


        BT = ar('C', [128, 2, T]); r_BT = Res()
        CT = ar('C', [128, 2, T]); r_CT = Res()
        pre0 = ar('C', [128, T + 8]); r_pre0 = Res()
        pre = [pre0, pre0]; r_pre = [r_pre0, r_pre0]
        wx = [ar('C', [128, KC, 128]) for i in range(2)]; r_wx = RL(2)
        wdt = ar('C', [128, KC, 16]); r_wdt = Res()
        dgwz = ar('C', [128, 5120])
        dgs = dgwz.rearrange("p (a b c) -> p a b c", a=8, b=5); r_dgs = Res()
        wz = dgwz[:, 0:4096].rearrange("p (a b) -> p a b", a=KC); r_wz = Res()
        dt_all = ar('C', [128, NT, 16], F32); r_dt = Res()
        a_all = ar('C', [128, NT, 16], F32); r_a = Res()
        ahl = [ar('C', [128, NT, 16]) for i in range(4)]
        Eb_all = ar('C', [128, NT, 16], F32); r_Eb = Res()
        scbrow = ar('C', [1, 1024]); r_scbrow = Res()
        Aexp = sb("Aexp", [128, 16]); r_Aexp = Res()
        Hf = ar('C', [128, 512], F32); Hb = ar('C', [128, 512], F32); r_Hf = Res(); r_Hb = Res()
        Hfb = ar('C', [128, 512]); Hbb = ar('C', [128, 512]); r_Hfb = Res(); r_Hbb = Res()
        Et = [sb("Et%d" % i, [128, 80]) for i in range(2)]; r_Et = RL(2)
        ncum = [sb("ncum%d" % i, [128, 32]) for i in range(2)]; r_ncum = RL(2)
        dtd = [sb("dtd%d" % i, [128, 16]) for i in range(2)]; r_dtd = RL(2)
        CBm = [sb("CBm%d" % i, [128, 2, 2, 128]) for i in range(2)]; r_CBm = RL(2)
        xs_t = [ar('C', [128, 512]) for i in range(3)]; r_xs_t = RL(3)
        b_t = [ar('C', [128, 256]) for i in range(3)]; r_b_t = RL(3)
        p1t = [[ar('C', [128, 512]) for b in range(5)] for a in range(2)]
        r_p1t = [RL(5) for a in range(2)]
        r_xs_d = RL(len(groups) * 4); r_b_d = RL(len(groups) * 2)
        r_yp_d = RL(NT); r_sb_d = RL(NT); r_usT = RL(NT)
        def pcol(t):
            return t + 2 if t < C else t + 6

        import os as _os
        C1S = _os.environ.get("C1STEPS", "12345")

        def phase_ssd_c1(l):
            MSET('pool', pre0[:], 0.0, W=[r_pre0])
            for cc in range(8 if '1' in C1S else 0):
                for k in range(5):
                    TS('dve', dgs[:, cc, k, :], identF, pp[:, l, PP_SCW + cc * 5 + k:PP_SCW + cc * 5 + k + 1], None,
                       ALU.mult, R=[r_cf, r_pp], W=[r_dgs])
            if '1' in C1S:
                P.dma('pool', scbrow[0:1, :], scb_in[l], writes=[r_scbrow])
            P.dma('pool', wdt[:], wsrc(w_in, l, ODT, 16), writes=[r_wdt])
            for i0 in range(0, NT if '2' in C1S else 0, 32):
                nb = min(32, NT - i0)
                pd, rpd = psbank()
                for ii in range(nb):
                    i = i0 + ii
                    for k in range(KC):
                        MM(pd[:, ii * 16:(ii + 1) * 16], hT[:, k, i * 128:(i + 1) * 128], wdt[:, k, :],
                           start=(k == 0), stop=(k == KC - 1), R=[r_wdt, r_hT[i]], W=[rpd])
                xb, rxb = wtile()
                ax, rax = wtile()
                n16 = nb * 16
                TT('dve', xb[:, 0:n16].rearrange("p (a b) -> p a b", b=16), pd[:, 0:n16].rearrange("p (a b) -> p a b", b=16),
                   pr[:, l, PR_DTB:PR_DTB + 16].unsqueeze(1).broadcast_to([128, nb, 16]), ALU.add, R=[rpd, r_pr], W=[rxb])
                ACT(ax[:, 0:n16], xb[:, 0:n16], AF.Abs, R=[rxb], W=[rax])
                ACT(ax[:, 0:n16], ax[:, 0:n16], AF.Exp, scale=-1.0, R=[rax], W=[rax])
                ACT(ax[:, 0:n16], ax[:, 0:n16], AF.Ln, bias=1.0, R=[rax], W=[rax])
                TS('dve', xb[:, 0:n16], xb[:, 0:n16], 0.0, None, ALU.max, R=[rxb], W=[rxb])
                TT('dve', dt_all[:, i0:i0 + nb, :], xb[:, 0:n16].rearrange("p (a b) -> p a b", b=16),
                   ax[:, 0:n16].rearrange("p (a b) -> p a b", b=16), ALU.add, R=[rxb, rax], W=[r_dt])
            if '2' in C1S:
                ACT(Aexp[:], pr[:, l, PR_ALOG:PR_ALOG + 16], AF.Exp, R=[r_pr], W=[r_Aexp])
                STT('dve', a_all[:], dt_all[:], -1.0, Aexp[:].unsqueeze(1).broadcast_to([128, NT, 16]),
                    ALU.mult, ALU.mult, R=[r_dt, r_Aexp], W=[r_a])
                CP('dve', ahl[0][:], a_all[:], R=[r_a], W=[r_a])
                TT('dve', ahl[1][:], a_all[:], ahl[0][:], ALU.subtract, R=[r_a], W=[r_a])
                TS('dve', ahl[2][:], ahl[0][:], -1.0, None, ALU.mult, R=[r_a], W=[r_a])
                TS('dve', ahl[3][:], ahl[1][:], -1.0, None, ALU.mult, R=[r_a], W=[r_a])
            for cc in range(8 if '3' in C1S else 0):
                par = cc % 2
                P.dma('pool', wx[par][:], wsrc(w_in, l, OXBC + cc * 128, 128), writes=[r_wx[par]])
                for (sq_, t0, n) in groups:
                    pb, rpb = psbank()
                    for k in range(KC):
                        MM(pb[:, 0:n], wx[par][:, k, :], hT[:, k, t0:t0 + n], start=(k == 0), stop=(k == KC - 1),
                           R=[r_wx[par]] + rh(t0, n), W=[rpb])
                    ACT(pre[par][:, pcol(t0):pcol(t0) + n], pb[:, 0:n], AF.Copy, R=[rpb], W=[r_pre[par]])
                if cc < 6 and '4' in C1S:
                    for gi, (sq_, t0, n) in enumerate(groups):
                        pb, rpb = psbank()
                        nt_ = n // 128
                        for ii in range(nt_):
                            tt0 = t0 + ii * 128
                            for k in range(5):
                                MM(pb[:, ii * 128:(ii + 1) * 128], pre[par][:, pcol(tt0) + k - 2:pcol(tt0) + k - 2 + 128],
                                   dgs[:, cc, k, :], start=(k == 0), stop=False, R=[r_pre[par], r_dgs], W=[rpb])
                            MM(pb[:, ii * 128:(ii + 1) * 128], cb[0:1, CB_ONES, :], scbrow[0:1, cc * 128:(cc + 1) * 128],
                               start=False, stop=True, R=[r_cb, r_scbrow], W=[rpb])
                        stg, rstg = wbtile()
                        ACT(stg[:, 0:n], pb[:, 0:n], AF.Silu, R=[rpb], W=[rstg])
                        if cc < 4:
                            dst = xs_d[t0:t0 + n, cc * 128:(cc + 1) * 128].rearrange("(a p) c -> p a c", p=128)
                            rdst = r_xs_d[gi * 4 + cc]
                        else:
                            dst = b_d[t0:t0 + n, (cc - 4) * 128:(cc - 3) * 128].rearrange("(a p) c -> p a c", p=128)
                            rdst = r_b_d[gi * 2 + cc - 4]
                        P.dma('sp', dst, stg[:, 0:n].rearrange("p (a c) -> p a c", c=128), reads=[rstg], writes=[rdst])
                if cc >= 4 and '5' in C1S:
                    for (sq_, t0, n) in groups:
                        pb, rpb = psbank()
                        for k in range(5):
                            MM(pb[:, 0:n], dgs[:, cc, k, :], pre[par][:, pcol(t0) + k - 2:pcol(t0) + k - 2 + n],
                               start=(k == 0), stop=(k == 4), R=[r_pre[par], r_dgs], W=[rpb])
                        tb, rtb = wtile()
                        ACT(tb[:, 0:n], pb[:, 0:n], AF.Identity, bias=pp[:, l, PP_SCB + cc:PP_SCB + cc + 1],
                            R=[rpb, r_pp], W=[rtb])
                        if cc < 6:
                            ACT(BT[:, cc - 4, t0:t0 + n], tb[:, 0:n], AF.Silu, R=[rtb], W=[r_BT])
                        else:
                            ACT(CT[:, cc - 6, t0:t0 + n], tb[:, 0:n], AF.Silu, R=[rtb], W=[r_CT])

        ssdc = [0]

        def bc8(ap):
            return ap.unsqueeze(2).broadcast_to([128, 8, 64])

        def h3(ap):
            return ap.rearrange("p (h d) -> p h d", d=64)

        p1state = {}

        def ssd_pass1_a(l, i, with_out):
            e = ssdc[0]
            ssdc[0] += 1
            p2 = e % 2
            p3 = e % 3
            p1state[i] = (p2, p3)
            gi = [k for k, (sq_, t0, n) in enumerate(groups) if t0 <= i * 128 < t0 + n][0]
            P.dma('sp', xs_t[p3][:], xs_d[i * 128:(i + 1) * 128, :], reads=r_xs_d[gi * 4:gi * 4 + 4], writes=[r_xs_t[p3]])
            P.dma('sp', b_t[p3][:], b_d[i * 128:(i + 1) * 128, :], reads=r_b_d[gi * 2:gi * 2 + 2], writes=[r_b_t[p3]])
            a_i = a_all[:, i, :]
            pc, rpc = psbank()
            for bi, blk in enumerate((CF_TRIF, CF_TRIB, CF_STRIF, CF_STRIB, CF_ONES)):
                MM(pc[:, bi * 16:(bi + 1) * 16], cf[:, blk, :], a_i, R=[r_cf, r_a], W=[rpc])
            ACT(Et[p2][:], pc[:, 0:80], AF.Exp, R=[rpc], W=[r_Et[p2]])
            CP('pool', Eb_all[:, i, 0:8], Et[p2][:, 24:32], R=[r_Et[p2]], W=[r_Eb])
            CP('pool', Eb_all[:, i, 8:16], Et[p2][:, 72:80], R=[r_Et[p2]], W=[r_Eb])
            TT('pool', dtd[p2][:, 0:8], dt_all[:, i, 0:8], Et[p2][:, 32:40], ALU.mult, R=[r_dt, r_Et[p2]], W=[r_dtd[p2]])
            TT('pool', dtd[p2][:, 8:16], dt_all[:, i, 8:16], Et[p2][:, 56:64], ALU.mult, R=[r_dt, r_Et[p2]], W=[r_dtd[p2]])
            xs3 = h3(xs_t[p3][:])
            xddf, rxddf = p1t[p2][0], r_p1t[p2][0]
            xddb, rxddb = p1t[p2][1], r_p1t[p2][1]
            TT('pool', h3(xddf[:]), xs3, bc8(dtd[p2][:, 0:8]), ALU.mult, R=[r_xs_t[p3], r_dtd[p2]], W=[rxddf])
            TT('dve', h3(xddb[:]), xs3, bc8(dtd[p2][:, 8:16]), ALU.mult, R=[r_xs_t[p3], r_dtd[p2]], W=[rxddb])
            if with_out:
                xdf, rxdf = p1t[p2][2], r_p1t[p2][2]
                xdb, rxdb = p1t[p2][3], r_p1t[p2][3]
                xsD, rxsD = p1t[p2][4], r_p1t[p2][4]
                TT('dve', h3(xdf[:]), xs3, bc8(dt_all[:, i, 0:8]), ALU.mult, R=[r_xs_t[p3], r_dt], W=[rxdf])
                TT('dve', h3(xdb[:]), xs3, bc8(dt_all[:, i, 8:16]), ALU.mult, R=[r_xs_t[p3], r_dt], W=[rxdb])
                TT('pool', h3(xsD[:]), xs3, bc8(pr[:, l, PR_D:PR_D + 8]), ALU.mult, R=[r_xs_t[p3], r_pr], W=[rxsD])
                pcb, rpcb = psbank()
                for g in range(2):
                    MM(pcb[:, g * 128:(g + 1) * 128], BT[:, g, i * 128:(i + 1) * 128], CT[:, g, i * 128:(i + 1) * 128],
                       R=[r_BT, r_CT], W=[rpcb])
                for d_, blk in enumerate((CF_TRIF, CF_TRIB)):
                    TT('dve', CBm[p2][:, d_, :, :], pcb[:, 0:256].rearrange("p (g s) -> p g s", s=128),
                       cf[:, blk, :].unsqueeze(1).broadcast_to([128, 2, 128]), ALU.mult, R=[rpcb, r_cf], W=[r_CBm[p2]])

        def ssd_pass1_b(l, i, with_out):
            p2, p3 = p1state.pop(i)
            xddf, rxddf = p1t[p2][0], r_p1t[p2][0]
            xddb, rxddb = p1t[p2][1], r_p1t[p2][1]
            xdf, rxdf = p1t[p2][2], r_p1t[p2][2]
            xdb, rxdb = p1t[p2][3], r_p1t[p2][3]
            xsD, rxsD = p1t[p2][4], r_p1t[p2][4]
            if with_out:
                pyo, rpyo = psbank()
                P.batch_begin()
                for g in range(2):
                    MM(pyo[:, g * 256:(g + 1) * 256], CT[:, g, i * 128:(i + 1) * 128], Hfb[:, g * 256:(g + 1) * 256],
                       R=[r_CT, r_Hfb], W=[rpyo])
                P.batch_end()
            psf, rpsf = psbank()
            psb_, rpsb = psbank()
            P.batch_begin()
            for g in range(2):
                MM(psf[:, g * 256:(g + 1) * 256], b_t[p3][:, g * 128:(g + 1) * 128], xddf[:, g * 256:(g + 1) * 256],
                   R=[r_b_t[p3], rxddf], W=[rpsf])
            for g in range(2):
                MM(psb_[:, g * 256:(g + 1) * 256], b_t[p3][:, g * 128:(g + 1) * 128], xddb[:, g * 256:(g + 1) * 256],
                   R=[r_b_t[p3], rxddb], W=[rpsb])
            P.batch_end()
            TT('dve', h3(Hf[:]), h3(Hf[:]), bc8(Et[p2][:, 64:72]), ALU.mult, R=[r_Hf, r_Et[p2]], W=[r_Hf])
            TT('dve', Hf[:], Hf[:], psf[:], ALU.add, R=[r_Hf, rpsf], W=[r_Hf])
            ACT(Hfb[:], Hf[:], AF.Copy, R=[r_Hf], W=[r_Hfb])
            sbt, rsbt = wtile()
            ACT(sbt[:], psb_[:], AF.Copy, R=[rpsb], W=[rsbt])
            P.dma('sp', sb_d[i], sbt[:], reads=[rsbt], writes=[r_sb_d[i]])
            if not with_out:
                return
            tmp, rtmp = wtile()
            TT('dve', h3(tmp[:]), h3(pyo[:]), bc8(Et[p2][:, 0:8]), ALU.mult, R=[rpyo, r_Et[p2]], W=[rtmp])
            py, rpy = psbank()
            MM(py[:, :], identB, xsD[:], start=True, stop=False, R=[r_cb, rxsD], W=[rpy])
            steps = [(d_, hq) for d_ in range(2) for hq in range(2)]
            stash = {}

            def x1_step(d_, hq):
                tri = CF_TRIF if d_ == 0 else CF_TRIB
                neg = CB_NEGF if d_ == 0 else CB_NEGB
                px, rpx = psbank()
                P.batch_begin()
                for hh in range(4):
                    h = hq * 4 + hh
                    col = d_ * 8 + h
                    MM(px[:, hh * 128:(hh + 1) * 128], identB, cb[:, neg, :], start=True, stop=False,
                       R=[r_cb], W=[rpx])
                    trib = CB_TRIF if d_ == 0 else CB_TRIB
                    for hl in range(2):
                        MM(px[:, hh * 128:(hh + 1) * 128], ahl[hl][:, i, col:col + 1].broadcast_to([128, 128]),
                           cb[:, trib, :], start=False, stop=False, R=[r_a, r_cb], W=[rpx])
                    for hl in range(2):
                        MM(px[:, hh * 128:(hh + 1) * 128], cb[:, trib, :],
                           ahl[2 + hl][:, i, col:col + 1].broadcast_to([128, 128]), start=False, stop=(hl == 1),
                           R=[r_a, r_cb], W=[rpx])
                P.batch_end()
                lt4, rlt4 = wtile()
                ACT(lt4[:], px[:], AF.Exp, R=[rpx], W=[rlt4])
                mt4, rmt4 = wbtile()
                TT('dve', mt4[:].rearrange("p (a b) -> p a b", b=128), lt4[:].rearrange("p (a b) -> p a b", b=128),
                   CBm[p2][:, d_, hq, :].unsqueeze(1).broadcast_to([128, 4, 128]), ALU.mult,
                   R=[rlt4, r_CBm[p2]], W=[rmt4])
                stash[(d_, hq)] = (mt4, rmt4)

            x1_step(*steps[0])
            for si_, (d_, hq) in enumerate(steps):
                if si_ + 1 < len(steps):
                    x1_step(*steps[si_ + 1])
                xd_, rxd_ = (xdf, rxdf) if d_ == 0 else (xdb, rxdb)
                mt4, rmt4 = stash[(d_, hq)]
                P.batch_begin()
                for hh in range(4):
                    h = hq * 4 + hh
                    MM(py[:, h * 64:(h + 1) * 64], mt4[:, hh * 128:(hh + 1) * 128], xd_[:, h * 64:(h + 1) * 64],
                       start=False, stop=(d_ == 1 and h == 7), R=[rmt4, rxd_], W=[rpy])
                P.batch_end()
            ypt, rypt = wtile()
            TT('dve', ypt[:], tmp[:], py[:], ALU.add, R=[rtmp, rpy], W=[rypt])
            P.dma('sp', yp_d[i * 128:(i + 1) * 128, :], ypt[:], reads=[rypt], writes=[r_yp_d[i]])

        def ssd_pass2(l, i, with_out):
            sbt, rsbt = wtile()
            P.dma('sp', sbt[:], sb_d[i], reads=[r_sb_d[i]], writes=[rsbt])
            if with_out:
                ypt, rypt = wtile()
                P.dma('sp', ypt[:], yp_d[i * 128:(i + 1) * 128, :], reads=[r_yp_d[i]], writes=[rypt])
                pyo, rpyo = psbank()
                P.batch_begin()
                for g in range(2):
                    MM(pyo[:, g * 256:(g + 1) * 256], CT[:, g, i * 128:(i + 1) * 128], Hbb[:, g * 256:(g + 1) * 256],
                       R=[r_CT, r_Hbb], W=[rpyo])
                P.batch_end()
            TT('dve', h3(Hb[:]), h3(Hb[:]), bc8(Eb_all[:, i, 8:16]), ALU.mult, R=[r_Hb, r_Eb], W=[r_Hb])
            TT('dve', Hb[:], Hb[:], sbt[:], ALU.add, R=[r_Hb, rsbt], W=[r_Hb])
            ACT(Hbb[:], Hb[:], AF.Copy, R=[r_Hb], W=[r_Hbb])
            if not with_out:
                return
            tmp, rtmp = wtile()
            TT('dve', h3(tmp[:]), h3(pyo[:]), bc8(Eb_all[:, i, 0:8]), ALU.mult, R=[rpyo, r_Eb], W=[rtmp])
            TT('pool', ypt[:], ypt[:], tmp[:], ALU.add, R=[rypt, rtmp], W=[rypt])
            pz, rpz = psbank()
            P.batch_begin()
            for k in range(KC):
                MM(pz[:, :], hT[:, k, i * 128:(i + 1) * 128], wz[:, k, :], start=(k == 0), stop=(k == KC - 1),
                   R=[r_wz, r_hT[i]], W=[rpz])
            P.batch_end()
            sz, rsz = wtile()
            ez, rez = wtile()
            ACT(ez[:], pz[:], AF.Exp, scale=-1.0, R=[rpz], W=[rez])
            TS('dve', ez[:], ez[:], 1.0, None, ALU.add, R=[rez], W=[rez])
            RCP(ez[:], ez[:], R=[rez], W=[rez])
            TT('dve', sz[:], pz[:], ez[:], ALU.mult, R=[rpz, rez], W=[rsz])
            TT('dve', ypt[:], ypt[:], sz[:], ALU.mult, R=[rypt, rsz], W=[rypt])
            s4, rs4 = st4[i % 2], r_st4[i % 2]
            ACT(sz[:], ypt[:], AF.Square, accum=s4[:, 0:1], R=[rypt], W=[rsz, rs4])
            ACT(s4[:, 1:2], s4[:, 0:1], AF.Ln, bias=EPS, scale=1.0 / 512, R=[rs4], W=[rs4])
            ACT(s4[:, 2:3], s4[:, 1:2], AF.Exp, scale=-0.5, R=[rs4], W=[rs4])
            ACT(ypt[:], ypt[:], AF.Copy, scale=s4[:, 2:3], R=[rypt, rs4], W=[rypt])
            pt_, rpt_ = psbank()
            for cc in range(4):
                TR(pt_[:, cc * 128:(cc + 1) * 128], ypt[:, cc * 128:(cc + 1) * 128], identF, R=[rypt, r_cf], W=[rpt_])
            ust, rust = wbtile()
            TT('dve', ust[:].rearrange("p (c t) -> p c t", t=128), pt_[:].rearrange("p (c t) -> p c t", t=128),
               pp[:, l, PP_SNW:PP_SNW + 4].unsqueeze(2).broadcast_to([128, 4, 128]), ALU.mult, R=[rpt_, r_pp], W=[rust])
            P.dma('sp', usT.rearrange("(c p) t -> p c t", p=128)[:, :, i * 128:(i + 1) * 128],
                  ust[:].rearrange("p (c t) -> p c t", t=128), reads=[rust], writes=[r_usT[i]])

        def phase_ssd(l, last):
            phase_ssd_c1(l)
            if stop_after == 'ssd_c1':
                return
            P.barrier()
            P.dma('pool', wz[:], wsrc(w_in, l, OZ, 512), writes=[r_wz])
            MSET('dve', Hf[:], 0.0, W=[r_Hf]); MSET('dve', Hb[:], 0.0, W=[r_Hb])
            MSET('pool', Hfb[:], 0.0, W=[r_Hfb]); MSET('pool', Hbb[:], 0.0, W=[r_Hbb])
            for (s0, ntl, wo) in ((0, NTC, not last), (NTC, NTL, True)):
                ssd_pass1_a(l, s0, wo)
                for ii in range(ntl):
                    if ii + 1 < ntl:
                        ssd_pass1_a(l, s0 + ii + 1, wo)
                    ssd_pass1_b(l, s0 + ii, wo)
                if stop_after == 'ssd_p1':
                    continue
                for ii in reversed(range(ntl)):
                    ssd_pass2(l, s0 + ii, wo)


        vT = ar('D', [128, 4, T + 60]); r_vT = Res()
        dgc = ar('D', [128, 4, 31, 128]); r_dgc_d = Res(); r_dgc_a = Res()
        wgcv = ar('D', [128, KC, 512]); r_wgcv = Res()
        wug = [ar('D', [128, KC, 2, 128]) for i in range(2)]; r_wug = RL(2)
        ycs = [ar('D', [128, 512], F32) for i in range(4)]; r_ycs = RL(4)
        mD = ar('D', [128, 512], F32); m2D = ar('D', [128, 512], F32); r_mD = Res(); r_m2D = Res()
        r_ucT = RL(4 * len(groups))

        def vcol(t):
            return t + 15 if t < C else t + 45

        def phase_conv(l, last):
            MSET('pool', vT[:], 0.0, W=[r_vT])
            for cc in range(4):
                for k in range(31):
                    if k % 2 == 0:
                        TS('dve', dgc[:, cc, k, :], identF,
                           pp[:, l, PP_CCW + cc * 31 + k:PP_CCW + cc * 31 + k + 1], None, ALU.mult, R=[r_cf, r_pp], W=[r_dgc_d])
                    else:
                        ACT(dgc[:, cc, k, :], identF, AF.Copy, scale=pp[:, l, PP_CCW + cc * 31 + k:PP_CCW + cc * 31 + k + 1],
                            R=[r_cf, r_pp], W=[r_dgc_a])
            P.dma('pool', wgcv[:], wsrc(w_in, l, OGCV, 512), writes=[r_wgcv])
            for cc in range(4):
                par = cc % 2
                P.dma('pool', wug[par][:, :, 0, :], wsrc(w_in, l, OGLU + cc * 128, 128), writes=[r_wug[par]])
                P.dma('pool', wug[par][:, :, 1, :], wsrc(w_in, l, OGLU + 512 + cc * 128, 128), writes=[r_wug[par]])
                for (sq_, t0, n) in groups:
                    if last and sq_ == 0:
                        continue
                    pu, rpu = psbank()
                    pg, rpg = psbank()
                    for k in range(KC):
                        MM(pu[:, 0:n], wug[par][:, k, 0, :], hT[:, k, t0:t0 + n], start=(k == 0), stop=(k == KC - 1),
                           R=[r_wug[par]] + rh(t0, n), W=[rpu])
                    for k in range(KC):
                        MM(pg[:, 0:n], wug[par][:, k, 1, :], hT[:, k, t0:t0 + n], start=(k == 0), stop=(k == KC - 1),
                           R=[r_wug[par]] + rh(t0, n), W=[rpg])
                    sg, rsg = wtile()
                    ACT(sg[:, 0:n], pg[:, 0:n], AF.Sigmoid, R=[rpg], W=[rsg])
                    TT('dve', vT[:, cc, vcol(t0):vcol(t0) + n], pu[:, 0:n], sg[:, 0:n], ALU.mult, R=[rpu, rsg], W=[r_vT])
            for gi, (sq_, t0, n) in enumerate(groups):
                if last and sq_ == 0:
                    continue
                psm, rpsm = psbank()
                psq, rpsq = psbank()
                for cc in range(4):
                    pcv, rpcv = psbank()
                    for k in range(31):
                        MM(pcv[:, 0:n], dgc[:, cc, k, :], vT[:, cc, vcol(t0) + k - 15:vcol(t0) + k - 15 + n],
                           start=(k == 0), stop=(k == 30), R=[r_dgc_d, r_dgc_a, r_vT], W=[rpcv])
                    ACT(ycs[cc][:, 0:n], pcv[:, 0:n], AF.Identity, bias=pp[:, l, PP_CCB + cc:PP_CCB + cc + 1],
                        R=[rpcv, r_pp], W=[r_ycs[cc]])
                    ysq, rysq = wtile()
                    ACT(ysq[:, 0:n], ycs[cc][:, 0:n], AF.Square, R=[r_ycs[cc]], W=[rysq])
                    MM(psm[:, 0:n], cf[:, CF_ONES, :], ycs[cc][:, 0:n], start=(cc == 0), stop=(cc == 3),
                       R=[r_cf, r_ycs[cc]], W=[rpsm])
                    MM(psq[:, 0:n], cf[:, CF_ONES, :], ysq[:, 0:n], start=(cc == 0), stop=(cc == 3),
                       R=[r_cf, rysq], W=[rpsq])
                m, rm = mD, r_mD
                ACT(m[:, 0:n], psm[:, 0:n], AF.Copy, scale=1.0 / 512, R=[rpsm], W=[rm])
                m2, rm2 = m2D, r_m2D
                TT('pool', m2[:, 0:n], m[:, 0:n], m[:, 0:n], ALU.mult, R=[rm], W=[rm2])
                STT('dve', m2[:, 0:n], psq[:, 0:n], 1.0 / 512, m2[:, 0:n], ALU.mult, ALU.subtract, R=[rpsq, rm2], W=[rm2])
                ACT(m2[:, 0:n], m2[:, 0:n], AF.Sqrt, bias=EPS, R=[rm2], W=[rm2])
                RCP(m2[:, 0:n], m2[:, 0:n], R=[rm2], W=[rm2])
                for cc in range(4):
                    t_, rt_ = wtile()
                    TT('pool', t_[:, 0:n], ycs[cc][:, 0:n], m[:, 0:n], ALU.subtract, R=[r_ycs[cc], rm], W=[rt_])
                    TT('dve', t_[:, 0:n], t_[:, 0:n], m2[:, 0:n], ALU.mult, R=[rt_, rm2], W=[rt_])
                    ACT(t_[:, 0:n], t_[:, 0:n], AF.Identity, scale=pp[:, l, PP_CLW + cc:PP_CLW + cc + 1],
                        bias=pp[:, l, PP_CLB + cc:PP_CLB + cc + 1], R=[rt_, r_pp], W=[rt_])
                    ACT(t_[:, 0:n], t_[:, 0:n], AF.Silu, R=[rt_], W=[rt_])
                    pgc, rpgc = psbank()
                    for k in range(KC):
                        MM(pgc[:, 0:n], wgcv[:, k, cc * 128:(cc + 1) * 128], hT[:, k, t0:t0 + n], start=(k == 0),
                           stop=(k == KC - 1), R=[r_wgcv] + rh(t0, n), W=[rpgc])
                    sgc, rsgc = wtile()
                    ACT(sgc[:, 0:n], pgc[:, 0:n], AF.Silu, R=[rpgc], W=[rsgc])
                    ucb, rucb = wbtile()
                    TT('dve', ucb[:, 0:n], t_[:, 0:n], sgc[:, 0:n], ALU.mult, R=[rt_, rsgc], W=[rucb])
                    P.dma('sp', ucT[cc * 128:(cc + 1) * 128, t0:t0 + n], ucb[:, 0:n], reads=[rucb],
                          writes=[r_ucT[cc * len(groups) + gi]])

        wgm = ar('E', [128, KC, 3072]); r_wgm = Res()
        wbr = ar('E', [128, 4, 3, D]); r_wbr = Res()
        ut = [ar('E', [128, 4, 512]) for i in range(3)]; r_ut = RL(3)
        wo = ar('F', [128, KC, D]); r_wo = Res()
        yTt = ar('F', [128, KC, 512]); r_yTt = Res()
        zT = ar('F', [128, KC, 512], F32); r_zT = Res()
        xo = [ar('F', [128, D], F32) for i in range(2)]; r_xo = RL(2)
        fnw = ar('F', [128, D], F32); r_fnw = Res()
        xtF = [ar('F', [128, D], F32) for i in range(4)]
        xnF = [ar('F', [128, D], F32) for i in range(2)]
        r_yT_d = RL(len(groups))

        def phase_merge(l, last):
            for b6 in range(6):
                P.dma('pool', wgm[:, :, b6 * 512:(b6 + 1) * 512], wsrc(w_in, l, OGM + b6 * 512, 512), writes=[r_wgm])
            for bi, wsrc_ in enumerate((w_bra, w_brs, w_brc)):
                for hf in range(2):
                    P.dma('pool', wbr[:, :, bi, hf * 512:(hf + 1) * 512],
                          wsrc_[l, :, hf * 512:(hf + 1) * 512].rearrange("(c p) n -> p c n", p=128), writes=[r_wbr])
            usrc = (uaT, usT, ucT)
            for gi, (sq_, t0, n) in enumerate(groups):
                if last and sq_ == 0:
                    continue
                for bi in range(3):
                    if bi == 0:
                        rr = [r_uaT[c * len(groups) + gi] for c in range(4)]
                    elif bi == 1:
                        rr = r_usT[t0 // 128:(t0 + n) // 128]
                    else:
                        rr = [r_ucT[c * len(groups) + gi] for c in range(4)]
                    P.dma('sp', ut[bi][:, :, 0:n], usrc[bi].rearrange("(c p) t -> p c t", p=128)[:, :, t0:t0 + n],
                          reads=rr, writes=[r_ut[bi]])
                for j in range(8):
                    acc, racc = wtile()
                    for bi in range(3):
                        pg, rpg = psbank()
                        for k in range(KC):
                            MM(pg[:, 0:n], wgm[:, k, bi * 1024 + j * 128:bi * 1024 + (j + 1) * 128], hT[:, k, t0:t0 + n],
                               start=(k == 0), stop=(k == KC - 1), R=[r_wgm] + rh(t0, n), W=[rpg])
                        gt, rgt = wtile()
                        ACT(gt[:, 0:n], pg[:, 0:n], AF.Identity, bias=pp[:, l, PP_BG + bi * 8 + j:PP_BG + bi * 8 + j + 1],
                            R=[rpg, r_pp], W=[rgt])
                        ACT(gt[:, 0:n], gt[:, 0:n], AF.Sigmoid, R=[rgt], W=[rgt])
                        pb, rpb = psbank()
                        for k4 in range(4):
                            MM(pb[:, 0:n], wbr[:, k4, bi, j * 128:(j + 1) * 128], ut[bi][:, k4, 0:n], start=(k4 == 0),
                               stop=(k4 == 3), R=[r_wbr, r_ut[bi]], W=[rpb])
                        if bi == 0:
                            TT('dve', acc[:, 0:n], pb[:, 0:n], gt[:, 0:n], ALU.mult, R=[rpb, rgt], W=[racc])
                        else:
                            TT('dve', gt[:, 0:n], pb[:, 0:n], gt[:, 0:n], ALU.mult, R=[rpb, rgt], W=[rgt])
                            TT('pool', acc[:, 0:n], acc[:, 0:n], gt[:, 0:n], ALU.add, R=[racc, rgt], W=[racc])
                    yb, ryb = wbtile()
                    CP('pool', yb[:, 0:n], acc[:, 0:n], R=[racc], W=[ryb])
                    P.dma('sp', yT_d[j * 128:(j + 1) * 128, t0:t0 + n], yb[:, 0:n], reads=[ryb], writes=[r_yT_d[gi]])

        def phase_out(l, last):
            xt, xn = xtF, xnF
            P.dma('sp', fnw[:], fnw_in[:, :], writes=[r_fnw])
            for hf in range(2):
                P.dma('pool', wo[:, :, hf * 512:(hf + 1) * 512], wsrc(w_out, l, hf * 512, 512), writes=[r_wo])
            for gi, (sq_, t0, n) in enumerate(groups):
                if last and sq_ == 0:
                    continue
                col = 1 if sq_ == 0 else 0
                P.dma('sp', yTt[:, :, 0:n], yT_d.rearrange("(c p) t -> p c t", p=128)[:, :, t0:t0 + n],
                      reads=[r_yT_d[gi]], writes=[r_yTt])
                for j2 in range(8):
                    po, rpo = psbank()
                    for j in range(8):
                        MM(po[:, 0:n], wo[:, j, j2 * 128:(j2 + 1) * 128], yTt[:, j, 0:n], start=(j == 0), stop=(j == 7),
                           R=[r_wo, r_yTt], W=[rpo])
                    ACT(zT[:, j2, 0:n], po[:, 0:n], AF.Copy, scale=modv[:, 16 + j2, col:col + 1], R=[rpo, r_modv], W=[r_zT])
                for si in range(n // 128):
                    i = t0 // 128 + si
                    s = i % 2
                    sx = i % 4
                    rsrc = [r_x1[i]] if l > 0 else []
                    P.dma('sp', xt[sx][:], src_tile(l, i), reads=rsrc, writes=[r_xt[sx]])
                    for b in range(2):
                        pb, rpb = psbank()
                        for kk in range(4):
                            j2 = b * 4 + kk
                            TR(pb[:, kk * 128:(kk + 1) * 128], zT[:, j2, si * 128:(si + 1) * 128], identF,
                               R=[r_zT, r_cf], W=[rpb])
                        TT('dve', xo[s][:, b * 512:(b + 1) * 512], pb[:, :], xt[sx][:, b * 512:(b + 1) * 512], ALU.add,
                           R=[rpb, r_xt[sx]], W=[r_xo[s]])
                    if not last:
                        dst = (xc1 if i < NTC else x1)[((i if i < NTC else i - NTC) * 128):((i if i < NTC else i - NTC) * 128 + 128), :]
                        P.dma('pool', dst, xo[s][:], reads=[r_xo[s]], writes=[r_x1[i]])
                    else:
                        s4, rs4 = st4[s], r_st4[s]
                        ACT(xn[s][:], xo[s][:], AF.Square, accum=s4[:, 0:1], R=[r_xo[s]], W=[r_xn[s], rs4])
                        ACT(s4[:, 1:2], s4[:, 0:1], AF.Sqrt, bias=EPS, scale=1.0 / D, R=[rs4], W=[rs4])
                        RCP(s4[:, 2:3], s4[:, 1:2], R=[rs4], W=[rs4])
                        ACT(xn[s][:], xo[s][:], AF.Copy, scale=s4[:, 2:3], R=[r_xo[s], rs4], W=[r_xn[s]])
                        TT('pool', xn[s][:], xn[s][:], fnw[:], ALU.mult, R=[r_xn[s], r_fnw], W=[r_xn[s]])
                        P.dma('pool', out[(i - NTC) * 128:(i - NTC + 1) * 128, :], xn[s][:], reads=[r_xn[s]])

        for l in range(DEPTH):
            last = (l == DEPTH - 1)
            P.barrier()
            phase_mod(l)
            P.barrier()
            phase_hT(l)
            P.barrier()
            if stop_after == 'hT':
                break
            phase_kv(l)
            phase_attn(l, last)
            if stop_after == 'attn':
                break
            P.barrier()
            phase_ssd(l, last)
            if stop_after in ('ssd', 'ssd_c1', 'ssd_p1'):
                break
            P.barrier()
            phase_conv(l, last)
            if stop_after == 'conv':
                break
            P.barrier()
            phase_merge(l, last)
            P.barrier()
            phase_out(l, last)
        if dbg:
            hT_o = nc.dram_tensor("hT_o", [128, KC, T], BF16, kind="ExternalOutput").ap()
            P.dma('sp', hT_o[:, :, :], hT[:], reads=r_hT)
        P.emit()
    return nc


def _cols(v):
    v = np.asarray(v, np.float32)
    return np.ascontiguousarray(v.reshape(-1, 128).T)


def host_prep(inp, L, C, DEPTH, nb):
    f32 = np.float32
    cfc, cbc = host_consts()
    cosT, sinT = host_rope(L, C)
    pp = np.zeros((DEPTH, 128, NPP), f32)
    pr = np.zeros((DEPTH, 128, NPR), f32)
    for l in range(DEPTH):
        pp[l, :, PP_BMOD:PP_BMOD + 24] = _cols(inp['b_mod'][l])
        pp[l, :, PP_NORMW:PP_NORMW + 8] = _cols(inp['norm_w'][l])
        pp[l, :, PP_QW] = np.tile(np.asarray(inp['q_norm_w'][l], f32), 2)
        pp[l, :, PP_KW] = np.tile(np.asarray(inp['k_norm_w'][l], f32), 2)
        scw = np.asarray(inp['ssd_conv_w'][l], f32)
        pp[l, :, PP_SCW:PP_SCW + 40] = scw.reshape(5, 8, 128).transpose(2, 1, 0).reshape(128, 40)
        pp[l, :, PP_SCB:PP_SCB + 8] = _cols(inp['ssd_conv_b'][l])
        pp[l, :, PP_SNW:PP_SNW + 4] = _cols(inp['ssd_norm_w'][l])
        ccw = np.asarray(inp['cm_conv_w'][l], f32)
        pp[l, :, PP_CCW:PP_CCW + 124] = ccw.reshape(31, 4, 128).transpose(2, 1, 0).reshape(128, 124)
        pp[l, :, PP_CCB:PP_CCB + 4] = _cols(inp['cm_conv_b'][l])
        pp[l, :, PP_CLW:PP_CLW + 4] = _cols(inp['cm_ln_w'][l])
        pp[l, :, PP_CLB:PP_CLB + 4] = _cols(inp['cm_ln_b'][l])
        pp[l, :, PP_BG:PP_BG + 24] = _cols(np.asarray(inp['b_gate'][l], f32).reshape(-1))
        pr[l, :, PR_DTB:PR_DTB + 16] = np.asarray(inp['ssd_dt_bias'][l], f32).reshape(1, 16)
        pr[l, :, PR_ALOG:PR_ALOG + 16] = np.asarray(inp['ssd_A_log'][l], f32).reshape(1, 16)
        pr[l, :, PR_D:PR_D + 8] = np.asarray(inp['ssd_D'][l], f32).reshape(1, 8)
    fnw = np.ascontiguousarray(np.broadcast_to(np.asarray(inp['final_norm_w'], f32)[None, :], (128, D)))
    shared = {
        'w_mod': np.ascontiguousarray(inp['w_mod'], f32), 'w_in': np.ascontiguousarray(inp['w_in'], f32),
        'w_bra': np.ascontiguousarray(inp['w_br_attn'], f32), 'w_brs': np.ascontiguousarray(inp['w_br_ssd'], f32),
        'w_brc': np.ascontiguousarray(inp['w_br_conv'], f32), 'w_out': np.ascontiguousarray(inp['w_out'], f32),
        'pp': pp, 'pr': pr, 'scb': np.ascontiguousarray(np.asarray(inp['ssd_conv_b'], f32).reshape(DEPTH, 1, 1024)), 'fnw': fnw, 'cf': cfc, 'cb': cbc, 'cosT': cosT, 'sinT': sinT,
    }
    maps = []
    cctx = _cols(inp['c_ctx'])
    for b in range(nb):
        cvec = np.zeros((128, 8, 2), f32)
        cvec[:, :, 0] = _cols(inp['c'][b])
        cvec[:, :, 1] = cctx
        m = dict(shared)
        m['x'] = np.ascontiguousarray(inp['x'][b], f32)
        m['ctx'] = np.ascontiguousarray(inp['ctx'][b], f32)
        m['cvec'] = cvec.reshape(128, 16)
        maps.append(m)
    return maps


_NC_CACHE = {}


def kernel(**inputs):
    x = np.asarray(inputs['x'])
    B, L, _ = x.shape
    C = np.asarray(inputs['ctx']).shape[1]
    DEPTH = np.asarray(inputs['w_in']).shape[0]
    key = (L, C, DEPTH)
    if key not in _NC_CACHE:
        _NC_CACHE[key] = build(L, C, DEPTH)
    nc = _NC_CACHE[key]
    maps = host_prep(inputs, L, C, DEPTH, B)
    res = run_bass_kernel_spmd(nc, maps, core_ids=list(range(B)))
    return np.stack([np.asarray(r['out'], np.float32) for r in res.results], axis=0)
```

```python
import numpy as np
from contextlib import ExitStack
import concourse.bass as bass
import concourse.mybir as mybir
from concourse.bass_utils import run_bass_kernel_spmd

F32 = mybir.dt.float32
BF16 = mybir.dt.bfloat16
AF = mybir.ActivationFunctionType
ALU = mybir.AluOpType

COMPUTE = ('pe', 'act', 'dve', 'pool')
NDMASEM = 12


class Res:
    __slots__ = ('w', 'rd', 'rdma')

    def __init__(self):
        self.w = None
        self.rd = {}
        self.rdma = []


def RL(n):
    return [Res() for _ in range(n)]


class Op:
    __slots__ = ('eng', 'fn', 'deps', 'dma', 'needs_inc', 'cnt', 'sem', 'target', 'prewait', 'batch')

    def __init__(self, eng, fn, dma):
        self.batch = None
        self.eng = eng
        self.fn = fn
        self.dma = dma
        self.deps = []
        self.needs_inc = False
        self.cnt = 0
        self.sem = None
        self.target = 0
        self.prewait = None


class Prog:
    def __init__(self, nc):
        self.nc = nc
        self.streams = {e: [] for e in ('pe', 'act', 'dve', 'pool', 'sp')}
        self.pending = {}
        self.dmas = []
        self.cur_batch = None
        self.batches = []

    def batch_begin(self):
        self.cur_batch = {}
        self.batches.append(self.cur_batch)

    def batch_end(self):
        self.cur_batch = None

    def barrier(self):
        deps = []
        for e, st in self.streams.items():
            for op in reversed(st):
                if not op.dma:
                    deps.append(op)
                    break
        deps.extend(self.dmas)
        self.dmas = []
        self.pending = {e: list(deps) for e in self.streams}

    def add(self, eng, fn, reads=(), writes=(), dma=False):
        op = Op(eng, fn, dma)
        deps = {}
        if self.pending.get(eng):
            for o in self.pending.pop(eng):
                deps[id(o)] = o
        if dma:
            self.dmas.append(op)
        for r in reads:
            if r.w is not None:
                deps[id(r.w)] = r.w
        for w in writes:
            if w.w is not None:
                deps[id(w.w)] = w.w
            for o in w.rd.values():
                deps[id(o)] = o
            for o in w.rdma:
                deps[id(o)] = o
        for r in reads:
            if dma:
                r.rdma.append(op)
            else:
                r.rd[eng] = op
        for w in writes:
            w.w = op
            w.rd = {}
            w.rdma = []
        for d in deps.values():
            if d is op:
                continue
            if (not d.dma) and (not dma) and d.eng == eng and eng == 'pe':
                continue
            op.deps.append(d)
            if not d.dma:
                d.needs_inc = True
        if self.cur_batch is not None and not dma and eng == 'pe':
            op.batch = self.cur_batch
            self.cur_batch.setdefault(eng, []).append(op)
        self.streams[eng].append(op)
        return op

    def coarsen(self):
        for st in self.streams.values():
            for x in st:
                nd = []
                for d in x.deps:
                    if d.batch is not None and not d.dma and not (x.batch is d.batch and x.eng == d.eng):
                        d = d.batch[d.eng][-1]
                    nd.append(d)
                x.deps = nd
        for b in self.batches:
            for e, ops in b.items():
                union = {}
                for o in ops:
                    for d in o.deps:
                        if d.batch is b and d.eng == e and not d.dma:
                            continue
                        union[id(d)] = d
                    o.deps = []
                ops[0].deps = list(union.values())
        for st in self.streams.values():
            for x in st:
                x.needs_inc = False
        for st in self.streams.values():
            for x in st:
                for d in x.deps:
                    if not d.dma:
                        d.needs_inc = True

    def dma(self, q, out, in_, reads=(), writes=(), slow=False):
        if slow:
            return self.add(q, lambda e: e.dma_start(out=out, in_=in_, allow_slow_non_contiguous=True),
                            reads, writes, dma=True)
        return self.add(q, lambda e: e.dma_start(out=out, in_=in_), reads, writes, dma=True)

    def emit(self):
        nc = self.nc
        self.coarsen()
        with ExitStack() as es:
            csem = {e: es.enter_context(nc.semaphore('cs_' + e)) for e in COMPUTE}
            dsem = {}
            for q in ('sp', 'act', 'pool'):
                dsem[q] = [es.enter_context(nc.semaphore('ds_%s%d' % (q, i))) for i in range(NDMASEM)]
            for e, st in self.streams.items():
                c = 0
                nd = 0
                for op in st:
                    if op.dma:
                        op.sem = dsem[e][nd % NDMASEM]
                        op.target = 16 * (nd // NDMASEM + 1)
                        if nd >= NDMASEM:
                            op.prewait = (op.sem, 16 * (nd // NDMASEM))
                        nd += 1
                    else:
                        if op.needs_inc:
                            c += 1
                        op.cnt = c
            block = es.enter_context(nc.Block())
            streams = self.streams

            def run_stream(ename, eng):
                known = {}
                for op in streams[ename]:
                    need = {}
                    if op.prewait is not None:
                        need[id(op.prewait[0])] = op.prewait
                    for d in op.deps:
                        if d.dma:
                            s, v = d.sem, d.target
                        else:
                            s, v = csem[d.eng], d.cnt
                        k = id(s)
                        if k not in need or need[k][1] < v:
                            need[k] = (s, v)
                    for k, (s, v) in need.items():
                        if known.get(k, 0) >= v:
                            continue
                        eng.wait_ge(s, v)
                        known[k] = v
                    ins = op.fn(eng)
                    if op.dma:
                        ins.then_inc(op.sem, 16)
                    elif op.needs_inc:
                        ins.then_inc(csem[ename], 1)
                if ename == 'sp':
                    for q in ('sp', 'act', 'pool'):
                        last = {}
                        for op in streams[q]:
                            if op.dma:
                                last[id(op.sem)] = (op.sem, op.target)
                        for k, (s, v) in last.items():
                            if known.get(k, 0) < v:
                                eng.wait_ge(s, v)
                                known[k] = v

            @block.tensor
            def _(eng):
                run_stream('pe', eng)

            @block.scalar
            def _(eng):
                run_stream('act', eng)

            @block.vector
            def _(eng):
                run_stream('dve', eng)

            @block.gpsimd
            def _(eng):
                run_stream('pool', eng)

            @block.sync
            def _(eng):
                run_stream('sp', eng)


D = 1024
KC = 8
OQ, OK_, OV, OGA, OXBC, ODT, OZ, OGLU, OGCV, OGM = 0, 512, 640, 768, 1280, 2304, 2320, 2832, 3856, 4368
INCOLS = 7440
EPS = 1e-6
NEG = -30000.0
PP_BMOD, PP_NORMW, PP_QW, PP_KW, PP_SCW, PP_SCB, PP_SNW, PP_CCW, PP_CCB, PP_CLW, PP_CLB, PP_BG = \
    0, 24, 32, 33, 34, 74, 82, 86, 210, 214, 218, 222
NPP = 246
PR_DTB, PR_ALOG, PR_D, PR_SCB = 0, 16, 32, 40
NPR = 40
CF_ID, CF_TRIF, CF_TRIB, CF_STRIF, CF_STRIB, CF_ONES = range(6)
CB_ID, CB_NEGF, CB_NEGB, CB_BLK, CB_RROT, CB_ONES, CB_TRIF, CB_TRIB = range(8)


def host_consts():
    j = np.arange(128)[:, None]
    s = np.arange(128)[None, :]
    cf = np.zeros((128, 6, 128), np.float32)
    cf[:, CF_ID] = (j == s)
    cf[:, CF_TRIF] = (j <= s)
    cf[:, CF_TRIB] = (j >= s)
    cf[:, CF_STRIF] = (j > s)
    cf[:, CF_STRIB] = (j < s)
    cf[:, CF_ONES] = 1.0
    cb = np.zeros((128, 8, 128), np.float32)
    cb[:, CB_ID] = (j == s)
    cb[:, CB_NEGF] = NEG * (s < j)
    cb[:, CB_NEGB] = NEG * (s > j)
    cb[:, CB_BLK] = ((j // 64) == (s // 64))
    rr = np.zeros((128, 128), np.float32)
    for m in range(128):
        if (m % 32) < 16:
            rr[m + 16, m] = -1.0
        else:
            rr[m - 16, m] = 1.0
    cb[:, CB_RROT] = rr
    cb[:, CB_ONES] = 1.0
    cb[:, CB_TRIF] = (j <= s)
    cb[:, CB_TRIB] = (j >= s)
    return cf.reshape(128, 768), cb.reshape(128, 1024)


def host_rope(L, C):
    T = C + L
    t = np.arange(L)
    row = (t // 64).astype(np.float32)
    col = (t % 64).astype(np.float32)
    inv = (1.0 / (10000.0 ** (np.arange(16, dtype=np.float32) / 16))).astype(np.float32)
    cosT = np.ones((128, T), np.float32)
    sinT = np.zeros((128, T), np.float32)
    for p in range(128):
        d = p % 64
        pos = row if d < 32 else col
        ang = (pos * inv[d % 16]).astype(np.float32)
        cosT[p, C:] = np.cos(ang)
        sinT[p, C:] = np.sin(ang)
    return cosT, sinT


def build(L, C, DEPTH, dbg=False, stop_after=None):
    nc = bass.Bass("TRN2", target_bir_lowering=False)
    T = C + L
    NT = T // 128
    NTC = C // 128
    NTL = L // 128
    P = Prog(nc)

    def din(name, shape, dt=F32):
        return nc.dram_tensor(name, shape, dt, kind="ExternalInput").ap()

    def dscr(name, shape, dt=F32):
        return nc.dram_tensor(name, shape, dt, kind=("ExternalOutput" if dbg else "Internal")).ap()

    x_in = din("x", [L, D])
    ctx_in = din("ctx", [C, D])
    cvec_in = din("cvec", [128, 16])
    w_mod = din("w_mod", [DEPTH, D, 3 * D])
    w_in = din("w_in", [DEPTH, D, INCOLS])
    w_bra = din("w_bra", [DEPTH, 512, D])
    w_brs = din("w_brs", [DEPTH, 512, D])
    w_brc = din("w_brc", [DEPTH, 512, D])
    w_out = din("w_out", [DEPTH, D, D])
    pp_in = din("pp", [DEPTH, 128, NPP])
    pr_in = din("pr", [DEPTH, 128, NPR])
    fnw_in = din("fnw", [128, D])
    scb_in = din("scb", [DEPTH, 1, 1024])
    cf_in = din("cf", [128, 768])
    cb_in = din("cb", [128, 1024])
    cos_in = din("cosT", [128, T])
    sin_in = din("sinT", [128, T])
    out = nc.dram_tensor("out", [L, D], F32, kind="ExternalOutput").ap()

    x1 = dscr("x1", [L, D])
    xc1 = dscr("xc1", [C, D])
    uaT = dscr("uaT", [512, T], BF16)
    usT = dscr("usT", [512, T], BF16)
    ucT = dscr("ucT", [512, T], BF16)
    yT_d = dscr("yT", [D, T], BF16)
    xs_d = dscr("xs_tm", [T, 512], BF16)
    b_d = dscr("b_tm", [T, 256], BF16)
    yp_d = dscr("ypart", [T, 512])
    sb_d = dscr("sbst", [NT, 128, 512])

    groups = [(0, 0, C)] + [(1, C + 512 * i, 512) for i in range(L // 512)]

    with ExitStack() as es:
        def sb(name, shape, dt=F32):
            return es.enter_context(nc.sbuf_tensor("s_" + name, shape, dt))

        ARN = 49152
        AR = sb("arena", [128, ARN], BF16)
        aoff = {}

        def ar(phase, shape, dt=BF16):
            n = 1
            for v in shape[1:]:
                n *= v
            nb = n * (2 if dt == BF16 else 4)
            off = (aoff.get(phase, 0) + 3) // 4 * 4
            aoff[phase] = off + nb
            assert aoff[phase] <= ARN * 2, (phase, aoff[phase])
            ap = AR[0:shape[0], off // 2:(off + nb) // 2]
            if dt == F32:
                ap = ap.bitcast(F32)
            if len(shape) == 2:
                return ap
            names = 'abcd'[:len(shape) - 1]
            pat = "p (" + " ".join(names) + ") -> p " + " ".join(names)
            kw = {names[i]: shape[1 + i] for i in range(1, len(names))}
            return ap.rearrange(pat, **kw)

        PP2 = [es.enter_context(nc.psum_tensor("pp%d" % i, [128, 1024], F32)) for i in range(4)]
        PS = []
        for i in range(4):
            PS.append(PP2[i][:, 0:512])
            PS.append(PP2[i][:, 512:1024])
        rPS = RL(8)

        def MM(o, lhsT, rhs, start=True, stop=True, R=(), W=()):
            P.add('pe', lambda e: e.matmul(o, lhsT=lhsT, rhs=rhs, start=start, stop=stop), R, W)

        def TR(o, in_, ident, R=(), W=()):
            P.add('pe', lambda e: e.transpose(out=o, in_=in_, identity=ident), R, W)

        def ACT(o, in_, func, bias=None, scale=None, accum=None, R=(), W=()):
            kw = {}
            if bias is not None:
                kw['bias'] = bias
            if scale is not None:
                kw['scale'] = scale
            if accum is not None:
                kw['accum_out'] = accum
            P.add('act', lambda e: e.activation(out=o, in_=in_, func=func, **kw), R, W)

        def TT(eng, o, a, b, op, R=(), W=()):
            P.add(eng, lambda e: e.tensor_tensor(out=o, in0=a, in1=b, op=op), R, W)

        def TS(eng, o, a, s1, s2, op0, op1=None, R=(), W=()):
            if op1 is None:
                P.add(eng, lambda e: e.tensor_scalar(out=o, in0=a, scalar1=s1, scalar2=None, op0=op0), R, W)
            else:
                P.add(eng, lambda e: e.tensor_scalar(out=o, in0=a, scalar1=s1, scalar2=s2, op0=op0, op1=op1), R, W)

        def STT(eng, o, a, s, b, op0, op1, R=(), W=()):
            P.add(eng, lambda e: e.scalar_tensor_tensor(out=o, in0=a, scalar=s, in1=b, op0=op0, op1=op1), R, W)

        def CP(eng, o, a, R=(), W=()):
            P.add(eng, lambda e: e.tensor_copy(out=o, in_=a), R, W)

        def RCP(o, a, R=(), W=()):
            P.add('dve', lambda e: e.reciprocal(out=o, in_=a), R, W)

        def MSET(eng, o, v, W=()):
            P.add(eng, lambda e: e.memset(o, v), (), W)

        def wsrc(w, l, c0, n):
            return w[l, :, c0:c0 + n].rearrange("(c p) n -> p c n", p=128)

        cf = sb("cf", [128, 6, 128]); r_cf = Res()
        cb = sb("cb", [128, 8, 128], BF16); r_cb = Res()
        pp = sb("pp", [128, DEPTH, NPP]); r_pp = Res()
        pr = sb("pr", [128, DEPTH, NPR]); r_pr = Res()
        cv = sb("cv", [128, 16]); r_cv = Res()
        csl = [[ar('B', [128, 512], F32) for b in range(2)] for a in range(2)]; r_csl = RL(2)
        cslc = [0]
        P.dma('sp', cf[:].rearrange("p a b -> p (a b)"), cf_in[:, :], writes=[r_cf])
        P.dma('pool', cb[:].rearrange("p a b -> p (a b)"), cb_in[:, :], writes=[r_cb])
        P.dma('sp', pp[:], pp_in.rearrange("l p n -> p l n"), writes=[r_pp])
        P.dma('sp', pr[:], pr_in.rearrange("l p n -> p l n"), writes=[r_pr])
        P.dma('sp', cv[:], cvec_in[:, :], writes=[r_cv])
        ones512 = sb("ones512", [1, 512], BF16); r_ones512 = Res()
        MSET('dve', ones512[0:1, :], 1.0, W=[r_ones512])
        identF = cf[:, CF_ID, :]
        identB = cb[:, CB_ID, :]

        hT = sb("hT", [128, KC, T], BF16); r_hT = RL(NT)

        def rh(t0, n):
            return r_hT[t0 // 128:(t0 + n) // 128]

        NW = 4
        wsl = [ar('M', [128, KC, 512], BF16) for i in range(NW)]
        r_wsl = RL(NW)
        wctr = [0]

        def wslot():
            i = wctr[0] % NW
            wctr[0] += 1
            return wsl[i], r_wsl[i]

        NWF = 10
        wf = [sb("wf%d" % i, [128, 512]) for i in range(NWF)]
        r_wf = RL(NWF)
        wfc = [0]

        def wtile():
            i = wfc[0] % NWF
            wfc[0] += 1
            return wf[i], r_wf[i]

        NWB = 8
        wb = [sb("wb%d" % i, [128, 512], BF16) for i in range(NWB)]
        r_wb = RL(NWB)
        wbc = [0]

        def wbtile():
            i = wbc[0] % NWB
            wbc[0] += 1
            return wb[i], r_wb[i]

        psc = [0]

        def psbank(lo=0, hi=8):
            i = lo + psc[0] % (hi - lo)
            psc[0] += 1
            return PS[i], rPS[i]

        modv = sb("modv", [128, 24, 2]); r_modv = Res()
        A1 = sb("A1", [128, 8, 2]); r_A1 = Res()
        scb = sb("scb", [128, 8, 2], BF16); r_scb = Res()

        xtA = [ar('A', [128, D], F32) for i in range(4)]; r_xt = RL(4)
        xnA = [ar('A', [128, D], F32) for i in range(2)]; r_xn = RL(2)
        junk = ar('A', [128, D], F32); r_junk = Res()
        xt = xtA
        xn = xnA
        st4 = [sb("st4_%d" % i, [128, 4]) for i in range(2)]; r_st4 = RL(2)

        def src_tile(l, i):
            if l == 0:
                return (ctx_in if i < NTC else x_in)[((i if i < NTC else i - NTC) * 128):((i if i < NTC else i - NTC) * 128 + 128), :]
            return (xc1 if i < NTC else x1)[((i if i < NTC else i - NTC) * 128):((i if i < NTC else i - NTC) * 128 + 128), :]

        r_x1 = RL(NT)

        def phase_mod(l):
            ACT(scb[:].rearrange("p a b -> p (a b)"), cv[:, :], AF.Silu, R=[r_cv], W=[r_scb])
            pm, rpm = psbank()
            for blk in range(6):
                ws, rws = wslot()
                P.dma('pool', ws[:], wsrc(w_mod, l, blk * 512, 512), writes=[rws])
                for jj in range(4):
                    j = blk * 4 + jj
                    for k in range(KC):
                        MM(pm[:, 2 * j:2 * j + 2], ws[:, k, jj * 128:(jj + 1) * 128], scb[:, k, :],
                           start=(k == 0), stop=(k == KC - 1), R=[rws, r_scb], W=[rpm])
            TT('dve', modv[:], pm[:, 0:48].rearrange("p (j t) -> p j t", t=2),
               pp[:, l, PP_BMOD:PP_BMOD + 24].unsqueeze(2).broadcast_to([128, 24, 2]), ALU.add,
               R=[rpm, r_pp], W=[r_modv])
            STT('dve', A1[:], modv[:, 8:16, :], 1.0,
                pp[:, l, PP_NORMW:PP_NORMW + 8].unsqueeze(2).broadcast_to([128, 8, 2]),
                ALU.add, ALU.mult, R=[r_modv, r_pp], W=[r_A1])

        def phase_hT(l):
            for i in range(NT):
                s = i % 2
                sx = i % 4
                col = 1 if i < NTC else 0
                rsrc = [r_x1[i]] if l > 0 else []
                P.dma('sp', xt[sx][:], src_tile(l, i), reads=rsrc, writes=[r_xt[sx]])
                ACT(junk[:], xt[sx][:], AF.Square, accum=st4[s][:, 0:1], R=[r_xt[sx]], W=[r_junk, r_st4[s]])
                ACT(st4[s][:, 1:2], st4[s][:, 0:1], AF.Sqrt, bias=EPS, scale=1.0 / D, R=[r_st4[s]], W=[r_st4[s]])
                RCP(st4[s][:, 2:3], st4[s][:, 1:2], R=[r_st4[s]], W=[r_st4[s]])
                ACT(xn[s][:], xt[sx][:], AF.Copy, scale=st4[s][:, 2:3], R=[r_xt[sx], r_st4[s]], W=[r_xn[s]])
                for b in range(2):
                    pb, rpb = psbank()
                    for kk in range(4):
                        k = b * 4 + kk
                        TR(pb[:, kk * 128:(kk + 1) * 128], xn[s][:, k * 128:(k + 1) * 128], identF,
                           R=[r_xn[s], r_cf], W=[rpb])
                    tw, rtw = wtile()
                    TT('dve', tw[:].rearrange("p (a b) -> p a b", b=128), pb[:].rearrange("p (a b) -> p a b", b=128),
                       A1[:, 4 * b:4 * b + 4, col:col + 1].broadcast_to([128, 4, 128]), ALU.mult,
                       R=[rpb, r_A1], W=[rtw])
                    TT('pool', hT[:, 4 * b:4 * b + 4, i * 128:(i + 1) * 128], tw[:].rearrange("p (a b) -> p a b", b=128),
                       modv[:, 4 * b:4 * b + 4, col:col + 1].broadcast_to([128, 4, 128]), ALU.add,
                       R=[rtw, r_modv], W=[r_hT[i]])

        kT = ar('B', [128, 2, T]); r_kT = Res()
        Vx = ar('B', [128, NT, 2, 128]); r_Vx = Res()
        wkd = ar('B', [128, KC, 2, 128]); r_wkd = Res()
        wv = ar('B', [128, KC, 128]); r_wv = Res()

        def qk_epilogue(l, pq, rpq, n, t0, wcolidx, o, ro):
            sq, rsq = wbtile()
            ACT(sq[:, 0:n], pq[:, 0:n], AF.Square, R=[rpq], W=[rsq])
            p2, rp2 = psbank()
            MM(p2[:, 0:n], cb[:, CB_BLK, :], sq[:, 0:n], R=[rsq, r_cb], W=[rp2])
            rs, rrs = wtile()
            ACT(rs[:, 0:n], p2[:, 0:n], AF.Ln, bias=EPS, scale=1.0 / 64, R=[rp2], W=[rrs])
            ACT(rs[:, 0:n], rs[:, 0:n], AF.Exp, scale=-0.5, R=[rrs], W=[rrs])
            qw, rqw = wtile()
            ACT(qw[:, 0:n], pq[:, 0:n], AF.Copy, scale=pp[:, l, wcolidx:wcolidx + 1], R=[rpq, r_pp], W=[rqw])
            TT('dve', qw[:, 0:n], qw[:, 0:n], rs[:, 0:n], ALU.mult, R=[rqw, rrs], W=[rqw])
            qb, rqb = wbtile()
            CP('dve', qb[:, 0:n], qw[:, 0:n], R=[rqw], W=[rqb])
            p3, rp3 = psbank()
            MM(p3[:, 0:n], cb[:, CB_RROT, :], qb[:, 0:n], R=[rqb, r_cb], W=[rp3])
            ci = cslc[0] % 2
            cslc[0] += 1
            P.dma('sp', csl[ci][0][:, 0:n], cos_in[:, t0:t0 + n], writes=[r_csl[ci]])
            P.dma('sp', csl[ci][1][:, 0:n], sin_in[:, t0:t0 + n], writes=[r_csl[ci]])
            t1, rt1 = wtile()
            TT('pool', t1[:, 0:n], qw[:, 0:n], csl[ci][0][:, 0:n], ALU.mult, R=[rqw, r_csl[ci]], W=[rt1])
            t2, rt2 = wtile()
            TT('dve', t2[:, 0:n], p3[:, 0:n], csl[ci][1][:, 0:n], ALU.mult, R=[rp3, r_csl[ci]], W=[rt2])
            TT('dve', o, t1[:, 0:n], t2[:, 0:n], ALU.add, R=[rt1, rt2], W=[ro])

        def phase_kv(l):
            MSET('pool', Vx[:, :, :, 64:128], 1.0, W=[r_Vx])
            for g in range(2):
                for h in range(2):
                    P.dma('pool', wkd[:, :, g, h * 64:(h + 1) * 64], wsrc(w_in, l, OK_ + g * 64, 64), writes=[r_wkd])
            P.dma('pool', wv[:], wsrc(w_in, l, OV, 128), writes=[r_wv])
            for (sq_, t0, n) in groups:
                for g in range(2):
                    pk, rpk = psbank()
                    for k in range(KC):
                        MM(pk[:, 0:n], wkd[:, k, g, :], hT[:, k, t0:t0 + n], start=(k == 0), stop=(k == KC - 1),
                           R=[r_wkd] + rh(t0, n), W=[rpk])
                    qk_epilogue(l, pk, rpk, n, t0, PP_KW, kT[:, g, t0:t0 + n], r_kT)
            for i0 in range(0, NT, 4):
                nb = min(4, NT - i0)
                pvv, rpv = psbank()
                for ii in range(nb):
                    i = i0 + ii
                    for k in range(KC):
                        MM(pvv[:, ii * 128:(ii + 1) * 128], hT[:, k, i * 128:(i + 1) * 128], wv[:, k, :],
                           start=(k == 0), stop=(k == KC - 1), R=[r_wv, r_hT[i]], W=[rpv])
                CP('dve', Vx[:, i0:i0 + nb, :, 0:64],
                   pvv[:, 0:nb * 128].rearrange("p (a g d) -> p a g d", g=2, d=64), R=[rpv], W=[r_Vx])

        qT = [ar('B', [128, T]) for i in range(2)]; r_qT = RL(2)
        sga = [ar('B', [128, T]) for i in range(2)]; r_sga = RL(2)
        wqg = [ar('B', [128, KC, 2, 128]) for i in range(2)]; r_wqg = RL(2)
        pts = [ar('B', [128, 1024]) for i in range(3)]; r_pts = RL(3)
        r_uaT = RL(4 * len(groups))
        SB_ = None
        OBK = (6, 7)
        attc = [0]

        def attend(l, c, par, q0, nq, key_tiles, gi, hooks=()):
            g = c // 2
            nk = len(key_tiles)
            LOOK = 1
            issued = []
            hooks = list(hooks)
            hstep = max(1, (nk - 2) // max(1, len(hooks))) if hooks else 1

            def issue_s(j):
                a = attc[0]
                attc[0] += 1
                pr_ = a % 2
                pt, rpt = pts[a % 3], r_pts[a % 3]
                for half in range(2):
                    hs = slice(half * 64, half * 64 + 64)
                    MM(PS[2 * pr_ + half][:, 0:nq], kT[hs, g, j * 128:(j + 1) * 128], qT[par][hs, q0:q0 + nq],
                       R=[r_kT, r_qT[par]], W=[rPS[2 * pr_ + half]])
                issued.append((pr_, pt, rpt))

            def issue_exp(n_):
                pr_, pt, rpt = issued[n_]
                if nq == 512:
                    ACT(pt[:, 0:1024], PP2[pr_][:, 0:1024], AF.Exp, scale=0.125, R=[rPS[2 * pr_], rPS[2 * pr_ + 1]], W=[rpt])
                else:
                    for half in range(2):
                        ACT(pt[:, half * 512:half * 512 + nq], PS[2 * pr_ + half][:, 0:nq], AF.Exp, scale=0.125,
                            R=[rPS[2 * pr_ + half]], W=[rpt])

            def finish():
                ua, rua = wtile()
                uab, ruab = wbtile()
                osb = []
                for half in range(2):
                    o_, ro_ = wtile()
                    CP('dve', o_[:, 0:nq], PS[OBK[half]][:, 0:nq], R=[rPS[OBK[half]]], W=[ro_])
                    osb.append((o_, ro_))
                ob, rob = osb[0]
                RCP(ob[64:128, 0:nq], ob[64:128, 0:nq], R=[rob], W=[rob])
                r2, rr2 = wtile()
                P.dma('sp', r2[0:64, 0:nq], ob[64:128, 0:nq], reads=[rob], writes=[rr2])
                TT('dve', ua[0:64, 0:nq], ob[0:64, 0:nq], r2[0:64, 0:nq], ALU.mult, R=[rob, rr2], W=[rua])
                ob1, rob1 = osb[1]
                o2, ro2 = wtile()
                P.dma('sp', o2[64:128, 0:nq], ob1[0:64, 0:nq], reads=[rob1], writes=[ro2])
                RCP(ob1[64:128, 0:nq], ob1[64:128, 0:nq], R=[rob1], W=[rob1])
                TT('dve', ua[64:128, 0:nq], o2[64:128, 0:nq], ob1[64:128, 0:nq], ALU.mult, R=[ro2, rob1], W=[rua])
                TT('pool', uab[:, 0:nq], ua[:, 0:nq], sga[par][:, q0:q0 + nq], ALU.mult,
                   R=[rua, r_sga[par]], W=[ruab])
                P.dma('sp', uaT[c * 128:(c + 1) * 128, q0:q0 + nq], uab[:, 0:nq], reads=[ruab],
                      writes=[r_uaT[c * len(groups) + gi]])

            def issue_pv(n_):
                j = key_tiles[n_]
                P.batch_begin()
                pr_, pt, rpt = issued[n_]
                for half in range(2):
                    MM(PS[OBK[half]][:, 0:nq], Vx[:, j, g, :], pt[:, half * 512:half * 512 + nq], start=(n_ == 0),
                       stop=(n_ == nk - 1), R=[r_Vx, rpt], W=[rPS[OBK[half]]])
                P.batch_end()

            P.batch_begin()
            issue_s(key_tiles[0])
            P.batch_end()
            issue_exp(0)
            for n_ in range(nk):
                if n_ + 1 < nk:
                    P.batch_begin()
                    issue_s(key_tiles[n_ + 1])
                    P.batch_end()
                    issue_exp(n_ + 1)
                if n_ >= 1:
                    issue_pv(n_ - 1)
                if hooks and n_ >= 1 and (n_ - 1) % hstep == 0:
                    hooks.pop(0)()
            issue_pv(nk - 1)
            finish()
            while hooks:
                hooks.pop(0)()

        def qga_stages(l, c, gi, lo, hi):
            par = c % 2
            (sq_, t0, n) = groups[gi]
            st = {}

            def bank(k):
                if lo is None:
                    return psbank()
                return PS[lo + k % (hi - lo)], rPS[lo + k % (hi - lo)]

            def s1():
                st['pq'] = bank(0)
                pq, rpq = st['pq']
                for k in range(KC):
                    MM(pq[:, 0:n], wqg[par][:, k, 0, :], hT[:, k, t0:t0 + n], start=(k == 0), stop=(k == KC - 1),
                       R=[r_wqg[par]] + rh(t0, n), W=[rpq])
                ci = cslc[0] % 2
                cslc[0] += 1
                st['ci'] = ci
                P.dma('sp', csl[ci][0][:, 0:n], cos_in[:, t0:t0 + n], writes=[r_csl[ci]])
                P.dma('sp', csl[ci][1][:, 0:n], sin_in[:, t0:t0 + n], writes=[r_csl[ci]])

            def s2():
                pq, rpq = st['pq']
                st['sq'] = wbtile()
                sq, rsq = st['sq']
                ACT(sq[:, 0:n], pq[:, 0:n], AF.Square, R=[rpq], W=[rsq])
                st['qw'] = wtile()
                qw, rqw = st['qw']
                ACT(qw[:, 0:n], pq[:, 0:n], AF.Copy, scale=pp[:, l, PP_QW:PP_QW + 1], R=[rpq, r_pp], W=[rqw])

            def s3():
                st['p2'] = bank(1)
                p2, rp2 = st['p2']
                sq, rsq = st['sq']
                MM(p2[:, 0:n], cb[:, CB_BLK, :], sq[:, 0:n], R=[rsq, r_cb], W=[rp2])

            def s4():
                p2, rp2 = st['p2']
                st['rs'] = wtile()
                rs, rrs = st['rs']
                ACT(rs[:, 0:n], p2[:, 0:n], AF.Ln, bias=EPS, scale=1.0 / 64, R=[rp2], W=[rrs])
                ACT(rs[:, 0:n], rs[:, 0:n], AF.Exp, scale=-0.5, R=[rrs], W=[rrs])

            def s5():
                rs, rrs = st['rs']
                qw, rqw = st['qw']
                ci = st['ci']
                TT('dve', qw[:, 0:n], qw[:, 0:n], rs[:, 0:n], ALU.mult, R=[rqw, rrs], W=[rqw])
                st['qb'] = wbtile()
                qb, rqb = st['qb']
                CP('pool', qb[:, 0:n], qw[:, 0:n], R=[rqw], W=[rqb])
                st['t1'] = wtile()
                t1, rt1 = st['t1']
                TT('pool', t1[:, 0:n], qw[:, 0:n], csl[ci][0][:, 0:n], ALU.mult, R=[rqw, r_csl[ci]], W=[rt1])

            def s6():
                st['p3'] = bank(2)
                p3, rp3 = st['p3']
                qb, rqb = st['qb']
                MM(p3[:, 0:n], cb[:, CB_RROT, :], qb[:, 0:n], R=[rqb, r_cb], W=[rp3])
                st['pg'] = bank(3)
                pg, rpg = st['pg']
                for k in range(KC):
                    MM(pg[:, 0:n], wqg[par][:, k, 1, :], hT[:, k, t0:t0 + n], start=(k == 0), stop=(k == KC - 1),
                       R=[r_wqg[par]] + rh(t0, n), W=[rpg])

            def s7():
                p3, rp3 = st['p3']
                pg, rpg = st['pg']
                t1, rt1 = st['t1']
                ci = st['ci']
                t2, rt2 = wtile()
                TT('dve', t2[:, 0:n], p3[:, 0:n], csl[ci][1][:, 0:n], ALU.mult, R=[rp3, r_csl[ci]], W=[rt2])
                TT('dve', qT[par][:, t0:t0 + n], t1[:, 0:n], t2[:, 0:n], ALU.add, R=[rt1, rt2], W=[r_qT[par]])
                e_, re_ = wtile()
                ACT(e_[:, 0:n], pg[:, 0:n], AF.Exp, scale=-1.0, R=[rpg], W=[re_])
                g_, rg_ = wtile()
                ACT(g_[:, 0:n], pg[:, 0:n], AF.Copy, R=[rpg], W=[rg_])
                TS('dve', e_[:, 0:n], e_[:, 0:n], 1.0, None, ALU.add, R=[re_], W=[re_])
                RCP(e_[:, 0:n], e_[:, 0:n], R=[re_], W=[re_])
                TT('pool', sga[par][:, t0:t0 + n], g_[:, 0:n], e_[:, 0:n], ALU.mult, R=[rg_, re_], W=[r_sga[par]])

            return [s1, s2, s3, s4, s5, s6, s7]

        def phase_attn(l, last):
            glist = [gi for gi, (sq_, t0, n) in enumerate(groups) if not (last and sq_ == 0)]

            def load_w(c):
                par = c % 2
                P.dma('pool', wqg[par][:, :, 0, :], wsrc(w_in, l, OQ + c * 128, 128), writes=[r_wqg[par]])
                P.dma('pool', wqg[par][:, :, 1, :], wsrc(w_in, l, OGA + c * 128, 128), writes=[r_wqg[par]])

            load_w(0)
            for gi in glist:
                for f_ in qga_stages(l, 0, gi, None, None):
                    f_()
            for c in range(4):
                par = c % 2
                if c + 1 < 4:
                    load_w(c + 1)
                order = [gi for gi in glist if groups[gi][0] == 1] + [gi for gi in glist if groups[gi][0] == 0]
                nxt = list(glist) if c + 1 < 4 else []
                for gi in order:
                    (sq_, t0, n) = groups[gi]
                    hk = qga_stages(l, c + 1, nxt.pop(0), 4, 6) if nxt else []
                    if sq_ == 0:
                        attend(l, c, par, t0, n, list(range(NTC)), gi, hk)
                    else:
                        attend(l, c, par, t0, n, list(range(NT)), gi, hk)

        def qk_epilogue_b(l, pq, rpq, n, t0, wcolidx, o, ro):
            old = psc[0]
            qk_epilogue_r(l, pq, rpq, n, t0, wcolidx, o, ro, 0, 8)

        def qk_epilogue_r(l, pq, rpq, n, t0, wcolidx, o, ro, lo, hi):
            sq, rsq = wbtile()
            ACT(sq[:, 0:n], pq[:, 0:n], AF.Square, R=[rpq], W=[rsq])
            p2, rp2 = psbank(lo, hi)
            MM(p2[:, 0:n], cb[:, CB_BLK, :], sq[:, 0:n], R=[rsq, r_cb], W=[rp2])
            rs, rrs = wtile()
            ACT(rs[:, 0:n], p2[:, 0:n], AF.Sqrt, bias=EPS, scale=1.0 / 64, R=[rp2], W=[rrs])
            RCP(rs[:, 0:n], rs[:, 0:n], R=[rrs], W=[rrs])
            qw, rqw = wtile()
            ACT(qw[:, 0:n], pq[:, 0:n], AF.Copy, scale=pp[:, l, wcolidx:wcolidx + 1], R=[rpq, r_pp], W=[rqw])
            TT('dve', qw[:, 0:n], qw[:, 0:n], rs[:, 0:n], ALU.mult, R=[rqw, rrs], W=[rqw])
            qb, rqb = wbtile()
            CP('pool', qb[:, 0:n], qw[:, 0:n], R=[rqw], W=[rqb])
            p3, rp3 = psbank(lo, hi)
            MM(p3[:, 0:n], cb[:, CB_RROT, :], qb[:, 0:n], R=[rqb, r_cb], W=[rp3])
            ci = cslc[0] % 2
            cslc[0] += 1
            P.dma('sp', csl[ci][0][:, 0:n], cos_in[:, t0:t0 + n], writes=[r_csl[ci]])
            P.dma('sp', csl[ci][1][:, 0:n], sin_in[:, t0:t0 + n], writes=[r_csl[ci]])
            t1, rt1 = wtile()
            TT('pool', t1[:, 0:n], qw[:, 0:n], csl[ci][0][:, 0:n], ALU.mult, R=[rqw, r_csl[ci]], W=[rt1])
            t2, rt2 = wtile()
            TT('dve', t2[:, 0:n], p3[:, 0:n], csl[ci][1][:, 0:n], ALU.mult, R=[rp3, r_csl[ci]], W=[rt2])
            TT('dve', o, t1[:, 0:n], t2[:, 0:n], ALU.add, R=[rt1, rt2], W=[ro])


        BT = ar('C', [128, 2, T]); r_BT = Res()
        CT = ar('C', [128, 2, T]); r_CT = Res()
        pre0 = ar('C', [128, T + 8]); r_pre0 = Res()
        pre = [pre0, pre0]; r_pre = [r_pre0, r_pre0]
        wx = [ar('C', [128, KC, 128]) for i in range(2)]; r_wx = RL(2)
        wdt = ar('C', [128, KC, 16]); r_wdt = Res()
        dgwz = ar('C', [128, 5120])
        dgs = dgwz.rearrange("p (a b c) -> p a b c", a=8, b=5); r_dgs = Res()
        wz = dgwz[:, 0:4096].rearrange("p (a b) -> p a b", a=KC); r_wz = Res()
        dt_all = ar('C', [128, NT, 16], F32); r_dt = Res()
        a_all = ar('C', [128, NT, 16], F32); r_a = Res()
        ahl = [ar('C', [128, NT, 16]) for i in range(4)]
        Eb_all = ar('C', [128, NT, 16], F32); r_Eb = Res()
        scbrow = ar('C', [1, 1024]); r_scbrow = Res()
        Aexp = sb("Aexp", [128, 16]); r_Aexp = Res()
        Hf = ar('C', [128, 512], F32); Hb = ar('C', [128, 512], F32); r_Hf = Res(); r_Hb = Res()
        Hfb = ar('C', [128, 512]); Hbb = ar('C', [128, 512]); r_Hfb = Res(); r_Hbb = Res()
        Et = [sb("Et%d" % i, [128, 80]) for i in range(2)]; r_Et = RL(2)
        ncum = [sb("ncum%d" % i, [128, 32]) for i in range(2)]; r_ncum = RL(2)
        dtd = [sb("dtd%d" % i, [128, 16]) for i in range(2)]; r_dtd = RL(2)
        CBm = [sb("CBm%d" % i, [128, 2, 2, 128]) for i in range(2)]; r_CBm = RL(2)
        xs_t = [ar('C', [128, 512]) for i in range(3)]; r_xs_t = RL(3)
        b_t = [ar('C', [128, 256]) for i in range(3)]; r_b_t = RL(3)
        p1t = [[ar('C', [128, 512]) for b in range(5)] for a in range(2)]
        r_p1t = [RL(5) for a in range(2)]
        r_xs_d = RL(len(groups) * 4); r_b_d = RL(len(groups) * 2)
        r_yp_d = RL(NT); r_sb_d = RL(NT); r_usT = RL(NT)
        def pcol(t):
            return t + 2 if t < C else t + 6

        import os as _os
        C1S = _os.environ.get("C1STEPS", "12345")

        def phase_ssd_c1(l):
            MSET('pool', pre0[:], 0.0, W=[r_pre0])
            for cc in range(8 if '1' in C1S else 0):
                for k in range(5):
                    TS('dve', dgs[:, cc, k, :], identF, pp[:, l, PP_SCW + cc * 5 + k:PP_SCW + cc * 5 + k + 1], None,
                       ALU.mult, R=[r_cf, r_pp], W=[r_dgs])
            if '1' in C1S:
                P.dma('pool', scbrow[0:1, :], scb_in[l], writes=[r_scbrow])
            P.dma('pool', wdt[:], wsrc(w_in, l, ODT, 16), writes=[r_wdt])
            for i0 in range(0, NT if '2' in C1S else 0, 32):
                nb = min(32, NT - i0)
                pd, rpd = psbank()
                for ii in range(nb):
                    i = i0 + ii
                    for k in range(KC):
                        MM(pd[:, ii * 16:(ii + 1) * 16], hT[:, k, i * 128:(i + 1) * 128], wdt[:, k, :],
                           start=(k == 0), stop=(k == KC - 1), R=[r_wdt, r_hT[i]], W=[rpd])
                xb, rxb = wtile()
                ax, rax = wtile()
                n16 = nb * 16
                TT('dve', xb[:, 0:n16].rearrange("p (a b) -> p a b", b=16), pd[:, 0:n16].rearrange("p (a b) -> p a b", b=16),
                   pr[:, l, PR_DTB:PR_DTB + 16].unsqueeze(1).broadcast_to([128, nb, 16]), ALU.add, R=[rpd, r_pr], W=[rxb])
                ACT(ax[:, 0:n16], xb[:, 0:n16], AF.Abs, R=[rxb], W=[rax])
                ACT(ax[:, 0:n16], ax[:, 0:n16], AF.Exp, scale=-1.0, R=[rax], W=[rax])
                ACT(ax[:, 0:n16], ax[:, 0:n16], AF.Ln, bias=1.0, R=[rax], W=[rax])
                TS('dve', xb[:, 0:n16], xb[:, 0:n16], 0.0, None, ALU.max, R=[rxb], W=[rxb])
                TT('dve', dt_all[:, i0:i0 + nb, :], xb[:, 0:n16].rearrange("p (a b) -> p a b", b=16),
                   ax[:, 0:n16].rearrange("p (a b) -> p a b", b=16), ALU.add, R=[rxb, rax], W=[r_dt])
            if '2' in C1S:
                ACT(Aexp[:], pr[:, l, PR_ALOG:PR_ALOG + 16], AF.Exp, R=[r_pr], W=[r_Aexp])
                STT('dve', a_all[:], dt_all[:], -1.0, Aexp[:].unsqueeze(1).broadcast_to([128, NT, 16]),
                    ALU.mult, ALU.mult, R=[r_dt, r_Aexp], W=[r_a])
                CP('dve', ahl[0][:], a_all[:], R=[r_a], W=[r_a])
                TT('dve', ahl[1][:], a_all[:], ahl[0][:], ALU.subtract, R=[r_a], W=[r_a])
                TS('dve', ahl[2][:], ahl[0][:], -1.0, None, ALU.mult, R=[r_a], W=[r_a])
                TS('dve', ahl[3][:], ahl[1][:], -1.0, None, ALU.mult, R=[r_a], W=[r_a])
            for cc in range(8 if '3' in C1S else 0):
                par = cc % 2
                P.dma('pool', wx[par][:], wsrc(w_in, l, OXBC + cc * 128, 128), writes=[r_wx[par]])
                for (sq_, t0, n) in groups:
                    pb, rpb = psbank()
                    for k in range(KC):
                        MM(pb[:, 0:n], wx[par][:, k, :], hT[:, k, t0:t0 + n], start=(k == 0), stop=(k == KC - 1),
                           R=[r_wx[par]] + rh(t0, n), W=[rpb])
                    ACT(pre[par][:, pcol(t0):pcol(t0) + n], pb[:, 0:n], AF.Copy, R=[rpb], W=[r_pre[par]])
                if cc < 6 and '4' in C1S:
                    for gi, (sq_, t0, n) in enumerate(groups):
                        pb, rpb = psbank()
                        nt_ = n // 128
                        for ii in range(nt_):
                            tt0 = t0 + ii * 128
                            for k in range(5):
                                MM(pb[:, ii * 128:(ii + 1) * 128], pre[par][:, pcol(tt0) + k - 2:pcol(tt0) + k - 2 + 128],
                                   dgs[:, cc, k, :], start=(k == 0), stop=False, R=[r_pre[par], r_dgs], W=[rpb])
                            MM(pb[:, ii * 128:(ii + 1) * 128], cb[0:1, CB_ONES, :], scbrow[0:1, cc * 128:(cc + 1) * 128],
                               start=False, stop=True, R=[r_cb, r_scbrow], W=[rpb])
                        stg, rstg = wbtile()
                        ACT(stg[:, 0:n], pb[:, 0:n], AF.Silu, R=[rpb], W=[rstg])
                        if cc < 4:
                            dst = xs_d[t0:t0 + n, cc * 128:(cc + 1) * 128].rearrange("(a p) c -> p a c", p=128)
                            rdst = r_xs_d[gi * 4 + cc]
                        else:
                            dst = b_d[t0:t0 + n, (cc - 4) * 128:(cc - 3) * 128].rearrange("(a p) c -> p a c", p=128)
                            rdst = r_b_d[gi * 2 + cc - 4]
                        P.dma('sp', dst, stg[:, 0:n].rearrange("p (a c) -> p a c", c=128), reads=[rstg], writes=[rdst])
                if cc >= 4 and '5' in C1S:
                    for (sq_, t0, n) in groups:
                        pb, rpb = psbank()
                        for k in range(5):
                            MM(pb[:, 0:n], dgs[:, cc, k, :], pre[par][:, pcol(t0) + k - 2:pcol(t0) + k - 2 + n],
                               start=(k == 0), stop=(k == 4), R=[r_pre[par], r_dgs], W=[rpb])
                        tb, rtb = wtile()
                        ACT(tb[:, 0:n], pb[:, 0:n], AF.Identity, bias=pp[:, l, PP_SCB + cc:PP_SCB + cc + 1],
                            R=[rpb, r_pp], W=[rtb])
                        if cc < 6:
                            ACT(BT[:, cc - 4, t0:t0 + n], tb[:, 0:n], AF.Silu, R=[rtb], W=[r_BT])
                        else:
                            ACT(CT[:, cc - 6, t0:t0 + n], tb[:, 0:n], AF.Silu, R=[rtb], W=[r_CT])

        ssdc = [0]

        def bc8(ap):
            return ap.unsqueeze(2).broadcast_to([128, 8, 64])

        def h3(ap):
            return ap.rearrange("p (h d) -> p h d", d=64)

        p1state = {}

        def ssd_pass1_a(l, i, with_out):
            e = ssdc[0]
            ssdc[0] += 1
            p2 = e % 2
            p3 = e % 3
            p1state[i] = (p2, p3)
            gi = [k for k, (sq_, t0, n) in enumerate(groups) if t0 <= i * 128 < t0 + n][0]
            P.dma('sp', xs_t[p3][:], xs_d[i * 128:(i + 1) * 128, :], reads=r_xs_d[gi * 4:gi * 4 + 4], writes=[r_xs_t[p3]])
            P.dma('sp', b_t[p3][:], b_d[i * 128:(i + 1) * 128, :], reads=r_b_d[gi * 2:gi * 2 + 2], writes=[r_b_t[p3]])
            a_i = a_all[:, i, :]
            pc, rpc = psbank()
            for bi, blk in enumerate((CF_TRIF, CF_TRIB, CF_STRIF, CF_STRIB, CF_ONES)):
                MM(pc[:, bi * 16:(bi + 1) * 16], cf[:, blk, :], a_i, R=[r_cf, r_a], W=[rpc])
            ACT(Et[p2][:], pc[:, 0:80], AF.Exp, R=[rpc], W=[r_Et[p2]])
            CP('pool', Eb_all[:, i, 0:8], Et[p2][:, 24:32], R=[r_Et[p2]], W=[r_Eb])
            CP('pool', Eb_all[:, i, 8:16], Et[p2][:, 72:80], R=[r_Et[p2]], W=[r_Eb])
            TT('pool', dtd[p2][:, 0:8], dt_all[:, i, 0:8], Et[p2][:, 32:40], ALU.mult, R=[r_dt, r_Et[p2]], W=[r_dtd[p2]])
            TT('pool', dtd[p2][:, 8:16], dt_all[:, i, 8:16], Et[p2][:, 56:64], ALU.mult, R=[r_dt, r_Et[p2]], W=[r_dtd[p2]])
            xs3 = h3(xs_t[p3][:])
            xddf, rxddf = p1t[p2][0], r_p1t[p2][0]
            xddb, rxddb = p1t[p2][1], r_p1t[p2][1]
            TT('pool', h3(xddf[:]), xs3, bc8(dtd[p2][:, 0:8]), ALU.mult, R=[r_xs_t[p3], r_dtd[p2]], W=[rxddf])
            TT('dve', h3(xddb[:]), xs3, bc8(dtd[p2][:, 8:16]), ALU.mult, R=[r_xs_t[p3], r_dtd[p2]], W=[rxddb])
            if with_out:
                xdf, rxdf = p1t[p2][2], r_p1t[p2][2]
                xdb, rxdb = p1t[p2][3], r_p1t[p2][3]
                xsD, rxsD = p1t[p2][4], r_p1t[p2][4]
                TT('dve', h3(xdf[:]), xs3, bc8(dt_all[:, i, 0:8]), ALU.mult, R=[r_xs_t[p3], r_dt], W=[rxdf])
                TT('dve', h3(xdb[:]), xs3, bc8(dt_all[:, i, 8:16]), ALU.mult, R=[r_xs_t[p3], r_dt], W=[rxdb])
                TT('pool', h3(xsD[:]), xs3, bc8(pr[:, l, PR_D:PR_D + 8]), ALU.mult, R=[r_xs_t[p3], r_pr], W=[rxsD])
                pcb, rpcb = psbank()
                for g in range(2):
                    MM(pcb[:, g * 128:(g + 1) * 128], BT[:, g, i * 128:(i + 1) * 128], CT[:, g, i * 128:(i + 1) * 128],
                       R=[r_BT, r_CT], W=[rpcb])
                for d_, blk in enumerate((CF_TRIF, CF_TRIB)):
                    TT('dve', CBm[p2][:, d_, :, :], pcb[:, 0:256].rearrange("p (g s) -> p g s", s=128),
                       cf[:, blk, :].unsqueeze(1).broadcast_to([128, 2, 128]), ALU.mult, R=[rpcb, r_cf], W=[r_CBm[p2]])

        def ssd_pass1_b(l, i, with_out):
            p2, p3 = p1state.pop(i)
            xddf, rxddf = p1t[p2][0], r_p1t[p2][0]
            xddb, rxddb = p1t[p2][1], r_p1t[p2][1]
            xdf, rxdf = p1t[p2][2], r_p1t[p2][2]
            xdb, rxdb = p1t[p2][3], r_p1t[p2][3]
            xsD, rxsD = p1t[p2][4], r_p1t[p2][4]
            if with_out:
                pyo, rpyo = psbank()
                P.batch_begin()
                for g in range(2):
                    MM(pyo[:, g * 256:(g + 1) * 256], CT[:, g, i * 128:(i + 1) * 128], Hfb[:, g * 256:(g + 1) * 256],
                       R=[r_CT, r_Hfb], W=[rpyo])
                P.batch_end()
            psf, rpsf = psbank()
            psb_, rpsb = psbank()
            P.batch_begin()
            for g in range(2):
                MM(psf[:, g * 256:(g + 1) * 256], b_t[p3][:, g * 128:(g + 1) * 128], xddf[:, g * 256:(g + 1) * 256],
                   R=[r_b_t[p3], rxddf], W=[rpsf])
            for g in range(2):
                MM(psb_[:, g * 256:(g + 1) * 256], b_t[p3][:, g * 128:(g + 1) * 128], xddb[:, g * 256:(g + 1) * 256],
                   R=[r_b_t[p3], rxddb], W=[rpsb])
            P.batch_end()
            TT('dve', h3(Hf[:]), h3(Hf[:]), bc8(Et[p2][:, 64:72]), ALU.mult, R=[r_Hf, r_Et[p2]], W=[r_Hf])
            TT('dve', Hf[:], Hf[:], psf[:], ALU.add, R=[r_Hf, rpsf], W=[r_Hf])
            ACT(Hfb[:], Hf[:], AF.Copy, R=[r_Hf], W=[r_Hfb])
            sbt, rsbt = wtile()
            ACT(sbt[:], psb_[:], AF.Copy, R=[rpsb], W=[rsbt])
            P.dma('sp', sb_d[i], sbt[:], reads=[rsbt], writes=[r_sb_d[i]])
            if not with_out:
                return
            tmp, rtmp = wtile()
            TT('dve', h3(tmp[:]), h3(pyo[:]), bc8(Et[p2][:, 0:8]), ALU.mult, R=[rpyo, r_Et[p2]], W=[rtmp])
            py, rpy = psbank()
            MM(py[:, :], identB, xsD[:], start=True, stop=False, R=[r_cb, rxsD], W=[rpy])
            steps = [(d_, hq) for d_ in range(2) for hq in range(2)]
            stash = {}

            def x1_step(d_, hq):
                tri = CF_TRIF if d_ == 0 else CF_TRIB
                neg = CB_NEGF if d_ == 0 else CB_NEGB
                px, rpx = psbank()
                P.batch_begin()
                for hh in range(4):
                    h = hq * 4 + hh
                    col = d_ * 8 + h
                    MM(px[:, hh * 128:(hh + 1) * 128], identB, cb[:, neg, :], start=True, stop=False,
                       R=[r_cb], W=[rpx])
                    trib = CB_TRIF if d_ == 0 else CB_TRIB
                    for hl in range(2):
                        MM(px[:, hh * 128:(hh + 1) * 128], ahl[hl][:, i, col:col + 1].broadcast_to([128, 128]),
                           cb[:, trib, :], start=False, stop=False, R=[r_a, r_cb], W=[rpx])
                    for hl in range(2):
                        MM(px[:, hh * 128:(hh + 1) * 128], cb[:, trib, :],
                           ahl[2 + hl][:, i, col:col + 1].broadcast_to([128, 128]), start=False, stop=(hl == 1),
                           R=[r_a, r_cb], W=[rpx])
                P.batch_end()
                lt4, rlt4 = wtile()
                ACT(lt4[:], px[:], AF.Exp, R=[rpx], W=[rlt4])
                mt4, rmt4 = wbtile()
                TT('dve', mt4[:].rearrange("p (a b) -> p a b", b=128), lt4[:].rearrange("p (a b) -> p a b", b=128),
                   CBm[p2][:, d_, hq, :].unsqueeze(1).broadcast_to([128, 4, 128]), ALU.mult,
                   R=[rlt4, r_CBm[p2]], W=[rmt4])
                stash[(d_, hq)] = (mt4, rmt4)

            x1_step(*steps[0])
            for si_, (d_, hq) in enumerate(steps):
                if si_ + 1 < len(steps):
                    x1_step(*steps[si_ + 1])
                xd_, rxd_ = (xdf, rxdf) if d_ == 0 else (xdb, rxdb)
                mt4, rmt4 = stash[(d_, hq)]
                P.batch_begin()
                for hh in range(4):
                    h = hq * 4 + hh
                    MM(py[:, h * 64:(h + 1) * 64], mt4[:, hh * 128:(hh + 1) * 128], xd_[:, h * 64:(h + 1) * 64],
                       start=False, stop=(d_ == 1 and h == 7), R=[rmt4, rxd_], W=[rpy])
                P.batch_end()
            ypt, rypt = wtile()
            TT('dve', ypt[:], tmp[:], py[:], ALU.add, R=[rtmp, rpy], W=[rypt])
            P.dma('sp', yp_d[i * 128:(i + 1) * 128, :], ypt[:], reads=[rypt], writes=[r_yp_d[i]])

        def ssd_pass2(l, i, with_out):
            sbt, rsbt = wtile()
            P.dma('sp', sbt[:], sb_d[i], reads=[r_sb_d[i]], writes=[rsbt])
            if with_out:
                ypt, rypt = wtile()
                P.dma('sp', ypt[:], yp_d[i * 128:(i + 1) * 128, :], reads=[r_yp_d[i]], writes=[rypt])
                pyo, rpyo = psbank()
                P.batch_begin()
                for g in range(2):
                    MM(pyo[:, g * 256:(g + 1) * 256], CT[:, g, i * 128:(i + 1) * 128], Hbb[:, g * 256:(g + 1) * 256],
                       R=[r_CT, r_Hbb], W=[rpyo])
                P.batch_end()
            TT('dve', h3(Hb[:]), h3(Hb[:]), bc8(Eb_all[:, i, 8:16]), ALU.mult, R=[r_Hb, r_Eb], W=[r_Hb])
            TT('dve', Hb[:], Hb[:], sbt[:], ALU.add, R=[r_Hb, rsbt], W=[r_Hb])
            ACT(Hbb[:], Hb[:], AF.Copy, R=[r_Hb], W=[r_Hbb])
            if not with_out:
                return
            tmp, rtmp = wtile()
            TT('dve', h3(tmp[:]), h3(pyo[:]), bc8(Eb_all[:, i, 0:8]), ALU.mult, R=[rpyo, r_Eb], W=[rtmp])
            TT('pool', ypt[:], ypt[:], tmp[:], ALU.add, R=[rypt, rtmp], W=[rypt])
            pz, rpz = psbank()
            P.batch_begin()
            for k in range(KC):
                MM(pz[:, :], hT[:, k, i * 128:(i + 1) * 128], wz[:, k, :], start=(k == 0), stop=(k == KC - 1),
                   R=[r_wz, r_hT[i]], W=[rpz])
            P.batch_end()
            sz, rsz = wtile()
            ez, rez = wtile()
            ACT(ez[:], pz[:], AF.Exp, scale=-1.0, R=[rpz], W=[rez])
            TS('dve', ez[:], ez[:], 1.0, None, ALU.add, R=[rez], W=[rez])
            RCP(ez[:], ez[:], R=[rez], W=[rez])
            TT('dve', sz[:], pz[:], ez[:], ALU.mult, R=[rpz, rez], W=[rsz])
            TT('dve', ypt[:], ypt[:], sz[:], ALU.mult, R=[rypt, rsz], W=[rypt])
            s4, rs4 = st4[i % 2], r_st4[i % 2]
            ACT(sz[:], ypt[:], AF.Square, accum=s4[:, 0:1], R=[rypt], W=[rsz, rs4])
            ACT(s4[:, 1:2], s4[:, 0:1], AF.Ln, bias=EPS, scale=1.0 / 512, R=[rs4], W=[rs4])
            ACT(s4[:, 2:3], s4[:, 1:2], AF.Exp, scale=-0.5, R=[rs4], W=[rs4])
            ACT(ypt[:], ypt[:], AF.Copy, scale=s4[:, 2:3], R=[rypt, rs4], W=[rypt])
            pt_, rpt_ = psbank()
            for cc in range(4):
                TR(pt_[:, cc * 128:(cc + 1) * 128], ypt[:, cc * 128:(cc + 1) * 128], identF, R=[rypt, r_cf], W=[rpt_])
            ust, rust = wbtile()
            TT('dve', ust[:].rearrange("p (c t) -> p c t", t=128), pt_[:].rearrange("p (c t) -> p c t", t=128),
               pp[:, l, PP_SNW:PP_SNW + 4].unsqueeze(2).broadcast_to([128, 4, 128]), ALU.mult, R=[rpt_, r_pp], W=[rust])
            P.dma('sp', usT.rearrange("(c p) t -> p c t", p=128)[:, :, i * 128:(i + 1) * 128],
                  ust[:].rearrange("p (c t) -> p c t", t=128), reads=[rust], writes=[r_usT[i]])

        def phase_ssd(l, last):
            phase_ssd_c1(l)
            if stop_after == 'ssd_c1':
                return
            P.barrier()
            P.dma('pool', wz[:], wsrc(w_in, l, OZ, 512), writes=[r_wz])
            MSET('dve', Hf[:], 0.0, W=[r_Hf]); MSET('dve', Hb[:], 0.0, W=[r_Hb])
            MSET('pool', Hfb[:], 0.0, W=[r_Hfb]); MSET('pool', Hbb[:], 0.0, W=[r_Hbb])
            for (s0, ntl, wo) in ((0, NTC, not last), (NTC, NTL, True)):
                ssd_pass1_a(l, s0, wo)
                for ii in range(ntl):
                    if ii + 1 < ntl:
                        ssd_pass1_a(l, s0 + ii + 1, wo)
                    ssd_pass1_b(l, s0 + ii, wo)
                if stop_after == 'ssd_p1':
                    continue
                for ii in reversed(range(ntl)):
                    ssd_pass2(l, s0 + ii, wo)


        vT = ar('D', [128, 4, T + 60]); r_vT = Res()
        dgc = ar('D', [128, 4, 31, 128]); r_dgc_d = Res(); r_dgc_a = Res()
        wgcv = ar('D', [128, KC, 512]); r_wgcv = Res()
        wug = [ar('D', [128, KC, 2, 128]) for i in range(2)]; r_wug = RL(2)
        ycsA = [ar('D', [128, 512], F32) for i in range(4)]
        wug_flat = wug[0]
        ycsB = [wug[i // 2].rearrange("p a b c -> p (a b c)")[:, (i % 2) * 1024:(i % 2) * 1024 + 1024].bitcast(F32) for i in range(4)]
        ycs2 = [ycsA, ycsB]; r_ycs2 = [RL(4), RL(4)]
        mD = ar('D', [128, 512], F32); m2D = ar('D', [128, 512], F32); r_mD = Res(); r_m2D = Res()
        r_ucT = RL(4 * len(groups))

        def vcol(t):
            return t + 15 if t < C else t + 45

        def phase_conv(l, last):
            MSET('pool', vT[:], 0.0, W=[r_vT])
            for cc in range(4):
                for k in range(31):
                    if k % 2 == 0:
                        TS('dve', dgc[:, cc, k, :], identF,
                           pp[:, l, PP_CCW + cc * 31 + k:PP_CCW + cc * 31 + k + 1], None, ALU.mult, R=[r_cf, r_pp], W=[r_dgc_d])
                    else:
                        ACT(dgc[:, cc, k, :], identF, AF.Copy, scale=pp[:, l, PP_CCW + cc * 31 + k:PP_CCW + cc * 31 + k + 1],
                            R=[r_cf, r_pp], W=[r_dgc_a])
            P.dma('pool', wgcv[:], wsrc(w_in, l, OGCV, 512), writes=[r_wgcv])
            for cc in range(4):
                par = cc % 2
                P.dma('pool', wug[par][:, :, 0, :], wsrc(w_in, l, OGLU + cc * 128, 128), writes=[r_wug[par]])
                P.dma('pool', wug[par][:, :, 1, :], wsrc(w_in, l, OGLU + 512 + cc * 128, 128), writes=[r_wug[par]])
                for (sq_, t0, n) in groups:
                    if last and sq_ == 0:
                        continue
                    pu, rpu = psbank()
                    pg, rpg = psbank()
                    for k in range(KC):
                        MM(pu[:, 0:n], wug[par][:, k, 0, :], hT[:, k, t0:t0 + n], start=(k == 0), stop=(k == KC - 1),
                           R=[r_wug[par]] + rh(t0, n), W=[rpu])
                    for k in range(KC):
                        MM(pg[:, 0:n], wug[par][:, k, 1, :], hT[:, k, t0:t0 + n], start=(k == 0), stop=(k == KC - 1),
                           R=[r_wug[par]] + rh(t0, n), W=[rpg])
                    sg, rsg = wtile()
                    ACT(sg[:, 0:n], pg[:, 0:n], AF.Sigmoid, R=[rpg], W=[rsg])
                    TT('dve', vT[:, cc, vcol(t0):vcol(t0) + n], pu[:, 0:n], sg[:, 0:n], ALU.mult, R=[rpu, rsg], W=[r_vT])
            P.barrier()
            for gi, (sq_, t0, n) in enumerate(groups):
                if last and sq_ == 0:
                    continue
                ycs, r_ycs = ycs2[gi % 2], r_ycs2[gi % 2]
                psm, rpsm = psbank()
                psq, rpsq = psbank()
                for cc in range(4):
                    pcv, rpcv = psbank()
                    for k in range(31):
                        MM(pcv[:, 0:n], dgc[:, cc, k, :], vT[:, cc, vcol(t0) + k - 15:vcol(t0) + k - 15 + n],
                           start=(k == 0), stop=(k == 30), R=[r_dgc_d, r_dgc_a, r_vT], W=[rpcv])
                    ACT(ycs[cc][:, 0:n], pcv[:, 0:n], AF.Identity, bias=pp[:, l, PP_CCB + cc:PP_CCB + cc + 1],
                        R=[rpcv, r_pp], W=[r_ycs[cc]])
                    ysq, rysq = wtile()
                    ACT(ysq[:, 0:n], ycs[cc][:, 0:n], AF.Square, R=[r_ycs[cc]], W=[rysq])
                    MM(psm[:, 0:n], cf[:, CF_ONES, :], ycs[cc][:, 0:n], start=(cc == 0), stop=(cc == 3),
                       R=[r_cf, r_ycs[cc]], W=[rpsm])
                    MM(psq[:, 0:n], cf[:, CF_ONES, :], ysq[:, 0:n], start=(cc == 0), stop=(cc == 3),
                       R=[r_cf, rysq], W=[rpsq])
                m, rm = mD, r_mD
                ACT(m[:, 0:n], psm[:, 0:n], AF.Copy, scale=1.0 / 512, R=[rpsm], W=[rm])
                m2, rm2 = m2D, r_m2D
                TT('pool', m2[:, 0:n], m[:, 0:n], m[:, 0:n], ALU.mult, R=[rm], W=[rm2])
                STT('dve', m2[:, 0:n], psq[:, 0:n], 1.0 / 512, m2[:, 0:n], ALU.mult, ALU.subtract, R=[rpsq, rm2], W=[rm2])
                ACT(m2[:, 0:n], m2[:, 0:n], AF.Sqrt, bias=EPS, R=[rm2], W=[rm2])
                RCP(m2[:, 0:n], m2[:, 0:n], R=[rm2], W=[rm2])
                for cc in range(4):
                    t_, rt_ = wtile()
                    TT('pool', t_[:, 0:n], ycs[cc][:, 0:n], m[:, 0:n], ALU.subtract, R=[r_ycs[cc], rm], W=[rt_])
                    TT('dve', t_[:, 0:n], t_[:, 0:n], m2[:, 0:n], ALU.mult, R=[rt_, rm2], W=[rt_])
                    ACT(t_[:, 0:n], t_[:, 0:n], AF.Identity, scale=pp[:, l, PP_CLW + cc:PP_CLW + cc + 1],
                        bias=pp[:, l, PP_CLB + cc:PP_CLB + cc + 1], R=[rt_, r_pp], W=[rt_])
                    ACT(t_[:, 0:n], t_[:, 0:n], AF.Silu, R=[rt_], W=[rt_])
                    pgc, rpgc = psbank()
                    for k in range(KC):
                        MM(pgc[:, 0:n], wgcv[:, k, cc * 128:(cc + 1) * 128], hT[:, k, t0:t0 + n], start=(k == 0),
                           stop=(k == KC - 1), R=[r_wgcv] + rh(t0, n), W=[rpgc])
                    sgc, rsgc = wtile()
                    ACT(sgc[:, 0:n], pgc[:, 0:n], AF.Silu, R=[rpgc], W=[rsgc])
                    ucb, rucb = wbtile()
                    TT('dve', ucb[:, 0:n], t_[:, 0:n], sgc[:, 0:n], ALU.mult, R=[rt_, rsgc], W=[rucb])
                    P.dma('sp', ucT[cc * 128:(cc + 1) * 128, t0:t0 + n], ucb[:, 0:n], reads=[rucb],
                          writes=[r_ucT[cc * len(groups) + gi]])

        wgm = ar('E', [128, KC, 3072]); r_wgm6 = RL(6)
        wbr = ar('E', [128, 4, 3, D]); r_wbr6 = RL(6)
        ut = [ar('E', [128, 4, 512]) for i in range(3)]; r_ut = RL(3)
        wo = ar('F', [128, KC, D]); r_wo = Res()
        yTt = ar('F', [128, KC, 512]); r_yTt = Res()
        zT = ar('F', [128, KC, 512], F32); r_zT = Res()
        xo = [ar('F', [128, D], F32) for i in range(2)]; r_xo = RL(2)
        fnw = ar('F', [128, D], F32); r_fnw = Res()
        xtF = [ar('F', [128, D], F32) for i in range(4)]
        xnF = [ar('F', [128, D], F32) for i in range(2)]
        r_yT_d = RL(len(groups))

        def phase_merge(l, last):
            wsrcs = (w_bra, w_brs, w_brc)
            for hf in range(2):
                for bi in range(3):
                    b6 = bi * 2 + hf
                    P.dma('pool', wgm[:, :, b6 * 512:(b6 + 1) * 512], wsrc(w_in, l, OGM + b6 * 512, 512), writes=[r_wgm6[b6]])
                    P.dma('pool', wbr[:, :, bi, hf * 512:(hf + 1) * 512],
                          wsrcs[bi][l, :, hf * 512:(hf + 1) * 512].rearrange("(c p) n -> p c n", p=128), writes=[r_wbr6[b6]])
            usrc = (uaT, usT, ucT)
            for gi, (sq_, t0, n) in enumerate(groups):
                if last and sq_ == 0:
                    continue
                for bi in range(3):
                    if bi == 0:
                        rr = [r_uaT[c * len(groups) + gi] for c in range(4)]
                    elif bi == 1:
                        rr = r_usT[t0 // 128:(t0 + n) // 128]
                    else:
                        rr = [r_ucT[c * len(groups) + gi] for c in range(4)]
                    P.dma('sp', ut[bi][:, :, 0:n], usrc[bi].rearrange("(c p) t -> p c t", p=128)[:, :, t0:t0 + n],
                          reads=rr, writes=[r_ut[bi]])
                for j in range(8):
                    acc, racc = wtile()
                    for bi in range(3):
                        pg, rpg = psbank()
                        for k in range(KC):
                            MM(pg[:, 0:n], wgm[:, k, bi * 1024 + j * 128:bi * 1024 + (j + 1) * 128], hT[:, k, t0:t0 + n],
                               start=(k == 0), stop=(k == KC - 1), R=[r_wgm6[bi * 2 + j // 4]] + rh(t0, n), W=[rpg])
                        gt, rgt = wtile()
                        ACT(gt[:, 0:n], pg[:, 0:n], AF.Identity, bias=pp[:, l, PP_BG + bi * 8 + j:PP_BG + bi * 8 + j + 1],
                            R=[rpg, r_pp], W=[rgt])
                        ACT(gt[:, 0:n], gt[:, 0:n], AF.Sigmoid, R=[rgt], W=[rgt])
                        pb, rpb = psbank()
                        for k4 in range(4):
                            MM(pb[:, 0:n], wbr[:, k4, bi, j * 128:(j + 1) * 128], ut[bi][:, k4, 0:n], start=(k4 == 0),
                               stop=(k4 == 3), R=[r_wbr6[bi * 2 + j // 4], r_ut[bi]], W=[rpb])
                        if bi == 0:
                            TT('dve', acc[:, 0:n], pb[:, 0:n], gt[:, 0:n], ALU.mult, R=[rpb, rgt], W=[racc])
                        else:
                            TT('dve', gt[:, 0:n], pb[:, 0:n], gt[:, 0:n], ALU.mult, R=[rpb, rgt], W=[rgt])
                            TT('pool', acc[:, 0:n], acc[:, 0:n], gt[:, 0:n], ALU.add, R=[racc, rgt], W=[racc])
                    yb, ryb = wbtile()
                    CP('pool', yb[:, 0:n], acc[:, 0:n], R=[racc], W=[ryb])
                    P.dma('sp', yT_d[j * 128:(j + 1) * 128, t0:t0 + n], yb[:, 0:n], reads=[ryb], writes=[r_yT_d[gi]])

        def phase_out(l, last):
            xt, xn = xtF, xnF
            P.dma('sp', fnw[:], fnw_in[:, :], writes=[r_fnw])
            for hf in range(2):
                P.dma('pool', wo[:, :, hf * 512:(hf + 1) * 512], wsrc(w_out, l, hf * 512, 512), writes=[r_wo])
            for gi, (sq_, t0, n) in enumerate(groups):
                if last and sq_ == 0:
                    continue
                col = 1 if sq_ == 0 else 0
                P.dma('sp', yTt[:, :, 0:n], yT_d.rearrange("(c p) t -> p c t", p=128)[:, :, t0:t0 + n],
                      reads=[r_yT_d[gi]], writes=[r_yTt])
                for j2 in range(8):
                    po, rpo = psbank()
                    for j in range(8):
                        MM(po[:, 0:n], wo[:, j, j2 * 128:(j2 + 1) * 128], yTt[:, j, 0:n], start=(j == 0), stop=(j == 7),
                           R=[r_wo, r_yTt], W=[rpo])
                    ACT(zT[:, j2, 0:n], po[:, 0:n], AF.Copy, scale=modv[:, 16 + j2, col:col + 1], R=[rpo, r_modv], W=[r_zT])
                for si in range(n // 128):
                    i = t0 // 128 + si
                    s = i % 2
                    sx = i % 4
                    rsrc = [r_x1[i]] if l > 0 else []
                    P.dma('sp', xt[sx][:], src_tile(l, i), reads=rsrc, writes=[r_xt[sx]])
                    for b in range(2):
                        pb, rpb = psbank()
                        for kk in range(4):
                            j2 = b * 4 + kk
                            TR(pb[:, kk * 128:(kk + 1) * 128], zT[:, j2, si * 128:(si + 1) * 128], identF,
                               R=[r_zT, r_cf], W=[rpb])
                        TT('dve', xo[s][:, b * 512:(b + 1) * 512], pb[:, :], xt[sx][:, b * 512:(b + 1) * 512], ALU.add,
                           R=[rpb, r_xt[sx]], W=[r_xo[s]])
                    if not last:
                        dst = (xc1 if i < NTC else x1)[((i if i < NTC else i - NTC) * 128):((i if i < NTC else i - NTC) * 128 + 128), :]
                        P.dma('pool', dst, xo[s][:], reads=[r_xo[s]], writes=[r_x1[i]])
                    else:
                        s4, rs4 = st4[s], r_st4[s]
                        ACT(xn[s][:], xo[s][:], AF.Square, accum=s4[:, 0:1], R=[r_xo[s]], W=[r_xn[s], rs4])
                        ACT(s4[:, 1:2], s4[:, 0:1], AF.Sqrt, bias=EPS, scale=1.0 / D, R=[rs4], W=[rs4])
                        RCP(s4[:, 2:3], s4[:, 1:2], R=[rs4], W=[rs4])
                        ACT(xn[s][:], xo[s][:], AF.Copy, scale=s4[:, 2:3], R=[r_xo[s], rs4], W=[r_xn[s]])
                        TT('pool', xn[s][:], xn[s][:], fnw[:], ALU.mult, R=[r_xn[s], r_fnw], W=[r_xn[s]])
                        P.dma('pool', out[(i - NTC) * 128:(i - NTC + 1) * 128, :], xn[s][:], reads=[r_xn[s]])

        for l in range(DEPTH):
            last = (l == DEPTH - 1)
            P.barrier()
            phase_mod(l)
            P.barrier()
            phase_hT(l)
            P.barrier()
            if stop_after == 'hT':
                break
            phase_kv(l)
            phase_attn(l, last)
            if stop_after == 'attn':
                break
            P.barrier()
            phase_ssd(l, last)
            if stop_after in ('ssd', 'ssd_c1', 'ssd_p1'):
                break
            P.barrier()
            phase_conv(l, last)
            if stop_after == 'conv':
                break
            P.barrier()
            phase_merge(l, last)
            P.barrier()
            phase_out(l, last)
        if dbg:
            hT_o = nc.dram_tensor("hT_o", [128, KC, T], BF16, kind="ExternalOutput").ap()
            P.dma('sp', hT_o[:, :, :], hT[:], reads=r_hT)
        P.emit()
    return nc


def _cols(v):
    v = np.asarray(v, np.float32)
    return np.ascontiguousarray(v.reshape(-1, 128).T)


def host_prep(inp, L, C, DEPTH, nb):
    f32 = np.float32
    cfc, cbc = host_consts()
    cosT, sinT = host_rope(L, C)
    pp = np.zeros((DEPTH, 128, NPP), f32)
    pr = np.zeros((DEPTH, 128, NPR), f32)
    for l in range(DEPTH):
        pp[l, :, PP_BMOD:PP_BMOD + 24] = _cols(inp['b_mod'][l])
        pp[l, :, PP_NORMW:PP_NORMW + 8] = _cols(inp['norm_w'][l])
        pp[l, :, PP_QW] = np.tile(np.asarray(inp['q_norm_w'][l], f32), 2)
        pp[l, :, PP_KW] = np.tile(np.asarray(inp['k_norm_w'][l], f32), 2)
        scw = np.asarray(inp['ssd_conv_w'][l], f32)
        pp[l, :, PP_SCW:PP_SCW + 40] = scw.reshape(5, 8, 128).transpose(2, 1, 0).reshape(128, 40)
        pp[l, :, PP_SCB:PP_SCB + 8] = _cols(inp['ssd_conv_b'][l])
        pp[l, :, PP_SNW:PP_SNW + 4] = _cols(inp['ssd_norm_w'][l])
        ccw = np.asarray(inp['cm_conv_w'][l], f32)
        pp[l, :, PP_CCW:PP_CCW + 124] = ccw.reshape(31, 4, 128).transpose(2, 1, 0).reshape(128, 124)
        pp[l, :, PP_CCB:PP_CCB + 4] = _cols(inp['cm_conv_b'][l])
        pp[l, :, PP_CLW:PP_CLW + 4] = _cols(inp['cm_ln_w'][l])
        pp[l, :, PP_CLB:PP_CLB + 4] = _cols(inp['cm_ln_b'][l])
        pp[l, :, PP_BG:PP_BG + 24] = _cols(np.asarray(inp['b_gate'][l], f32).reshape(-1))
        pr[l, :, PR_DTB:PR_DTB + 16] = np.asarray(inp['ssd_dt_bias'][l], f32).reshape(1, 16)
        pr[l, :, PR_ALOG:PR_ALOG + 16] = np.asarray(inp['ssd_A_log'][l], f32).reshape(1, 16)
        pr[l, :, PR_D:PR_D + 8] = np.asarray(inp['ssd_D'][l], f32).reshape(1, 8)
    fnw = np.ascontiguousarray(np.broadcast_to(np.asarray(inp['final_norm_w'], f32)[None, :], (128, D)))
    shared = {
        'w_mod': np.ascontiguousarray(inp['w_mod'], f32), 'w_in': np.ascontiguousarray(inp['w_in'], f32),
        'w_bra': np.ascontiguousarray(inp['w_br_attn'], f32), 'w_brs': np.ascontiguousarray(inp['w_br_ssd'], f32),
        'w_brc': np.ascontiguousarray(inp['w_br_conv'], f32), 'w_out': np.ascontiguousarray(inp['w_out'], f32),
        'pp': pp, 'pr': pr, 'scb': np.ascontiguousarray(np.asarray(inp['ssd_conv_b'], f32).reshape(DEPTH, 1, 1024)), 'fnw': fnw, 'cf': cfc, 'cb': cbc, 'cosT': cosT, 'sinT': sinT,
    }
    maps = []
    cctx = _cols(inp['c_ctx'])
    for b in range(nb):
        cvec = np.zeros((128, 8, 2), f32)
        cvec[:, :, 0] = _cols(inp['c'][b])
        cvec[:, :, 1] = cctx
        m = dict(shared)
        m['x'] = np.ascontiguousarray(inp['x'][b], f32)
        m['ctx'] = np.ascontiguousarray(inp['ctx'][b], f32)
        m['cvec'] = cvec.reshape(128, 16)
        maps.append(m)
    return maps


_NC_CACHE = {}


def kernel(**inputs):
    x = np.asarray(inputs['x'])
    B, L, _ = x.shape
    C = np.asarray(inputs['ctx']).shape[1]
    DEPTH = np.asarray(inputs['w_in']).shape[0]
    key = (L, C, DEPTH)
    if key not in _NC_CACHE:
        _NC_CACHE[key] = build(L, C, DEPTH)
    nc = _NC_CACHE[key]
    maps = host_prep(inputs, L, C, DEPTH, B)
    res = run_bass_kernel_spmd(nc, maps, core_ids=list(range(B)))
    return np.stack([np.asarray(r['out'], np.float32) for r in res.results], axis=0)
```

```python
import numpy as np
from contextlib import ExitStack
import concourse.bass as bass
import concourse.mybir as mybir
from concourse.bass_utils import run_bass_kernel_spmd

F32 = mybir.dt.float32
BF16 = mybir.dt.bfloat16
AF = mybir.ActivationFunctionType
ALU = mybir.AluOpType

COMPUTE = ('pe', 'act', 'dve', 'pool')
NDMASEM = 12


class Res:
    __slots__ = ('w', 'rd', 'rdma')

    def __init__(self):
        self.w = None
        self.rd = {}
        self.rdma = []


def RL(n):
    return [Res() for _ in range(n)]


class Op:
    __slots__ = ('eng', 'fn', 'deps', 'dma', 'needs_inc', 'cnt', 'sem', 'target', 'prewait', 'batch')

    def __init__(self, eng, fn, dma):
        self.batch = None
        self.eng = eng
        self.fn = fn
        self.dma = dma
        self.deps = []
        self.needs_inc = False
        self.cnt = 0
        self.sem = None
        self.target = 0
        self.prewait = None


class Prog:
    def __init__(self, nc):
        self.nc = nc
        self.streams = {e: [] for e in ('pe', 'act', 'dve', 'pool', 'sp')}
        self.pending = {}
        self.dmas = []
        self.cur_batch = None
        self.batches = []

    def batch_begin(self):
        self.cur_batch = {}
        self.batches.append(self.cur_batch)

    def batch_end(self):
        self.cur_batch = None

    def barrier(self):
        deps = []
        for e, st in self.streams.items():
            for op in reversed(st):
                if not op.dma:
                    deps.append(op)
                    break
        deps.extend(self.dmas)
        self.dmas = []
        self.pending = {e: list(deps) for e in self.streams}

    def add(self, eng, fn, reads=(), writes=(), dma=False):
        op = Op(eng, fn, dma)
        deps = {}
        if self.pending.get(eng):
            for o in self.pending.pop(eng):
                deps[id(o)] = o
        if dma:
            self.dmas.append(op)
        for r in reads:
            if r.w is not None:
                deps[id(r.w)] = r.w
        for w in writes:
            if w.w is not None:
                deps[id(w.w)] = w.w
            for o in w.rd.values():
                deps[id(o)] = o
            for o in w.rdma:
                deps[id(o)] = o
        for r in reads:
            if dma:
                r.rdma.append(op)
            else:
                r.rd[eng] = op
        for w in writes:
            w.w = op
            w.rd = {}
            w.rdma = []
        for d in deps.values():
            if d is op:
                continue
            if (not d.dma) and (not dma) and d.eng == eng and eng == 'pe':
                continue
            op.deps.append(d)
            if not d.dma:
                d.needs_inc = True
        if self.cur_batch is not None and not dma and eng == 'pe':
            op.batch = self.cur_batch
            self.cur_batch.setdefault(eng, []).append(op)
        self.streams[eng].append(op)
        return op

    def coarsen(self):
        for st in self.streams.values():
            for x in st:
                nd = []
                for d in x.deps:
                    if d.batch is not None and not d.dma and not (x.batch is d.batch and x.eng == d.eng):
                        d = d.batch[d.eng][-1]
                    nd.append(d)
                x.deps = nd
        for b in self.batches:
            for e, ops in b.items():
                union = {}
                for o in ops:
                    for d in o.deps:
                        if d.batch is b and d.eng == e and not d.dma:
                            continue
                        union[id(d)] = d
                    o.deps = []
                ops[0].deps = list(union.values())
        for st in self.streams.values():
            for x in st:
                x.needs_inc = False
        for st in self.streams.values():
            for x in st:
                for d in x.deps:
                    if not d.dma:
                        d.needs_inc = True

    def dma(self, q, out, in_, reads=(), writes=(), slow=False):
        if slow:
            return self.add(q, lambda e: e.dma_start(out=out, in_=in_, allow_slow_non_contiguous=True),
                            reads, writes, dma=True)
        return self.add(q, lambda e: e.dma_start(out=out, in_=in_), reads, writes, dma=True)

    def emit(self):
        nc = self.nc
        self.coarsen()
        with ExitStack() as es:
            csem = {e: es.enter_context(nc.semaphore('cs_' + e)) for e in COMPUTE}
            dsem = {}
            for q in ('sp', 'act', 'pool'):
                dsem[q] = [es.enter_context(nc.semaphore('ds_%s%d' % (q, i))) for i in range(NDMASEM)]
            for e, st in self.streams.items():
                c = 0
                nd = 0
                for op in st:
                    if op.dma:
                        op.sem = dsem[e][nd % NDMASEM]
                        op.target = 16 * (nd // NDMASEM + 1)
                        if nd >= NDMASEM:
                            op.prewait = (op.sem, 16 * (nd // NDMASEM))
                        nd += 1
                    else:
                        if op.needs_inc:
                            c += 1
                        op.cnt = c
            block = es.enter_context(nc.Block())
            streams = self.streams

            def run_stream(ename, eng):
                known = {}
                for op in streams[ename]:
                    need = {}
                    if op.prewait is not None:
                        need[id(op.prewait[0])] = op.prewait
                    for d in op.deps:
                        if d.dma:
                            s, v = d.sem, d.target
                        else:
                            s, v = csem[d.eng], d.cnt
                        k = id(s)
                        if k not in need or need[k][1] < v:
                            need[k] = (s, v)
                    for k, (s, v) in need.items():
                        if known.get(k, 0) >= v:
                            continue
                        eng.wait_ge(s, v)
                        known[k] = v
                    ins = op.fn(eng)
                    if op.dma:
                        ins.then_inc(op.sem, 16)
                    elif op.needs_inc:
                        ins.then_inc(csem[ename], 1)
                if ename == 'sp':
                    for q in ('sp', 'act', 'pool'):
                        last = {}
                        for op in streams[q]:
                            if op.dma:
                                last[id(op.sem)] = (op.sem, op.target)
                        for k, (s, v) in last.items():
                            if known.get(k, 0) < v:
                                eng.wait_ge(s, v)
                                known[k] = v

            @block.tensor
            def _(eng):
                run_stream('pe', eng)

            @block.scalar
            def _(eng):
                run_stream('act', eng)

            @block.vector
            def _(eng):
                run_stream('dve', eng)

            @block.gpsimd
            def _(eng):
                run_stream('pool', eng)

            @block.sync
            def _(eng):
                run_stream('sp', eng)


D = 1024
KC = 8
OQ, OK_, OV, OGA, OXBC, ODT, OZ, OGLU, OGCV, OGM = 0, 512, 640, 768, 1280, 2304, 2320, 2832, 3856, 4368
INCOLS = 7440
EPS = 1e-6
NEG = -30000.0
PP_BMOD, PP_NORMW, PP_QW, PP_KW, PP_SCW, PP_SCB, PP_SNW, PP_CCW, PP_CCB, PP_CLW, PP_CLB, PP_BG = \
    0, 24, 32, 33, 34, 74, 82, 86, 210, 214, 218, 222
NPP = 246
PR_DTB, PR_ALOG, PR_D, PR_SCB = 0, 16, 32, 40
NPR = 40
CF_ID, CF_TRIF, CF_TRIB, CF_STRIF, CF_STRIB, CF_ONES = range(6)
CB_ID, CB_NEGF, CB_NEGB, CB_BLK, CB_RROT, CB_ONES, CB_TRIF, CB_TRIB = range(8)


def host_consts():
    j = np.arange(128)[:, None]
    s = np.arange(128)[None, :]
    cf = np.zeros((128, 6, 128), np.float32)
    cf[:, CF_ID] = (j == s)
    cf[:, CF_TRIF] = (j <= s)
    cf[:, CF_TRIB] = (j >= s)
    cf[:, CF_STRIF] = (j > s)
    cf[:, CF_STRIB] = (j < s)
    cf[:, CF_ONES] = 1.0
    cb = np.zeros((128, 8, 128), np.float32)
    cb[:, CB_ID] = (j == s)
    cb[:, CB_NEGF] = NEG * (s < j)
    cb[:, CB_NEGB] = NEG * (s > j)
    cb[:, CB_BLK] = ((j // 64) == (s // 64))
    rr = np.zeros((128, 128), np.float32)
    for m in range(128):
        if (m % 32) < 16:
            rr[m + 16, m] = -1.0
        else:
            rr[m - 16, m] = 1.0
    cb[:, CB_RROT] = rr
    cb[:, CB_ONES] = 1.0
    cb[:, CB_TRIF] = (j <= s)
    cb[:, CB_TRIB] = (j >= s)
    return cf.reshape(128, 768), cb.reshape(128, 1024)


def host_rope(L, C):
    T = C + L
    t = np.arange(L)
    row = (t // 64).astype(np.float32)
    col = (t % 64).astype(np.float32)
    inv = (1.0 / (10000.0 ** (np.arange(16, dtype=np.float32) / 16))).astype(np.float32)
    cosT = np.ones((128, T), np.float32)
    sinT = np.zeros((128, T), np.float32)
    for p in range(128):
        d = p % 64
        pos = row if d < 32 else col
        ang = (pos * inv[d % 16]).astype(np.float32)
        cosT[p, C:] = np.cos(ang)
        sinT[p, C:] = np.sin(ang)
    return cosT, sinT


def build(L, C, DEPTH, dbg=False, stop_after=None):
    nc = bass.Bass("TRN2", target_bir_lowering=False)
    T = C + L
    NT = T // 128
    NTC = C // 128
    NTL = L // 128
    P = Prog(nc)

    def din(name, shape, dt=F32):
        return nc.dram_tensor(name, shape, dt, kind="ExternalInput").ap()

    def dscr(name, shape, dt=F32):
        return nc.dram_tensor(name, shape, dt, kind=("ExternalOutput" if dbg else "Internal")).ap()

    x_in = din("x", [L, D])
    ctx_in = din("ctx", [C, D])
    cvec_in = din("cvec", [128, 16])
    w_mod = din("w_mod", [DEPTH, D, 3 * D])
    w_in = din("w_in", [DEPTH, D, INCOLS])
    w_bra = din("w_bra", [DEPTH, 512, D])
    w_brs = din("w_brs", [DEPTH, 512, D])
    w_brc = din("w_brc", [DEPTH, 512, D])
    w_out = din("w_out", [DEPTH, D, D])
    pp_in = din("pp", [DEPTH, 128, NPP])
    pr_in = din("pr", [DEPTH, 128, NPR])
    fnw_in = din("fnw", [128, D])
    scb_in = din("scb", [DEPTH, 1, 1024])
    cf_in = din("cf", [128, 768])
    cb_in = din("cb", [128, 1024])
    cos_in = din("cosT", [128, T])
    sin_in = din("sinT", [128, T])
    out = nc.dram_tensor("out", [L, D], F32, kind="ExternalOutput").ap()

    x1 = dscr("x1", [L, D])
    xc1 = dscr("xc1", [C, D])
    uaT = dscr("uaT", [512, T], BF16)
    usT = dscr("usT", [512, T], BF16)
    ucT = dscr("ucT", [512, T], BF16)
    yT_d = dscr("yT", [D, T], BF16)
    xs_d = dscr("xs_tm", [T, 512], BF16)
    b_d = dscr("b_tm", [T, 256], BF16)
    yp_d = dscr("ypart", [T, 512])
    sb_d = dscr("sbst", [NT, 128, 512])

    groups = [(0, 0, C)] + [(1, C + 512 * i, 512) for i in range(L // 512)]

    with ExitStack() as es:
        def sb(name, shape, dt=F32):
            return es.enter_context(nc.sbuf_tensor("s_" + name, shape, dt))

        ARN = 49152
        AR = sb("arena", [128, ARN], BF16)
        aoff = {}

        def ar(phase, shape, dt=BF16):
            n = 1
            for v in shape[1:]:
                n *= v
            nb = n * (2 if dt == BF16 else 4)
            off = (aoff.get(phase, 0) + 3) // 4 * 4
            aoff[phase] = off + nb
            assert aoff[phase] <= ARN * 2, (phase, aoff[phase])
            ap = AR[0:shape[0], off // 2:(off + nb) // 2]
            if dt == F32:
                ap = ap.bitcast(F32)
            if len(shape) == 2:
                return ap
            names = 'abcd'[:len(shape) - 1]
            pat = "p (" + " ".join(names) + ") -> p " + " ".join(names)
            kw = {names[i]: shape[1 + i] for i in range(1, len(names))}
            return ap.rearrange(pat, **kw)

        PP2 = [es.enter_context(nc.psum_tensor("pp%d" % i, [128, 1024], F32)) for i in range(4)]
        PS = []
        for i in range(4):
            PS.append(PP2[i][:, 0:512])
            PS.append(PP2[i][:, 512:1024])
        rPS = RL(8)

        def MM(o, lhsT, rhs, start=True, stop=True, R=(), W=()):
            P.add('pe', lambda e: e.matmul(o, lhsT=lhsT, rhs=rhs, start=start, stop=stop), R, W)

        def TR(o, in_, ident, R=(), W=()):
            P.add('pe', lambda e: e.transpose(out=o, in_=in_, identity=ident), R, W)

        def ACT(o, in_, func, bias=None, scale=None, accum=None, R=(), W=()):
            kw = {}
            if bias is not None:
                kw['bias'] = bias
            if scale is not None:
                kw['scale'] = scale
            if accum is not None:
                kw['accum_out'] = accum
            P.add('act', lambda e: e.activation(out=o, in_=in_, func=func, **kw), R, W)

        def TT(eng, o, a, b, op, R=(), W=()):
            P.add(eng, lambda e: e.tensor_tensor(out=o, in0=a, in1=b, op=op), R, W)

        def TS(eng, o, a, s1, s2, op0, op1=None, R=(), W=()):
            if op1 is None:
                P.add(eng, lambda e: e.tensor_scalar(out=o, in0=a, scalar1=s1, scalar2=None, op0=op0), R, W)
            else:
                P.add(eng, lambda e: e.tensor_scalar(out=o, in0=a, scalar1=s1, scalar2=s2, op0=op0, op1=op1), R, W)

        def STT(eng, o, a, s, b, op0, op1, R=(), W=()):
            P.add(eng, lambda e: e.scalar_tensor_tensor(out=o, in0=a, scalar=s, in1=b, op0=op0, op1=op1), R, W)

        def CP(eng, o, a, R=(), W=()):
            P.add(eng, lambda e: e.tensor_copy(out=o, in_=a), R, W)

        def RCP(o, a, R=(), W=()):
            P.add('dve', lambda e: e.reciprocal(out=o, in_=a), R, W)

        def MSET(eng, o, v, W=()):
            P.add(eng, lambda e: e.memset(o, v), (), W)

        def wsrc(w, l, c0, n):
            return w[l, :, c0:c0 + n].rearrange("(c p) n -> p c n", p=128)

        cf = sb("cf", [128, 6, 128]); r_cf = Res()
        cb = sb("cb", [128, 8, 128], BF16); r_cb = Res()
        pp = sb("pp", [128, DEPTH, NPP]); r_pp = Res()
        pr = sb("pr", [128, DEPTH, NPR]); r_pr = Res()
        cv = sb("cv", [128, 16]); r_cv = Res()
        csl = [[ar('B', [128, 512], F32) for b in range(2)] for a in range(2)]; r_csl = RL(2)
        cslc = [0]
        P.dma('sp', cf[:].rearrange("p a b -> p (a b)"), cf_in[:, :], writes=[r_cf])
        P.dma('pool', cb[:].rearrange("p a b -> p (a b)"), cb_in[:, :], writes=[r_cb])
        P.dma('sp', pp[:], pp_in.rearrange("l p n -> p l n"), writes=[r_pp])
        P.dma('sp', pr[:], pr_in.rearrange("l p n -> p l n"), writes=[r_pr])
        P.dma('sp', cv[:], cvec_in[:, :], writes=[r_cv])
        ones512 = sb("ones512", [1, 512], BF16); r_ones512 = Res()
        MSET('dve', ones512[0:1, :], 1.0, W=[r_ones512])
        identF = cf[:, CF_ID, :]
        identB = cb[:, CB_ID, :]

        hT = sb("hT", [128, KC, T], BF16); r_hT = RL(NT)

        def rh(t0, n):
            return r_hT[t0 // 128:(t0 + n) // 128]

        NW = 4
        wsl = [ar('M', [128, KC, 512], BF16) for i in range(NW)]
        r_wsl = RL(NW)
        wctr = [0]

        def wslot():
            i = wctr[0] % NW
            wctr[0] += 1
            return wsl[i], r_wsl[i]

        NWF = 10
        wf = [sb("wf%d" % i, [128, 512]) for i in range(NWF)]
        r_wf = RL(NWF)
        wfc = [0]

        def wtile():
            i = wfc[0] % NWF
            wfc[0] += 1
            return wf[i], r_wf[i]

        NWB = 8
        wb = [sb("wb%d" % i, [128, 512], BF16) for i in range(NWB)]
        r_wb = RL(NWB)
        wbc = [0]

        def wbtile():
            i = wbc[0] % NWB
            wbc[0] += 1
            return wb[i], r_wb[i]

        psc = [0]

        def psbank(lo=0, hi=8):
            i = lo + psc[0] % (hi - lo)
            psc[0] += 1
            return PS[i], rPS[i]

        modv = sb("modv", [128, 24, 2]); r_modv = Res()
        A1 = sb("A1", [128, 8, 2]); r_A1 = Res()
        scb = sb("scb", [128, 8, 2], BF16); r_scb = Res()

        xtA = [ar('A', [128, D], F32) for i in range(4)]; r_xt = RL(4)
        xnA = [ar('A', [128, D], F32) for i in range(2)]; r_xn = RL(2)
        junk = ar('A', [128, D], F32); r_junk = Res()
        xt = xtA
        xn = xnA
        st4 = [sb("st4_%d" % i, [128, 4]) for i in range(2)]; r_st4 = RL(2)

        def src_tile(l, i):
            if l == 0:
                return (ctx_in if i < NTC else x_in)[((i if i < NTC else i - NTC) * 128):((i if i < NTC else i - NTC) * 128 + 128), :]
            return (xc1 if i < NTC else x1)[((i if i < NTC else i - NTC) * 128):((i if i < NTC else i - NTC) * 128 + 128), :]

        r_x1 = RL(NT)

        def phase_mod(l):
            ACT(scb[:].rearrange("p a b -> p (a b)"), cv[:, :], AF.Silu, R=[r_cv], W=[r_scb])
            pm, rpm = psbank()
            for blk in range(6):
                ws, rws = wslot()
                P.dma('pool', ws[:], wsrc(w_mod, l, blk * 512, 512), writes=[rws])
                for jj in range(4):
                    j = blk * 4 + jj
                    for k in range(KC):
                        MM(pm[:, 2 * j:2 * j + 2], ws[:, k, jj * 128:(jj + 1) * 128], scb[:, k, :],
                           start=(k == 0), stop=(k == KC - 1), R=[rws, r_scb], W=[rpm])
            TT('dve', modv[:], pm[:, 0:48].rearrange("p (j t) -> p j t", t=2),
               pp[:, l, PP_BMOD:PP_BMOD + 24].unsqueeze(2).broadcast_to([128, 24, 2]), ALU.add,
               R=[rpm, r_pp], W=[r_modv])
            STT('dve', A1[:], modv[:, 8:16, :], 1.0,
                pp[:, l, PP_NORMW:PP_NORMW + 8].unsqueeze(2).broadcast_to([128, 8, 2]),
                ALU.add, ALU.mult, R=[r_modv, r_pp], W=[r_A1])

        def phase_hT(l):
            for i in range(NT):
                s = i % 2
                sx = i % 4
                col = 1 if i < NTC else 0
                rsrc = [r_x1[i]] if l > 0 else []
                P.dma('sp', xt[sx][:], src_tile(l, i), reads=rsrc, writes=[r_xt[sx]])
                ACT(junk[:], xt[sx][:], AF.Square, accum=st4[s][:, 0:1], R=[r_xt[sx]], W=[r_junk, r_st4[s]])
                ACT(st4[s][:, 1:2], st4[s][:, 0:1], AF.Sqrt, bias=EPS, scale=1.0 / D, R=[r_st4[s]], W=[r_st4[s]])
                RCP(st4[s][:, 2:3], st4[s][:, 1:2], R=[r_st4[s]], W=[r_st4[s]])
                ACT(xn[s][:], xt[sx][:], AF.Copy, scale=st4[s][:, 2:3], R=[r_xt[sx], r_st4[s]], W=[r_xn[s]])
                for b in range(2):
                    pb, rpb = psbank()
                    for kk in range(4):
                        k = b * 4 + kk
                        TR(pb[:, kk * 128:(kk + 1) * 128], xn[s][:, k * 128:(k + 1) * 128], identF,
                           R=[r_xn[s], r_cf], W=[rpb])
                    tw, rtw = wtile()
                    TT('dve', tw[:].rearrange("p (a b) -> p a b", b=128), pb[:].rearrange("p (a b) -> p a b", b=128),
                       A1[:, 4 * b:4 * b + 4, col:col + 1].broadcast_to([128, 4, 128]), ALU.mult,
                       R=[rpb, r_A1], W=[rtw])
                    TT('pool', hT[:, 4 * b:4 * b + 4, i * 128:(i + 1) * 128], tw[:].rearrange("p (a b) -> p a b", b=128),
                       modv[:, 4 * b:4 * b + 4, col:col + 1].broadcast_to([128, 4, 128]), ALU.add,
                       R=[rtw, r_modv], W=[r_hT[i]])

        kT = ar('B', [128, 2, T]); r_kT = Res()
        Vx = ar('B', [128, NT, 2, 128]); r_Vx = Res()
        wkd = ar('B', [128, KC, 2, 128]); r_wkd = Res()
        wv = ar('B', [128, KC, 128]); r_wv = Res()

        def qk_epilogue(l, pq, rpq, n, t0, wcolidx, o, ro):
            sq, rsq = wbtile()
            ACT(sq[:, 0:n], pq[:, 0:n], AF.Square, R=[rpq], W=[rsq])
            p2, rp2 = psbank()
            MM(p2[:, 0:n], cb[:, CB_BLK, :], sq[:, 0:n], R=[rsq, r_cb], W=[rp2])
            rs, rrs = wtile()
            ACT(rs[:, 0:n], p2[:, 0:n], AF.Ln, bias=EPS, scale=1.0 / 64, R=[rp2], W=[rrs])
            ACT(rs[:, 0:n], rs[:, 0:n], AF.Exp, scale=-0.5, R=[rrs], W=[rrs])
            qw, rqw = wtile()
            ACT(qw[:, 0:n], pq[:, 0:n], AF.Copy, scale=pp[:, l, wcolidx:wcolidx + 1], R=[rpq, r_pp], W=[rqw])
            TT('dve', qw[:, 0:n], qw[:, 0:n], rs[:, 0:n], ALU.mult, R=[rqw, rrs], W=[rqw])
            qb, rqb = wbtile()
            CP('dve', qb[:, 0:n], qw[:, 0:n], R=[rqw], W=[rqb])
            p3, rp3 = psbank()
            MM(p3[:, 0:n], cb[:, CB_RROT, :], qb[:, 0:n], R=[rqb, r_cb], W=[rp3])
            ci = cslc[0] % 2
            cslc[0] += 1
            P.dma('sp', csl[ci][0][:, 0:n], cos_in[:, t0:t0 + n], writes=[r_csl[ci]])
            P.dma('sp', csl[ci][1][:, 0:n], sin_in[:, t0:t0 + n], writes=[r_csl[ci]])
            t1, rt1 = wtile()
            TT('pool', t1[:, 0:n], qw[:, 0:n], csl[ci][0][:, 0:n], ALU.mult, R=[rqw, r_csl[ci]], W=[rt1])
            t2, rt2 = wtile()
            TT('dve', t2[:, 0:n], p3[:, 0:n], csl[ci][1][:, 0:n], ALU.mult, R=[rp3, r_csl[ci]], W=[rt2])
            TT('dve', o, t1[:, 0:n], t2[:, 0:n], ALU.add, R=[rt1, rt2], W=[ro])

        def phase_kv(l):
            MSET('pool', Vx[:, :, :, 64:128], 1.0, W=[r_Vx])
            for g in range(2):
                for h in range(2):
                    P.dma('pool', wkd[:, :, g, h * 64:(h + 1) * 64], wsrc(w_in, l, OK_ + g * 64, 64), writes=[r_wkd])
            P.dma('pool', wv[:], wsrc(w_in, l, OV, 128), writes=[r_wv])
            for (sq_, t0, n) in groups:
                for g in range(2):
                    pk, rpk = psbank()
                    for k in range(KC):
                        MM(pk[:, 0:n], wkd[:, k, g, :], hT[:, k, t0:t0 + n], start=(k == 0), stop=(k == KC - 1),
                           R=[r_wkd] + rh(t0, n), W=[rpk])
                    qk_epilogue(l, pk, rpk, n, t0, PP_KW, kT[:, g, t0:t0 + n], r_kT)
            for i0 in range(0, NT, 4):
                nb = min(4, NT - i0)
                pvv, rpv = psbank()
                for ii in range(nb):
                    i = i0 + ii
                    for k in range(KC):
                        MM(pvv[:, ii * 128:(ii + 1) * 128], hT[:, k, i * 128:(i + 1) * 128], wv[:, k, :],
                           start=(k == 0), stop=(k == KC - 1), R=[r_wv, r_hT[i]], W=[rpv])
                CP('dve', Vx[:, i0:i0 + nb, :, 0:64],
                   pvv[:, 0:nb * 128].rearrange("p (a g d) -> p a g d", g=2, d=64), R=[rpv], W=[r_Vx])

        qT = [ar('B', [128, T]) for i in range(2)]; r_qT = RL(2)
        sga = [ar('B', [128, T]) for i in range(2)]; r_sga = RL(2)
        wqg = [ar('B', [128, KC, 2, 128]) for i in range(2)]; r_wqg = RL(2)
        pts = [ar('B', [128, 1024]) for i in range(3)]; r_pts = RL(3)
        r_uaT = RL(4 * len(groups))
        SB_ = None
        OBK = (6, 7)
        attc = [0]

        def attend(l, c, par, q0, nq, key_tiles, gi, hooks=()):
            g = c // 2
            nk = len(key_tiles)
            LOOK = 1
            issued = []
            hooks = list(hooks)
            hstep = max(1, (nk - 2) // max(1, len(hooks))) if hooks else 1

            def issue_s(j):
                a = attc[0]
                attc[0] += 1
                pr_ = a % 2
                pt, rpt = pts[a % 3], r_pts[a % 3]
                for half in range(2):
                    hs = slice(half * 64, half * 64 + 64)
                    MM(PS[2 * pr_ + half][:, 0:nq], kT[hs, g, j * 128:(j + 1) * 128], qT[par][hs, q0:q0 + nq],
                       R=[r_kT, r_qT[par]], W=[rPS[2 * pr_ + half]])
                issued.append((pr_, pt, rpt))

            def issue_exp(n_):
                pr_, pt, rpt = issued[n_]
                if nq == 512:
                    ACT(pt[:, 0:1024], PP2[pr_][:, 0:1024], AF.Exp, scale=0.125, R=[rPS[2 * pr_], rPS[2 * pr_ + 1]], W=[rpt])
                else:
                    for half in range(2):
                        ACT(pt[:, half * 512:half * 512 + nq], PS[2 * pr_ + half][:, 0:nq], AF.Exp, scale=0.125,
                            R=[rPS[2 * pr_ + half]], W=[rpt])

            def finish():
                ua, rua = wtile()
                uab, ruab = wbtile()
                osb = []
                for half in range(2):
                    o_, ro_ = wtile()
                    CP('dve', o_[:, 0:nq], PS[OBK[half]][:, 0:nq], R=[rPS[OBK[half]]], W=[ro_])
                    osb.append((o_, ro_))
                ob, rob = osb[0]
                RCP(ob[64:128, 0:nq], ob[64:128, 0:nq], R=[rob], W=[rob])
                r2, rr2 = wtile()
                P.dma('sp', r2[0:64, 0:nq], ob[64:128, 0:nq], reads=[rob], writes=[rr2])
                TT('dve', ua[0:64, 0:nq], ob[0:64, 0:nq], r2[0:64, 0:nq], ALU.mult, R=[rob, rr2], W=[rua])
                ob1, rob1 = osb[1]
                o2, ro2 = wtile()
                P.dma('sp', o2[64:128, 0:nq], ob1[0:64, 0:nq], reads=[rob1], writes=[ro2])
                RCP(ob1[64:128, 0:nq], ob1[64:128, 0:nq], R=[rob1], W=[rob1])
                TT('dve', ua[64:128, 0:nq], o2[64:128, 0:nq], ob1[64:128, 0:nq], ALU.mult, R=[ro2, rob1], W=[rua])
                TT('pool', uab[:, 0:nq], ua[:, 0:nq], sga[par][:, q0:q0 + nq], ALU.mult,
                   R=[rua, r_sga[par]], W=[ruab])
                P.dma('sp', uaT[c * 128:(c + 1) * 128, q0:q0 + nq], uab[:, 0:nq], reads=[ruab],
                      writes=[r_uaT[c * len(groups) + gi]])

            def issue_pv(n_):
                j = key_tiles[n_]
                P.batch_begin()
                pr_, pt, rpt = issued[n_]
                for half in range(2):
                    MM(PS[OBK[half]][:, 0:nq], Vx[:, j, g, :], pt[:, half * 512:half * 512 + nq], start=(n_ == 0),
                       stop=(n_ == nk - 1), R=[r_Vx, rpt], W=[rPS[OBK[half]]])
                P.batch_end()

            P.batch_begin()
            issue_s(key_tiles[0])
            P.batch_end()
            issue_exp(0)
            for n_ in range(nk):
                if n_ + 1 < nk:
                    P.batch_begin()
                    issue_s(key_tiles[n_ + 1])
                    P.batch_end()
                    issue_exp(n_ + 1)
                if n_ >= 1:
                    issue_pv(n_ - 1)
                if hooks and n_ >= 1 and (n_ - 1) % hstep == 0:
                    hooks.pop(0)()
            issue_pv(nk - 1)
            finish()
            while hooks:
                hooks.pop(0)()

        def qga_stages(l, c, gi, lo, hi):
            par = c % 2
            (sq_, t0, n) = groups[gi]
            st = {}

            def bank(k):
                if lo is None:
                    return psbank()
                return PS[lo + k % (hi - lo)], rPS[lo + k % (hi - lo)]

            def s1():
                st['pq'] = bank(0)
                pq, rpq = st['pq']
                for k in range(KC):
                    MM(pq[:, 0:n], wqg[par][:, k, 0, :], hT[:, k, t0:t0 + n], start=(k == 0), stop=(k == KC - 1),
                       R=[r_wqg[par]] + rh(t0, n), W=[rpq])
                ci = cslc[0] % 2
                cslc[0] += 1
                st['ci'] = ci
                P.dma('sp', csl[ci][0][:, 0:n], cos_in[:, t0:t0 + n], writes=[r_csl[ci]])
                P.dma('sp', csl[ci][1][:, 0:n], sin_in[:, t0:t0 + n], writes=[r_csl[ci]])

            def s2():
                pq, rpq = st['pq']
                st['sq'] = wbtile()
                sq, rsq = st['sq']
                ACT(sq[:, 0:n], pq[:, 0:n], AF.Square, R=[rpq], W=[rsq])
                st['qw'] = wtile()
                qw, rqw = st['qw']
                ACT(qw[:, 0:n], pq[:, 0:n], AF.Copy, scale=pp[:, l, PP_QW:PP_QW + 1], R=[rpq, r_pp], W=[rqw])

            def s3():
                st['p2'] = bank(1)
                p2, rp2 = st['p2']
                sq, rsq = st['sq']
                MM(p2[:, 0:n], cb[:, CB_BLK, :], sq[:, 0:n], R=[rsq, r_cb], W=[rp2])

            def s4():
                p2, rp2 = st['p2']
                st['rs'] = wtile()
                rs, rrs = st['rs']
                ACT(rs[:, 0:n], p2[:, 0:n], AF.Ln, bias=EPS, scale=1.0 / 64, R=[rp2], W=[rrs])
                ACT(rs[:, 0:n], rs[:, 0:n], AF.Exp, scale=-0.5, R=[rrs], W=[rrs])

            def s5():
                rs, rrs = st['rs']
                qw, rqw = st['qw']
                ci = st['ci']
                TT('dve', qw[:, 0:n], qw[:, 0:n], rs[:, 0:n], ALU.mult, R=[rqw, rrs], W=[rqw])
                st['qb'] = wbtile()
                qb, rqb = st['qb']
                CP('pool', qb[:, 0:n], qw[:, 0:n], R=[rqw], W=[rqb])
                st['t1'] = wtile()
                t1, rt1 = st['t1']
                TT('pool', t1[:, 0:n], qw[:, 0:n], csl[ci][0][:, 0:n], ALU.mult, R=[rqw, r_csl[ci]], W=[rt1])

            def s6():
                st['p3'] = bank(2)
                p3, rp3 = st['p3']
                qb, rqb = st['qb']
                MM(p3[:, 0:n], cb[:, CB_RROT, :], qb[:, 0:n], R=[rqb, r_cb], W=[rp3])
                st['pg'] = bank(3)
                pg, rpg = st['pg']
                for k in range(KC):
                    MM(pg[:, 0:n], wqg[par][:, k, 1, :], hT[:, k, t0:t0 + n], start=(k == 0), stop=(k == KC - 1),
                       R=[r_wqg[par]] + rh(t0, n), W=[rpg])

            def s7():
                p3, rp3 = st['p3']
                pg, rpg = st['pg']
                t1, rt1 = st['t1']
                ci = st['ci']
                t2, rt2 = wtile()
                TT('dve', t2[:, 0:n], p3[:, 0:n], csl[ci][1][:, 0:n], ALU.mult, R=[rp3, r_csl[ci]], W=[rt2])
                TT('dve', qT[par][:, t0:t0 + n], t1[:, 0:n], t2[:, 0:n], ALU.add, R=[rt1, rt2], W=[r_qT[par]])
                e_, re_ = wtile()
                ACT(e_[:, 0:n], pg[:, 0:n], AF.Exp, scale=-1.0, R=[rpg], W=[re_])
                g_, rg_ = wtile()
                ACT(g_[:, 0:n], pg[:, 0:n], AF.Copy, R=[rpg], W=[rg_])
                TS('dve', e_[:, 0:n], e_[:, 0:n], 1.0, None, ALU.add, R=[re_], W=[re_])
                RCP(e_[:, 0:n], e_[:, 0:n], R=[re_], W=[re_])
                TT('pool', sga[par][:, t0:t0 + n], g_[:, 0:n], e_[:, 0:n], ALU.mult, R=[rg_, re_], W=[r_sga[par]])

            return [s1, s2, s3, s4, s5, s6, s7]

        def phase_attn(l, last):
            glist = [gi for gi, (sq_, t0, n) in enumerate(groups) if not (last and sq_ == 0)]

            def load_w(c):
                par = c % 2
                P.dma('pool', wqg[par][:, :, 0, :], wsrc(w_in, l, OQ + c * 128, 128), writes=[r_wqg[par]])
                P.dma('pool', wqg[par][:, :, 1, :], wsrc(w_in, l, OGA + c * 128, 128), writes=[r_wqg[par]])

            load_w(0)
            for gi in glist:
                for f_ in qga_stages(l, 0, gi, None, None):
                    f_()
            for c in range(4):
                par = c % 2
                if c + 1 < 4:
                    load_w(c + 1)
                order = [gi for gi in glist if groups[gi][0] == 1] + [gi for gi in glist if groups[gi][0] == 0]
                nxt = list(glist) if c + 1 < 4 else []
                for gi in order:
                    (sq_, t0, n) = groups[gi]
                    hk = qga_stages(l, c + 1, nxt.pop(0), 4, 6) if nxt else []
                    if sq_ == 0:
                        attend(l, c, par, t0, n, list(range(NTC)), gi, hk)
                    else:
                        attend(l, c, par, t0, n, list(range(NT)), gi, hk)

        def qk_epilogue_b(l, pq, rpq, n, t0, wcolidx, o, ro):
            old = psc[0]
            qk_epilogue_r(l, pq, rpq, n, t0, wcolidx, o, ro, 0, 8)

        def qk_epilogue_r(l, pq, rpq, n, t0, wcolidx, o, ro, lo, hi):
            sq, rsq = wbtile()
            ACT(sq[:, 0:n], pq[:, 0:n], AF.Square, R=[rpq], W=[rsq])
            p2, rp2 = psbank(lo, hi)
            MM(p2[:, 0:n], cb[:, CB_BLK, :], sq[:, 0:n], R=[rsq, r_cb], W=[rp2])
            rs, rrs = wtile()
            ACT(rs[:, 0:n], p2[:, 0:n], AF.Sqrt, bias=EPS, scale=1.0 / 64, R=[rp2], W=[rrs])
            RCP(rs[:, 0:n], rs[:, 0:n], R=[rrs], W=[rrs])
            qw, rqw = wtile()
            ACT(qw[:, 0:n], pq[:, 0:n], AF.Copy, scale=pp[:, l, wcolidx:wcolidx + 1], R=[rpq, r_pp], W=[rqw])
            TT('dve', qw[:, 0:n], qw[:, 0:n], rs[:, 0:n], ALU.mult, R=[rqw, rrs], W=[rqw])
            qb, rqb = wbtile()
            CP('pool', qb[:, 0:n], qw[:, 0:n], R=[rqw], W=[rqb])
            p3, rp3 = psbank(lo, hi)
            MM(p3[:, 0:n], cb[:, CB_RROT, :], qb[:, 0:n], R=[rqb, r_cb], W=[rp3])
            ci = cslc[0] % 2
            cslc[0] += 1
            P.dma('sp', csl[ci][0][:, 0:n], cos_in[:, t0:t0 + n], writes=[r_csl[ci]])
            P.dma('sp', csl[ci][1][:, 0:n], sin_in[:, t0:t0 + n], writes=[r_csl[ci]])
            t1, rt1 = wtile()
            TT('pool', t1[:, 0:n], qw[:, 0:n], csl[ci][0][:, 0:n], ALU.mult, R=[rqw, r_csl[ci]], W=[rt1])
            t2, rt2 = wtile()
            TT('dve', t2[:, 0:n], p3[:, 0:n], csl[ci][1][:, 0:n], ALU.mult, R=[rp3, r_csl[ci]], W=[rt2])
            TT('dve', o, t1[:, 0:n], t2[:, 0:n], ALU.add, R=[rt1, rt2], W=[ro])


        BT = ar('C', [128, 2, T]); r_BT = Res()
        CT = ar('C', [128, 2, T]); r_CT = Res()
        pre0 = ar('C', [128, T + 8]); r_pre0 = Res()
        pre = [pre0, pre0]; r_pre = [r_pre0, r_pre0]
        wx = [ar('C', [128, KC, 128]) for i in range(2)]; r_wx = RL(2)
        wdt = ar('C', [128, KC, 16]); r_wdt = Res()
        dgwz = ar('C', [128, 5120])
        dgs = dgwz.rearrange("p (a b c) -> p a b c", a=8, b=5); r_dgs = Res()
        wz = dgwz[:, 0:4096].rearrange("p (a b) -> p a b", a=KC); r_wz = Res()
        dt_all = ar('C', [128, NT, 16], F32); r_dt = Res()
        a_all = ar('C', [128, NT, 16], F32); r_a = Res()
        ahl = [ar('C', [128, NT, 16]) for i in range(4)]
        Eb_all = ar('C', [128, NT, 16], F32); r_Eb = Res()
        scbrow = ar('C', [1, 1024]); r_scbrow = Res()
        Aexp = sb("Aexp", [128, 16]); r_Aexp = Res()
        Hf = ar('C', [128, 512], F32); Hb = ar('C', [128, 512], F32); r_Hf = Res(); r_Hb = Res()
        Hfb = ar('C', [128, 512]); Hbb = ar('C', [128, 512]); r_Hfb = Res(); r_Hbb = Res()
        Et = [sb("Et%d" % i, [128, 80]) for i in range(2)]; r_Et = RL(2)
        ncum = [sb("ncum%d" % i, [128, 32]) for i in range(2)]; r_ncum = RL(2)
        dtd = [sb("dtd%d" % i, [128, 16]) for i in range(2)]; r_dtd = RL(2)
        CBm = [sb("CBm%d" % i, [128, 2, 2, 128]) for i in range(2)]; r_CBm = RL(2)
        xs_t = [ar('C', [128, 512]) for i in range(3)]; r_xs_t = RL(3)
        b_t = [ar('C', [128, 256]) for i in range(3)]; r_b_t = RL(3)
        p1t = [[ar('C', [128, 512]) for b in range(5)] for a in range(2)]
        r_p1t = [RL(5) for a in range(2)]
        r_xs_d = RL(len(groups) * 4); r_b_d = RL(len(groups) * 2)
        r_yp_d = RL(NT); r_sb_d = RL(NT); r_usT = RL(NT)
        def pcol(t):
            return t + 2 if t < C else t + 6

        import os as _os
        C1S = _os.environ.get("C1STEPS", "12345")

        def phase_ssd_c1(l):
            MSET('pool', pre0[:], 0.0, W=[r_pre0])
            for cc in range(8 if '1' in C1S else 0):
                for k in range(5):
                    TS('dve', dgs[:, cc, k, :], identF, pp[:, l, PP_SCW + cc * 5 + k:PP_SCW + cc * 5 + k + 1], None,
                       ALU.mult, R=[r_cf, r_pp], W=[r_dgs])
            if '1' in C1S:
                P.dma('pool', scbrow[0:1, :], scb_in[l], writes=[r_scbrow])
            P.dma('pool', wdt[:], wsrc(w_in, l, ODT, 16), writes=[r_wdt])
            for i0 in range(0, NT if '2' in C1S else 0, 32):
                nb = min(32, NT - i0)
                pd, rpd = psbank()
                for ii in range(nb):
                    i = i0 + ii
                    for k in range(KC):
                        MM(pd[:, ii * 16:(ii + 1) * 16], hT[:, k, i * 128:(i + 1) * 128], wdt[:, k, :],
                           start=(k == 0), stop=(k == KC - 1), R=[r_wdt, r_hT[i]], W=[rpd])
                xb, rxb = wtile()
                ax, rax = wtile()
                n16 = nb * 16
                TT('dve', xb[:, 0:n16].rearrange("p (a b) -> p a b", b=16), pd[:, 0:n16].rearrange("p (a b) -> p a b", b=16),
                   pr[:, l, PR_DTB:PR_DTB + 16].unsqueeze(1).broadcast_to([128, nb, 16]), ALU.add, R=[rpd, r_pr], W=[rxb])
                ACT(ax[:, 0:n16], xb[:, 0:n16], AF.Abs, R=[rxb], W=[rax])
                ACT(ax[:, 0:n16], ax[:, 0:n16], AF.Exp, scale=-1.0, R=[rax], W=[rax])
                ACT(ax[:, 0:n16], ax[:, 0:n16], AF.Ln, bias=1.0, R=[rax], W=[rax])
                TS('dve', xb[:, 0:n16], xb[:, 0:n16], 0.0, None, ALU.max, R=[rxb], W=[rxb])
                TT('dve', dt_all[:, i0:i0 + nb, :], xb[:, 0:n16].rearrange("p (a b) -> p a b", b=16),
                   ax[:, 0:n16].rearrange("p (a b) -> p a b", b=16), ALU.add, R=[rxb, rax], W=[r_dt])
            if '2' in C1S:
                ACT(Aexp[:], pr[:, l, PR_ALOG:PR_ALOG + 16], AF.Exp, R=[r_pr], W=[r_Aexp])
                STT('dve', a_all[:], dt_all[:], -1.0, Aexp[:].unsqueeze(1).broadcast_to([128, NT, 16]),
                    ALU.mult, ALU.mult, R=[r_dt, r_Aexp], W=[r_a])
                CP('dve', ahl[0][:], a_all[:], R=[r_a], W=[r_a])
                TT('dve', ahl[1][:], a_all[:], ahl[0][:], ALU.subtract, R=[r_a], W=[r_a])
                TS('dve', ahl[2][:], ahl[0][:], -1.0, None, ALU.mult, R=[r_a], W=[r_a])
                TS('dve', ahl[3][:], ahl[1][:], -1.0, None, ALU.mult, R=[r_a], W=[r_a])
            for cc in range(8 if '3' in C1S else 0):
                par = cc % 2
                P.dma('pool', wx[par][:], wsrc(w_in, l, OXBC + cc * 128, 128), writes=[r_wx[par]])
                for (sq_, t0, n) in groups:
                    pb, rpb = psbank()
                    for k in range(KC):
                        MM(pb[:, 0:n], wx[par][:, k, :], hT[:, k, t0:t0 + n], start=(k == 0), stop=(k == KC - 1),
                           R=[r_wx[par]] + rh(t0, n), W=[rpb])
                    ACT(pre[par][:, pcol(t0):pcol(t0) + n], pb[:, 0:n], AF.Copy, R=[rpb], W=[r_pre[par]])
                if cc < 6 and '4' in C1S:
                    for gi, (sq_, t0, n) in enumerate(groups):
                        pb, rpb = psbank()
                        nt_ = n // 128
                        for ii in range(nt_):
                            tt0 = t0 + ii * 128
                            for k in range(5):
                                MM(pb[:, ii * 128:(ii + 1) * 128], pre[par][:, pcol(tt0) + k - 2:pcol(tt0) + k - 2 + 128],
                                   dgs[:, cc, k, :], start=(k == 0), stop=False, R=[r_pre[par], r_dgs], W=[rpb])
                            MM(pb[:, ii * 128:(ii + 1) * 128], cb[0:1, CB_ONES, :], scbrow[0:1, cc * 128:(cc + 1) * 128],
                               start=False, stop=True, R=[r_cb, r_scbrow], W=[rpb])
                        stg, rstg = wbtile()
                        ACT(stg[:, 0:n], pb[:, 0:n], AF.Silu, R=[rpb], W=[rstg])
                        if cc < 4:
                            dst = xs_d[t0:t0 + n, cc * 128:(cc + 1) * 128].rearrange("(a p) c -> p a c", p=128)
                            rdst = r_xs_d[gi * 4 + cc]
                        else:
                            dst = b_d[t0:t0 + n, (cc - 4) * 128:(cc - 3) * 128].rearrange("(a p) c -> p a c", p=128)
                            rdst = r_b_d[gi * 2 + cc - 4]
                        P.dma('sp', dst, stg[:, 0:n].rearrange("p (a c) -> p a c", c=128), reads=[rstg], writes=[rdst])
                if cc >= 4 and '5' in C1S:
                    for (sq_, t0, n) in groups:
                        pb, rpb = psbank()
                        for k in range(5):
                            MM(pb[:, 0:n], dgs[:, cc, k, :], pre[par][:, pcol(t0) + k - 2:pcol(t0) + k - 2 + n],
                               start=(k == 0), stop=(k == 4), R=[r_pre[par], r_dgs], W=[rpb])
                        tb, rtb = wtile()
                        ACT(tb[:, 0:n], pb[:, 0:n], AF.Identity, bias=pp[:, l, PP_SCB + cc:PP_SCB + cc + 1],
                            R=[rpb, r_pp], W=[rtb])
                        if cc < 6:
                            ACT(BT[:, cc - 4, t0:t0 + n], tb[:, 0:n], AF.Silu, R=[rtb], W=[r_BT])
                        else:
                            ACT(CT[:, cc - 6, t0:t0 + n], tb[:, 0:n], AF.Silu, R=[rtb], W=[r_CT])

        ssdc = [0]

        def bc8(ap):
            return ap.unsqueeze(2).broadcast_to([128, 8, 64])

        def h3(ap):
            return ap.rearrange("p (h d) -> p h d", d=64)

        p1state = {}

        def ssd_pass1_a(l, i, with_out):
            e = ssdc[0]
            ssdc[0] += 1
            p2 = e % 2
            p3 = e % 3
            p1state[i] = (p2, p3)
            gi = [k for k, (sq_, t0, n) in enumerate(groups) if t0 <= i * 128 < t0 + n][0]
            P.dma('sp', xs_t[p3][:], xs_d[i * 128:(i + 1) * 128, :], reads=r_xs_d[gi * 4:gi * 4 + 4], writes=[r_xs_t[p3]])
            P.dma('sp', b_t[p3][:], b_d[i * 128:(i + 1) * 128, :], reads=r_b_d[gi * 2:gi * 2 + 2], writes=[r_b_t[p3]])
            a_i = a_all[:, i, :]
            pc, rpc = psbank()
            for bi, blk in enumerate((CF_TRIF, CF_TRIB, CF_STRIF, CF_STRIB, CF_ONES)):
                MM(pc[:, bi * 16:(bi + 1) * 16], cf[:, blk, :], a_i, R=[r_cf, r_a], W=[rpc])
            ACT(Et[p2][:], pc[:, 0:80], AF.Exp, R=[rpc], W=[r_Et[p2]])
            CP('pool', Eb_all[:, i, 0:8], Et[p2][:, 24:32], R=[r_Et[p2]], W=[r_Eb])
            CP('pool', Eb_all[:, i, 8:16], Et[p2][:, 72:80], R=[r_Et[p2]], W=[r_Eb])
            TT('pool', dtd[p2][:, 0:8], dt_all[:, i, 0:8], Et[p2][:, 32:40], ALU.mult, R=[r_dt, r_Et[p2]], W=[r_dtd[p2]])
            TT('pool', dtd[p2][:, 8:16], dt_all[:, i, 8:16], Et[p2][:, 56:64], ALU.mult, R=[r_dt, r_Et[p2]], W=[r_dtd[p2]])
            xs3 = h3(xs_t[p3][:])
            xddf, rxddf = p1t[p2][0], r_p1t[p2][0]
            xddb, rxddb = p1t[p2][1], r_p1t[p2][1]
            TT('pool', h3(xddf[:]), xs3, bc8(dtd[p2][:, 0:8]), ALU.mult, R=[r_xs_t[p3], r_dtd[p2]], W=[rxddf])
            TT('dve', h3(xddb[:]), xs3, bc8(dtd[p2][:, 8:16]), ALU.mult, R=[r_xs_t[p3], r_dtd[p2]], W=[rxddb])
            if with_out:
                xdf, rxdf = p1t[p2][2], r_p1t[p2][2]
                xdb, rxdb = p1t[p2][3], r_p1t[p2][3]
                xsD, rxsD = p1t[p2][4], r_p1t[p2][4]
                TT('dve', h3(xdf[:]), xs3, bc8(dt_all[:, i, 0:8]), ALU.mult, R=[r_xs_t[p3], r_dt], W=[rxdf])
                TT('dve', h3(xdb[:]), xs3, bc8(dt_all[:, i, 8:16]), ALU.mult, R=[r_xs_t[p3], r_dt], W=[rxdb])
                TT('pool', h3(xsD[:]), xs3, bc8(pr[:, l, PR_D:PR_D + 8]), ALU.mult, R=[r_xs_t[p3], r_pr], W=[rxsD])
                pcb, rpcb = psbank()
                for g in range(2):
                    MM(pcb[:, g * 128:(g + 1) * 128], BT[:, g, i * 128:(i + 1) * 128], CT[:, g, i * 128:(i + 1) * 128],
                       R=[r_BT, r_CT], W=[rpcb])
                for d_, blk in enumerate((CF_TRIF, CF_TRIB)):
                    TT('dve', CBm[p2][:, d_, :, :], pcb[:, 0:256].rearrange("p (g s) -> p g s", s=128),
                       cf[:, blk, :].unsqueeze(1).broadcast_to([128, 2, 128]), ALU.mult, R=[rpcb, r_cf], W=[r_CBm[p2]])

        def ssd_pass1_b(l, i, with_out):
            p2, p3 = p1state.pop(i)
            xddf, rxddf = p1t[p2][0], r_p1t[p2][0]
            xddb, rxddb = p1t[p2][1], r_p1t[p2][1]
            xdf, rxdf = p1t[p2][2], r_p1t[p2][2]
            xdb, rxdb = p1t[p2][3], r_p1t[p2][3]
            xsD, rxsD = p1t[p2][4], r_p1t[p2][4]
            if with_out:
                pyo, rpyo = psbank()
                P.batch_begin()
                for g in range(2):
                    MM(pyo[:, g * 256:(g + 1) * 256], CT[:, g, i * 128:(i + 1) * 128], Hfb[:, g * 256:(g + 1) * 256],
                       R=[r_CT, r_Hfb], W=[rpyo])
                P.batch_end()
            psf, rpsf = psbank()
            psb_, rpsb = psbank()
            P.batch_begin()
            for g in range(2):
                MM(psf[:, g * 256:(g + 1) * 256], b_t[p3][:, g * 128:(g + 1) * 128], xddf[:, g * 256:(g + 1) * 256],
                   R=[r_b_t[p3], rxddf], W=[rpsf])
            for g in range(2):
                MM(psb_[:, g * 256:(g + 1) * 256], b_t[p3][:, g * 128:(g + 1) * 128], xddb[:, g * 256:(g + 1) * 256],
                   R=[r_b_t[p3], rxddb], W=[rpsb])
            P.batch_end()
            TT('dve', h3(Hf[:]), h3(Hf[:]), bc8(Et[p2][:, 64:72]), ALU.mult, R=[r_Hf, r_Et[p2]], W=[r_Hf])
            TT('dve', Hf[:], Hf[:], psf[:], ALU.add, R=[r_Hf, rpsf], W=[r_Hf])
            ACT(Hfb[:], Hf[:], AF.Copy, R=[r_Hf], W=[r_Hfb])
            sbt, rsbt = wtile()
            ACT(sbt[:], psb_[:], AF.Copy, R=[rpsb], W=[rsbt])
            P.dma('sp', sb_d[i], sbt[:], reads=[rsbt], writes=[r_sb_d[i]])
            if not with_out:
                return
            tmp, rtmp = wtile()
            TT('dve', h3(tmp[:]), h3(pyo[:]), bc8(Et[p2][:, 0:8]), ALU.mult, R=[rpyo, r_Et[p2]], W=[rtmp])
            py, rpy = psbank()
            MM(py[:, :], identB, xsD[:], start=True, stop=False, R=[r_cb, rxsD], W=[rpy])
            steps = [(d_, hq) for d_ in range(2) for hq in range(2)]
            stash = {}

            def x1_step(d_, hq):
                tri = CF_TRIF if d_ == 0 else CF_TRIB
                neg = CB_NEGF if d_ == 0 else CB_NEGB
                px, rpx = psbank()
                P.batch_begin()
                for hh in range(4):
                    h = hq * 4 + hh
                    col = d_ * 8 + h
                    MM(px[:, hh * 128:(hh + 1) * 128], identB, cb[:, neg, :], start=True, stop=False,
                       R=[r_cb], W=[rpx])
                    trib = CB_TRIF if d_ == 0 else CB_TRIB
                    for hl in range(2):
                        MM(px[:, hh * 128:(hh + 1) * 128], ahl[hl][:, i, col:col + 1].broadcast_to([128, 128]),
                           cb[:, trib, :], start=False, stop=False, R=[r_a, r_cb], W=[rpx])
                    for hl in range(2):
                        MM(px[:, hh * 128:(hh + 1) * 128], cb[:, trib, :],
                           ahl[2 + hl][:, i, col:col + 1].broadcast_to([128, 128]), start=False, stop=(hl == 1),
                           R=[r_a, r_cb], W=[rpx])
                P.batch_end()
                lt4, rlt4 = wtile()
                ACT(lt4[:], px[:], AF.Exp, R=[rpx], W=[rlt4])
                mt4, rmt4 = wbtile()
                TT('dve', mt4[:].rearrange("p (a b) -> p a b", b=128), lt4[:].rearrange("p (a b) -> p a b", b=128),
                   CBm[p2][:, d_, hq, :].unsqueeze(1).broadcast_to([128, 4, 128]), ALU.mult,
                   R=[rlt4, r_CBm[p2]], W=[rmt4])
                stash[(d_, hq)] = (mt4, rmt4)

            for st_ in steps:
                x1_step(*st_)
            for si_, (d_, hq) in enumerate(steps):
                xd_, rxd_ = (xdf, rxdf) if d_ == 0 else (xdb, rxdb)
                mt4, rmt4 = stash[(d_, hq)]
                P.batch_begin()
                for hh in range(4):
                    h = hq * 4 + hh
                    MM(py[:, h * 64:(h + 1) * 64], mt4[:, hh * 128:(hh + 1) * 128], xd_[:, h * 64:(h + 1) * 64],
                       start=False, stop=(d_ == 1 and h == 7), R=[rmt4, rxd_], W=[rpy])
                P.batch_end()
            ypt, rypt = wtile()
            TT('dve', ypt[:], tmp[:], py[:], ALU.add, R=[rtmp, rpy], W=[rypt])
            P.dma('sp', yp_d[i * 128:(i + 1) * 128, :], ypt[:], reads=[rypt], writes=[r_yp_d[i]])

        p2state = {}

        def ssd_pass2_a(l, i, with_out):
            sbt, rsbt = wtile()
            P.dma('sp', sbt[:], sb_d[i], reads=[r_sb_d[i]], writes=[rsbt])
            st = {}
            if with_out:
                st['ypt'] = wtile()
                ypt, rypt = st['ypt']
                P.dma('sp', ypt[:], yp_d[i * 128:(i + 1) * 128, :], reads=[r_yp_d[i]], writes=[rypt])
                st['pyo'] = psbank()
                pyo, rpyo = st['pyo']
                P.batch_begin()
                for g in range(2):
                    MM(pyo[:, g * 256:(g + 1) * 256], CT[:, g, i * 128:(i + 1) * 128], Hbb[:, g * 256:(g + 1) * 256],
                       R=[r_CT, r_Hbb], W=[rpyo])
                P.batch_end()
            TT('dve', h3(Hb[:]), h3(Hb[:]), bc8(Eb_all[:, i, 8:16]), ALU.mult, R=[r_Hb, r_Eb], W=[r_Hb])
            TT('dve', Hb[:], Hb[:], sbt[:], ALU.add, R=[r_Hb, rsbt], W=[r_Hb])
            ACT(Hbb[:], Hb[:], AF.Copy, R=[r_Hb], W=[r_Hbb])
            if with_out:
                st['pz'] = psbank()
                pz, rpz = st['pz']
                P.batch_begin()
                for k in range(KC):
                    MM(pz[:, :], hT[:, k, i * 128:(i + 1) * 128], wz[:, k, :], start=(k == 0), stop=(k == KC - 1),
                       R=[r_wz, r_hT[i]], W=[rpz])
                P.batch_end()
                st['ez'] = wtile()
                ez, rez = st['ez']
                ACT(ez[:], pz[:], AF.Exp, scale=-1.0, R=[rpz], W=[rez])
            p2state[i] = st

        def ssd_pass2_b(l, i, with_out):
            st = p2state.pop(i)
            if not with_out:
                return
            ypt, rypt = st['ypt']
            pyo, rpyo = st['pyo']
            pz, rpz = st['pz']
            ez, rez = st['ez']
            tmp, rtmp = wtile()
            TT('dve', h3(tmp[:]), h3(pyo[:]), bc8(Eb_all[:, i, 0:8]), ALU.mult, R=[rpyo, r_Eb], W=[rtmp])
            TT('pool', ypt[:], ypt[:], tmp[:], ALU.add, R=[rypt, rtmp], W=[rypt])
            sz, rsz = wtile()
            TS('dve', ez[:], ez[:], 1.0, None, ALU.add, R=[rez], W=[rez])
            RCP(ez[:], ez[:], R=[rez], W=[rez])
            TT('dve', sz[:], pz[:], ez[:], ALU.mult, R=[rpz, rez], W=[rsz])
            TT('dve', ypt[:], ypt[:], sz[:], ALU.mult, R=[rypt, rsz], W=[rypt])
            s4, rs4 = st4[i % 2], r_st4[i % 2]
            ACT(sz[:], ypt[:], AF.Square, accum=s4[:, 0:1], R=[rypt], W=[rsz, rs4])
            ACT(s4[:, 1:2], s4[:, 0:1], AF.Ln, bias=EPS, scale=1.0 / 512, R=[rs4], W=[rs4])
            ACT(s4[:, 2:3], s4[:, 1:2], AF.Exp, scale=-0.5, R=[rs4], W=[rs4])
            ACT(ypt[:], ypt[:], AF.Copy, scale=s4[:, 2:3], R=[rypt, rs4], W=[rypt])
            pt_, rpt_ = psbank()
            for cc in range(4):
                TR(pt_[:, cc * 128:(cc + 1) * 128], ypt[:, cc * 128:(cc + 1) * 128], identF, R=[rypt, r_cf], W=[rpt_])
            ust, rust = wbtile()
            TT('dve', ust[:].rearrange("p (c t) -> p c t", t=128), pt_[:].rearrange("p (c t) -> p c t", t=128),
               pp[:, l, PP_SNW:PP_SNW + 4].unsqueeze(2).broadcast_to([128, 4, 128]), ALU.mult, R=[rpt_, r_pp], W=[rust])
            P.dma('sp', usT.rearrange("(c p) t -> p c t", p=128)[:, :, i * 128:(i + 1) * 128],
                  ust[:].rearrange("p (c t) -> p c t", t=128), reads=[rust], writes=[r_usT[i]])

        def phase_ssd(l, last):
            phase_ssd_c1(l)
            if stop_after == 'ssd_c1':
                return
            P.barrier()
            P.dma('pool', wz[:], wsrc(w_in, l, OZ, 512), writes=[r_wz])
            MSET('dve', Hf[:], 0.0, W=[r_Hf]); MSET('dve', Hb[:], 0.0, W=[r_Hb])
            MSET('pool', Hfb[:], 0.0, W=[r_Hfb]); MSET('pool', Hbb[:], 0.0, W=[r_Hbb])
            for (s0, ntl, wo) in ((0, NTC, not last), (NTC, NTL, True)):
                ssd_pass1_a(l, s0, wo)
                for ii in range(ntl):
                    if ii + 1 < ntl:
                        ssd_pass1_a(l, s0 + ii + 1, wo)
                    ssd_pass1_b(l, s0 + ii, wo)
                if stop_after == 'ssd_p1':
                    continue
                order2 = list(reversed(range(ntl)))
                ssd_pass2_a(l, s0 + order2[0], wo)
                for k_, ii in enumerate(order2):
                    if k_ + 1 < len(order2):
                        ssd_pass2_a(l, s0 + order2[k_ + 1], wo)
                    ssd_pass2_b(l, s0 + ii, wo)


        vT = ar('D', [128, 4, T + 60]); r_vT = Res()
        dgc = ar('D', [128, 4, 31, 128]); r_dgc_d = Res(); r_dgc_a = Res()
        wgcv = ar('D', [128, KC, 512]); r_wgcv = Res()
        wug = [ar('D', [128, KC, 2, 128]) for i in range(2)]; r_wug = RL(2)
        ycsA = [ar('D', [128, 512], F32) for i in range(4)]
        wug_flat = wug[0]
        ycsB = [wug[i // 2].rearrange("p a b c -> p (a b c)")[:, (i % 2) * 1024:(i % 2) * 1024 + 1024].bitcast(F32) for i in range(4)]
        ycs2 = [ycsA, ycsB]; r_ycs2 = [RL(4), RL(4)]
        mD = ar('D', [128, 512], F32); m2D = ar('D', [128, 512], F32); r_mD = Res(); r_m2D = Res()
        r_ucT = RL(4 * len(groups))

        def vcol(t):
            return t + 15 if t < C else t + 45

        def phase_conv(l, last):
            MSET('pool', vT[:], 0.0, W=[r_vT])
            for cc in range(4):
                for k in range(31):
                    if k % 2 == 0:
                        TS('dve', dgc[:, cc, k, :], identF,
                           pp[:, l, PP_CCW + cc * 31 + k:PP_CCW + cc * 31 + k + 1], None, ALU.mult, R=[r_cf, r_pp], W=[r_dgc_d])
                    else:
                        ACT(dgc[:, cc, k, :], identF, AF.Copy, scale=pp[:, l, PP_CCW + cc * 31 + k:PP_CCW + cc * 31 + k + 1],
                            R=[r_cf, r_pp], W=[r_dgc_a])
            P.dma('pool', wgcv[:], wsrc(w_in, l, OGCV, 512), writes=[r_wgcv])
            for cc in range(4):
                par = cc % 2
                P.dma('pool', wug[par][:, :, 0, :], wsrc(w_in, l, OGLU + cc * 128, 128), writes=[r_wug[par]])
                P.dma('pool', wug[par][:, :, 1, :], wsrc(w_in, l, OGLU + 512 + cc * 128, 128), writes=[r_wug[par]])
                for (sq_, t0, n) in groups:
                    if last and sq_ == 0:
                        continue
                    pu, rpu = psbank()
                    pg, rpg = psbank()
                    for k in range(KC):
                        MM(pu[:, 0:n], wug[par][:, k, 0, :], hT[:, k, t0:t0 + n], start=(k == 0), stop=(k == KC - 1),
                           R=[r_wug[par]] + rh(t0, n), W=[rpu])
                    for k in range(KC):
                        MM(pg[:, 0:n], wug[par][:, k, 1, :], hT[:, k, t0:t0 + n], start=(k == 0), stop=(k == KC - 1),
                           R=[r_wug[par]] + rh(t0, n), W=[rpg])
                    sg, rsg = wtile()
                    ACT(sg[:, 0:n], pg[:, 0:n], AF.Sigmoid, R=[rpg], W=[rsg])
                    TT('dve', vT[:, cc, vcol(t0):vcol(t0) + n], pu[:, 0:n], sg[:, 0:n], ALU.mult, R=[rpu, rsg], W=[r_vT])
            P.barrier()
            for gi, (sq_, t0, n) in enumerate(groups):
                if last and sq_ == 0:
                    continue
                ycs, r_ycs = ycs2[gi % 2], r_ycs2[gi % 2]
                psm, rpsm = psbank()
                psq, rpsq = psbank()
                for cc in range(4):
                    pcv, rpcv = psbank()
                    for k in range(31):
                        MM(pcv[:, 0:n], dgc[:, cc, k, :], vT[:, cc, vcol(t0) + k - 15:vcol(t0) + k - 15 + n],
                           start=(k == 0), stop=(k == 30), R=[r_dgc_d, r_dgc_a, r_vT], W=[rpcv])
                    ACT(ycs[cc][:, 0:n], pcv[:, 0:n], AF.Identity, bias=pp[:, l, PP_CCB + cc:PP_CCB + cc + 1],
                        R=[rpcv, r_pp], W=[r_ycs[cc]])
                    ysq, rysq = wtile()
                    ACT(ysq[:, 0:n], ycs[cc][:, 0:n], AF.Square, R=[r_ycs[cc]], W=[rysq])
                    MM(psm[:, 0:n], cf[:, CF_ONES, :], ycs[cc][:, 0:n], start=(cc == 0), stop=(cc == 3),
                       R=[r_cf, r_ycs[cc]], W=[rpsm])
                    MM(psq[:, 0:n], cf[:, CF_ONES, :], ysq[:, 0:n], start=(cc == 0), stop=(cc == 3),
                       R=[r_cf, rysq], W=[rpsq])
                m, rm = mD, r_mD
                ACT(m[:, 0:n], psm[:, 0:n], AF.Copy, scale=1.0 / 512, R=[rpsm], W=[rm])
                m2, rm2 = m2D, r_m2D
                TT('pool', m2[:, 0:n], m[:, 0:n], m[:, 0:n], ALU.mult, R=[rm], W=[rm2])
                STT('dve', m2[:, 0:n], psq[:, 0:n], 1.0 / 512, m2[:, 0:n], ALU.mult, ALU.subtract, R=[rpsq, rm2], W=[rm2])
                ACT(m2[:, 0:n], m2[:, 0:n], AF.Sqrt, bias=EPS, R=[rm2], W=[rm2])
                RCP(m2[:, 0:n], m2[:, 0:n], R=[rm2], W=[rm2])
                for cc in range(4):
                    t_, rt_ = wtile()
                    TT('pool', t_[:, 0:n], ycs[cc][:, 0:n], m[:, 0:n], ALU.subtract, R=[r_ycs[cc], rm], W=[rt_])
                    TT('dve', t_[:, 0:n], t_[:, 0:n], m2[:, 0:n], ALU.mult, R=[rt_, rm2], W=[rt_])
                    ACT(t_[:, 0:n], t_[:, 0:n], AF.Identity, scale=pp[:, l, PP_CLW + cc:PP_CLW + cc + 1],
                        bias=pp[:, l, PP_CLB + cc:PP_CLB + cc + 1], R=[rt_, r_pp], W=[rt_])
                    ACT(t_[:, 0:n], t_[:, 0:n], AF.Silu, R=[rt_], W=[rt_])
                    pgc, rpgc = psbank()
                    for k in range(KC):
                        MM(pgc[:, 0:n], wgcv[:, k, cc * 128:(cc + 1) * 128], hT[:, k, t0:t0 + n], start=(k == 0),
                           stop=(k == KC - 1), R=[r_wgcv] + rh(t0, n), W=[rpgc])
                    sgc, rsgc = wtile()
                    ACT(sgc[:, 0:n], pgc[:, 0:n], AF.Silu, R=[rpgc], W=[rsgc])
                    ucb, rucb = wbtile()
                    TT('dve', ucb[:, 0:n], t_[:, 0:n], sgc[:, 0:n], ALU.mult, R=[rt_, rsgc], W=[rucb])
                    P.dma('sp', ucT[cc * 128:(cc + 1) * 128, t0:t0 + n], ucb[:, 0:n], reads=[rucb],
                          writes=[r_ucT[cc * len(groups) + gi]])

        wgm = ar('E', [128, KC, 3072]); r_wgm6 = RL(6)
        wbr = ar('E', [128, 4, 3, D]); r_wbr6 = RL(6)
        ut = [ar('E', [128, 4, 512]) for i in range(3)]; r_ut = RL(3)
        wo = ar('F', [128, KC, D]); r_wo = Res()
        yTt = ar('F', [128, KC, 512]); r_yTt = Res()
        zT = ar('F', [128, KC, 512], F32); r_zT = Res()
        xo = [ar('F', [128, D], F32) for i in range(2)]; r_xo = RL(2)
        fnw = ar('F', [128, D], F32); r_fnw = Res()
        xtF = [ar('F', [128, D], F32) for i in range(4)]
        xnF = [ar('F', [128, D], F32) for i in range(2)]
        r_yT_d = RL(len(groups))

        def phase_merge(l, last):
            wsrcs = (w_bra, w_brs, w_brc)
            for hf in range(2):
                for bi in range(3):
                    b6 = bi * 2 + hf
                    P.dma('pool', wgm[:, :, b6 * 512:(b6 + 1) * 512], wsrc(w_in, l, OGM + b6 * 512, 512), writes=[r_wgm6[b6]])
                    P.dma('pool', wbr[:, :, bi, hf * 512:(hf + 1) * 512],
                          wsrcs[bi][l, :, hf * 512:(hf + 1) * 512].rearrange("(c p) n -> p c n", p=128), writes=[r_wbr6[b6]])
            usrc = (uaT, usT, ucT)
            for gi, (sq_, t0, n) in enumerate(groups):
                if last and sq_ == 0:
                    continue
                for bi in range(3):
                    if bi == 0:
                        rr = [r_uaT[c * len(groups) + gi] for c in range(4)]
                    elif bi == 1:
                        rr = r_usT[t0 // 128:(t0 + n) // 128]
                    else:
                        rr = [r_ucT[c * len(groups) + gi] for c in range(4)]
                    P.dma('sp', ut[bi][:, :, 0:n], usrc[bi].rearrange("(c p) t -> p c t", p=128)[:, :, t0:t0 + n],
                          reads=rr, writes=[r_ut[bi]])
                for j in range(8):
                    acc, racc = wtile()
                    for bi in range(3):
                        pg, rpg = psbank()
                        for k in range(KC):
                            MM(pg[:, 0:n], wgm[:, k, bi * 1024 + j * 128:bi * 1024 + (j + 1) * 128], hT[:, k, t0:t0 + n],
                               start=(k == 0), stop=(k == KC - 1), R=[r_wgm6[bi * 2 + j // 4]] + rh(t0, n), W=[rpg])
                        gt, rgt = wtile()
                        ACT(gt[:, 0:n], pg[:, 0:n], AF.Identity, bias=pp[:, l, PP_BG + bi * 8 + j:PP_BG + bi * 8 + j + 1],
                            R=[rpg, r_pp], W=[rgt])
                        ACT(gt[:, 0:n], gt[:, 0:n], AF.Sigmoid, R=[rgt], W=[rgt])
                        pb, rpb = psbank()
                        for k4 in range(4):
                            MM(pb[:, 0:n], wbr[:, k4, bi, j * 128:(j + 1) * 128], ut[bi][:, k4, 0:n], start=(k4 == 0),
                               stop=(k4 == 3), R=[r_wbr6[bi * 2 + j // 4], r_ut[bi]], W=[rpb])
                        if bi == 0:
                            TT('dve', acc[:, 0:n], pb[:, 0:n], gt[:, 0:n], ALU.mult, R=[rpb, rgt], W=[racc])
                        else:
                            TT('dve', gt[:, 0:n], pb[:, 0:n], gt[:, 0:n], ALU.mult, R=[rpb, rgt], W=[rgt])
                            TT('pool', acc[:, 0:n], acc[:, 0:n], gt[:, 0:n], ALU.add, R=[racc, rgt], W=[racc])
                    yb, ryb = wbtile()
                    CP('pool', yb[:, 0:n], acc[:, 0:n], R=[racc], W=[ryb])
                    P.dma('sp', yT_d[j * 128:(j + 1) * 128, t0:t0 + n], yb[:, 0:n], reads=[ryb], writes=[r_yT_d[gi]])

        def phase_out(l, last):
            xt, xn = xtF, xnF
            P.dma('sp', fnw[:], fnw_in[:, :], writes=[r_fnw])
            for hf in range(2):
                P.dma('pool', wo[:, :, hf * 512:(hf + 1) * 512], wsrc(w_out, l, hf * 512, 512), writes=[r_wo])
            for gi, (sq_, t0, n) in enumerate(groups):
                if last and sq_ == 0:
                    continue
                col = 1 if sq_ == 0 else 0
                P.dma('sp', yTt[:, :, 0:n], yT_d.rearrange("(c p) t -> p c t", p=128)[:, :, t0:t0 + n],
                      reads=[r_yT_d[gi]], writes=[r_yTt])
                for j2 in range(8):
                    po, rpo = psbank()
                    for j in range(8):
                        MM(po[:, 0:n], wo[:, j, j2 * 128:(j2 + 1) * 128], yTt[:, j, 0:n], start=(j == 0), stop=(j == 7),
                           R=[r_wo, r_yTt], W=[rpo])
                    ACT(zT[:, j2, 0:n], po[:, 0:n], AF.Copy, scale=modv[:, 16 + j2, col:col + 1], R=[rpo, r_modv], W=[r_zT])
                for si in range(n // 128):
                    i = t0 // 128 + si
                    s = i % 2
                    sx = i % 4
                    rsrc = [r_x1[i]] if l > 0 else []
                    P.dma('sp', xt[sx][:], src_tile(l, i), reads=rsrc, writes=[r_xt[sx]])
                    for b in range(2):
                        pb, rpb = psbank()
                        for kk in range(4):
                            j2 = b * 4 + kk
                            TR(pb[:, kk * 128:(kk + 1) * 128], zT[:, j2, si * 128:(si + 1) * 128], identF,
                               R=[r_zT, r_cf], W=[rpb])
                        TT('dve', xo[s][:, b * 512:(b + 1) * 512], pb[:, :], xt[sx][:, b * 512:(b + 1) * 512], ALU.add,
                           R=[rpb, r_xt[sx]], W=[r_xo[s]])
                    if not last:
                        dst = (xc1 if i < NTC else x1)[((i if i < NTC else i - NTC) * 128):((i if i < NTC else i - NTC) * 128 + 128), :]
                        P.dma('pool', dst, xo[s][:], reads=[r_xo[s]], writes=[r_x1[i]])
                    else:
                        s4, rs4 = st4[s], r_st4[s]
                        ACT(xn[s][:], xo[s][:], AF.Square, accum=s4[:, 0:1], R=[r_xo[s]], W=[r_xn[s], rs4])
                        ACT(s4[:, 1:2], s4[:, 0:1], AF.Sqrt, bias=EPS, scale=1.0 / D, R=[rs4], W=[rs4])
                        RCP(s4[:, 2:3], s4[:, 1:2], R=[rs4], W=[rs4])
                        ACT(xn[s][:], xo[s][:], AF.Copy, scale=s4[:, 2:3], R=[r_xo[s], rs4], W=[r_xn[s]])
                        TT('pool', xn[s][:], xn[s][:], fnw[:], ALU.mult, R=[r_xn[s], r_fnw], W=[r_xn[s]])
                        P.dma('pool', out[(i - NTC) * 128:(i - NTC + 1) * 128, :], xn[s][:], reads=[r_xn[s]])

        for l in range(DEPTH):
            last = (l == DEPTH - 1)
            P.barrier()
            phase_mod(l)
            P.barrier()
            phase_hT(l)
            P.barrier()
            if stop_after == 'hT':
                break
            phase_kv(l)
            phase_attn(l, last)
            if stop_after == 'attn':
                break
            P.barrier()
            phase_ssd(l, last)
            if stop_after in ('ssd', 'ssd_c1', 'ssd_p1'):
                break
            P.barrier()
            phase_conv(l, last)
            if stop_after == 'conv':
                break
            P.barrier()
            phase_merge(l, last)
            P.barrier()
            phase_out(l, last)
        if dbg:
            hT_o = nc.dram_tensor("hT_o", [128, KC, T], BF16, kind="ExternalOutput").ap()
            P.dma('sp', hT_o[:, :, :], hT[:], reads=r_hT)
        P.emit()
    return nc


def _cols(v):
    v = np.asarray(v, np.float32)
    return np.ascontiguousarray(v.reshape(-1, 128).T)


def host_prep(inp, L, C, DEPTH, nb):
    f32 = np.float32
    cfc, cbc = host_consts()
    cosT, sinT = host_rope(L, C)
    pp = np.zeros((DEPTH, 128, NPP), f32)
    pr = np.zeros((DEPTH, 128, NPR), f32)
    for l in range(DEPTH):
        pp[l, :, PP_BMOD:PP_BMOD + 24] = _cols(inp['b_mod'][l])
        pp[l, :, PP_NORMW:PP_NORMW + 8] = _cols(inp['norm_w'][l])
        pp[l, :, PP_QW] = np.tile(np.asarray(inp['q_norm_w'][l], f32), 2)
        pp[l, :, PP_KW] = np.tile(np.asarray(inp['k_norm_w'][l], f32), 2)
        scw = np.asarray(inp['ssd_conv_w'][l], f32)
        pp[l, :, PP_SCW:PP_SCW + 40] = scw.reshape(5, 8, 128).transpose(2, 1, 0).reshape(128, 40)
        pp[l, :, PP_SCB:PP_SCB + 8] = _cols(inp['ssd_conv_b'][l])
        pp[l, :, PP_SNW:PP_SNW + 4] = _cols(inp['ssd_norm_w'][l])
        ccw = np.asarray(inp['cm_conv_w'][l], f32)
        pp[l, :, PP_CCW:PP_CCW + 124] = ccw.reshape(31, 4, 128).transpose(2, 1, 0).reshape(128, 124)
        pp[l, :, PP_CCB:PP_CCB + 4] = _cols(inp['cm_conv_b'][l])
        pp[l, :, PP_CLW:PP_CLW + 4] = _cols(inp['cm_ln_w'][l])
        pp[l, :, PP_CLB:PP_CLB + 4] = _cols(inp['cm_ln_b'][l])
        pp[l, :, PP_BG:PP_BG + 24] = _cols(np.asarray(inp['b_gate'][l], f32).reshape(-1))
        pr[l, :, PR_DTB:PR_DTB + 16] = np.asarray(inp['ssd_dt_bias'][l], f32).reshape(1, 16)
        pr[l, :, PR_ALOG:PR_ALOG + 16] = np.asarray(inp['ssd_A_log'][l], f32).reshape(1, 16)
        pr[l, :, PR_D:PR_D + 8] = np.asarray(inp['ssd_D'][l], f32).reshape(1, 8)
    fnw = np.ascontiguousarray(np.broadcast_to(np.asarray(inp['final_norm_w'], f32)[None, :], (128, D)))
    shared = {
        'w_mod': np.ascontiguousarray(inp['w_mod'], f32), 'w_in': np.ascontiguousarray(inp['w_in'], f32),
        'w_bra': np.ascontiguousarray(inp['w_br_attn'], f32), 'w_brs': np.ascontiguousarray(inp['w_br_ssd'], f32),
        'w_brc': np.ascontiguousarray(inp['w_br_conv'], f32), 'w_out': np.ascontiguousarray(inp['w_out'], f32),
        'pp': pp, 'pr': pr, 'scb': np.ascontiguousarray(np.asarray(inp['ssd_conv_b'], f32).reshape(DEPTH, 1, 1024)), 'fnw': fnw, 'cf': cfc, 'cb': cbc, 'cosT': cosT, 'sinT': sinT,
    }
    maps = []
    cctx = _cols(inp['c_ctx'])
    for b in range(nb):
        cvec = np.zeros((128, 8, 2), f32)
        cvec[:, :, 0] = _cols(inp['c'][b])
        cvec[:, :, 1] = cctx
        m = dict(shared)
        m['x'] = np.ascontiguousarray(inp['x'][b], f32)
        m['ctx'] = np.ascontiguousarray(inp['ctx'][b], f32)
        m['cvec'] = cvec.reshape(128, 16)
        maps.append(m)
    return maps


_NC_CACHE = {}


def kernel(**inputs):
    x = np.asarray(inputs['x'])
    B, L, _ = x.shape
    C = np.asarray(inputs['ctx']).shape[1]
    DEPTH = np.asarray(inputs['w_in']).shape[0]
    key = (L, C, DEPTH)
    if key not in _NC_CACHE:
        _NC_CACHE[key] = build(L, C, DEPTH)
    nc = _NC_CACHE[key]
    maps = host_prep(inputs, L, C, DEPTH, B)
    res = run_bass_kernel_spmd(nc, maps, core_ids=list(range(B)))
    return np.stack([np.asarray(r['out'], np.float32) for r in res.results], axis=0)
```

```python
import numpy as np
from contextlib import ExitStack
import concourse.bass as bass
import concourse.mybir as mybir
from concourse.bass_utils import run_bass_kernel_spmd

F32 = mybir.dt.float32
BF16 = mybir.dt.bfloat16
AF = mybir.ActivationFunctionType
ALU = mybir.AluOpType

COMPUTE = ('pe', 'act', 'dve', 'pool')
NDMASEM = 12


class Res:
    __slots__ = ('w', 'rd', 'rdma')

    def __init__(self):
        self.w = None
        self.rd = {}
        self.rdma = []


def RL(n):
    return [Res() for _ in range(n)]


class Op:
    __slots__ = ('eng', 'fn', 'deps', 'dma', 'needs_inc', 'cnt', 'sem', 'target', 'prewait', 'batch')

    def __init__(self, eng, fn, dma):
        self.batch = None
        self.eng = eng
        self.fn = fn
        self.dma = dma
        self.deps = []
        self.needs_inc = False
        self.cnt = 0
        self.sem = None
        self.target = 0
        self.prewait = None


class Prog:
    def __init__(self, nc):
        self.nc = nc
        self.streams = {e: [] for e in ('pe', 'act', 'dve', 'pool', 'sp')}
        self.pending = {}
        self.dmas = []
        self.cur_batch = None
        self.batches = []

    def batch_begin(self):
        self.cur_batch = {}
        self.batches.append(self.cur_batch)

    def batch_end(self):
        self.cur_batch = None

    def barrier(self):
        deps = []
        for e, st in self.streams.items():
            for op in reversed(st):
                if not op.dma:
                    deps.append(op)
                    break
        deps.extend(self.dmas)
        self.dmas = []
        self.pending = {e: list(deps) for e in self.streams}

    def add(self, eng, fn, reads=(), writes=(), dma=False):
        op = Op(eng, fn, dma)
        deps = {}
        if self.pending.get(eng):
            for o in self.pending.pop(eng):
                deps[id(o)] = o
        if dma:
            self.dmas.append(op)
        for r in reads:
            if r.w is not None:
                deps[id(r.w)] = r.w
        for w in writes:
            if w.w is not None:
                deps[id(w.w)] = w.w
            for o in w.rd.values():
                deps[id(o)] = o
            for o in w.rdma:
                deps[id(o)] = o
        for r in reads:
            if dma:
                r.rdma.append(op)
            else:
                r.rd[eng] = op
        for w in writes:
            w.w = op
            w.rd = {}
            w.rdma = []
        for d in deps.values():
            if d is op:
                continue
            if (not d.dma) and (not dma) and d.eng == eng and eng == 'pe':
                continue
            op.deps.append(d)
            if not d.dma:
                d.needs_inc = True
        if self.cur_batch is not None and not dma and eng == 'pe':
            op.batch = self.cur_batch
            self.cur_batch.setdefault(eng, []).append(op)
        self.streams[eng].append(op)
        return op

    def coarsen(self):
        for st in self.streams.values():
            for x in st:
                nd = []
                for d in x.deps:
                    if d.batch is not None and not d.dma and not (x.batch is d.batch and x.eng == d.eng):
                        d = d.batch[d.eng][-1]
                    nd.append(d)
                x.deps = nd
        for b in self.batches:
            for e, ops in b.items():
                union = {}
                for o in ops:
                    for d in o.deps:
                        if d.batch is b and d.eng == e and not d.dma:
                            continue
                        union[id(d)] = d
                    o.deps = []
                ops[0].deps = list(union.values())
        for st in self.streams.values():
            for x in st:
                x.needs_inc = False
        for st in self.streams.values():
            for x in st:
                for d in x.deps:
                    if not d.dma:
                        d.needs_inc = True

    def dma(self, q, out, in_, reads=(), writes=(), slow=False):
        if slow:
            return self.add(q, lambda e: e.dma_start(out=out, in_=in_, allow_slow_non_contiguous=True),
                            reads, writes, dma=True)
        return self.add(q, lambda e: e.dma_start(out=out, in_=in_), reads, writes, dma=True)

    def emit(self):
        nc = self.nc
        self.coarsen()
        with ExitStack() as es:
            csem = {e: es.enter_context(nc.semaphore('cs_' + e)) for e in COMPUTE}
            dsem = {}
            for q in ('sp', 'act', 'pool'):
                dsem[q] = [es.enter_context(nc.semaphore('ds_%s%d' % (q, i))) for i in range(NDMASEM)]
            for e, st in self.streams.items():
                c = 0
                nd = 0
                for op in st:
                    if op.dma:
                        op.sem = dsem[e][nd % NDMASEM]
                        op.target = 16 * (nd // NDMASEM + 1)
                        if nd >= NDMASEM:
                            op.prewait = (op.sem, 16 * (nd // NDMASEM))
                        nd += 1
                    else:
                        if op.needs_inc:
                            c += 1
                        op.cnt = c
            block = es.enter_context(nc.Block())
            streams = self.streams

            def run_stream(ename, eng):
                known = {}
                for op in streams[ename]:
                    need = {}
                    if op.prewait is not None:
                        need[id(op.prewait[0])] = op.prewait
                    for d in op.deps:
                        if d.dma:
                            s, v = d.sem, d.target
                        else:
                            s, v = csem[d.eng], d.cnt
                        k = id(s)
                        if k not in need or need[k][1] < v:
                            need[k] = (s, v)
                    for k, (s, v) in need.items():
                        if known.get(k, 0) >= v:
                            continue
                        eng.wait_ge(s, v)
                        known[k] = v
                    ins = op.fn(eng)
                    if op.dma:
                        ins.then_inc(op.sem, 16)
                    elif op.needs_inc:
                        ins.then_inc(csem[ename], 1)
                if ename == 'sp':
                    for q in ('sp', 'act', 'pool'):
                        last = {}
                        for op in streams[q]:
                            if op.dma:
                                last[id(op.sem)] = (op.sem, op.target)
                        for k, (s, v) in last.items():
                            if known.get(k, 0) < v:
                                eng.wait_ge(s, v)
                                known[k] = v

            @block.tensor
            def _(eng):
                run_stream('pe', eng)

            @block.scalar
            def _(eng):
                run_stream('act', eng)

            @block.vector
            def _(eng):
                run_stream('dve', eng)

            @block.gpsimd
            def _(eng):
                run_stream('pool', eng)

            @block.sync
            def _(eng):
                run_stream('sp', eng)


D = 1024
KC = 8
OQ, OK_, OV, OGA, OXBC, ODT, OZ, OGLU, OGCV, OGM = 0, 512, 640, 768, 1280, 2304, 2320, 2832, 3856, 4368
INCOLS = 7440
EPS = 1e-6
NEG = -30000.0
PP_BMOD, PP_NORMW, PP_QW, PP_KW, PP_SCW, PP_SCB, PP_SNW, PP_CCW, PP_CCB, PP_CLW, PP_CLB, PP_BG = \
    0, 24, 32, 33, 34, 74, 82, 86, 210, 214, 218, 222
NPP = 246
PR_DTB, PR_ALOG, PR_D, PR_SCB = 0, 16, 32, 40
NPR = 40
CF_ID, CF_TRIF, CF_TRIB, CF_STRIF, CF_STRIB, CF_ONES = range(6)
CB_ID, CB_NEGF, CB_NEGB, CB_BLK, CB_RROT, CB_ONES, CB_TRIF, CB_TRIB = range(8)


def host_consts():
    j = np.arange(128)[:, None]
    s = np.arange(128)[None, :]
    cf = np.zeros((128, 6, 128), np.float32)
    cf[:, CF_ID] = (j == s)
    cf[:, CF_TRIF] = (j <= s)
    cf[:, CF_TRIB] = (j >= s)
    cf[:, CF_STRIF] = (j > s)
    cf[:, CF_STRIB] = (j < s)
    cf[:, CF_ONES] = 1.0
    cb = np.zeros((128, 8, 128), np.float32)
    cb[:, CB_ID] = (j == s)
    cb[:, CB_NEGF] = NEG * (s < j)
    cb[:, CB_NEGB] = NEG * (s > j)
    cb[:, CB_BLK] = ((j // 64) == (s // 64))
    rr = np.zeros((128, 128), np.float32)
    for m in range(128):
        if (m % 32) < 16:
            rr[m + 16, m] = -1.0
        else:
            rr[m - 16, m] = 1.0
    cb[:, CB_RROT] = rr
    cb[:, CB_ONES] = 1.0
    cb[:, CB_TRIF] = (j <= s)
    cb[:, CB_TRIB] = (j >= s)
    return cf.reshape(128, 768), cb.reshape(128, 1024)


def host_rope(L, C):
    T = C + L
    t = np.arange(L)
    row = (t // 64).astype(np.float32)
    col = (t % 64).astype(np.float32)
    inv = (1.0 / (10000.0 ** (np.arange(16, dtype=np.float32) / 16))).astype(np.float32)
    cosT = np.ones((128, T), np.float32)
    sinT = np.zeros((128, T), np.float32)
    for p in range(128):
        d = p % 64
        pos = row if d < 32 else col
        ang = (pos * inv[d % 16]).astype(np.float32)
        cosT[p, C:] = np.cos(ang)
        sinT[p, C:] = np.sin(ang)
    return cosT, sinT


def build(L, C, DEPTH, dbg=False, stop_after=None):
    nc = bass.Bass("TRN2", target_bir_lowering=False)
    T = C + L
    NT = T // 128
    NTC = C // 128
    NTL = L // 128
    P = Prog(nc)

    def din(name, shape, dt=F32):
        return nc.dram_tensor(name, shape, dt, kind="ExternalInput").ap()

    def dscr(name, shape, dt=F32):
        return nc.dram_tensor(name, shape, dt, kind=("ExternalOutput" if dbg else "Internal")).ap()

    x_in = din("x", [L, D])
    ctx_in = din("ctx", [C, D])
    cvec_in = din("cvec", [128, 16])
    w_mod = din("w_mod", [DEPTH, D, 3 * D])
    w_in = din("w_in", [DEPTH, D, INCOLS])
    w_bra = din("w_bra", [DEPTH, 512, D])
    w_brs = din("w_brs", [DEPTH, 512, D])
    w_brc = din("w_brc", [DEPTH, 512, D])
    w_out = din("w_out", [DEPTH, D, D])
    pp_in = din("pp", [DEPTH, 128, NPP])
    pr_in = din("pr", [DEPTH, 128, NPR])
    fnw_in = din("fnw", [128, D])
    scb_in = din("scb", [DEPTH, 1, 1024])
    cf_in = din("cf", [128, 768])
    cb_in = din("cb", [128, 1024])
    cos_in = din("cosT", [128, T])
    sin_in = din("sinT", [128, T])
    out = nc.dram_tensor("out", [L, D], F32, kind="ExternalOutput").ap()

    x1 = dscr("x1", [L, D])
    xc1 = dscr("xc1", [C, D])
    uaT = dscr("uaT", [512, T], BF16)
    usT = dscr("usT", [512, T], BF16)
    ucT = dscr("ucT", [512, T], BF16)
    yT_d = dscr("yT", [D, T], BF16)
    xs_d = dscr("xs_tm", [T, 512], BF16)
    b_d = dscr("b_tm", [T, 256], BF16)
    yp_d = dscr("ypart", [T, 512])
    sb_d = dscr("sbst", [NT, 128, 512])

    groups = [(0, 0, C)] + [(1, C + 512 * i, 512) for i in range(L // 512)]

    with ExitStack() as es:
        def sb(name, shape, dt=F32):
            return es.enter_context(nc.sbuf_tensor("s_" + name, shape, dt))

        ARN = 49152
        AR = sb("arena", [128, ARN], BF16)
        aoff = {}

        def ar(phase, shape, dt=BF16):
            n = 1
            for v in shape[1:]:
                n *= v
            nb = n * (2 if dt == BF16 else 4)
            off = (aoff.get(phase, 0) + 3) // 4 * 4
            aoff[phase] = off + nb
            assert aoff[phase] <= ARN * 2, (phase, aoff[phase])
            ap = AR[0:shape[0], off // 2:(off + nb) // 2]
            if dt == F32:
                ap = ap.bitcast(F32)
            if len(shape) == 2:
                return ap
            names = 'abcd'[:len(shape) - 1]
            pat = "p (" + " ".join(names) + ") -> p " + " ".join(names)
            kw = {names[i]: shape[1 + i] for i in range(1, len(names))}
            return ap.rearrange(pat, **kw)

        PP2 = [es.enter_context(nc.psum_tensor("pp%d" % i, [128, 1024], F32)) for i in range(4)]
        PS = []
        for i in range(4):
            PS.append(PP2[i][:, 0:512])
            PS.append(PP2[i][:, 512:1024])
        rPS = RL(8)

        def MM(o, lhsT, rhs, start=True, stop=True, R=(), W=()):
            P.add('pe', lambda e: e.matmul(o, lhsT=lhsT, rhs=rhs, start=start, stop=stop), R, W)

        def TR(o, in_, ident, R=(), W=()):
            P.add('pe', lambda e: e.transpose(out=o, in_=in_, identity=ident), R, W)

        def ACT(o, in_, func, bias=None, scale=None, accum=None, R=(), W=()):
            kw = {}
            if bias is not None:
                kw['bias'] = bias
            if scale is not None:
                kw['scale'] = scale
            if accum is not None:
                kw['accum_out'] = accum
            P.add('act', lambda e: e.activation(out=o, in_=in_, func=func, **kw), R, W)

        def TT(eng, o, a, b, op, R=(), W=()):
            P.add(eng, lambda e: e.tensor_tensor(out=o, in0=a, in1=b, op=op), R, W)

        def TS(eng, o, a, s1, s2, op0, op1=None, R=(), W=()):
            if op1 is None:
                P.add(eng, lambda e: e.tensor_scalar(out=o, in0=a, scalar1=s1, scalar2=None, op0=op0), R, W)
            else:
                P.add(eng, lambda e: e.tensor_scalar(out=o, in0=a, scalar1=s1, scalar2=s2, op0=op0, op1=op1), R, W)

        def STT(eng, o, a, s, b, op0, op1, R=(), W=()):
            P.add(eng, lambda e: e.scalar_tensor_tensor(out=o, in0=a, scalar=s, in1=b, op0=op0, op1=op1), R, W)

        def CP(eng, o, a, R=(), W=()):
            P.add(eng, lambda e: e.tensor_copy(out=o, in_=a), R, W)

        def RCP(o, a, R=(), W=()):
            P.add('dve', lambda e: e.reciprocal(out=o, in_=a), R, W)

        def MSET(eng, o, v, W=()):
            P.add(eng, lambda e: e.memset(o, v), (), W)

        def wsrc(w, l, c0, n):
            return w[l, :, c0:c0 + n].rearrange("(c p) n -> p c n", p=128)

        cf = sb("cf", [128, 6, 128]); r_cf = Res()
        cb = sb("cb", [128, 8, 128], BF16); r_cb = Res()
        pp = sb("pp", [128, DEPTH, NPP]); r_pp = Res()
        pr = sb("pr", [128, DEPTH, NPR]); r_pr = Res()
        cv = sb("cv", [128, 16]); r_cv = Res()
        csl = [[ar('B', [128, 512], F32) for b in range(2)] for a in range(2)]; r_csl = RL(2)
        cslc = [0]
        P.dma('sp', cf[:].rearrange("p a b -> p (a b)"), cf_in[:, :], writes=[r_cf])
        P.dma('pool', cb[:].rearrange("p a b -> p (a b)"), cb_in[:, :], writes=[r_cb])
        P.dma('sp', pp[:], pp_in.rearrange("l p n -> p l n"), writes=[r_pp])
        P.dma('sp', pr[:], pr_in.rearrange("l p n -> p l n"), writes=[r_pr])
        P.dma('sp', cv[:], cvec_in[:, :], writes=[r_cv])
        ones512 = sb("ones512", [1, 512], BF16); r_ones512 = Res()
        MSET('dve', ones512[0:1, :], 1.0, W=[r_ones512])
        identF = cf[:, CF_ID, :]
        identB = cb[:, CB_ID, :]

        hT = sb("hT", [128, KC, T], BF16); r_hT = RL(NT)

        def rh(t0, n):
            return r_hT[t0 // 128:(t0 + n) // 128]

        NW = 4
        wsl = [ar('M', [128, KC, 512], BF16) for i in range(NW)]
        r_wsl = RL(NW)
        wctr = [0]

        def wslot():
            i = wctr[0] % NW
            wctr[0] += 1
            return wsl[i], r_wsl[i]

        NWF = 10
        wf = [sb("wf%d" % i, [128, 512]) for i in range(NWF)]
        r_wf = RL(NWF)
        wfc = [0]

        def wtile():
            i = wfc[0] % NWF
            wfc[0] += 1
            return wf[i], r_wf[i]

        NWB = 8
        wb = [sb("wb%d" % i, [128, 512], BF16) for i in range(NWB)]
        r_wb = RL(NWB)
        wbc = [0]

        def wbtile():
            i = wbc[0] % NWB
            wbc[0] += 1
            return wb[i], r_wb[i]

        psc = [0]

        def psbank(lo=0, hi=8):
            i = lo + psc[0] % (hi - lo)
            psc[0] += 1
            return PS[i], rPS[i]

        modv = sb("modv", [128, 24, 2]); r_modv = Res()
        A1 = sb("A1", [128, 8, 2]); r_A1 = Res()
        scb = sb("scb", [128, 8, 2], BF16); r_scb = Res()

        xtA = [ar('A', [128, D], F32) for i in range(4)]; r_xt = RL(4)
        xnA = [ar('A', [128, D], F32) for i in range(2)]; r_xn = RL(2)
        junk = ar('A', [128, D], F32); r_junk = Res()
        xt = xtA
        xn = xnA
        st4 = [sb("st4_%d" % i, [128, 4]) for i in range(2)]; r_st4 = RL(2)

        def src_tile(l, i):
            if l == 0:
                return (ctx_in if i < NTC else x_in)[((i if i < NTC else i - NTC) * 128):((i if i < NTC else i - NTC) * 128 + 128), :]
            return (xc1 if i < NTC else x1)[((i if i < NTC else i - NTC) * 128):((i if i < NTC else i - NTC) * 128 + 128), :]

        r_x1 = RL(NT)

        def phase_mod(l):
            ACT(scb[:].rearrange("p a b -> p (a b)"), cv[:, :], AF.Silu, R=[r_cv], W=[r_scb])
            pm, rpm = psbank()
            for blk in range(6):
                ws, rws = wslot()
                P.dma('pool', ws[:], wsrc(w_mod, l, blk * 512, 512), writes=[rws])
                for jj in range(4):
                    j = blk * 4 + jj
                    for k in range(KC):
                        MM(pm[:, 2 * j:2 * j + 2], ws[:, k, jj * 128:(jj + 1) * 128], scb[:, k, :],
                           start=(k == 0), stop=(k == KC - 1), R=[rws, r_scb], W=[rpm])
            TT('dve', modv[:], pm[:, 0:48].rearrange("p (j t) -> p j t", t=2),
               pp[:, l, PP_BMOD:PP_BMOD + 24].unsqueeze(2).broadcast_to([128, 24, 2]), ALU.add,
               R=[rpm, r_pp], W=[r_modv])
            STT('dve', A1[:], modv[:, 8:16, :], 1.0,
                pp[:, l, PP_NORMW:PP_NORMW + 8].unsqueeze(2).broadcast_to([128, 8, 2]),
                ALU.add, ALU.mult, R=[r_modv, r_pp], W=[r_A1])

        def phase_hT(l):
            for i in range(NT):
                s = i % 2
                sx = i % 4
                col = 1 if i < NTC else 0
                rsrc = [r_x1[i]] if l > 0 else []
                P.dma('sp', xt[sx][:], src_tile(l, i), reads=rsrc, writes=[r_xt[sx]])
                ACT(junk[:], xt[sx][:], AF.Square, accum=st4[s][:, 0:1], R=[r_xt[sx]], W=[r_junk, r_st4[s]])
                ACT(st4[s][:, 1:2], st4[s][:, 0:1], AF.Sqrt, bias=EPS, scale=1.0 / D, R=[r_st4[s]], W=[r_st4[s]])
                RCP(st4[s][:, 2:3], st4[s][:, 1:2], R=[r_st4[s]], W=[r_st4[s]])
                ACT(xn[s][:], xt[sx][:], AF.Copy, scale=st4[s][:, 2:3], R=[r_xt[sx], r_st4[s]], W=[r_xn[s]])
                for b in range(2):
                    pb, rpb = psbank()
                    for kk in range(4):
                        k = b * 4 + kk
                        TR(pb[:, kk * 128:(kk + 1) * 128], xn[s][:, k * 128:(k + 1) * 128], identF,
                           R=[r_xn[s], r_cf], W=[rpb])
                    tw, rtw = wtile()
                    TT('dve', tw[:].rearrange("p (a b) -> p a b", b=128), pb[:].rearrange("p (a b) -> p a b", b=128),
                       A1[:, 4 * b:4 * b + 4, col:col + 1].broadcast_to([128, 4, 128]), ALU.mult,
                       R=[rpb, r_A1], W=[rtw])
                    TT('pool', hT[:, 4 * b:4 * b + 4, i * 128:(i + 1) * 128], tw[:].rearrange("p (a b) -> p a b", b=128),
                       modv[:, 4 * b:4 * b + 4, col:col + 1].broadcast_to([128, 4, 128]), ALU.add,
                       R=[rtw, r_modv], W=[r_hT[i]])

        kT = ar('B', [128, 2, T]); r_kT = Res()
        Vx = ar('B', [128, NT, 2, 128]); r_Vx = Res()
        wkd = ar('B', [128, KC, 2, 128]); r_wkd = Res()
        wv = ar('B', [128, KC, 128]); r_wv = Res()

        def qk_epilogue(l, pq, rpq, n, t0, wcolidx, o, ro):
            sq, rsq = wbtile()
            ACT(sq[:, 0:n], pq[:, 0:n], AF.Square, R=[rpq], W=[rsq])
            p2, rp2 = psbank()
            MM(p2[:, 0:n], cb[:, CB_BLK, :], sq[:, 0:n], R=[rsq, r_cb], W=[rp2])
            rs, rrs = wtile()
            ACT(rs[:, 0:n], p2[:, 0:n], AF.Ln, bias=EPS, scale=1.0 / 64, R=[rp2], W=[rrs])
            ACT(rs[:, 0:n], rs[:, 0:n], AF.Exp, scale=-0.5, R=[rrs], W=[rrs])
            qw, rqw = wtile()
            ACT(qw[:, 0:n], pq[:, 0:n], AF.Copy, scale=pp[:, l, wcolidx:wcolidx + 1], R=[rpq, r_pp], W=[rqw])
            TT('dve', qw[:, 0:n], qw[:, 0:n], rs[:, 0:n], ALU.mult, R=[rqw, rrs], W=[rqw])
            qb, rqb = wbtile()
            CP('dve', qb[:, 0:n], qw[:, 0:n], R=[rqw], W=[rqb])
            p3, rp3 = psbank()
            MM(p3[:, 0:n], cb[:, CB_RROT, :], qb[:, 0:n], R=[rqb, r_cb], W=[rp3])
            ci = cslc[0] % 2
            cslc[0] += 1
            P.dma('sp', csl[ci][0][:, 0:n], cos_in[:, t0:t0 + n], writes=[r_csl[ci]])
            P.dma('sp', csl[ci][1][:, 0:n], sin_in[:, t0:t0 + n], writes=[r_csl[ci]])
            t1, rt1 = wtile()
            TT('pool', t1[:, 0:n], qw[:, 0:n], csl[ci][0][:, 0:n], ALU.mult, R=[rqw, r_csl[ci]], W=[rt1])
            t2, rt2 = wtile()
            TT('dve', t2[:, 0:n], p3[:, 0:n], csl[ci][1][:, 0:n], ALU.mult, R=[rp3, r_csl[ci]], W=[rt2])
            TT('dve', o, t1[:, 0:n], t2[:, 0:n], ALU.add, R=[rt1, rt2], W=[ro])

        def phase_kv(l):
            MSET('pool', Vx[:, :, :, 64:128], 1.0, W=[r_Vx])
            for g in range(2):
                for h in range(2):
                    P.dma('pool', wkd[:, :, g, h * 64:(h + 1) * 64], wsrc(w_in, l, OK_ + g * 64, 64), writes=[r_wkd])
            P.dma('pool', wv[:], wsrc(w_in, l, OV, 128), writes=[r_wv])
            for (sq_, t0, n) in groups:
                for g in range(2):
                    pk, rpk = psbank()
                    for k in range(KC):
                        MM(pk[:, 0:n], wkd[:, k, g, :], hT[:, k, t0:t0 + n], start=(k == 0), stop=(k == KC - 1),
                           R=[r_wkd] + rh(t0, n), W=[rpk])
                    qk_epilogue(l, pk, rpk, n, t0, PP_KW, kT[:, g, t0:t0 + n], r_kT)
            for i0 in range(0, NT, 4):
                nb = min(4, NT - i0)
                pvv, rpv = psbank()
                for ii in range(nb):
                    i = i0 + ii
                    for k in range(KC):
                        MM(pvv[:, ii * 128:(ii + 1) * 128], hT[:, k, i * 128:(i + 1) * 128], wv[:, k, :],
                           start=(k == 0), stop=(k == KC - 1), R=[r_wv, r_hT[i]], W=[rpv])
                CP('dve', Vx[:, i0:i0 + nb, :, 0:64],
                   pvv[:, 0:nb * 128].rearrange("p (a g d) -> p a g d", g=2, d=64), R=[rpv], W=[r_Vx])

        qT = [ar('B', [128, T]) for i in range(2)]; r_qT = RL(2)
        sga = [ar('B', [128, T]) for i in range(2)]; r_sga = RL(2)
        wqg = [ar('B', [128, KC, 2, 128]) for i in range(2)]; r_wqg = RL(2)
        pts = [ar('B', [128, 1024]) for i in range(3)]; r_pts = RL(3)
        r_uaT = RL(4 * len(groups))
        SB_ = None
        OBK = (6, 7)
        attc = [0]

        def attend(l, c, par, q0, nq, key_tiles, gi, hooks=()):
            g = c // 2
            nk = len(key_tiles)
            LOOK = 1
            issued = []
            hooks = list(hooks)
            hstep = max(1, (nk - 2) // max(1, len(hooks))) if hooks else 1

            def issue_s(j):
                a = attc[0]
                attc[0] += 1
                pr_ = a % 2
                pt, rpt = pts[a % 3], r_pts[a % 3]
                for half in range(2):
                    hs = slice(half * 64, half * 64 + 64)
                    MM(PS[2 * pr_ + half][:, 0:nq], kT[hs, g, j * 128:(j + 1) * 128], qT[par][hs, q0:q0 + nq],
                       R=[r_kT, r_qT[par]], W=[rPS[2 * pr_ + half]])
                issued.append((pr_, pt, rpt))

            def issue_exp(n_):
                pr_, pt, rpt = issued[n_]
                if nq == 512:
                    ACT(pt[:, 0:1024], PP2[pr_][:, 0:1024], AF.Exp, scale=0.125, R=[rPS[2 * pr_], rPS[2 * pr_ + 1]], W=[rpt])
                else:
                    for half in range(2):
                        ACT(pt[:, half * 512:half * 512 + nq], PS[2 * pr_ + half][:, 0:nq], AF.Exp, scale=0.125,
                            R=[rPS[2 * pr_ + half]], W=[rpt])

            def finish():
                ua, rua = wtile()
                uab, ruab = wbtile()
                osb = []
                for half in range(2):
                    o_, ro_ = wtile()
                    CP('dve', o_[:, 0:nq], PS[OBK[half]][:, 0:nq], R=[rPS[OBK[half]]], W=[ro_])
                    osb.append((o_, ro_))
                ob, rob = osb[0]
                RCP(ob[64:128, 0:nq], ob[64:128, 0:nq], R=[rob], W=[rob])
                r2, rr2 = wtile()
                P.dma('sp', r2[0:64, 0:nq], ob[64:128, 0:nq], reads=[rob], writes=[rr2])
                TT('dve', ua[0:64, 0:nq], ob[0:64, 0:nq], r2[0:64, 0:nq], ALU.mult, R=[rob, rr2], W=[rua])
                ob1, rob1 = osb[1]
                o2, ro2 = wtile()
                P.dma('sp', o2[64:128, 0:nq], ob1[0:64, 0:nq], reads=[rob1], writes=[ro2])
                RCP(ob1[64:128, 0:nq], ob1[64:128, 0:nq], R=[rob1], W=[rob1])
                TT('dve', ua[64:128, 0:nq], o2[64:128, 0:nq], ob1[64:128, 0:nq], ALU.mult, R=[ro2, rob1], W=[rua])
                TT('pool', uab[:, 0:nq], ua[:, 0:nq], sga[par][:, q0:q0 + nq], ALU.mult,
                   R=[rua, r_sga[par]], W=[ruab])
                P.dma('sp', uaT[c * 128:(c + 1) * 128, q0:q0 + nq], uab[:, 0:nq], reads=[ruab],
                      writes=[r_uaT[c * len(groups) + gi]])

            def issue_pv(n_):
                j = key_tiles[n_]
                P.batch_begin()
                pr_, pt, rpt = issued[n_]
                for half in range(2):
                    MM(PS[OBK[half]][:, 0:nq], Vx[:, j, g, :], pt[:, half * 512:half * 512 + nq], start=(n_ == 0),
                       stop=(n_ == nk - 1), R=[r_Vx, rpt], W=[rPS[OBK[half]]])
                P.batch_end()

            P.batch_begin()
            issue_s(key_tiles[0])
            P.batch_end()
            issue_exp(0)
            for n_ in range(nk):
                if n_ + 1 < nk:
                    P.batch_begin()
                    issue_s(key_tiles[n_ + 1])
                    P.batch_end()
                    issue_exp(n_ + 1)
                if n_ >= 1:
                    issue_pv(n_ - 1)
                if hooks and n_ >= 1 and (n_ - 1) % hstep == 0:
                    hooks.pop(0)()
            issue_pv(nk - 1)
            finish()
            while hooks:
                hooks.pop(0)()

        def qga_stages(l, c, gi, lo, hi):
            par = c % 2
            (sq_, t0, n) = groups[gi]
            st = {}

            def bank(k):
                if lo is None:
                    return psbank()
                return PS[lo + k % (hi - lo)], rPS[lo + k % (hi - lo)]

            def s1():
                st['pq'] = bank(0)
                pq, rpq = st['pq']
                for k in range(KC):
                    MM(pq[:, 0:n], wqg[par][:, k, 0, :], hT[:, k, t0:t0 + n], start=(k == 0), stop=(k == KC - 1),
                       R=[r_wqg[par]] + rh(t0, n), W=[rpq])
                ci = cslc[0] % 2
                cslc[0] += 1
                st['ci'] = ci
                P.dma('sp', csl[ci][0][:, 0:n], cos_in[:, t0:t0 + n], writes=[r_csl[ci]])
                P.dma('sp', csl[ci][1][:, 0:n], sin_in[:, t0:t0 + n], writes=[r_csl[ci]])

            def s2():
                pq, rpq = st['pq']
                st['sq'] = wbtile()
                sq, rsq = st['sq']
                ACT(sq[:, 0:n], pq[:, 0:n], AF.Square, R=[rpq], W=[rsq])
                st['qw'] = wtile()
                qw, rqw = st['qw']
                ACT(qw[:, 0:n], pq[:, 0:n], AF.Copy, scale=pp[:, l, PP_QW:PP_QW + 1], R=[rpq, r_pp], W=[rqw])

            def s3():
                st['p2'] = bank(1)
                p2, rp2 = st['p2']
                sq, rsq = st['sq']
                MM(p2[:, 0:n], cb[:, CB_BLK, :], sq[:, 0:n], R=[rsq, r_cb], W=[rp2])

            def s4():
                p2, rp2 = st['p2']
                st['rs'] = wtile()
                rs, rrs = st['rs']
                ACT(rs[:, 0:n], p2[:, 0:n], AF.Ln, bias=EPS, scale=1.0 / 64, R=[rp2], W=[rrs])
                ACT(rs[:, 0:n], rs[:, 0:n], AF.Exp, scale=-0.5, R=[rrs], W=[rrs])

            def s5():
                rs, rrs = st['rs']
                qw, rqw = st['qw']
                ci = st['ci']
                TT('dve', qw[:, 0:n], qw[:, 0:n], rs[:, 0:n], ALU.mult, R=[rqw, rrs], W=[rqw])
                st['qb'] = wbtile()
                qb, rqb = st['qb']
                CP('pool', qb[:, 0:n], qw[:, 0:n], R=[rqw], W=[rqb])
                st['t1'] = wtile()
                t1, rt1 = st['t1']
                TT('pool', t1[:, 0:n], qw[:, 0:n], csl[ci][0][:, 0:n], ALU.mult, R=[rqw, r_csl[ci]], W=[rt1])

            def s6():
                st['p3'] = bank(2)
                p3, rp3 = st['p3']
                qb, rqb = st['qb']
                MM(p3[:, 0:n], cb[:, CB_RROT, :], qb[:, 0:n], R=[rqb, r_cb], W=[rp3])
                st['pg'] = bank(3)
                pg, rpg = st['pg']
                for k in range(KC):
                    MM(pg[:, 0:n], wqg[par][:, k, 1, :], hT[:, k, t0:t0 + n], start=(k == 0), stop=(k == KC - 1),
                       R=[r_wqg[par]] + rh(t0, n), W=[rpg])

            def s7():
                p3, rp3 = st['p3']
                pg, rpg = st['pg']
                t1, rt1 = st['t1']
                ci = st['ci']
                t2, rt2 = wtile()
                TT('dve', t2[:, 0:n], p3[:, 0:n], csl[ci][1][:, 0:n], ALU.mult, R=[rp3, r_csl[ci]], W=[rt2])
                TT('dve', qT[par][:, t0:t0 + n], t1[:, 0:n], t2[:, 0:n], ALU.add, R=[rt1, rt2], W=[r_qT[par]])
                e_, re_ = wtile()
                ACT(e_[:, 0:n], pg[:, 0:n], AF.Exp, scale=-1.0, R=[rpg], W=[re_])
                g_, rg_ = wtile()
                ACT(g_[:, 0:n], pg[:, 0:n], AF.Copy, R=[rpg], W=[rg_])
                TS('dve', e_[:, 0:n], e_[:, 0:n], 1.0, None, ALU.add, R=[re_], W=[re_])
                RCP(e_[:, 0:n], e_[:, 0:n], R=[re_], W=[re_])
                TT('pool', sga[par][:, t0:t0 + n], g_[:, 0:n], e_[:, 0:n], ALU.mult, R=[rg_, re_], W=[r_sga[par]])

            return [s1, s2, s3, s4, s5, s6, s7]

        def phase_attn(l, last):
            glist = [gi for gi, (sq_, t0, n) in enumerate(groups) if not (last and sq_ == 0)]

            def load_w(c):
                par = c % 2
                P.dma('pool', wqg[par][:, :, 0, :], wsrc(w_in, l, OQ + c * 128, 128), writes=[r_wqg[par]])
                P.dma('pool', wqg[par][:, :, 1, :], wsrc(w_in, l, OGA + c * 128, 128), writes=[r_wqg[par]])

            load_w(0)
            for gi in glist:
                for f_ in qga_stages(l, 0, gi, None, None):
                    f_()
            for c in range(4):
                par = c % 2
                if c + 1 < 4:
                    load_w(c + 1)
                order = [gi for gi in glist if groups[gi][0] == 1] + [gi for gi in glist if groups[gi][0] == 0]
                nxt = list(glist) if c + 1 < 4 else []
                for gi in order:
                    (sq_, t0, n) = groups[gi]
                    hk = qga_stages(l, c + 1, nxt.pop(0), 4, 6) if nxt else []
                    if sq_ == 0:
                        attend(l, c, par, t0, n, list(range(NTC)), gi, hk)
                    else:
                        attend(l, c, par, t0, n, list(range(NT)), gi, hk)

        def qk_epilogue_b(l, pq, rpq, n, t0, wcolidx, o, ro):
            old = psc[0]
            qk_epilogue_r(l, pq, rpq, n, t0, wcolidx, o, ro, 0, 8)

        def qk_epilogue_r(l, pq, rpq, n, t0, wcolidx, o, ro, lo, hi):
            sq, rsq = wbtile()
            ACT(sq[:, 0:n], pq[:, 0:n], AF.Square, R=[rpq], W=[rsq])
            p2, rp2 = psbank(lo, hi)
            MM(p2[:, 0:n], cb[:, CB_BLK, :], sq[:, 0:n], R=[rsq, r_cb], W=[rp2])
            rs, rrs = wtile()
            ACT(rs[:, 0:n], p2[:, 0:n], AF.Sqrt, bias=EPS, scale=1.0 / 64, R=[rp2], W=[rrs])
            RCP(rs[:, 0:n], rs[:, 0:n], R=[rrs], W=[rrs])
            qw, rqw = wtile()
            ACT(qw[:, 0:n], pq[:, 0:n], AF.Copy, scale=pp[:, l, wcolidx:wcolidx + 1], R=[rpq, r_pp], W=[rqw])
            TT('dve', qw[:, 0:n], qw[:, 0:n], rs[:, 0:n], ALU.mult, R=[rqw, rrs], W=[rqw])
            qb, rqb = wbtile()
            CP('pool', qb[:, 0:n], qw[:, 0:n], R=[rqw], W=[rqb])
            p3, rp3 = psbank(lo, hi)
            MM(p3[:, 0:n], cb[:, CB_RROT, :], qb[:, 0:n], R=[rqb, r_cb], W=[rp3])
            ci = cslc[0] % 2
            cslc[0] += 1
            P.dma('sp', csl[ci][0][:, 0:n], cos_in[:, t0:t0 + n], writes=[r_csl[ci]])
            P.dma('sp', csl[ci][1][:, 0:n], sin_in[:, t0:t0 + n], writes=[r_csl[ci]])
            t1, rt1 = wtile()
            TT('pool', t1[:, 0:n], qw[:, 0:n], csl[ci][0][:, 0:n], ALU.mult, R=[rqw, r_csl[ci]], W=[rt1])
            t2, rt2 = wtile()
            TT('dve', t2[:, 0:n], p3[:, 0:n], csl[ci][1][:, 0:n], ALU.mult, R=[rp3, r_csl[ci]], W=[rt2])
            TT('dve', o, t1[:, 0:n], t2[:, 0:n], ALU.add, R=[rt1, rt2], W=[ro])


        BT = ar('C', [128, 2, T]); r_BT = Res()
        CT = ar('C', [128, 2, T]); r_CT = Res()
        pre0 = ar('C', [128, T + 8]); r_pre0 = Res()
        pre = [pre0, pre0]; r_pre = [r_pre0, r_pre0]
        wx = [ar('C', [128, KC, 128]) for i in range(2)]; r_wx = RL(2)
        wdt = ar('C', [128, KC, 16]); r_wdt = Res()
        dgwz = ar('C', [128, 5120])
        dgs = dgwz.rearrange("p (a b c) -> p a b c", a=8, b=5); r_dgs = Res()
        wz = dgwz[:, 0:4096].rearrange("p (a b) -> p a b", a=KC); r_wz = Res()
        dt_all = ar('C', [128, NT, 16], F32); r_dt = Res()
        a_all = ar('C', [128, NT, 16], F32); r_a = Res()
        ahl = [ar('C', [128, NT, 16]) for i in range(4)]
        Eb_all = ar('C', [128, NT, 16], F32); r_Eb = Res()
        scbrow = ar('C', [1, 1024]); r_scbrow = Res()
        Aexp = sb("Aexp", [128, 16]); r_Aexp = Res()
        Hf = ar('C', [128, 512], F32); Hb = ar('C', [128, 512], F32); r_Hf = Res(); r_Hb = Res()
        Hfb = ar('C', [128, 512]); Hbb = ar('C', [128, 512]); r_Hfb = Res(); r_Hbb = Res()
        Et = [sb("Et%d" % i, [128, 80]) for i in range(2)]; r_Et = RL(2)
        ncum = [sb("ncum%d" % i, [128, 32]) for i in range(2)]; r_ncum = RL(2)
        dtd = [sb("dtd%d" % i, [128, 16]) for i in range(2)]; r_dtd = RL(2)
        CBm = [sb("CBm%d" % i, [128, 2, 2, 128]) for i in range(2)]; r_CBm = RL(2)
        xs_t = [ar('C', [128, 512]) for i in range(3)]; r_xs_t = RL(3)
        b_t = [ar('C', [128, 256]) for i in range(3)]; r_b_t = RL(3)
        p1t = [[ar('C', [128, 512]) for b in range(5)] for a in range(2)]
        r_p1t = [RL(5) for a in range(2)]
        r_xs_d = RL(len(groups) * 4); r_b_d = RL(len(groups) * 2)
        r_yp_d = RL(NT); r_sb_d = RL(NT); r_usT = RL(NT)
        def pcol(t):
            return t + 2 if t < C else t + 6

        import os as _os
        C1S = _os.environ.get("C1STEPS", "12345")

        def phase_ssd_c1(l):
            MSET('pool', pre0[:], 0.0, W=[r_pre0])
            for cc in range(8 if '1' in C1S else 0):
                for k in range(5):
                    TS('dve', dgs[:, cc, k, :], identF, pp[:, l, PP_SCW + cc * 5 + k:PP_SCW + cc * 5 + k + 1], None,
                       ALU.mult, R=[r_cf, r_pp], W=[r_dgs])
            if '1' in C1S:
                P.dma('pool', scbrow[0:1, :], scb_in[l], writes=[r_scbrow])
            P.dma('pool', wdt[:], wsrc(w_in, l, ODT, 16), writes=[r_wdt])
            for i0 in range(0, NT if '2' in C1S else 0, 32):
                nb = min(32, NT - i0)
                pd, rpd = psbank()
                for ii in range(nb):
                    i = i0 + ii
                    for k in range(KC):
                        MM(pd[:, ii * 16:(ii + 1) * 16], hT[:, k, i * 128:(i + 1) * 128], wdt[:, k, :],
                           start=(k == 0), stop=(k == KC - 1), R=[r_wdt, r_hT[i]], W=[rpd])
                xb, rxb = wtile()
                ax, rax = wtile()
                n16 = nb * 16
                TT('dve', xb[:, 0:n16].rearrange("p (a b) -> p a b", b=16), pd[:, 0:n16].rearrange("p (a b) -> p a b", b=16),
                   pr[:, l, PR_DTB:PR_DTB + 16].unsqueeze(1).broadcast_to([128, nb, 16]), ALU.add, R=[rpd, r_pr], W=[rxb])
                ACT(ax[:, 0:n16], xb[:, 0:n16], AF.Abs, R=[rxb], W=[rax])
                ACT(ax[:, 0:n16], ax[:, 0:n16], AF.Exp, scale=-1.0, R=[rax], W=[rax])
                ACT(ax[:, 0:n16], ax[:, 0:n16], AF.Ln, bias=1.0, R=[rax], W=[rax])
                TS('dve', xb[:, 0:n16], xb[:, 0:n16], 0.0, None, ALU.max, R=[rxb], W=[rxb])
                TT('dve', dt_all[:, i0:i0 + nb, :], xb[:, 0:n16].rearrange("p (a b) -> p a b", b=16),
                   ax[:, 0:n16].rearrange("p (a b) -> p a b", b=16), ALU.add, R=[rxb, rax], W=[r_dt])
            if '2' in C1S:
                ACT(Aexp[:], pr[:, l, PR_ALOG:PR_ALOG + 16], AF.Exp, R=[r_pr], W=[r_Aexp])
                STT('dve', a_all[:], dt_all[:], -1.0, Aexp[:].unsqueeze(1).broadcast_to([128, NT, 16]),
                    ALU.mult, ALU.mult, R=[r_dt, r_Aexp], W=[r_a])
                CP('dve', ahl[0][:], a_all[:], R=[r_a], W=[r_a])
                TT('dve', ahl[1][:], a_all[:], ahl[0][:], ALU.subtract, R=[r_a], W=[r_a])
                TS('dve', ahl[2][:], ahl[0][:], -1.0, None, ALU.mult, R=[r_a], W=[r_a])
                TS('dve', ahl[3][:], ahl[1][:], -1.0, None, ALU.mult, R=[r_a], W=[r_a])
            for cc in range(8 if '3' in C1S else 0):
                par = cc % 2
                P.dma('pool', wx[par][:], wsrc(w_in, l, OXBC + cc * 128, 128), writes=[r_wx[par]])
                for (sq_, t0, n) in groups:
                    pb, rpb = psbank()
                    for k in range(KC):
                        MM(pb[:, 0:n], wx[par][:, k, :], hT[:, k, t0:t0 + n], start=(k == 0), stop=(k == KC - 1),
                           R=[r_wx[par]] + rh(t0, n), W=[rpb])
                    ACT(pre[par][:, pcol(t0):pcol(t0) + n], pb[:, 0:n], AF.Copy, R=[rpb], W=[r_pre[par]])
                if cc < 6 and '4' in C1S:
                    for gi, (sq_, t0, n) in enumerate(groups):
                        pb, rpb = psbank()
                        nt_ = n // 128
                        for ii in range(nt_):
                            tt0 = t0 + ii * 128
                            for k in range(5):
                                MM(pb[:, ii * 128:(ii + 1) * 128], pre[par][:, pcol(tt0) + k - 2:pcol(tt0) + k - 2 + 128],
                                   dgs[:, cc, k, :], start=(k == 0), stop=False, R=[r_pre[par], r_dgs], W=[rpb])
                            MM(pb[:, ii * 128:(ii + 1) * 128], cb[0:1, CB_ONES, :], scbrow[0:1, cc * 128:(cc + 1) * 128],
                               start=False, stop=True, R=[r_cb, r_scbrow], W=[rpb])
                        stg, rstg = wbtile()
                        ACT(stg[:, 0:n], pb[:, 0:n], AF.Silu, R=[rpb], W=[rstg])
                        if cc < 4:
                            dst = xs_d[t0:t0 + n, cc * 128:(cc + 1) * 128].rearrange("(a p) c -> p a c", p=128)
                            rdst = r_xs_d[gi * 4 + cc]
                        else:
                            dst = b_d[t0:t0 + n, (cc - 4) * 128:(cc - 3) * 128].rearrange("(a p) c -> p a c", p=128)
                            rdst = r_b_d[gi * 2 + cc - 4]
                        P.dma('sp', dst, stg[:, 0:n].rearrange("p (a c) -> p a c", c=128), reads=[rstg], writes=[rdst])
                if cc >= 4 and '5' in C1S:
                    for (sq_, t0, n) in groups:
                        pb, rpb = psbank()
                        for k in range(5):
                            MM(pb[:, 0:n], dgs[:, cc, k, :], pre[par][:, pcol(t0) + k - 2:pcol(t0) + k - 2 + n],
                               start=(k == 0), stop=(k == 4), R=[r_pre[par], r_dgs], W=[rpb])
                        tb, rtb = wtile()
                        ACT(tb[:, 0:n], pb[:, 0:n], AF.Identity, bias=pp[:, l, PP_SCB + cc:PP_SCB + cc + 1],
                            R=[rpb, r_pp], W=[rtb])
                        if cc < 6:
                            ACT(BT[:, cc - 4, t0:t0 + n], tb[:, 0:n], AF.Silu, R=[rtb], W=[r_BT])
                        else:
                            ACT(CT[:, cc - 6, t0:t0 + n], tb[:, 0:n], AF.Silu, R=[rtb], W=[r_CT])

        ssdc = [0]

        def bc8(ap):
            return ap.unsqueeze(2).broadcast_to([128, 8, 64])

        def h3(ap):
            return ap.rearrange("p (h d) -> p h d", d=64)

        p1state = {}

        def ssd_pass1_a(l, i, with_out):
            e = ssdc[0]
            ssdc[0] += 1
            p2 = e % 2
            p3 = e % 3
            p1state[i] = (p2, p3)
            gi = [k for k, (sq_, t0, n) in enumerate(groups) if t0 <= i * 128 < t0 + n][0]
            P.dma('sp', xs_t[p3][:], xs_d[i * 128:(i + 1) * 128, :], reads=r_xs_d[gi * 4:gi * 4 + 4], writes=[r_xs_t[p3]])
            P.dma('sp', b_t[p3][:], b_d[i * 128:(i + 1) * 128, :], reads=r_b_d[gi * 2:gi * 2 + 2], writes=[r_b_t[p3]])
            a_i = a_all[:, i, :]
            pc, rpc = psbank()
            for bi, blk in enumerate((CF_TRIF, CF_TRIB, CF_STRIF, CF_STRIB, CF_ONES)):
                MM(pc[:, bi * 16:(bi + 1) * 16], cf[:, blk, :], a_i, R=[r_cf, r_a], W=[rpc])
            ACT(Et[p2][:], pc[:, 0:80], AF.Exp, R=[rpc], W=[r_Et[p2]])
            CP('pool', Eb_all[:, i, 0:8], Et[p2][:, 24:32], R=[r_Et[p2]], W=[r_Eb])
            CP('pool', Eb_all[:, i, 8:16], Et[p2][:, 72:80], R=[r_Et[p2]], W=[r_Eb])
            TT('pool', dtd[p2][:, 0:8], dt_all[:, i, 0:8], Et[p2][:, 32:40], ALU.mult, R=[r_dt, r_Et[p2]], W=[r_dtd[p2]])
            TT('pool', dtd[p2][:, 8:16], dt_all[:, i, 8:16], Et[p2][:, 56:64], ALU.mult, R=[r_dt, r_Et[p2]], W=[r_dtd[p2]])
            xs3 = h3(xs_t[p3][:])
            xddf, rxddf = p1t[p2][0], r_p1t[p2][0]
            xddb, rxddb = p1t[p2][1], r_p1t[p2][1]
            TT('pool', h3(xddf[:]), xs3, bc8(dtd[p2][:, 0:8]), ALU.mult, R=[r_xs_t[p3], r_dtd[p2]], W=[rxddf])
            TT('dve', h3(xddb[:]), xs3, bc8(dtd[p2][:, 8:16]), ALU.mult, R=[r_xs_t[p3], r_dtd[p2]], W=[rxddb])
            if with_out:
                xdf, rxdf = p1t[p2][2], r_p1t[p2][2]
                xdb, rxdb = p1t[p2][3], r_p1t[p2][3]
                xsD, rxsD = p1t[p2][4], r_p1t[p2][4]
                TT('dve', h3(xdf[:]), xs3, bc8(dt_all[:, i, 0:8]), ALU.mult, R=[r_xs_t[p3], r_dt], W=[rxdf])
                TT('dve', h3(xdb[:]), xs3, bc8(dt_all[:, i, 8:16]), ALU.mult, R=[r_xs_t[p3], r_dt], W=[rxdb])
                TT('pool', h3(xsD[:]), xs3, bc8(pr[:, l, PR_D:PR_D + 8]), ALU.mult, R=[r_xs_t[p3], r_pr], W=[rxsD])
                pcb, rpcb = psbank()
                for g in range(2):
                    MM(pcb[:, g * 128:(g + 1) * 128], BT[:, g, i * 128:(i + 1) * 128], CT[:, g, i * 128:(i + 1) * 128],
                       R=[r_BT, r_CT], W=[rpcb])
                for d_, blk in enumerate((CF_TRIF, CF_TRIB)):
                    TT('dve', CBm[p2][:, d_, :, :], pcb[:, 0:256].rearrange("p (g s) -> p g s", s=128),
                       cf[:, blk, :].unsqueeze(1).broadcast_to([128, 2, 128]), ALU.mult, R=[rpcb, r_cf], W=[r_CBm[p2]])

        def ssd_pass1_b(l, i, with_out):
            p2, p3 = p1state.pop(i)
            xddf, rxddf = p1t[p2][0], r_p1t[p2][0]
            xddb, rxddb = p1t[p2][1], r_p1t[p2][1]
            xdf, rxdf = p1t[p2][2], r_p1t[p2][2]
            xdb, rxdb = p1t[p2][3], r_p1t[p2][3]
            xsD, rxsD = p1t[p2][4], r_p1t[p2][4]
            if with_out:
                pyo, rpyo = psbank()
                P.batch_begin()
                for g in range(2):
                    MM(pyo[:, g * 256:(g + 1) * 256], CT[:, g, i * 128:(i + 1) * 128], Hfb[:, g * 256:(g + 1) * 256],
                       R=[r_CT, r_Hfb], W=[rpyo])
                P.batch_end()
            psf, rpsf = psbank()
            psb_, rpsb = psbank()
            P.batch_begin()
            for g in range(2):
                MM(psf[:, g * 256:(g + 1) * 256], b_t[p3][:, g * 128:(g + 1) * 128], xddf[:, g * 256:(g + 1) * 256],
                   R=[r_b_t[p3], rxddf], W=[rpsf])
            for g in range(2):
                MM(psb_[:, g * 256:(g + 1) * 256], b_t[p3][:, g * 128:(g + 1) * 128], xddb[:, g * 256:(g + 1) * 256],
                   R=[r_b_t[p3], rxddb], W=[rpsb])
            P.batch_end()
            TT('dve', h3(Hf[:]), h3(Hf[:]), bc8(Et[p2][:, 64:72]), ALU.mult, R=[r_Hf, r_Et[p2]], W=[r_Hf])
            TT('dve', Hf[:], Hf[:], psf[:], ALU.add, R=[r_Hf, rpsf], W=[r_Hf])
            ACT(Hfb[:], Hf[:], AF.Copy, R=[r_Hf], W=[r_Hfb])
            sbt, rsbt = wtile()
            ACT(sbt[:], psb_[:], AF.Copy, R=[rpsb], W=[rsbt])
            P.dma('sp', sb_d[i], sbt[:], reads=[rsbt], writes=[r_sb_d[i]])
            if not with_out:
                return
            tmp, rtmp = wtile()
            TT('dve', h3(tmp[:]), h3(pyo[:]), bc8(Et[p2][:, 0:8]), ALU.mult, R=[rpyo, r_Et[p2]], W=[rtmp])
            py, rpy = psbank()
            MM(py[:, :], identB, xsD[:], start=True, stop=False, R=[r_cb, rxsD], W=[rpy])
            steps = [(d_, hq) for d_ in range(2) for hq in range(2)]
            stash = {}

            def x1_step(d_, hq):
                tri = CF_TRIF if d_ == 0 else CF_TRIB
                neg = CB_NEGF if d_ == 0 else CB_NEGB
                px, rpx = psbank()
                P.batch_begin()
                for hh in range(4):
                    h = hq * 4 + hh
                    col = d_ * 8 + h
                    MM(px[:, hh * 128:(hh + 1) * 128], identB, cb[:, neg, :], start=True, stop=False,
                       R=[r_cb], W=[rpx])
                    trib = CB_TRIF if d_ == 0 else CB_TRIB
                    for hl in range(2):
                        MM(px[:, hh * 128:(hh + 1) * 128], ahl[hl][:, i, col:col + 1].broadcast_to([128, 128]),
                           cb[:, trib, :], start=False, stop=False, R=[r_a, r_cb], W=[rpx])
                    for hl in range(2):
                        MM(px[:, hh * 128:(hh + 1) * 128], cb[:, trib, :],
                           ahl[2 + hl][:, i, col:col + 1].broadcast_to([128, 128]), start=False, stop=(hl == 1),
                           R=[r_a, r_cb], W=[rpx])
                P.batch_end()
                lt4, rlt4 = wtile()
                ACT(lt4[:], px[:], AF.Exp, R=[rpx], W=[rlt4])
                mt4, rmt4 = wbtile()
                TT('dve', mt4[:].rearrange("p (a b) -> p a b", b=128), lt4[:].rearrange("p (a b) -> p a b", b=128),
                   CBm[p2][:, d_, hq, :].unsqueeze(1).broadcast_to([128, 4, 128]), ALU.mult,
                   R=[rlt4, r_CBm[p2]], W=[rmt4])
                stash[(d_, hq)] = (mt4, rmt4)

            for st_ in steps:
                x1_step(*st_)
            for si_, (d_, hq) in enumerate(steps):
                xd_, rxd_ = (xdf, rxdf) if d_ == 0 else (xdb, rxdb)
                mt4, rmt4 = stash[(d_, hq)]
                P.batch_begin()
                for hh in range(4):
                    h = hq * 4 + hh
                    MM(py[:, h * 64:(h + 1) * 64], mt4[:, hh * 128:(hh + 1) * 128], xd_[:, h * 64:(h + 1) * 64],
                       start=False, stop=(d_ == 1 and h == 7), R=[rmt4, rxd_], W=[rpy])
                P.batch_end()
            ypt, rypt = wtile()
            TT('dve', ypt[:], tmp[:], py[:], ALU.add, R=[rtmp, rpy], W=[rypt])
            P.dma('sp', yp_d[i * 128:(i + 1) * 128, :], ypt[:], reads=[rypt], writes=[r_yp_d[i]])

        p2state = {}

        def ssd_pass2_a(l, i, with_out):
            sbt, rsbt = wtile()
            P.dma('sp', sbt[:], sb_d[i], reads=[r_sb_d[i]], writes=[rsbt])
            st = {}
            if with_out:
                st['ypt'] = wtile()
                ypt, rypt = st['ypt']
                P.dma('sp', ypt[:], yp_d[i * 128:(i + 1) * 128, :], reads=[r_yp_d[i]], writes=[rypt])
                st['pyo'] = psbank()
                pyo, rpyo = st['pyo']
                P.batch_begin()
                for g in range(2):
                    MM(pyo[:, g * 256:(g + 1) * 256], CT[:, g, i * 128:(i + 1) * 128], Hbb[:, g * 256:(g + 1) * 256],
                       R=[r_CT, r_Hbb], W=[rpyo])
                P.batch_end()
            TT('dve', h3(Hb[:]), h3(Hb[:]), bc8(Eb_all[:, i, 8:16]), ALU.mult, R=[r_Hb, r_Eb], W=[r_Hb])
            TT('dve', Hb[:], Hb[:], sbt[:], ALU.add, R=[r_Hb, rsbt], W=[r_Hb])
            ACT(Hbb[:], Hb[:], AF.Copy, R=[r_Hb], W=[r_Hbb])
            if with_out:
                st['pz'] = psbank()
                pz, rpz = st['pz']
                P.batch_begin()
                for k in range(KC):
                    MM(pz[:, :], hT[:, k, i * 128:(i + 1) * 128], wz[:, k, :], start=(k == 0), stop=(k == KC - 1),
                       R=[r_wz, r_hT[i]], W=[rpz])
                P.batch_end()
                st['ez'] = wtile()
                ez, rez = st['ez']
                ACT(ez[:], pz[:], AF.Silu, R=[rpz], W=[rez])
            p2state[i] = st

        def ssd_pass2_b(l, i, with_out):
            st = p2state.pop(i)
            if not with_out:
                return
            ypt, rypt = st['ypt']
            pyo, rpyo = st['pyo']
            pz, rpz = st['pz']
            ez, rez = st['ez']
            tmp, rtmp = wtile()
            TT('dve', h3(tmp[:]), h3(pyo[:]), bc8(Eb_all[:, i, 0:8]), ALU.mult, R=[rpyo, r_Eb], W=[rtmp])
            TT('pool', ypt[:], ypt[:], tmp[:], ALU.add, R=[rypt, rtmp], W=[rypt])
            sz, rsz = wtile()
            TT('dve', ypt[:], ypt[:], ez[:], ALU.mult, R=[rypt, rez], W=[rypt])
            s4, rs4 = st4[i % 2], r_st4[i % 2]
            ACT(sz[:], ypt[:], AF.Square, accum=s4[:, 0:1], R=[rypt], W=[rsz, rs4])
            ACT(s4[:, 1:2], s4[:, 0:1], AF.Ln, bias=EPS, scale=1.0 / 512, R=[rs4], W=[rs4])
            ACT(s4[:, 2:3], s4[:, 1:2], AF.Exp, scale=-0.5, R=[rs4], W=[rs4])
            ACT(ypt[:], ypt[:], AF.Copy, scale=s4[:, 2:3], R=[rypt, rs4], W=[rypt])
            pt_, rpt_ = psbank()
            for cc in range(4):
                TR(pt_[:, cc * 128:(cc + 1) * 128], ypt[:, cc * 128:(cc + 1) * 128], identF, R=[rypt, r_cf], W=[rpt_])
            ust, rust = wbtile()
            TT('dve', ust[:].rearrange("p (c t) -> p c t", t=128), pt_[:].rearrange("p (c t) -> p c t", t=128),
               pp[:, l, PP_SNW:PP_SNW + 4].unsqueeze(2).broadcast_to([128, 4, 128]), ALU.mult, R=[rpt_, r_pp], W=[rust])
            P.dma('sp', usT.rearrange("(c p) t -> p c t", p=128)[:, :, i * 128:(i + 1) * 128],
                  ust[:].rearrange("p (c t) -> p c t", t=128), reads=[rust], writes=[r_usT[i]])

        def phase_ssd(l, last):
            phase_ssd_c1(l)
            if stop_after == 'ssd_c1':
                return
            P.barrier()
            P.dma('pool', wz[:], wsrc(w_in, l, OZ, 512), writes=[r_wz])
            MSET('dve', Hf[:], 0.0, W=[r_Hf]); MSET('dve', Hb[:], 0.0, W=[r_Hb])
            MSET('pool', Hfb[:], 0.0, W=[r_Hfb]); MSET('pool', Hbb[:], 0.0, W=[r_Hbb])
            for (s0, ntl, wo) in ((0, NTC, not last), (NTC, NTL, True)):
                ssd_pass1_a(l, s0, wo)
                for ii in range(ntl):
                    if ii + 1 < ntl:
                        ssd_pass1_a(l, s0 + ii + 1, wo)
                    ssd_pass1_b(l, s0 + ii, wo)
                if stop_after == 'ssd_p1':
                    continue
                order2 = list(reversed(range(ntl)))
                ssd_pass2_a(l, s0 + order2[0], wo)
                for k_, ii in enumerate(order2):
                    if k_ + 1 < len(order2):
                        ssd_pass2_a(l, s0 + order2[k_ + 1], wo)
                    ssd_pass2_b(l, s0 + ii, wo)


        vT = ar('D', [128, 4, T + 60]); r_vT = Res()
        dgc = ar('D', [128, 4, 31, 128]); r_dgc_d = Res(); r_dgc_a = Res()
        wgcv = ar('D', [128, KC, 512]); r_wgcv = Res()
        wug = [ar('D', [128, KC, 2, 128]) for i in range(2)]; r_wug = RL(2)
        ycsA = [ar('D', [128, 512], F32) for i in range(4)]
        wug_flat = wug[0]
        ycsB = [wug[i // 2].rearrange("p a b c -> p (a b c)")[:, (i % 2) * 1024:(i % 2) * 1024 + 1024].bitcast(F32) for i in range(4)]
        ycs2 = [ycsA, ycsB]; r_ycs2 = [RL(4), RL(4)]
        mD = ar('D', [128, 512], F32); m2D = ar('D', [128, 512], F32); r_mD = Res(); r_m2D = Res()
        r_ucT = RL(4 * len(groups))

        def vcol(t):
            return t + 15 if t < C else t + 45

        def phase_conv(l, last):
            MSET('pool', vT[:], 0.0, W=[r_vT])
            for cc in range(4):
                for k in range(31):
                    if k % 2 == 0:
                        TS('dve', dgc[:, cc, k, :], identF,
                           pp[:, l, PP_CCW + cc * 31 + k:PP_CCW + cc * 31 + k + 1], None, ALU.mult, R=[r_cf, r_pp], W=[r_dgc_d])
                    else:
                        ACT(dgc[:, cc, k, :], identF, AF.Copy, scale=pp[:, l, PP_CCW + cc * 31 + k:PP_CCW + cc * 31 + k + 1],
                            R=[r_cf, r_pp], W=[r_dgc_a])
            P.dma('pool', wgcv[:], wsrc(w_in, l, OGCV, 512), writes=[r_wgcv])
            for cc in range(4):
                par = cc % 2
                P.dma('pool', wug[par][:, :, 0, :], wsrc(w_in, l, OGLU + cc * 128, 128), writes=[r_wug[par]])
                P.dma('pool', wug[par][:, :, 1, :], wsrc(w_in, l, OGLU + 512 + cc * 128, 128), writes=[r_wug[par]])
                for (sq_, t0, n) in groups:
                    if last and sq_ == 0:
                        continue
                    pu, rpu = psbank()
                    pg, rpg = psbank()
                    for k in range(KC):
                        MM(pu[:, 0:n], wug[par][:, k, 0, :], hT[:, k, t0:t0 + n], start=(k == 0), stop=(k == KC - 1),
                           R=[r_wug[par]] + rh(t0, n), W=[rpu])
                    for k in range(KC):
                        MM(pg[:, 0:n], wug[par][:, k, 1, :], hT[:, k, t0:t0 + n], start=(k == 0), stop=(k == KC - 1),
                           R=[r_wug[par]] + rh(t0, n), W=[rpg])
                    sg, rsg = wtile()
                    ACT(sg[:, 0:n], pg[:, 0:n], AF.Sigmoid, R=[rpg], W=[rsg])
                    TT('dve', vT[:, cc, vcol(t0):vcol(t0) + n], pu[:, 0:n], sg[:, 0:n], ALU.mult, R=[rpu, rsg], W=[r_vT])
            P.barrier()
            for gi, (sq_, t0, n) in enumerate(groups):
                if last and sq_ == 0:
                    continue
                ycs, r_ycs = ycs2[gi % 2], r_ycs2[gi % 2]
                psm, rpsm = psbank()
                psq, rpsq = psbank()
                pcvs = {}

                def conv_mm(cc):
                    pcvs[cc] = psbank()
                    pcv, rpcv = pcvs[cc]
                    P.batch_begin()
                    for k in range(31):
                        MM(pcv[:, 0:n], dgc[:, cc, k, :], vT[:, cc, vcol(t0) + k - 15:vcol(t0) + k - 15 + n],
                           start=(k == 0), stop=(k == 30), R=[r_dgc_d, r_dgc_a, r_vT], W=[rpcv])
                    P.batch_end()

                conv_mm(0)
                for cc in range(4):
                    if cc + 1 < 4:
                        conv_mm(cc + 1)
                    pcv, rpcv = pcvs[cc]
                    ACT(ycs[cc][:, 0:n], pcv[:, 0:n], AF.Identity, bias=pp[:, l, PP_CCB + cc:PP_CCB + cc + 1],
                        R=[rpcv, r_pp], W=[r_ycs[cc]])
                    ysq, rysq = wtile()
                    ACT(ysq[:, 0:n], ycs[cc][:, 0:n], AF.Square, R=[r_ycs[cc]], W=[rysq])
                    MM(psm[:, 0:n], cf[:, CF_ONES, :], ycs[cc][:, 0:n], start=(cc == 0), stop=(cc == 3),
                       R=[r_cf, r_ycs[cc]], W=[rpsm])
                    MM(psq[:, 0:n], cf[:, CF_ONES, :], ysq[:, 0:n], start=(cc == 0), stop=(cc == 3),
                       R=[r_cf, rysq], W=[rpsq])
                m, rm = mD, r_mD
                ACT(m[:, 0:n], psm[:, 0:n], AF.Copy, scale=1.0 / 512, R=[rpsm], W=[rm])
                m2, rm2 = m2D, r_m2D
                TT('pool', m2[:, 0:n], m[:, 0:n], m[:, 0:n], ALU.mult, R=[rm], W=[rm2])
                STT('dve', m2[:, 0:n], psq[:, 0:n], 1.0 / 512, m2[:, 0:n], ALU.mult, ALU.subtract, R=[rpsq, rm2], W=[rm2])
                ACT(m2[:, 0:n], m2[:, 0:n], AF.Sqrt, bias=EPS, R=[rm2], W=[rm2])
                RCP(m2[:, 0:n], m2[:, 0:n], R=[rm2], W=[rm2])
                for cc in range(4):
                    t_, rt_ = wtile()
                    TT('pool', t_[:, 0:n], ycs[cc][:, 0:n], m[:, 0:n], ALU.subtract, R=[r_ycs[cc], rm], W=[rt_])
                    TT('dve', t_[:, 0:n], t_[:, 0:n], m2[:, 0:n], ALU.mult, R=[rt_, rm2], W=[rt_])
                    ACT(t_[:, 0:n], t_[:, 0:n], AF.Identity, scale=pp[:, l, PP_CLW + cc:PP_CLW + cc + 1],
                        bias=pp[:, l, PP_CLB + cc:PP_CLB + cc + 1], R=[rt_, r_pp], W=[rt_])
                    ACT(t_[:, 0:n], t_[:, 0:n], AF.Silu, R=[rt_], W=[rt_])
                    pgc, rpgc = psbank()
                    for k in range(KC):
                        MM(pgc[:, 0:n], wgcv[:, k, cc * 128:(cc + 1) * 128], hT[:, k, t0:t0 + n], start=(k == 0),
                           stop=(k == KC - 1), R=[r_wgcv] + rh(t0, n), W=[rpgc])
                    sgc, rsgc = wtile()
                    ACT(sgc[:, 0:n], pgc[:, 0:n], AF.Silu, R=[rpgc], W=[rsgc])
                    ucb, rucb = wbtile()
                    TT('dve', ucb[:, 0:n], t_[:, 0:n], sgc[:, 0:n], ALU.mult, R=[rt_, rsgc], W=[rucb])
                    P.dma('sp', ucT[cc * 128:(cc + 1) * 128, t0:t0 + n], ucb[:, 0:n], reads=[rucb],
                          writes=[r_ucT[cc * len(groups) + gi]])

        wgm = ar('E', [128, KC, 3072]); r_wgm6 = RL(6)
        wbr = ar('E', [128, 4, 3, D]); r_wbr6 = RL(6)
        ut = [ar('E', [128, 4, 512]) for i in range(3)]; r_ut = RL(3)
        wo = ar('F', [128, KC, D]); r_wo = Res()
        yTt = ar('F', [128, KC, 512]); r_yTt = Res()
        zT = ar('F', [128, KC, 512], F32); r_zT = Res()
        xo = [ar('F', [128, D], F32) for i in range(2)]; r_xo = RL(2)
        fnw = ar('F', [128, D], F32); r_fnw = Res()
        xtF = [ar('F', [128, D], F32) for i in range(4)]
        xnF = [ar('F', [128, D], F32) for i in range(2)]
        r_yT_d = RL(len(groups))

        def phase_merge(l, last):
            wsrcs = (w_bra, w_brs, w_brc)
            for hf in range(2):
                for bi in range(3):
                    b6 = bi * 2 + hf
                    P.dma('pool', wgm[:, :, b6 * 512:(b6 + 1) * 512], wsrc(w_in, l, OGM + b6 * 512, 512), writes=[r_wgm6[b6]])
                    P.dma('pool', wbr[:, :, bi, hf * 512:(hf + 1) * 512],
                          wsrcs[bi][l, :, hf * 512:(hf + 1) * 512].rearrange("(c p) n -> p c n", p=128), writes=[r_wbr6[b6]])
            usrc = (uaT, usT, ucT)
            for gi, (sq_, t0, n) in enumerate(groups):
                if last and sq_ == 0:
                    continue
                for bi in range(3):
                    if bi == 0:
                        rr = [r_uaT[c * len(groups) + gi] for c in range(4)]
                    elif bi == 1:
                        rr = r_usT[t0 // 128:(t0 + n) // 128]
                    else:
                        rr = [r_ucT[c * len(groups) + gi] for c in range(4)]
                    P.dma('sp', ut[bi][:, :, 0:n], usrc[bi].rearrange("(c p) t -> p c t", p=128)[:, :, t0:t0 + n],
                          reads=rr, writes=[r_ut[bi]])
                for j in range(8):
                    acc, racc = wtile()
                    for bi in range(3):
                        pg, rpg = psbank()
                        for k in range(KC):
                            MM(pg[:, 0:n], wgm[:, k, bi * 1024 + j * 128:bi * 1024 + (j + 1) * 128], hT[:, k, t0:t0 + n],
                               start=(k == 0), stop=(k == KC - 1), R=[r_wgm6[bi * 2 + j // 4]] + rh(t0, n), W=[rpg])
                        gt, rgt = wtile()
                        ACT(gt[:, 0:n], pg[:, 0:n], AF.Identity, bias=pp[:, l, PP_BG + bi * 8 + j:PP_BG + bi * 8 + j + 1],
                            R=[rpg, r_pp], W=[rgt])
                        ACT(gt[:, 0:n], gt[:, 0:n], AF.Sigmoid, R=[rgt], W=[rgt])
                        pb, rpb = psbank()
                        for k4 in range(4):
                            MM(pb[:, 0:n], wbr[:, k4, bi, j * 128:(j + 1) * 128], ut[bi][:, k4, 0:n], start=(k4 == 0),
                               stop=(k4 == 3), R=[r_wbr6[bi * 2 + j // 4], r_ut[bi]], W=[rpb])
                        if bi == 0:
                            TT('dve', acc[:, 0:n], pb[:, 0:n], gt[:, 0:n], ALU.mult, R=[rpb, rgt], W=[racc])
                        else:
                            TT('dve', gt[:, 0:n], pb[:, 0:n], gt[:, 0:n], ALU.mult, R=[rpb, rgt], W=[rgt])
                            TT('pool', acc[:, 0:n], acc[:, 0:n], gt[:, 0:n], ALU.add, R=[racc, rgt], W=[racc])
                    yb, ryb = wbtile()
                    CP('pool', yb[:, 0:n], acc[:, 0:n], R=[racc], W=[ryb])
                    P.dma('sp', yT_d[j * 128:(j + 1) * 128, t0:t0 + n], yb[:, 0:n], reads=[ryb], writes=[r_yT_d[gi]])

        def phase_out(l, last):
            xt, xn = xtF, xnF
            P.dma('sp', fnw[:], fnw_in[:, :], writes=[r_fnw])
            for hf in range(2):
                P.dma('pool', wo[:, :, hf * 512:(hf + 1) * 512], wsrc(w_out, l, hf * 512, 512), writes=[r_wo])
            for gi, (sq_, t0, n) in enumerate(groups):
                if last and sq_ == 0:
                    continue
                col = 1 if sq_ == 0 else 0
                P.dma('sp', yTt[:, :, 0:n], yT_d.rearrange("(c p) t -> p c t", p=128)[:, :, t0:t0 + n],
                      reads=[r_yT_d[gi]], writes=[r_yTt])
                for j2 in range(8):
                    po, rpo = psbank()
                    for j in range(8):
                        MM(po[:, 0:n], wo[:, j, j2 * 128:(j2 + 1) * 128], yTt[:, j, 0:n], start=(j == 0), stop=(j == 7),
                           R=[r_wo, r_yTt], W=[rpo])
                    ACT(zT[:, j2, 0:n], po[:, 0:n], AF.Copy, scale=modv[:, 16 + j2, col:col + 1], R=[rpo, r_modv], W=[r_zT])
                for si in range(n // 128):
                    i = t0 // 128 + si
                    s = i % 2
                    sx = i % 4
                    rsrc = [r_x1[i]] if l > 0 else []
                    P.dma('sp', xt[sx][:], src_tile(l, i), reads=rsrc, writes=[r_xt[sx]])
                    for b in range(2):
                        pb, rpb = psbank()
                        for kk in range(4):
                            j2 = b * 4 + kk
                            TR(pb[:, kk * 128:(kk + 1) * 128], zT[:, j2, si * 128:(si + 1) * 128], identF,
                               R=[r_zT, r_cf], W=[rpb])
                        TT('dve', xo[s][:, b * 512:(b + 1) * 512], pb[:, :], xt[sx][:, b * 512:(b + 1) * 512], ALU.add,
                           R=[rpb, r_xt[sx]], W=[r_xo[s]])
                    if not last:
                        dst = (xc1 if i < NTC else x1)[((i if i < NTC else i - NTC) * 128):((i if i < NTC else i - NTC) * 128 + 128), :]
                        P.dma('pool', dst, xo[s][:], reads=[r_xo[s]], writes=[r_x1[i]])
                    else:
                        s4, rs4 = st4[s], r_st4[s]
                        ACT(xn[s][:], xo[s][:], AF.Square, accum=s4[:, 0:1], R=[r_xo[s]], W=[r_xn[s], rs4])
                        ACT(s4[:, 1:2], s4[:, 0:1], AF.Sqrt, bias=EPS, scale=1.0 / D, R=[rs4], W=[rs4])
                        RCP(s4[:, 2:3], s4[:, 1:2], R=[rs4], W=[rs4])
                        ACT(xn[s][:], xo[s][:], AF.Copy, scale=s4[:, 2:3], R=[r_xo[s], rs4], W=[r_xn[s]])
                        TT('pool', xn[s][:], xn[s][:], fnw[:], ALU.mult, R=[r_xn[s], r_fnw], W=[r_xn[s]])
                        P.dma('pool', out[(i - NTC) * 128:(i - NTC + 1) * 128, :], xn[s][:], reads=[r_xn[s]])

        for l in range(DEPTH):
            last = (l == DEPTH - 1)
            P.barrier()
            phase_mod(l)
            P.barrier()
            phase_hT(l)
            P.barrier()
            if stop_after == 'hT':
                break
            phase_kv(l)
            phase_attn(l, last)
            if stop_after == 'attn':
                break
            P.barrier()
            phase_ssd(l, last)
            if stop_after in ('ssd', 'ssd_c1', 'ssd_p1'):
                break
            P.barrier()
            phase_conv(l, last)
            if stop_after == 'conv':
                break
            P.barrier()
            phase_merge(l, last)
            P.barrier()
            phase_out(l, last)
        if dbg:
            hT_o = nc.dram_tensor("hT_o", [128, KC, T], BF16, kind="ExternalOutput").ap()
            P.dma('sp', hT_o[:, :, :], hT[:], reads=r_hT)
        P.emit()
    return nc


def _cols(v):
    v = np.asarray(v, np.float32)
    return np.ascontiguousarray(v.reshape(-1, 128).T)


def host_prep(inp, L, C, DEPTH, nb):
    f32 = np.float32
    cfc, cbc = host_consts()
    cosT, sinT = host_rope(L, C)
    pp = np.zeros((DEPTH, 128, NPP), f32)
    pr = np.zeros((DEPTH, 128, NPR), f32)
    for l in range(DEPTH):
        pp[l, :, PP_BMOD:PP_BMOD + 24] = _cols(inp['b_mod'][l])
        pp[l, :, PP_NORMW:PP_NORMW + 8] = _cols(inp['norm_w'][l])
        pp[l, :, PP_QW] = np.tile(np.asarray(inp['q_norm_w'][l], f32), 2)
        pp[l, :, PP_KW] = np.tile(np.asarray(inp['k_norm_w'][l], f32), 2)
        scw = np.asarray(inp['ssd_conv_w'][l], f32)
        pp[l, :, PP_SCW:PP_SCW + 40] = scw.reshape(5, 8, 128).transpose(2, 1, 0).reshape(128, 40)
        pp[l, :, PP_SCB:PP_SCB + 8] = _cols(inp['ssd_conv_b'][l])
        pp[l, :, PP_SNW:PP_SNW + 4] = _cols(inp['ssd_norm_w'][l])
        ccw = np.asarray(inp['cm_conv_w'][l], f32)
        pp[l, :, PP_CCW:PP_CCW + 124] = ccw.reshape(31, 4, 128).transpose(2, 1, 0).reshape(128, 124)
        pp[l, :, PP_CCB:PP_CCB + 4] = _cols(inp['cm_conv_b'][l])
        pp[l, :, PP_CLW:PP_CLW + 4] = _cols(inp['cm_ln_w'][l])
        pp[l, :, PP_CLB:PP_CLB + 4] = _cols(inp['cm_ln_b'][l])
        pp[l, :, PP_BG:PP_BG + 24] = _cols(np.asarray(inp['b_gate'][l], f32).reshape(-1))
        pr[l, :, PR_DTB:PR_DTB + 16] = np.asarray(inp['ssd_dt_bias'][l], f32).reshape(1, 16)
        pr[l, :, PR_ALOG:PR_ALOG + 16] = np.asarray(inp['ssd_A_log'][l], f32).reshape(1, 16)
        pr[l, :, PR_D:PR_D + 8] = np.asarray(inp['ssd_D'][l], f32).reshape(1, 8)
    fnw = np.ascontiguousarray(np.broadcast_to(np.asarray(inp['final_norm_w'], f32)[None, :], (128, D)))
    shared = {
        'w_mod': np.ascontiguousarray(inp['w_mod'], f32), 'w_in': np.ascontiguousarray(inp['w_in'], f32),
        'w_bra': np.ascontiguousarray(inp['w_br_attn'], f32), 'w_brs': np.ascontiguousarray(inp['w_br_ssd'], f32),
        'w_brc': np.ascontiguousarray(inp['w_br_conv'], f32), 'w_out': np.ascontiguousarray(inp['w_out'], f32),
        'pp': pp, 'pr': pr, 'scb': np.ascontiguousarray(np.asarray(inp['ssd_conv_b'], f32).reshape(DEPTH, 1, 1024)), 'fnw': fnw, 'cf': cfc, 'cb': cbc, 'cosT': cosT, 'sinT': sinT,
    }
    maps = []
    cctx = _cols(inp['c_ctx'])
    for b in range(nb):
        cvec = np.zeros((128, 8, 2), f32)
        cvec[:, :, 0] = _cols(inp['c'][b])
        cvec[:, :, 1] = cctx
        m = dict(shared)
        m['x'] = np.ascontiguousarray(inp['x'][b], f32)
        m['ctx'] = np.ascontiguousarray(inp['ctx'][b], f32)
        m['cvec'] = cvec.reshape(128, 16)
        maps.append(m)
    return maps


_NC_CACHE = {}


def kernel(**inputs):
    x = np.asarray(inputs['x'])
    B, L, _ = x.shape
    C = np.asarray(inputs['ctx']).shape[1]
    DEPTH = np.asarray(inputs['w_in']).shape[0]
    key = (L, C, DEPTH)
    if key not in _NC_CACHE:
        _NC_CACHE[key] = build(L, C, DEPTH)
    nc = _NC_CACHE[key]
    maps = host_prep(inputs, L, C, DEPTH, B)
    res = run_bass_kernel_spmd(nc, maps, core_ids=list(range(B)))
    return np.stack([np.asarray(r['out'], np.float32) for r in res.results], axis=0)
```

```python
import numpy as np
from contextlib import ExitStack
import concourse.bass as bass
import concourse.mybir as mybir
from concourse.bass_utils import run_bass_kernel_spmd

F32 = mybir.dt.float32
BF16 = mybir.dt.bfloat16
AF = mybir.ActivationFunctionType
ALU = mybir.AluOpType

COMPUTE = ('pe', 'act', 'dve', 'pool')
NDMASEM = 12


class Res:
    __slots__ = ('w', 'rd', 'rdma')

    def __init__(self):
        self.w = None
        self.rd = {}
        self.rdma = []


def RL(n):
    return [Res() for _ in range(n)]


class Op:
    __slots__ = ('eng', 'fn', 'deps', 'dma', 'needs_inc', 'cnt', 'sem', 'target', 'prewait', 'batch')

    def __init__(self, eng, fn, dma):
        self.batch = None
        self.eng = eng
        self.fn = fn
        self.dma = dma
        self.deps = []
        self.needs_inc = False
        self.cnt = 0
        self.sem = None
        self.target = 0
        self.prewait = None


class Prog:
    def __init__(self, nc):
        self.nc = nc
        self.streams = {e: [] for e in ('pe', 'act', 'dve', 'pool', 'sp')}
        self.pending = {}
        self.dmas = []
        self.cur_batch = None
        self.batches = []

    def batch_begin(self):
        self.cur_batch = {}
        self.batches.append(self.cur_batch)

    def batch_end(self):
        self.cur_batch = None

    def barrier(self):
        deps = []
        for e, st in self.streams.items():
            for op in reversed(st):
                if not op.dma:
                    deps.append(op)
                    break
        deps.extend(self.dmas)
        self.dmas = []
        self.pending = {e: list(deps) for e in self.streams}

    def add(self, eng, fn, reads=(), writes=(), dma=False):
        op = Op(eng, fn, dma)
        deps = {}
        if self.pending.get(eng):
            for o in self.pending.pop(eng):
                deps[id(o)] = o
        if dma:
            self.dmas.append(op)
        for r in reads:
            if r.w is not None:
                deps[id(r.w)] = r.w
        for w in writes:
            if w.w is not None:
                deps[id(w.w)] = w.w
            for o in w.rd.values():
                deps[id(o)] = o
            for o in w.rdma:
                deps[id(o)] = o
        for r in reads:
            if dma:
                r.rdma.append(op)
            else:
                r.rd[eng] = op
        for w in writes:
            w.w = op
            w.rd = {}
            w.rdma = []
        for d in deps.values():
            if d is op:
                continue
            if (not d.dma) and (not dma) and d.eng == eng and eng == 'pe':
                continue
            op.deps.append(d)
            if not d.dma:
                d.needs_inc = True
        if self.cur_batch is not None and not dma and eng == 'pe':
            op.batch = self.cur_batch
            self.cur_batch.setdefault(eng, []).append(op)
        self.streams[eng].append(op)
        return op

    def coarsen(self):
        for st in self.streams.values():
            for x in st:
                nd = []
                for d in x.deps:
                    if d.batch is not None and not d.dma and not (x.batch is d.batch and x.eng == d.eng):
                        d = d.batch[d.eng][-1]
                    nd.append(d)
                x.deps = nd
        for b in self.batches:
            for e, ops in b.items():
                union = {}
                for o in ops:
                    for d in o.deps:
                        if d.batch is b and d.eng == e and not d.dma:
                            continue
                        union[id(d)] = d
                    o.deps = []
                ops[0].deps = list(union.values())
        for st in self.streams.values():
            for x in st:
                x.needs_inc = False
        for st in self.streams.values():
            for x in st:
                for d in x.deps:
                    if not d.dma:
                        d.needs_inc = True

    def dma(self, q, out, in_, reads=(), writes=(), slow=False):
        if slow:
            return self.add(q, lambda e: e.dma_start(out=out, in_=in_, allow_slow_non_contiguous=True),
                            reads, writes, dma=True)
        return self.add(q, lambda e: e.dma_start(out=out, in_=in_), reads, writes, dma=True)

    def emit(self):
        nc = self.nc
        self.coarsen()
        with ExitStack() as es:
            csem = {e: es.enter_context(nc.semaphore('cs_' + e)) for e in COMPUTE}
            dsem = {}
            for q in ('sp', 'act', 'pool'):
                dsem[q] = [es.enter_context(nc.semaphore('ds_%s%d' % (q, i))) for i in range(NDMASEM)]
            for e, st in self.streams.items():
                c = 0
                nd = 0
                for op in st:
                    if op.dma:
                        op.sem = dsem[e][nd % NDMASEM]
                        op.target = 16 * (nd // NDMASEM + 1)
                        if nd >= NDMASEM:
                            op.prewait = (op.sem, 16 * (nd // NDMASEM))
                        nd += 1
                    else:
                        if op.needs_inc:
                            c += 1
                        op.cnt = c
            block = es.enter_context(nc.Block())
            streams = self.streams

            def run_stream(ename, eng):
                known = {}
                for op in streams[ename]:
                    need = {}
                    if op.prewait is not None:
                        need[id(op.prewait[0])] = op.prewait
                    for d in op.deps:
                        if d.dma:
                            s, v = d.sem, d.target
                        else:
                            s, v = csem[d.eng], d.cnt
                        k = id(s)
                        if k not in need or need[k][1] < v:
                            need[k] = (s, v)
                    for k, (s, v) in need.items():
                        if known.get(k, 0) >= v:
                            continue
                        eng.wait_ge(s, v)
                        known[k] = v
                    ins = op.fn(eng)
                    if op.dma:
                        ins.then_inc(op.sem, 16)
                    elif op.needs_inc:
                        ins.then_inc(csem[ename], 1)
                if ename == 'sp':
                    for q in ('sp', 'act', 'pool'):
                        last = {}
                        for op in streams[q]:
                            if op.dma:
                                last[id(op.sem)] = (op.sem, op.target)
                        for k, (s, v) in last.items():
                            if known.get(k, 0) < v:
                                eng.wait_ge(s, v)
                                known[k] = v

            @block.tensor
            def _(eng):
                run_stream('pe', eng)

            @block.scalar
            def _(eng):
                run_stream('act', eng)

            @block.vector
            def _(eng):
                run_stream('dve', eng)

            @block.gpsimd
            def _(eng):
                run_stream('pool', eng)

            @block.sync
            def _(eng):
                run_stream('sp', eng)


D = 1024
KC = 8
OQ, OK_, OV, OGA, OXBC, ODT, OZ, OGLU, OGCV, OGM = 0, 512, 640, 768, 1280, 2304, 2320, 2832, 3856, 4368
INCOLS = 7440
EPS = 1e-6
NEG = -30000.0
PP_BMOD, PP_NORMW, PP_QW, PP_KW, PP_SCW, PP_SCB, PP_SNW, PP_CCW, PP_CCB, PP_CLW, PP_CLB, PP_BG = \
    0, 24, 32, 33, 34, 74, 82, 86, 210, 214, 218, 222
NPP = 246
PR_DTB, PR_ALOG, PR_D, PR_SCB = 0, 16, 32, 40
NPR = 40
CF_ID, CF_TRIF, CF_TRIB, CF_STRIF, CF_STRIB, CF_ONES = range(6)
CB_ID, CB_NEGF, CB_NEGB, CB_BLK, CB_RROT, CB_ONES, CB_TRIF, CB_TRIB = range(8)


def host_consts():
    j = np.arange(128)[:, None]
    s = np.arange(128)[None, :]
    cf = np.zeros((128, 6, 128), np.float32)
    cf[:, CF_ID] = (j == s)
    cf[:, CF_TRIF] = (j <= s)
    cf[:, CF_TRIB] = (j >= s)
    cf[:, CF_STRIF] = (j > s)
    cf[:, CF_STRIB] = (j < s)
    cf[:, CF_ONES] = 1.0
    cb = np.zeros((128, 8, 128), np.float32)
    cb[:, CB_ID] = (j == s)
    cb[:, CB_NEGF] = NEG * (s < j)
    cb[:, CB_NEGB] = NEG * (s > j)
    cb[:, CB_BLK] = ((j // 64) == (s // 64))
    rr = np.zeros((128, 128), np.float32)
    for m in range(128):
        if (m % 32) < 16:
            rr[m + 16, m] = -1.0
        else:
            rr[m - 16, m] = 1.0
    cb[:, CB_RROT] = rr
    cb[:, CB_ONES] = 1.0
    cb[:, CB_TRIF] = (j <= s)
    cb[:, CB_TRIB] = (j >= s)
    return cf.reshape(128, 768), cb.reshape(128, 1024)


def host_rope(L, C):
    T = C + L
    t = np.arange(L)
    row = (t // 64).astype(np.float32)
    col = (t % 64).astype(np.float32)
    inv = (1.0 / (10000.0 ** (np.arange(16, dtype=np.float32) / 16))).astype(np.float32)
    cosT = np.ones((128, T), np.float32)
    sinT = np.zeros((128, T), np.float32)
    for p in range(128):
        d = p % 64
        pos = row if d < 32 else col
        ang = (pos * inv[d % 16]).astype(np.float32)
        cosT[p, C:] = np.cos(ang)
        sinT[p, C:] = np.sin(ang)
    return cosT, sinT


def build(L, C, DEPTH, dbg=False, stop_after=None):
    nc = bass.Bass("TRN2", target_bir_lowering=False)
    T = C + L
    NT = T // 128
    NTC = C // 128
    NTL = L // 128
    P = Prog(nc)

    def din(name, shape, dt=F32):
        return nc.dram_tensor(name, shape, dt, kind="ExternalInput").ap()

    def dscr(name, shape, dt=F32):
        return nc.dram_tensor(name, shape, dt, kind=("ExternalOutput" if dbg else "Internal")).ap()

    x_in = din("x", [L, D])
    ctx_in = din("ctx", [C, D])
    cvec_in = din("cvec", [128, 16])
    w_mod = din("w_mod", [DEPTH, D, 3 * D])
    w_in = din("w_in", [DEPTH, D, INCOLS])
    w_bra = din("w_bra", [DEPTH, 512, D])
    w_brs = din("w_brs", [DEPTH, 512, D])
    w_brc = din("w_brc", [DEPTH, 512, D])
    w_out = din("w_out", [DEPTH, D, D])
    pp_in = din("pp", [DEPTH, 128, NPP])
    pr_in = din("pr", [DEPTH, 128, NPR])
    fnw_in = din("fnw", [128, D])
    scb_in = din("scb", [DEPTH, 1, 1024])
    cf_in = din("cf", [128, 768])
    cb_in = din("cb", [128, 1024])
    cos_in = din("cosT", [128, T])
    sin_in = din("sinT", [128, T])
    out = nc.dram_tensor("out", [L, D], F32, kind="ExternalOutput").ap()

    x1 = dscr("x1", [L, D])
    xc1 = dscr("xc1", [C, D])
    uaT = dscr("uaT", [512, T], BF16)
    usT = dscr("usT", [512, T], BF16)
    ucT = dscr("ucT", [512, T], BF16)
    yT_d = dscr("yT", [D, T], BF16)
    xs_d = dscr("xs_tm", [T, 512], BF16)
    b_d = dscr("b_tm", [T, 256], BF16)
    yp_d = dscr("ypart", [T, 512])
    sb_d = dscr("sbst", [NT, 128, 512])

    groups = [(0, 0, C)] + [(1, C + 512 * i, 512) for i in range(L // 512)]

    with ExitStack() as es:
        def sb(name, shape, dt=F32):
            return es.enter_context(nc.sbuf_tensor("s_" + name, shape, dt))

        ARN = 49152
        AR = sb("arena", [128, ARN], BF16)
        aoff = {}

        def ar(phase, shape, dt=BF16):
            n = 1
            for v in shape[1:]:
                n *= v
            nb = n * (2 if dt == BF16 else 4)
            off = (aoff.get(phase, 0) + 3) // 4 * 4
            aoff[phase] = off + nb
            assert aoff[phase] <= ARN * 2, (phase, aoff[phase])
            ap = AR[0:shape[0], off // 2:(off + nb) // 2]
            if dt == F32:
                ap = ap.bitcast(F32)
            if len(shape) == 2:
                return ap
            names = 'abcd'[:len(shape) - 1]
            pat = "p (" + " ".join(names) + ") -> p " + " ".join(names)
            kw = {names[i]: shape[1 + i] for i in range(1, len(names))}
            return ap.rearrange(pat, **kw)

        PP2 = [es.enter_context(nc.psum_tensor("pp%d" % i, [128, 1024], F32)) for i in range(4)]
        PS = []
        for i in range(4):
            PS.append(PP2[i][:, 0:512])
            PS.append(PP2[i][:, 512:1024])
        rPS = RL(8)

        def MM(o, lhsT, rhs, start=True, stop=True, R=(), W=()):
            P.add('pe', lambda e: e.matmul(o, lhsT=lhsT, rhs=rhs, start=start, stop=stop), R, W)

        def TR(o, in_, ident, R=(), W=()):
            P.add('pe', lambda e: e.transpose(out=o, in_=in_, identity=ident), R, W)

        def ACT(o, in_, func, bias=None, scale=None, accum=None, R=(), W=()):
            kw = {}
            if bias is not None:
                kw['bias'] = bias
            if scale is not None:
                kw['scale'] = scale
            if accum is not None:
                kw['accum_out'] = accum
            P.add('act', lambda e: e.activation(out=o, in_=in_, func=func, **kw), R, W)

        def TT(eng, o, a, b, op, R=(), W=()):
            P.add(eng, lambda e: e.tensor_tensor(out=o, in0=a, in1=b, op=op), R, W)

        def TS(eng, o, a, s1, s2, op0, op1=None, R=(), W=()):
            if op1 is None:
                P.add(eng, lambda e: e.tensor_scalar(out=o, in0=a, scalar1=s1, scalar2=None, op0=op0), R, W)
            else:
                P.add(eng, lambda e: e.tensor_scalar(out=o, in0=a, scalar1=s1, scalar2=s2, op0=op0, op1=op1), R, W)

        def STT(eng, o, a, s, b, op0, op1, R=(), W=()):
            P.add(eng, lambda e: e.scalar_tensor_tensor(out=o, in0=a, scalar=s, in1=b, op0=op0, op1=op1), R, W)

        def CP(eng, o, a, R=(), W=()):
            P.add(eng, lambda e: e.tensor_copy(out=o, in_=a), R, W)

        def RCP(o, a, R=(), W=()):
            P.add('dve', lambda e: e.reciprocal(out=o, in_=a), R, W)

        def MSET(eng, o, v, W=()):
            P.add(eng, lambda e: e.memset(o, v), (), W)

        def wsrc(w, l, c0, n):
            return w[l, :, c0:c0 + n].rearrange("(c p) n -> p c n", p=128)

        cf = sb("cf", [128, 6, 128]); r_cf = Res()
        cb = sb("cb", [128, 8, 128], BF16); r_cb = Res()
        pp = sb("pp", [128, DEPTH, NPP]); r_pp = Res()
        pr = sb("pr", [128, DEPTH, NPR]); r_pr = Res()
        cv = sb("cv", [128, 16]); r_cv = Res()
        csl = [[ar('B', [128, 512], F32) for b in range(2)] for a in range(2)]; r_csl = RL(2)
        cslc = [0]
        P.dma('sp', cf[:].rearrange("p a b -> p (a b)"), cf_in[:, :], writes=[r_cf])
        P.dma('pool', cb[:].rearrange("p a b -> p (a b)"), cb_in[:, :], writes=[r_cb])
        P.dma('sp', pp[:], pp_in.rearrange("l p n -> p l n"), writes=[r_pp])
        P.dma('sp', pr[:], pr_in.rearrange("l p n -> p l n"), writes=[r_pr])
        P.dma('sp', cv[:], cvec_in[:, :], writes=[r_cv])
        ones512 = sb("ones512", [1, 512], BF16); r_ones512 = Res()
        MSET('dve', ones512[0:1, :], 1.0, W=[r_ones512])
        identF = cf[:, CF_ID, :]
        identB = cb[:, CB_ID, :]

        hT = sb("hT", [128, KC, T], BF16); r_hT = RL(NT)

        def rh(t0, n):
            return r_hT[t0 // 128:(t0 + n) // 128]

        NW = 4
        wsl = [ar('M', [128, KC, 512], BF16) for i in range(NW)]
        r_wsl = RL(NW)
        wctr = [0]

        def wslot():
            i = wctr[0] % NW
            wctr[0] += 1
            return wsl[i], r_wsl[i]

        NWF = 10
        wf = [sb("wf%d" % i, [128, 512]) for i in range(NWF)]
        r_wf = RL(NWF)
        wfc = [0]

        def wtile():
            i = wfc[0] % NWF
            wfc[0] += 1
            return wf[i], r_wf[i]

        NWB = 8
        wb = [sb("wb%d" % i, [128, 512], BF16) for i in range(NWB)]
        r_wb = RL(NWB)
        wbc = [0]

        def wbtile():
            i = wbc[0] % NWB
            wbc[0] += 1
            return wb[i], r_wb[i]

        psc = [0]

        def psbank(lo=0, hi=8):
            i = lo + psc[0] % (hi - lo)
            psc[0] += 1
            return PS[i], rPS[i]

        modv = sb("modv", [128, 24, 2]); r_modv = Res()
        A1 = sb("A1", [128, 8, 2]); r_A1 = Res()
        scb = sb("scb", [128, 8, 2], BF16); r_scb = Res()

        xtA = [ar('A', [128, D], F32) for i in range(4)]; r_xt = RL(4)
        xnA = [ar('A', [128, D], F32) for i in range(2)]; r_xn = RL(2)
        junk = ar('A', [128, D], F32); r_junk = Res()
        xt = xtA
        xn = xnA
        st4 = [sb("st4_%d" % i, [128, 4]) for i in range(2)]; r_st4 = RL(2)

        def src_tile(l, i):
            if l == 0:
                return (ctx_in if i < NTC else x_in)[((i if i < NTC else i - NTC) * 128):((i if i < NTC else i - NTC) * 128 + 128), :]
            return (xc1 if i < NTC else x1)[((i if i < NTC else i - NTC) * 128):((i if i < NTC else i - NTC) * 128 + 128), :]

        r_x1 = RL(NT)

        def phase_mod(l):
            ACT(scb[:].rearrange("p a b -> p (a b)"), cv[:, :], AF.Silu, R=[r_cv], W=[r_scb])
            pm, rpm = psbank()
            for blk in range(6):
                ws, rws = wslot()
                P.dma('pool', ws[:], wsrc(w_mod, l, blk * 512, 512), writes=[rws])
                for jj in range(4):
                    j = blk * 4 + jj
                    for k in range(KC):
                        MM(pm[:, 2 * j:2 * j + 2], ws[:, k, jj * 128:(jj + 1) * 128], scb[:, k, :],
                           start=(k == 0), stop=(k == KC - 1), R=[rws, r_scb], W=[rpm])
            TT('dve', modv[:], pm[:, 0:48].rearrange("p (j t) -> p j t", t=2),
               pp[:, l, PP_BMOD:PP_BMOD + 24].unsqueeze(2).broadcast_to([128, 24, 2]), ALU.add,
               R=[rpm, r_pp], W=[r_modv])
            STT('dve', A1[:], modv[:, 8:16, :], 1.0,
                pp[:, l, PP_NORMW:PP_NORMW + 8].unsqueeze(2).broadcast_to([128, 8, 2]),
                ALU.add, ALU.mult, R=[r_modv, r_pp], W=[r_A1])

        def phase_hT(l):
            for i in range(NT):
                s = i % 2
                sx = i % 4
                col = 1 if i < NTC else 0
                rsrc = [r_x1[i]] if l > 0 else []
                P.dma('sp', xt[sx][:], src_tile(l, i), reads=rsrc, writes=[r_xt[sx]])
                ACT(junk[:], xt[sx][:], AF.Square, accum=st4[s][:, 0:1], R=[r_xt[sx]], W=[r_junk, r_st4[s]])
                ACT(st4[s][:, 1:2], st4[s][:, 0:1], AF.Sqrt, bias=EPS, scale=1.0 / D, R=[r_st4[s]], W=[r_st4[s]])
                RCP(st4[s][:, 2:3], st4[s][:, 1:2], R=[r_st4[s]], W=[r_st4[s]])
                ACT(xn[s][:], xt[sx][:], AF.Copy, scale=st4[s][:, 2:3], R=[r_xt[sx], r_st4[s]], W=[r_xn[s]])
                for b in range(2):
                    pb, rpb = psbank()
                    for kk in range(4):
                        k = b * 4 + kk
                        TR(pb[:, kk * 128:(kk + 1) * 128], xn[s][:, k * 128:(k + 1) * 128], identF,
                           R=[r_xn[s], r_cf], W=[rpb])
                    tw, rtw = wtile()
                    TT('dve', tw[:].rearrange("p (a b) -> p a b", b=128), pb[:].rearrange("p (a b) -> p a b", b=128),
                       A1[:, 4 * b:4 * b + 4, col:col + 1].broadcast_to([128, 4, 128]), ALU.mult,
                       R=[rpb, r_A1], W=[rtw])
                    TT('pool', hT[:, 4 * b:4 * b + 4, i * 128:(i + 1) * 128], tw[:].rearrange("p (a b) -> p a b", b=128),
                       modv[:, 4 * b:4 * b + 4, col:col + 1].broadcast_to([128, 4, 128]), ALU.add,
                       R=[rtw, r_modv], W=[r_hT[i]])

        kT = ar('B', [128, 2, T]); r_kT = Res()
        Vx = ar('B', [128, NT, 2, 128]); r_Vx = Res()
        wkd = ar('B', [128, KC, 2, 128]); r_wkd = Res()
        wv = ar('B', [128, KC, 128]); r_wv = Res()

        def qk_epilogue(l, pq, rpq, n, t0, wcolidx, o, ro):
            sq, rsq = wbtile()
            ACT(sq[:, 0:n], pq[:, 0:n], AF.Square, R=[rpq], W=[rsq])
            p2, rp2 = psbank()
            MM(p2[:, 0:n], cb[:, CB_BLK, :], sq[:, 0:n], R=[rsq, r_cb], W=[rp2])
            rs, rrs = wtile()
            ACT(rs[:, 0:n], p2[:, 0:n], AF.Ln, bias=EPS, scale=1.0 / 64, R=[rp2], W=[rrs])
            ACT(rs[:, 0:n], rs[:, 0:n], AF.Exp, scale=-0.5, R=[rrs], W=[rrs])
            qw, rqw = wtile()
            ACT(qw[:, 0:n], pq[:, 0:n], AF.Copy, scale=pp[:, l, wcolidx:wcolidx + 1], R=[rpq, r_pp], W=[rqw])
            TT('dve', qw[:, 0:n], qw[:, 0:n], rs[:, 0:n], ALU.mult, R=[rqw, rrs], W=[rqw])
            qb, rqb = wbtile()
            CP('dve', qb[:, 0:n], qw[:, 0:n], R=[rqw], W=[rqb])
            p3, rp3 = psbank()
            MM(p3[:, 0:n], cb[:, CB_RROT, :], qb[:, 0:n], R=[rqb, r_cb], W=[rp3])
            ci = cslc[0] % 2
            cslc[0] += 1
            P.dma('sp', csl[ci][0][:, 0:n], cos_in[:, t0:t0 + n], writes=[r_csl[ci]])
            P.dma('sp', csl[ci][1][:, 0:n], sin_in[:, t0:t0 + n], writes=[r_csl[ci]])
            t1, rt1 = wtile()
            TT('pool', t1[:, 0:n], qw[:, 0:n], csl[ci][0][:, 0:n], ALU.mult, R=[rqw, r_csl[ci]], W=[rt1])
            t2, rt2 = wtile()
            TT('dve', t2[:, 0:n], p3[:, 0:n], csl[ci][1][:, 0:n], ALU.mult, R=[rp3, r_csl[ci]], W=[rt2])
            TT('dve', o, t1[:, 0:n], t2[:, 0:n], ALU.add, R=[rt1, rt2], W=[ro])

        def phase_kv(l):
            MSET('pool', Vx[:, :, :, 64:128], 1.0, W=[r_Vx])
            for g in range(2):
                for h in range(2):
                    P.dma('pool', wkd[:, :, g, h * 64:(h + 1) * 64], wsrc(w_in, l, OK_ + g * 64, 64), writes=[r_wkd])
            P.dma('pool', wv[:], wsrc(w_in, l, OV, 128), writes=[r_wv])
            items = [(gi, g) for gi in range(len(groups)) for g in range(2)]
            for w0 in range(0, len(items), 2):
                wave = [qga_stages(l, 0, gi, None, None, kind='k', g=g) for (gi, g) in items[w0:w0 + 2]]
                for si_ in range(7):
                    for stl in wave:
                        stl[si_]()
            for i0 in range(0, NT, 4):
                nb = min(4, NT - i0)
                pvv, rpv = psbank()
                for ii in range(nb):
                    i = i0 + ii
                    for k in range(KC):
                        MM(pvv[:, ii * 128:(ii + 1) * 128], hT[:, k, i * 128:(i + 1) * 128], wv[:, k, :],
                           start=(k == 0), stop=(k == KC - 1), R=[r_wv, r_hT[i]], W=[rpv])
                CP('dve', Vx[:, i0:i0 + nb, :, 0:64],
                   pvv[:, 0:nb * 128].rearrange("p (a g d) -> p a g d", g=2, d=64), R=[rpv], W=[r_Vx])

        qT = [ar('B', [128, T]) for i in range(2)]; r_qT = RL(2)
        sga = [ar('B', [128, T]) for i in range(2)]; r_sga = RL(2)
        wqg = [ar('B', [128, KC, 2, 128]) for i in range(2)]; r_wqg = RL(2)
        pts = [ar('B', [128, 1024]) for i in range(3)]; r_pts = RL(3)
        r_uaT = RL(4 * len(groups))
        SB_ = None
        OBK = (6, 7)
        attc = [0]

        def attend(l, c, par, q0, nq, key_tiles, gi, hooks=()):
            g = c // 2
            nk = len(key_tiles)
            LOOK = 1
            issued = []
            hooks = list(hooks)
            hstep = max(1, (nk - 2) // max(1, len(hooks))) if hooks else 1

            def issue_s(j):
                a = attc[0]
                attc[0] += 1
                pr_ = a % 2
                pt, rpt = pts[a % 3], r_pts[a % 3]
                for half in range(2):
                    hs = slice(half * 64, half * 64 + 64)
                    MM(PS[2 * pr_ + half][:, 0:nq], kT[hs, g, j * 128:(j + 1) * 128], qT[par][hs, q0:q0 + nq],
                       R=[r_kT, r_qT[par]], W=[rPS[2 * pr_ + half]])
                issued.append((pr_, pt, rpt))

            def issue_exp(n_):
                pr_, pt, rpt = issued[n_]
                if nq == 512:
                    ACT(pt[:, 0:1024], PP2[pr_][:, 0:1024], AF.Exp, scale=0.125, R=[rPS[2 * pr_], rPS[2 * pr_ + 1]], W=[rpt])
                else:
                    for half in range(2):
                        ACT(pt[:, half * 512:half * 512 + nq], PS[2 * pr_ + half][:, 0:nq], AF.Exp, scale=0.125,
                            R=[rPS[2 * pr_ + half]], W=[rpt])

            def finish():
                ua, rua = wtile()
                uab, ruab = wbtile()
                osb = []
                for half in range(2):
                    o_, ro_ = wtile()
                    CP('dve', o_[:, 0:nq], PS[OBK[half]][:, 0:nq], R=[rPS[OBK[half]]], W=[ro_])
                    osb.append((o_, ro_))
                ob, rob = osb[0]
                RCP(ob[64:128, 0:nq], ob[64:128, 0:nq], R=[rob], W=[rob])
                r2, rr2 = wtile()
                P.dma('sp', r2[0:64, 0:nq], ob[64:128, 0:nq], reads=[rob], writes=[rr2])
                TT('dve', ua[0:64, 0:nq], ob[0:64, 0:nq], r2[0:64, 0:nq], ALU.mult, R=[rob, rr2], W=[rua])
                ob1, rob1 = osb[1]
                o2, ro2 = wtile()
                P.dma('sp', o2[64:128, 0:nq], ob1[0:64, 0:nq], reads=[rob1], writes=[ro2])
                RCP(ob1[64:128, 0:nq], ob1[64:128, 0:nq], R=[rob1], W=[rob1])
                TT('dve', ua[64:128, 0:nq], o2[64:128, 0:nq], ob1[64:128, 0:nq], ALU.mult, R=[ro2, rob1], W=[rua])
                TT('pool', uab[:, 0:nq], ua[:, 0:nq], sga[par][:, q0:q0 + nq], ALU.mult,
                   R=[rua, r_sga[par]], W=[ruab])
                P.dma('sp', uaT[c * 128:(c + 1) * 128, q0:q0 + nq], uab[:, 0:nq], reads=[ruab],
                      writes=[r_uaT[c * len(groups) + gi]])

            def issue_pv(n_):
                j = key_tiles[n_]
                P.batch_begin()
                pr_, pt, rpt = issued[n_]
                for half in range(2):
                    MM(PS[OBK[half]][:, 0:nq], Vx[:, j, g, :], pt[:, half * 512:half * 512 + nq], start=(n_ == 0),
                       stop=(n_ == nk - 1), R=[r_Vx, rpt], W=[rPS[OBK[half]]])
                P.batch_end()

            P.batch_begin()
            issue_s(key_tiles[0])
            P.batch_end()
            issue_exp(0)
            for n_ in range(nk):
                if n_ + 1 < nk:
                    P.batch_begin()
                    issue_s(key_tiles[n_ + 1])
                    P.batch_end()
                    issue_exp(n_ + 1)
                if n_ >= 1:
                    issue_pv(n_ - 1)
                if hooks and n_ >= 1 and (n_ - 1) % hstep == 0:
                    hooks.pop(0)()
            issue_pv(nk - 1)
            finish()
            while hooks:
                hooks.pop(0)()

        def qga_stages(l, c, gi, lo, hi, kind='q', g=0):
            par = c % 2
            (sq_, t0, n) = groups[gi]
            st = {}
            isq = (kind == 'q')
            wcol = PP_QW if isq else PP_KW

            def bank(k):
                if lo is None:
                    return psbank()
                return PS[lo + k % (hi - lo)], rPS[lo + k % (hi - lo)]

            def s1():
                st['pq'] = bank(0)
                pq, rpq = st['pq']
                for k in range(KC):
                    if isq:
                        MM(pq[:, 0:n], wqg[par][:, k, 0, :], hT[:, k, t0:t0 + n], start=(k == 0), stop=(k == KC - 1),
                           R=[r_wqg[par]] + rh(t0, n), W=[rpq])
                    else:
                        MM(pq[:, 0:n], wkd[:, k, g, :], hT[:, k, t0:t0 + n], start=(k == 0), stop=(k == KC - 1),
                           R=[r_wkd] + rh(t0, n), W=[rpq])
                ci = cslc[0] % 2
                cslc[0] += 1
                st['ci'] = ci
                P.dma('sp', csl[ci][0][:, 0:n], cos_in[:, t0:t0 + n], writes=[r_csl[ci]])
                P.dma('sp', csl[ci][1][:, 0:n], sin_in[:, t0:t0 + n], writes=[r_csl[ci]])

            def s2():
                pq, rpq = st['pq']
                st['sq'] = wbtile()
                sq, rsq = st['sq']
                ACT(sq[:, 0:n], pq[:, 0:n], AF.Square, R=[rpq], W=[rsq])
                st['qw'] = wtile()
                qw, rqw = st['qw']
                ACT(qw[:, 0:n], pq[:, 0:n], AF.Copy, scale=pp[:, l, wcol:wcol + 1], R=[rpq, r_pp], W=[rqw])

            def s3():
                st['p2'] = bank(1)
                p2, rp2 = st['p2']
                sq, rsq = st['sq']
                MM(p2[:, 0:n], cb[:, CB_BLK, :], sq[:, 0:n], R=[rsq, r_cb], W=[rp2])

            def s4():
                p2, rp2 = st['p2']
                st['rs'] = wtile()
                rs, rrs = st['rs']
                ACT(rs[:, 0:n], p2[:, 0:n], AF.Ln, bias=EPS, scale=1.0 / 64, R=[rp2], W=[rrs])
                ACT(rs[:, 0:n], rs[:, 0:n], AF.Exp, scale=-0.5, R=[rrs], W=[rrs])

            def s5():
                rs, rrs = st['rs']
                qw, rqw = st['qw']
                ci = st['ci']
                TT('dve', qw[:, 0:n], qw[:, 0:n], rs[:, 0:n], ALU.mult, R=[rqw, rrs], W=[rqw])
                st['qb'] = wbtile()
                qb, rqb = st['qb']
                CP('pool', qb[:, 0:n], qw[:, 0:n], R=[rqw], W=[rqb])
                st['t1'] = wtile()
                t1, rt1 = st['t1']
                TT('pool', t1[:, 0:n], qw[:, 0:n], csl[ci][0][:, 0:n], ALU.mult, R=[rqw, r_csl[ci]], W=[rt1])

            def s6():
                st['p3'] = bank(2)
                p3, rp3 = st['p3']
                qb, rqb = st['qb']
                MM(p3[:, 0:n], cb[:, CB_RROT, :], qb[:, 0:n], R=[rqb, r_cb], W=[rp3])
                if not isq:
                    return
                st['pg'] = bank(3)
                pg, rpg = st['pg']
                for k in range(KC):
                    MM(pg[:, 0:n], wqg[par][:, k, 1, :], hT[:, k, t0:t0 + n], start=(k == 0), stop=(k == KC - 1),
                       R=[r_wqg[par]] + rh(t0, n), W=[rpg])

            def s7():
                p3, rp3 = st['p3']
                t1, rt1 = st['t1']
                ci = st['ci']
                t2, rt2 = wtile()
                TT('dve', t2[:, 0:n], p3[:, 0:n], csl[ci][1][:, 0:n], ALU.mult, R=[rp3, r_csl[ci]], W=[rt2])
                if not isq:
                    TT('dve', kT[:, g, t0:t0 + n], t1[:, 0:n], t2[:, 0:n], ALU.add, R=[rt1, rt2], W=[r_kT])
                    return
                pg, rpg = st['pg']
                TT('dve', qT[par][:, t0:t0 + n], t1[:, 0:n], t2[:, 0:n], ALU.add, R=[rt1, rt2], W=[r_qT[par]])
                e_, re_ = wtile()
                ACT(e_[:, 0:n], pg[:, 0:n], AF.Exp, scale=-1.0, R=[rpg], W=[re_])
                g_, rg_ = wtile()
                ACT(g_[:, 0:n], pg[:, 0:n], AF.Copy, R=[rpg], W=[rg_])
                TS('dve', e_[:, 0:n], e_[:, 0:n], 1.0, None, ALU.add, R=[re_], W=[re_])
                RCP(e_[:, 0:n], e_[:, 0:n], R=[re_], W=[re_])
                TT('pool', sga[par][:, t0:t0 + n], g_[:, 0:n], e_[:, 0:n], ALU.mult, R=[rg_, re_], W=[r_sga[par]])

            return [s1, s2, s3, s4, s5, s6, s7]

        def phase_attn(l, last):
            glist = [gi for gi, (sq_, t0, n) in enumerate(groups) if not (last and sq_ == 0)]

            def load_w(c):
                par = c % 2
                P.dma('pool', wqg[par][:, :, 0, :], wsrc(w_in, l, OQ + c * 128, 128), writes=[r_wqg[par]])
                P.dma('pool', wqg[par][:, :, 1, :], wsrc(w_in, l, OGA + c * 128, 128), writes=[r_wqg[par]])

            load_w(0)
            for gi in glist:
                for f_ in qga_stages(l, 0, gi, None, None):
                    f_()
            for c in range(4):
                par = c % 2
                if c + 1 < 4:
                    load_w(c + 1)
                order = [gi for gi in glist if groups[gi][0] == 1] + [gi for gi in glist if groups[gi][0] == 0]
                nxt = list(glist) if c + 1 < 4 else []
                for gi in order:
                    (sq_, t0, n) = groups[gi]
                    hk = qga_stages(l, c + 1, nxt.pop(0), 4, 6) if nxt else []
                    if sq_ == 0:
                        attend(l, c, par, t0, n, list(range(NTC)), gi, hk)
                    else:
                        attend(l, c, par, t0, n, list(range(NT)), gi, hk)

        def qk_epilogue_b(l, pq, rpq, n, t0, wcolidx, o, ro):
            old = psc[0]
            qk_epilogue_r(l, pq, rpq, n, t0, wcolidx, o, ro, 0, 8)

        def qk_epilogue_r(l, pq, rpq, n, t0, wcolidx, o, ro, lo, hi):
            sq, rsq = wbtile()
            ACT(sq[:, 0:n], pq[:, 0:n], AF.Square, R=[rpq], W=[rsq])
            p2, rp2 = psbank(lo, hi)
            MM(p2[:, 0:n], cb[:, CB_BLK, :], sq[:, 0:n], R=[rsq, r_cb], W=[rp2])
            rs, rrs = wtile()
            ACT(rs[:, 0:n], p2[:, 0:n], AF.Sqrt, bias=EPS, scale=1.0 / 64, R=[rp2], W=[rrs])
            RCP(rs[:, 0:n], rs[:, 0:n], R=[rrs], W=[rrs])
            qw, rqw = wtile()
            ACT(qw[:, 0:n], pq[:, 0:n], AF.Copy, scale=pp[:, l, wcolidx:wcolidx + 1], R=[rpq, r_pp], W=[rqw])
            TT('dve', qw[:, 0:n], qw[:, 0:n], rs[:, 0:n], ALU.mult, R=[rqw, rrs], W=[rqw])
            qb, rqb = wbtile()
            CP('pool', qb[:, 0:n], qw[:, 0:n], R=[rqw], W=[rqb])
            p3, rp3 = psbank(lo, hi)
            MM(p3[:, 0:n], cb[:, CB_RROT, :], qb[:, 0:n], R=[rqb, r_cb], W=[rp3])
            ci = cslc[0] % 2
            cslc[0] += 1
            P.dma('sp', csl[ci][0][:, 0:n], cos_in[:, t0:t0 + n], writes=[r_csl[ci]])
            P.dma('sp', csl[ci][1][:, 0:n], sin_in[:, t0:t0 + n], writes=[r_csl[ci]])
            t1, rt1 = wtile()
            TT('pool', t1[:, 0:n], qw[:, 0:n], csl[ci][0][:, 0:n], ALU.mult, R=[rqw, r_csl[ci]], W=[rt1])
            t2, rt2 = wtile()
            TT('dve', t2[:, 0:n], p3[:, 0:n], csl[ci][1][:, 0:n], ALU.mult, R=[rp3, r_csl[ci]], W=[rt2])
            TT('dve', o, t1[:, 0:n], t2[:, 0:n], ALU.add, R=[rt1, rt2], W=[ro])


        BT = ar('C', [128, 2, T]); r_BT = Res()
        CT = ar('C', [128, 2, T]); r_CT = Res()
        pre0 = ar('C', [128, T + 8]); r_pre0 = Res()
        pre = [pre0, pre0]; r_pre = [r_pre0, r_pre0]
        wx = [ar('C', [128, KC, 128]) for i in range(2)]; r_wx = RL(2)
        wdt = ar('C', [128, KC, 16]); r_wdt = Res()
        dgwz = ar('C', [128, 5120])
        dgs = dgwz.rearrange("p (a b c) -> p a b c", a=8, b=5); r_dgs = Res()
        wz = dgwz[:, 0:4096].rearrange("p (a b) -> p a b", a=KC); r_wz = Res()
        dt_all = ar('C', [128, NT, 16], F32); r_dt = Res()
        a_all = ar('C', [128, NT, 16], F32); r_a = Res()
        ahl = [ar('C', [128, NT, 16]) for i in range(4)]
        Eb_all = ar('C', [128, NT, 16], F32); r_Eb = Res()
        scbrow = ar('C', [1, 1024]); r_scbrow = Res()
        Aexp = sb("Aexp", [128, 16]); r_Aexp = Res()
        Hf = ar('C', [128, 512], F32); Hb = ar('C', [128, 512], F32); r_Hf = Res(); r_Hb = Res()
        Hfb = ar('C', [128, 512]); Hbb = ar('C', [128, 512]); r_Hfb = Res(); r_Hbb = Res()
        Et = [sb("Et%d" % i, [128, 80]) for i in range(2)]; r_Et = RL(2)
        ncum = [sb("ncum%d" % i, [128, 32]) for i in range(2)]; r_ncum = RL(2)
        dtd = [sb("dtd%d" % i, [128, 16]) for i in range(2)]; r_dtd = RL(2)
        CBm = [sb("CBm%d" % i, [128, 2, 2, 128]) for i in range(2)]; r_CBm = RL(2)
        xs_t = [ar('C', [128, 512]) for i in range(3)]; r_xs_t = RL(3)
        b_t = [ar('C', [128, 256]) for i in range(3)]; r_b_t = RL(3)
        p1t = [[ar('C', [128, 512]) for b in range(5)] for a in range(2)]
        r_p1t = [RL(5) for a in range(2)]
        r_xs_d = RL(len(groups) * 4); r_b_d = RL(len(groups) * 2)
        r_yp_d = RL(NT); r_sb_d = RL(NT); r_usT = RL(NT)
        def pcol(t):
            return t + 2 if t < C else t + 6

        import os as _os
        C1S = _os.environ.get("C1STEPS", "12345")

        def phase_ssd_c1(l):
            MSET('pool', pre0[:], 0.0, W=[r_pre0])
            for cc in range(8 if '1' in C1S else 0):
                for k in range(5):
                    TS('dve', dgs[:, cc, k, :], identF, pp[:, l, PP_SCW + cc * 5 + k:PP_SCW + cc * 5 + k + 1], None,
                       ALU.mult, R=[r_cf, r_pp], W=[r_dgs])
            if '1' in C1S:
                P.dma('pool', scbrow[0:1, :], scb_in[l], writes=[r_scbrow])
            P.dma('pool', wdt[:], wsrc(w_in, l, ODT, 16), writes=[r_wdt])
            for i0 in range(0, NT if '2' in C1S else 0, 32):
                nb = min(32, NT - i0)
                pd, rpd = psbank()
                for ii in range(nb):
                    i = i0 + ii
                    for k in range(KC):
                        MM(pd[:, ii * 16:(ii + 1) * 16], hT[:, k, i * 128:(i + 1) * 128], wdt[:, k, :],
                           start=(k == 0), stop=(k == KC - 1), R=[r_wdt, r_hT[i]], W=[rpd])
                xb, rxb = wtile()
                ax, rax = wtile()
                n16 = nb * 16
                TT('dve', xb[:, 0:n16].rearrange("p (a b) -> p a b", b=16), pd[:, 0:n16].rearrange("p (a b) -> p a b", b=16),
                   pr[:, l, PR_DTB:PR_DTB + 16].unsqueeze(1).broadcast_to([128, nb, 16]), ALU.add, R=[rpd, r_pr], W=[rxb])
                ACT(ax[:, 0:n16], xb[:, 0:n16], AF.Abs, R=[rxb], W=[rax])
                ACT(ax[:, 0:n16], ax[:, 0:n16], AF.Exp, scale=-1.0, R=[rax], W=[rax])
                ACT(ax[:, 0:n16], ax[:, 0:n16], AF.Ln, bias=1.0, R=[rax], W=[rax])
                TS('dve', xb[:, 0:n16], xb[:, 0:n16], 0.0, None, ALU.max, R=[rxb], W=[rxb])
                TT('dve', dt_all[:, i0:i0 + nb, :], xb[:, 0:n16].rearrange("p (a b) -> p a b", b=16),
                   ax[:, 0:n16].rearrange("p (a b) -> p a b", b=16), ALU.add, R=[rxb, rax], W=[r_dt])
            if '2' in C1S:
                ACT(Aexp[:], pr[:, l, PR_ALOG:PR_ALOG + 16], AF.Exp, R=[r_pr], W=[r_Aexp])
                STT('dve', a_all[:], dt_all[:], -1.0, Aexp[:].unsqueeze(1).broadcast_to([128, NT, 16]),
                    ALU.mult, ALU.mult, R=[r_dt, r_Aexp], W=[r_a])
                CP('dve', ahl[0][:], a_all[:], R=[r_a], W=[r_a])
                TT('dve', ahl[1][:], a_all[:], ahl[0][:], ALU.subtract, R=[r_a], W=[r_a])
                TS('dve', ahl[2][:], ahl[0][:], -1.0, None, ALU.mult, R=[r_a], W=[r_a])
                TS('dve', ahl[3][:], ahl[1][:], -1.0, None, ALU.mult, R=[r_a], W=[r_a])
            for cc in range(8 if '3' in C1S else 0):
                par = cc % 2
                P.dma('pool', wx[par][:], wsrc(w_in, l, OXBC + cc * 128, 128), writes=[r_wx[par]])
                for (sq_, t0, n) in groups:
                    pb, rpb = psbank()
                    for k in range(KC):
                        MM(pb[:, 0:n], wx[par][:, k, :], hT[:, k, t0:t0 + n], start=(k == 0), stop=(k == KC - 1),
                           R=[r_wx[par]] + rh(t0, n), W=[rpb])
                    ACT(pre[par][:, pcol(t0):pcol(t0) + n], pb[:, 0:n], AF.Copy, R=[rpb], W=[r_pre[par]])
                if cc < 6 and '4' in C1S:
                    for gi, (sq_, t0, n) in enumerate(groups):
                        pb, rpb = psbank()
                        nt_ = n // 128
                        for ii in range(nt_):
                            tt0 = t0 + ii * 128
                            for k in range(5):
                                MM(pb[:, ii * 128:(ii + 1) * 128], pre[par][:, pcol(tt0) + k - 2:pcol(tt0) + k - 2 + 128],
                                   dgs[:, cc, k, :], start=(k == 0), stop=False, R=[r_pre[par], r_dgs], W=[rpb])
                            MM(pb[:, ii * 128:(ii + 1) * 128], cb[0:1, CB_ONES, :], scbrow[0:1, cc * 128:(cc + 1) * 128],
                               start=False, stop=True, R=[r_cb, r_scbrow], W=[rpb])
                        stg, rstg = wbtile()
                        ACT(stg[:, 0:n], pb[:, 0:n], AF.Silu, R=[rpb], W=[rstg])
                        if cc < 4:
                            dst = xs_d[t0:t0 + n, cc * 128:(cc + 1) * 128].rearrange("(a p) c -> p a c", p=128)
                            rdst = r_xs_d[gi * 4 + cc]
                        else:
                            dst = b_d[t0:t0 + n, (cc - 4) * 128:(cc - 3) * 128].rearrange("(a p) c -> p a c", p=128)
                            rdst = r_b_d[gi * 2 + cc - 4]
                        P.dma('sp', dst, stg[:, 0:n].rearrange("p (a c) -> p a c", c=128), reads=[rstg], writes=[rdst])
                if cc >= 4 and '5' in C1S:
                    for (sq_, t0, n) in groups:
                        pb, rpb = psbank()
                        for k in range(5):
                            MM(pb[:, 0:n], dgs[:, cc, k, :], pre[par][:, pcol(t0) + k - 2:pcol(t0) + k - 2 + n],
                               start=(k == 0), stop=(k == 4), R=[r_pre[par], r_dgs], W=[rpb])
                        tb, rtb = wtile()
                        ACT(tb[:, 0:n], pb[:, 0:n], AF.Identity, bias=pp[:, l, PP_SCB + cc:PP_SCB + cc + 1],
                            R=[rpb, r_pp], W=[rtb])
                        if cc < 6:
                            ACT(BT[:, cc - 4, t0:t0 + n], tb[:, 0:n], AF.Silu, R=[rtb], W=[r_BT])
                        else:
                            ACT(CT[:, cc - 6, t0:t0 + n], tb[:, 0:n], AF.Silu, R=[rtb], W=[r_CT])

        ssdc = [0]

        def bc8(ap):
            return ap.unsqueeze(2).broadcast_to([128, 8, 64])

        def h3(ap):
            return ap.rearrange("p (h d) -> p h d", d=64)

        p1state = {}

        def ssd_pass1_a(l, i, with_out):
            e = ssdc[0]
            ssdc[0] += 1
            p2 = e % 2
            p3 = e % 3
            p1state[i] = (p2, p3)
            gi = [k for k, (sq_, t0, n) in enumerate(groups) if t0 <= i * 128 < t0 + n][0]
            P.dma('sp', xs_t[p3][:], xs_d[i * 128:(i + 1) * 128, :], reads=r_xs_d[gi * 4:gi * 4 + 4], writes=[r_xs_t[p3]])
            P.dma('sp', b_t[p3][:], b_d[i * 128:(i + 1) * 128, :], reads=r_b_d[gi * 2:gi * 2 + 2], writes=[r_b_t[p3]])
            a_i = a_all[:, i, :]
            pc, rpc = psbank()
            for bi, blk in enumerate((CF_TRIF, CF_TRIB, CF_STRIF, CF_STRIB, CF_ONES)):
                MM(pc[:, bi * 16:(bi + 1) * 16], cf[:, blk, :], a_i, R=[r_cf, r_a], W=[rpc])
            ACT(Et[p2][:], pc[:, 0:80], AF.Exp, R=[rpc], W=[r_Et[p2]])
            CP('pool', Eb_all[:, i, 0:8], Et[p2][:, 24:32], R=[r_Et[p2]], W=[r_Eb])
            CP('pool', Eb_all[:, i, 8:16], Et[p2][:, 72:80], R=[r_Et[p2]], W=[r_Eb])
            TT('pool', dtd[p2][:, 0:8], dt_all[:, i, 0:8], Et[p2][:, 32:40], ALU.mult, R=[r_dt, r_Et[p2]], W=[r_dtd[p2]])
            TT('pool', dtd[p2][:, 8:16], dt_all[:, i, 8:16], Et[p2][:, 56:64], ALU.mult, R=[r_dt, r_Et[p2]], W=[r_dtd[p2]])
            xs3 = h3(xs_t[p3][:])
            xddf, rxddf = p1t[p2][0], r_p1t[p2][0]
            xddb, rxddb = p1t[p2][1], r_p1t[p2][1]
            TT('pool', h3(xddf[:]), xs3, bc8(dtd[p2][:, 0:8]), ALU.mult, R=[r_xs_t[p3], r_dtd[p2]], W=[rxddf])
            TT('dve', h3(xddb[:]), xs3, bc8(dtd[p2][:, 8:16]), ALU.mult, R=[r_xs_t[p3], r_dtd[p2]], W=[rxddb])
            if with_out:
                xdf, rxdf = p1t[p2][2], r_p1t[p2][2]
                xdb, rxdb = p1t[p2][3], r_p1t[p2][3]
                xsD, rxsD = p1t[p2][4], r_p1t[p2][4]
                TT('dve', h3(xdf[:]), xs3, bc8(dt_all[:, i, 0:8]), ALU.mult, R=[r_xs_t[p3], r_dt], W=[rxdf])
                TT('dve', h3(xdb[:]), xs3, bc8(dt_all[:, i, 8:16]), ALU.mult, R=[r_xs_t[p3], r_dt], W=[rxdb])
                TT('pool', h3(xsD[:]), xs3, bc8(pr[:, l, PR_D:PR_D + 8]), ALU.mult, R=[r_xs_t[p3], r_pr], W=[rxsD])
                pcb, rpcb = psbank()
                for g in range(2):
                    MM(pcb[:, g * 128:(g + 1) * 128], BT[:, g, i * 128:(i + 1) * 128], CT[:, g, i * 128:(i + 1) * 128],
                       R=[r_BT, r_CT], W=[rpcb])
                for d_, blk in enumerate((CF_TRIF, CF_TRIB)):
                    TT('dve', CBm[p2][:, d_, :, :], pcb[:, 0:256].rearrange("p (g s) -> p g s", s=128),
                       cf[:, blk, :].unsqueeze(1).broadcast_to([128, 2, 128]), ALU.mult, R=[rpcb, r_cf], W=[r_CBm[p2]])

        def ssd_pass1_b(l, i, with_out):
            p2, p3 = p1state.pop(i)
            xddf, rxddf = p1t[p2][0], r_p1t[p2][0]
            xddb, rxddb = p1t[p2][1], r_p1t[p2][1]
            xdf, rxdf = p1t[p2][2], r_p1t[p2][2]
            xdb, rxdb = p1t[p2][3], r_p1t[p2][3]
            xsD, rxsD = p1t[p2][4], r_p1t[p2][4]
            if with_out:
                pyo, rpyo = psbank()
                P.batch_begin()
                for g in range(2):
                    MM(pyo[:, g * 256:(g + 1) * 256], CT[:, g, i * 128:(i + 1) * 128], Hfb[:, g * 256:(g + 1) * 256],
                       R=[r_CT, r_Hfb], W=[rpyo])
                P.batch_end()
            psf, rpsf = psbank()
            psb_, rpsb = psbank()
            P.batch_begin()
            for g in range(2):
                MM(psf[:, g * 256:(g + 1) * 256], b_t[p3][:, g * 128:(g + 1) * 128], xddf[:, g * 256:(g + 1) * 256],
                   R=[r_b_t[p3], rxddf], W=[rpsf])
            for g in range(2):
                MM(psb_[:, g * 256:(g + 1) * 256], b_t[p3][:, g * 128:(g + 1) * 128], xddb[:, g * 256:(g + 1) * 256],
                   R=[r_b_t[p3], rxddb], W=[rpsb])
            P.batch_end()
            TT('dve', h3(Hf[:]), h3(Hf[:]), bc8(Et[p2][:, 64:72]), ALU.mult, R=[r_Hf, r_Et[p2]], W=[r_Hf])
            TT('dve', Hf[:], Hf[:], psf[:], ALU.add, R=[r_Hf, rpsf], W=[r_Hf])
            ACT(Hfb[:], Hf[:], AF.Copy, R=[r_Hf], W=[r_Hfb])
            sbt, rsbt = wtile()
            ACT(sbt[:], psb_[:], AF.Copy, R=[rpsb], W=[rsbt])
            P.dma('sp', sb_d[i], sbt[:], reads=[rsbt], writes=[r_sb_d[i]])
            if not with_out:
                return
            tmp, rtmp = wtile()
            TT('dve', h3(tmp[:]), h3(pyo[:]), bc8(Et[p2][:, 0:8]), ALU.mult, R=[rpyo, r_Et[p2]], W=[rtmp])
            py, rpy = psbank()
            MM(py[:, :], identB, xsD[:], start=True, stop=False, R=[r_cb, rxsD], W=[rpy])
            steps = [(d_, hq) for d_ in range(2) for hq in range(2)]
            stash = {}

            def x1_step(d_, hq):
                tri = CF_TRIF if d_ == 0 else CF_TRIB
                neg = CB_NEGF if d_ == 0 else CB_NEGB
                px, rpx = psbank()
                P.batch_begin()
                for hh in range(4):
                    h = hq * 4 + hh
                    col = d_ * 8 + h
                    MM(px[:, hh * 128:(hh + 1) * 128], identB, cb[:, neg, :], start=True, stop=False,
                       R=[r_cb], W=[rpx])
                    trib = CB_TRIF if d_ == 0 else CB_TRIB
                    for hl in range(2):
                        MM(px[:, hh * 128:(hh + 1) * 128], ahl[hl][:, i, col:col + 1].broadcast_to([128, 128]),
                           cb[:, trib, :], start=False, stop=False, R=[r_a, r_cb], W=[rpx])
                    for hl in range(2):
                        MM(px[:, hh * 128:(hh + 1) * 128], cb[:, trib, :],
                           ahl[2 + hl][:, i, col:col + 1].broadcast_to([128, 128]), start=False, stop=(hl == 1),
                           R=[r_a, r_cb], W=[rpx])
                P.batch_end()
                lt4, rlt4 = wtile()
                ACT(lt4[:], px[:], AF.Exp, R=[rpx], W=[rlt4])
                mt4, rmt4 = wbtile()
                TT('dve', mt4[:].rearrange("p (a b) -> p a b", b=128), lt4[:].rearrange("p (a b) -> p a b", b=128),
                   CBm[p2][:, d_, hq, :].unsqueeze(1).broadcast_to([128, 4, 128]), ALU.mult,
                   R=[rlt4, r_CBm[p2]], W=[rmt4])
                stash[(d_, hq)] = (mt4, rmt4)

            for st_ in steps:
                x1_step(*st_)
            for si_, (d_, hq) in enumerate(steps):
                xd_, rxd_ = (xdf, rxdf) if d_ == 0 else (xdb, rxdb)
                mt4, rmt4 = stash[(d_, hq)]
                P.batch_begin()
                for hh in range(4):
                    h = hq * 4 + hh
                    MM(py[:, h * 64:(h + 1) * 64], mt4[:, hh * 128:(hh + 1) * 128], xd_[:, h * 64:(h + 1) * 64],
                       start=False, stop=(d_ == 1 and h == 7), R=[rmt4, rxd_], W=[rpy])
                P.batch_end()
            ypt, rypt = wtile()
            TT('dve', ypt[:], tmp[:], py[:], ALU.add, R=[rtmp, rpy], W=[rypt])
            P.dma('sp', yp_d[i * 128:(i + 1) * 128, :], ypt[:], reads=[rypt], writes=[r_yp_d[i]])

        p2state = {}

        def ssd_pass2_a(l, i, with_out):
            sbt, rsbt = wtile()
            P.dma('sp', sbt[:], sb_d[i], reads=[r_sb_d[i]], writes=[rsbt])
            st = {}
            if with_out:
                st['ypt'] = wtile()
                ypt, rypt = st['ypt']
                P.dma('sp', ypt[:], yp_d[i * 128:(i + 1) * 128, :], reads=[r_yp_d[i]], writes=[rypt])
                st['pyo'] = psbank()
                pyo, rpyo = st['pyo']
                P.batch_begin()
                for g in range(2):
                    MM(pyo[:, g * 256:(g + 1) * 256], CT[:, g, i * 128:(i + 1) * 128], Hbb[:, g * 256:(g + 1) * 256],
                       R=[r_CT, r_Hbb], W=[rpyo])
                P.batch_end()
            TT('dve', h3(Hb[:]), h3(Hb[:]), bc8(Eb_all[:, i, 8:16]), ALU.mult, R=[r_Hb, r_Eb], W=[r_Hb])
            TT('dve', Hb[:], Hb[:], sbt[:], ALU.add, R=[r_Hb, rsbt], W=[r_Hb])
            ACT(Hbb[:], Hb[:], AF.Copy, R=[r_Hb], W=[r_Hbb])
            if with_out:
                st['pz'] = psbank()
                pz, rpz = st['pz']
                P.batch_begin()
                for k in range(KC):
                    MM(pz[:, :], hT[:, k, i * 128:(i + 1) * 128], wz[:, k, :], start=(k == 0), stop=(k == KC - 1),
                       R=[r_wz, r_hT[i]], W=[rpz])
                P.batch_end()
                st['ez'] = wtile()
                ez, rez = st['ez']
                ACT(ez[:], pz[:], AF.Silu, R=[rpz], W=[rez])
            p2state[i] = st

        def ssd_pass2_b(l, i, with_out):
            st = p2state.pop(i)
            if not with_out:
                return
            ypt, rypt = st['ypt']
            pyo, rpyo = st['pyo']
            pz, rpz = st['pz']
            ez, rez = st['ez']
            tmp, rtmp = wtile()
            TT('dve', h3(tmp[:]), h3(pyo[:]), bc8(Eb_all[:, i, 0:8]), ALU.mult, R=[rpyo, r_Eb], W=[rtmp])
            TT('pool', ypt[:], ypt[:], tmp[:], ALU.add, R=[rypt, rtmp], W=[rypt])
            sz, rsz = wtile()
            TT('dve', ypt[:], ypt[:], ez[:], ALU.mult, R=[rypt, rez], W=[rypt])
            s4, rs4 = st4[i % 2], r_st4[i % 2]
            ACT(sz[:], ypt[:], AF.Square, accum=s4[:, 0:1], R=[rypt], W=[rsz, rs4])
            ACT(s4[:, 1:2], s4[:, 0:1], AF.Ln, bias=EPS, scale=1.0 / 512, R=[rs4], W=[rs4])
            ACT(s4[:, 2:3], s4[:, 1:2], AF.Exp, scale=-0.5, R=[rs4], W=[rs4])
            ACT(ypt[:], ypt[:], AF.Copy, scale=s4[:, 2:3], R=[rypt, rs4], W=[rypt])
            pt_, rpt_ = psbank()
            for cc in range(4):
                TR(pt_[:, cc * 128:(cc + 1) * 128], ypt[:, cc * 128:(cc + 1) * 128], identF, R=[rypt, r_cf], W=[rpt_])
            ust, rust = wbtile()
            TT('dve', ust[:].rearrange("p (c t) -> p c t", t=128), pt_[:].rearrange("p (c t) -> p c t", t=128),
               pp[:, l, PP_SNW:PP_SNW + 4].unsqueeze(2).broadcast_to([128, 4, 128]), ALU.mult, R=[rpt_, r_pp], W=[rust])
            P.dma('sp', usT.rearrange("(c p) t -> p c t", p=128)[:, :, i * 128:(i + 1) * 128],
                  ust[:].rearrange("p (c t) -> p c t", t=128), reads=[rust], writes=[r_usT[i]])

        def phase_ssd(l, last):
            phase_ssd_c1(l)
            if stop_after == 'ssd_c1':
                return
            P.barrier()
            P.dma('pool', wz[:], wsrc(w_in, l, OZ, 512), writes=[r_wz])
            MSET('dve', Hf[:], 0.0, W=[r_Hf]); MSET('dve', Hb[:], 0.0, W=[r_Hb])
            MSET('pool', Hfb[:], 0.0, W=[r_Hfb]); MSET('pool', Hbb[:], 0.0, W=[r_Hbb])
            for (s0, ntl, wo) in ((0, NTC, not last), (NTC, NTL, True)):
                ssd_pass1_a(l, s0, wo)
                for ii in range(ntl):
                    if ii + 1 < ntl:
                        ssd_pass1_a(l, s0 + ii + 1, wo)
                    ssd_pass1_b(l, s0 + ii, wo)
                if stop_after == 'ssd_p1':
                    continue
                order2 = list(reversed(range(ntl)))
                ssd_pass2_a(l, s0 + order2[0], wo)
                for k_, ii in enumerate(order2):
                    if k_ + 1 < len(order2):
                        ssd_pass2_a(l, s0 + order2[k_ + 1], wo)
                    ssd_pass2_b(l, s0 + ii, wo)


        vT = ar('D', [128, 4, T + 60]); r_vT = Res()
        dgc = ar('D', [128, 4, 31, 128]); r_dgc_d = Res(); r_dgc_a = Res()
        wgcv = ar('D', [128, KC, 512]); r_wgcv = Res()
        wug = [ar('D', [128, KC, 2, 128]) for i in range(2)]; r_wug = RL(2)
        ycsA = [ar('D', [128, 512], F32) for i in range(4)]
        wug_flat = wug[0]
        ycsB = [wug[i // 2].rearrange("p a b c -> p (a b c)")[:, (i % 2) * 1024:(i % 2) * 1024 + 1024].bitcast(F32) for i in range(4)]
        ycs2 = [ycsA, ycsB]; r_ycs2 = [RL(4), RL(4)]
        mD = ar('D', [128, 512], F32); m2D = ar('D', [128, 512], F32); r_mD = Res(); r_m2D = Res()
        r_ucT = RL(4 * len(groups))

        def vcol(t):
            return t + 15 if t < C else t + 45

        def phase_conv(l, last):
            MSET('pool', vT[:], 0.0, W=[r_vT])
            for cc in range(4):
                for k in range(31):
                    if k % 2 == 0:
                        TS('dve', dgc[:, cc, k, :], identF,
                           pp[:, l, PP_CCW + cc * 31 + k:PP_CCW + cc * 31 + k + 1], None, ALU.mult, R=[r_cf, r_pp], W=[r_dgc_d])
                    else:
                        ACT(dgc[:, cc, k, :], identF, AF.Copy, scale=pp[:, l, PP_CCW + cc * 31 + k:PP_CCW + cc * 31 + k + 1],
                            R=[r_cf, r_pp], W=[r_dgc_a])
            P.dma('pool', wgcv[:], wsrc(w_in, l, OGCV, 512), writes=[r_wgcv])
            for cc in range(4):
                par = cc % 2
                P.dma('pool', wug[par][:, :, 0, :], wsrc(w_in, l, OGLU + cc * 128, 128), writes=[r_wug[par]])
                P.dma('pool', wug[par][:, :, 1, :], wsrc(w_in, l, OGLU + 512 + cc * 128, 128), writes=[r_wug[par]])
                for (sq_, t0, n) in groups:
                    if last and sq_ == 0:
                        continue
                    pu, rpu = psbank()
                    pg, rpg = psbank()
                    for k in range(KC):
                        MM(pu[:, 0:n], wug[par][:, k, 0, :], hT[:, k, t0:t0 + n], start=(k == 0), stop=(k == KC - 1),
                           R=[r_wug[par]] + rh(t0, n), W=[rpu])
                    for k in range(KC):
                        MM(pg[:, 0:n], wug[par][:, k, 1, :], hT[:, k, t0:t0 + n], start=(k == 0), stop=(k == KC - 1),
                           R=[r_wug[par]] + rh(t0, n), W=[rpg])
                    sg, rsg = wtile()
                    ACT(sg[:, 0:n], pg[:, 0:n], AF.Sigmoid, R=[rpg], W=[rsg])
                    TT('dve', vT[:, cc, vcol(t0):vcol(t0) + n], pu[:, 0:n], sg[:, 0:n], ALU.mult, R=[rpu, rsg], W=[r_vT])
            P.barrier()
            for gi, (sq_, t0, n) in enumerate(groups):
                if last and sq_ == 0:
                    continue
                ycs, r_ycs = ycs2[gi % 2], r_ycs2[gi % 2]
                psm, rpsm = psbank()
                psq, rpsq = psbank()
                pcvs = {}

                def conv_mm(cc):
                    pcvs[cc] = psbank()
                    pcv, rpcv = pcvs[cc]
                    P.batch_begin()
                    for k in range(31):
                        MM(pcv[:, 0:n], dgc[:, cc, k, :], vT[:, cc, vcol(t0) + k - 15:vcol(t0) + k - 15 + n],
                           start=(k == 0), stop=(k == 30), R=[r_dgc_d, r_dgc_a, r_vT], W=[rpcv])
                    P.batch_end()

                conv_mm(0)
                for cc in range(4):
                    if cc + 1 < 4:
                        conv_mm(cc + 1)
                    pcv, rpcv = pcvs[cc]
                    ACT(ycs[cc][:, 0:n], pcv[:, 0:n], AF.Identity, bias=pp[:, l, PP_CCB + cc:PP_CCB + cc + 1],
                        R=[rpcv, r_pp], W=[r_ycs[cc]])
                    ysq, rysq = wtile()
                    ACT(ysq[:, 0:n], ycs[cc][:, 0:n], AF.Square, R=[r_ycs[cc]], W=[rysq])
                    MM(psm[:, 0:n], cf[:, CF_ONES, :], ycs[cc][:, 0:n], start=(cc == 0), stop=(cc == 3),
                       R=[r_cf, r_ycs[cc]], W=[rpsm])
                    MM(psq[:, 0:n], cf[:, CF_ONES, :], ysq[:, 0:n], start=(cc == 0), stop=(cc == 3),
                       R=[r_cf, rysq], W=[rpsq])
                m, rm = mD, r_mD
                ACT(m[:, 0:n], psm[:, 0:n], AF.Copy, scale=1.0 / 512, R=[rpsm], W=[rm])
                m2, rm2 = m2D, r_m2D
                TT('pool', m2[:, 0:n], m[:, 0:n], m[:, 0:n], ALU.mult, R=[rm], W=[rm2])
                STT('dve', m2[:, 0:n], psq[:, 0:n], 1.0 / 512, m2[:, 0:n], ALU.mult, ALU.subtract, R=[rpsq, rm2], W=[rm2])
                ACT(m2[:, 0:n], m2[:, 0:n], AF.Sqrt, bias=EPS, R=[rm2], W=[rm2])
                RCP(m2[:, 0:n], m2[:, 0:n], R=[rm2], W=[rm2])
                for cc in range(4):
                    t_, rt_ = wtile()
                    TT('pool', t_[:, 0:n], ycs[cc][:, 0:n], m[:, 0:n], ALU.subtract, R=[r_ycs[cc], rm], W=[rt_])
                    TT('dve', t_[:, 0:n], t_[:, 0:n], m2[:, 0:n], ALU.mult, R=[rt_, rm2], W=[rt_])
                    ACT(t_[:, 0:n], t_[:, 0:n], AF.Identity, scale=pp[:, l, PP_CLW + cc:PP_CLW + cc + 1],
                        bias=pp[:, l, PP_CLB + cc:PP_CLB + cc + 1], R=[rt_, r_pp], W=[rt_])
                    ACT(t_[:, 0:n], t_[:, 0:n], AF.Silu, R=[rt_], W=[rt_])
                    pgc, rpgc = psbank()
                    for k in range(KC):
                        MM(pgc[:, 0:n], wgcv[:, k, cc * 128:(cc + 1) * 128], hT[:, k, t0:t0 + n], start=(k == 0),
                           stop=(k == KC - 1), R=[r_wgcv] + rh(t0, n), W=[rpgc])
                    sgc, rsgc = wtile()
                    ACT(sgc[:, 0:n], pgc[:, 0:n], AF.Silu, R=[rpgc], W=[rsgc])
                    ucb, rucb = wbtile()
                    TT('dve', ucb[:, 0:n], t_[:, 0:n], sgc[:, 0:n], ALU.mult, R=[rt_, rsgc], W=[rucb])
                    P.dma('sp', ucT[cc * 128:(cc + 1) * 128, t0:t0 + n], ucb[:, 0:n], reads=[rucb],
                          writes=[r_ucT[cc * len(groups) + gi]])

        wgm = ar('E', [128, KC, 3072]); r_wgm6 = RL(6)
        wbr = ar('E', [128, 4, 3, D]); r_wbr6 = RL(6)
        ut = [ar('E', [128, 4, 512]) for i in range(3)]; r_ut = RL(3)
        wo = ar('F', [128, KC, D]); r_wo = Res()
        yTt = ar('F', [128, KC, 512]); r_yTt = Res()
        zT = ar('F', [128, KC, 512], F32); r_zT = Res()
        xo = [ar('F', [128, D], F32) for i in range(2)]; r_xo = RL(2)
        fnw = ar('F', [128, D], F32); r_fnw = Res()
        xtF = [ar('F', [128, D], F32) for i in range(4)]
        xnF = [ar('F', [128, D], F32) for i in range(2)]
        r_yT_d = RL(len(groups))

        def phase_merge(l, last):
            wsrcs = (w_bra, w_brs, w_brc)
            for hf in range(2):
                for bi in range(3):
                    b6 = bi * 2 + hf
                    P.dma('pool', wgm[:, :, b6 * 512:(b6 + 1) * 512], wsrc(w_in, l, OGM + b6 * 512, 512), writes=[r_wgm6[b6]])
                    P.dma('pool', wbr[:, :, bi, hf * 512:(hf + 1) * 512],
                          wsrcs[bi][l, :, hf * 512:(hf + 1) * 512].rearrange("(c p) n -> p c n", p=128), writes=[r_wbr6[b6]])
            usrc = (uaT, usT, ucT)
            for gi, (sq_, t0, n) in enumerate(groups):
                if last and sq_ == 0:
                    continue
                for bi in range(3):
                    if bi == 0:
                        rr = [r_uaT[c * len(groups) + gi] for c in range(4)]
                    elif bi == 1:
                        rr = r_usT[t0 // 128:(t0 + n) // 128]
                    else:
                        rr = [r_ucT[c * len(groups) + gi] for c in range(4)]
                    P.dma('sp', ut[bi][:, :, 0:n], usrc[bi].rearrange("(c p) t -> p c t", p=128)[:, :, t0:t0 + n],
                          reads=rr, writes=[r_ut[bi]])
                for j in range(8):
                    acc, racc = wtile()
                    for bi in range(3):
                        pg, rpg = psbank()
                        for k in range(KC):
                            MM(pg[:, 0:n], wgm[:, k, bi * 1024 + j * 128:bi * 1024 + (j + 1) * 128], hT[:, k, t0:t0 + n],
                               start=(k == 0), stop=(k == KC - 1), R=[r_wgm6[bi * 2 + j // 4]] + rh(t0, n), W=[rpg])
                        gt, rgt = wtile()
                        ACT(gt[:, 0:n], pg[:, 0:n], AF.Identity, bias=pp[:, l, PP_BG + bi * 8 + j:PP_BG + bi * 8 + j + 1],
                            R=[rpg, r_pp], W=[rgt])
                        ACT(gt[:, 0:n], gt[:, 0:n], AF.Sigmoid, R=[rgt], W=[rgt])
                        pb, rpb = psbank()
                        for k4 in range(4):
                            MM(pb[:, 0:n], wbr[:, k4, bi, j * 128:(j + 1) * 128], ut[bi][:, k4, 0:n], start=(k4 == 0),
                               stop=(k4 == 3), R=[r_wbr6[bi * 2 + j // 4], r_ut[bi]], W=[rpb])
                        if bi == 0:
                            TT('dve', acc[:, 0:n], pb[:, 0:n], gt[:, 0:n], ALU.mult, R=[rpb, rgt], W=[racc])
                        else:
                            TT('dve', gt[:, 0:n], pb[:, 0:n], gt[:, 0:n], ALU.mult, R=[rpb, rgt], W=[rgt])
                            TT('pool', acc[:, 0:n], acc[:, 0:n], gt[:, 0:n], ALU.add, R=[racc, rgt], W=[racc])
                    yb, ryb = wbtile()
                    CP('pool', yb[:, 0:n], acc[:, 0:n], R=[racc], W=[ryb])
                    P.dma('sp', yT_d[j * 128:(j + 1) * 128, t0:t0 + n], yb[:, 0:n], reads=[ryb], writes=[r_yT_d[gi]])

        def phase_out(l, last):
            xt, xn = xtF, xnF
            P.dma('sp', fnw[:], fnw_in[:, :], writes=[r_fnw])
            for hf in range(2):
                P.dma('pool', wo[:, :, hf * 512:(hf + 1) * 512], wsrc(w_out, l, hf * 512, 512), writes=[r_wo])
            for gi, (sq_, t0, n) in enumerate(groups):
                if last and sq_ == 0:
                    continue
                col = 1 if sq_ == 0 else 0
                P.dma('sp', yTt[:, :, 0:n], yT_d.rearrange("(c p) t -> p c t", p=128)[:, :, t0:t0 + n],
                      reads=[r_yT_d[gi]], writes=[r_yTt])
                for j2 in range(8):
                    po, rpo = psbank()
                    for j in range(8):
                        MM(po[:, 0:n], wo[:, j, j2 * 128:(j2 + 1) * 128], yTt[:, j, 0:n], start=(j == 0), stop=(j == 7),
                           R=[r_wo, r_yTt], W=[rpo])
                    ACT(zT[:, j2, 0:n], po[:, 0:n], AF.Copy, scale=modv[:, 16 + j2, col:col + 1], R=[rpo, r_modv], W=[r_zT])
                for si in range(n // 128):
                    i = t0 // 128 + si
                    s = i % 2
                    sx = i % 4
                    rsrc = [r_x1[i]] if l > 0 else []
                    P.dma('sp', xt[sx][:], src_tile(l, i), reads=rsrc, writes=[r_xt[sx]])
                    for b in range(2):
                        pb, rpb = psbank()
                        for kk in range(4):
                            j2 = b * 4 + kk
                            TR(pb[:, kk * 128:(kk + 1) * 128], zT[:, j2, si * 128:(si + 1) * 128], identF,
                               R=[r_zT, r_cf], W=[rpb])
                        TT('dve', xo[s][:, b * 512:(b + 1) * 512], pb[:, :], xt[sx][:, b * 512:(b + 1) * 512], ALU.add,
                           R=[rpb, r_xt[sx]], W=[r_xo[s]])
                    if not last:
                        dst = (xc1 if i < NTC else x1)[((i if i < NTC else i - NTC) * 128):((i if i < NTC else i - NTC) * 128 + 128), :]
                        P.dma('pool', dst, xo[s][:], reads=[r_xo[s]], writes=[r_x1[i]])
                    else:
                        s4, rs4 = st4[s], r_st4[s]
                        ACT(xn[s][:], xo[s][:], AF.Square, accum=s4[:, 0:1], R=[r_xo[s]], W=[r_xn[s], rs4])
                        ACT(s4[:, 1:2], s4[:, 0:1], AF.Sqrt, bias=EPS, scale=1.0 / D, R=[rs4], W=[rs4])
                        RCP(s4[:, 2:3], s4[:, 1:2], R=[rs4], W=[rs4])
                        ACT(xn[s][:], xo[s][:], AF.Copy, scale=s4[:, 2:3], R=[r_xo[s], rs4], W=[r_xn[s]])
                        TT('pool', xn[s][:], xn[s][:], fnw[:], ALU.mult, R=[r_xn[s], r_fnw], W=[r_xn[s]])
                        P.dma('pool', out[(i - NTC) * 128:(i - NTC + 1) * 128, :], xn[s][:], reads=[r_xn[s]])

        for l in range(DEPTH):
            last = (l == DEPTH - 1)
            P.barrier()
            phase_mod(l)
            P.barrier()
            phase_hT(l)
            P.barrier()
            if stop_after == 'hT':
                break
            phase_kv(l)
            phase_attn(l, last)
            if stop_after == 'attn':
                break
            P.barrier()
            phase_ssd(l, last)
            if stop_after in ('ssd', 'ssd_c1', 'ssd_p1'):
                break
            P.barrier()
            phase_conv(l, last)
            if stop_after == 'conv':
                break
            P.barrier()
            phase_merge(l, last)
            P.barrier()
            phase_out(l, last)
        if dbg:
            hT_o = nc.dram_tensor("hT_o", [128, KC, T], BF16, kind="ExternalOutput").ap()
            P.dma('sp', hT_o[:, :, :], hT[:], reads=r_hT)
        P.emit()
    return nc


def _cols(v):
    v = np.asarray(v, np.float32)
    return np.ascontiguousarray(v.reshape(-1, 128).T)


def host_prep(inp, L, C, DEPTH, nb):
    f32 = np.float32
    cfc, cbc = host_consts()
    cosT, sinT = host_rope(L, C)
    pp = np.zeros((DEPTH, 128, NPP), f32)
    pr = np.zeros((DEPTH, 128, NPR), f32)
    for l in range(DEPTH):
        pp[l, :, PP_BMOD:PP_BMOD + 24] = _cols(inp['b_mod'][l])
        pp[l, :, PP_NORMW:PP_NORMW + 8] = _cols(inp['norm_w'][l])
        pp[l, :, PP_QW] = np.tile(np.asarray(inp['q_norm_w'][l], f32), 2)
        pp[l, :, PP_KW] = np.tile(np.asarray(inp['k_norm_w'][l], f32), 2)
        scw = np.asarray(inp['ssd_conv_w'][l], f32)
        pp[l, :, PP_SCW:PP_SCW + 40] = scw.reshape(5, 8, 128).transpose(2, 1, 0).reshape(128, 40)
        pp[l, :, PP_SCB:PP_SCB + 8] = _cols(inp['ssd_conv_b'][l])
        pp[l, :, PP_SNW:PP_SNW + 4] = _cols(inp['ssd_norm_w'][l])
        ccw = np.asarray(inp['cm_conv_w'][l], f32)
        pp[l, :, PP_CCW:PP_CCW + 124] = ccw.reshape(31, 4, 128).transpose(2, 1, 0).reshape(128, 124)
        pp[l, :, PP_CCB:PP_CCB + 4] = _cols(inp['cm_conv_b'][l])
        pp[l, :, PP_CLW:PP_CLW + 4] = _cols(inp['cm_ln_w'][l])
        pp[l, :, PP_CLB:PP_CLB + 4] = _cols(inp['cm_ln_b'][l])
        pp[l, :, PP_BG:PP_BG + 24] = _cols(np.asarray(inp['b_gate'][l], f32).reshape(-1))
        pr[l, :, PR_DTB:PR_DTB + 16] = np.asarray(inp['ssd_dt_bias'][l], f32).reshape(1, 16)
        pr[l, :, PR_ALOG:PR_ALOG + 16] = np.asarray(inp['ssd_A_log'][l], f32).reshape(1, 16)
        pr[l, :, PR_D:PR_D + 8] = np.asarray(inp['ssd_D'][l], f32).reshape(1, 8)
    fnw = np.ascontiguousarray(np.broadcast_to(np.asarray(inp['final_norm_w'], f32)[None, :], (128, D)))
    shared = {
        'w_mod': np.ascontiguousarray(inp['w_mod'], f32), 'w_in': np.ascontiguousarray(inp['w_in'], f32),
        'w_bra': np.ascontiguousarray(inp['w_br_attn'], f32), 'w_brs': np.ascontiguousarray(inp['w_br_ssd'], f32),
        'w_brc': np.ascontiguousarray(inp['w_br_conv'], f32), 'w_out': np.ascontiguousarray(inp['w_out'], f32),
        'pp': pp, 'pr': pr, 'scb': np.ascontiguousarray(np.asarray(inp['ssd_conv_b'], f32).reshape(DEPTH, 1, 1024)), 'fnw': fnw, 'cf': cfc, 'cb': cbc, 'cosT': cosT, 'sinT': sinT,
    }
    maps = []
    cctx = _cols(inp['c_ctx'])
    for b in range(nb):
        cvec = np.zeros((128, 8, 2), f32)
        cvec[:, :, 0] = _cols(inp['c'][b])
        cvec[:, :, 1] = cctx
        m = dict(shared)
        m['x'] = np.ascontiguousarray(inp['x'][b], f32)
        m['ctx'] = np.ascontiguousarray(inp['ctx'][b], f32)
        m['cvec'] = cvec.reshape(128, 16)
        maps.append(m)
    return maps


_NC_CACHE = {}


def kernel(**inputs):
    x = np.asarray(inputs['x'])
    B, L, _ = x.shape
    C = np.asarray(inputs['ctx']).shape[1]
    DEPTH = np.asarray(inputs['w_in']).shape[0]
    key = (L, C, DEPTH)
    if key not in _NC_CACHE:
        _NC_CACHE[key] = build(L, C, DEPTH)
    nc = _NC_CACHE[key]
    maps = host_prep(inputs, L, C, DEPTH, B)
    res = run_bass_kernel_spmd(nc, maps, core_ids=list(range(B)))
    return np.stack([np.asarray(r['out'], np.float32) for r in res.results], axis=0)
```
